# Optimizing a Trainium2 kernel written in Bass

```python
import math
import jax
import jax.numpy as jnp
from jax import lax
import numpy as np

D_MODEL = 1024
BATCH = 2
SEQ = 8192
DEPTH = 4

GRID_W = 64
CTX_LEN = 256
N_MIXERS = 3
N_A = (DEPTH + 2) // N_MIXERS
N_B = (DEPTH + 1) // N_MIXERS
N_C = DEPTH // N_MIXERS
NORM_EPS = 1e-6

RW_HEAD = 64
RW_HEADS = D_MODEL // RW_HEAD
RW_DECAY_LORA = 64
RW_AAA_LORA = 64
RW_GATE_LORA = 160
RW_LNX_EPS = 6.4e-4

ML_HEADS = 4
ML_QK = D_MODEL // (2 * ML_HEADS)
ML_V = D_MODEL // ML_HEADS
ML_CHUNK = 128
ML_IN = 2 * ML_HEADS * ML_QK + ML_HEADS * ML_V + D_MODEL + 4 * ML_HEADS

DA_HEADS = 8
DA_HEAD = D_MODEL // (2 * DA_HEADS)
DA_QBLOCK = 128
ROPE_BASE = 10000.0

FFN_HIDDEN = 2816
CONV_W = 3

kernel_name = "hybrid_rwkv7_mlstm_diffattn_adaln"


def _prev(x):
    return jnp.pad(x, ((0, 0), (1, 0), (0, 0)))[:, :-1]


def _next(x):
    return jnp.pad(x, ((0, 0), (0, 1), (0, 0)))[:, 1:]


def rms_norm(x, g):
    xf = x.astype(jnp.float32)
    y = xf * lax.rsqrt(jnp.mean(xf * xf, axis=-1, keepdims=True) + NORM_EPS)
    return (y * g.astype(jnp.float32)).astype(x.dtype)


def conv_ffn(h, w_in, conv_w, conv_b, w_out):
    val, gate = jnp.split(h @ w_in, 2, axis=-1)
    gate = conv_w[0] * _prev(gate) + conv_w[1] * gate + conv_w[2] * _next(gate) + conv_b
    return (jax.nn.gelu(gate) * val) @ w_out


def rwkv7_scan(r, w, k, v, kk, a, state, reverse):
    def step(S, inp):
        r_t, w_t, k_t, v_t, kk_t, a_t = inp
        sa = jnp.einsum('bhvk,bhk->bhv', S, kk_t)
        S = (S * w_t[:, :, None, :] - sa[..., None] * (kk_t * a_t)[:, :, None, :]
             + v_t[..., None] * k_t[:, :, None, :])
        return S, jnp.einsum('bhvk,bhk->bhv', S, r_t)
    xs = tuple(jnp.moveaxis(t, 1, 0) for t in (r, w, k, v, kk, a))
    S, ys = lax.scan(step, state, xs, reverse=reverse)
    return jnp.moveaxis(ys, 0, 1), S


def rwkv7_mixer(h_lat, h_ctx, mix, w_rkv, w0, w1, w2, a0, a1, a2, g1, g2,
                k_k, k_a, r_k, lnx_g, lnx_b, w_out, with_ctx):
    H, N = RW_HEADS, RW_HEAD
    f32 = jnp.float32

    def lora(xin, p1, p2, act):
        return jnp.einsum('zbnl,zld->zbnd', act(jnp.einsum('bnd,zdl->zbnl', xin, p1)), p2)

    def prep(h):
        B, n, D = h.shape
        xx = 0.5 * (_prev(h) + _next(h)) - h
        xs = h[:, :, None, :] + xx[:, :, None, :] * mix
        rkv = jnp.einsum('bnjd,jde->bnje', xs[:, :, :3], w_rkv)
        r, k, v = rkv[:, :, 0], rkv[:, :, 1], rkv[:, :, 2]
        w_pre = (w0[:, None, None, :] + lora(xs[:, :, 3], w1, w2, jnp.tanh)).astype(f32)
        decay = jnp.exp(-jnp.exp(-jax.nn.softplus(-w_pre) - 0.5))
        a = jax.nn.sigmoid((a0[:, None, None, :]
                            + lora(xs[:, :, 4], a1, a2, lambda t: t)).astype(f32))
        g = jax.nn.sigmoid(xs[:, :, 5] @ g1) @ g2
        kk = (k * k_k).astype(f32).reshape(B, n, H, N)
        kk = kk / jnp.maximum(jnp.linalg.norm(kk, axis=-1, keepdims=True), 1e-12)
        kd = k.astype(f32) * (1.0 + (a - 1.0) * k_a.astype(f32))
        split = lambda t: t.reshape(t.shape[:-1] + (H, N))
        return (split(r.astype(f32)), split(decay), split(kd), split(v.astype(f32)),
                kk, split(a), g)

    def post(y, pr, h):
        r, _, kd, v, _, _, g = pr
        B, n, D = h.shape
        mu = jnp.mean(y, axis=-1, keepdims=True)
        var = jnp.mean(jnp.square(y - mu), axis=-1, keepdims=True)
        y = ((y - mu) * lax.rsqrt(var + RW_LNX_EPS)).reshape(B, n, D) * lnx_g + lnx_b
        bonus = jnp.sum(jnp.sum(r * kd * r_k, axis=-1, keepdims=True) * v, axis=0)
        return ((y + bonus.reshape(B, n, D)) * g).astype(h.dtype) @ w_out

    lat, ctx = prep(h_lat), prep(h_ctx)
    state0 = jnp.zeros((h_lat.shape[0], H, N, N), f32)
    ys_lat, ys_ctx = [], []
    for d, rev in ((0, False), (1, True)):
        yc, s_ctx = rwkv7_scan(ctx[0], ctx[1][d], ctx[2][d], ctx[3], ctx[4], ctx[5][d], state0, rev)
        yl, _ = rwkv7_scan(lat[0], lat[1][d], lat[2][d], lat[3], lat[4], lat[5][d], s_ctx, rev)
        ys_lat.append(yl)
        ys_ctx.append(yc)
    out_lat = post(ys_lat[0] + ys_lat[1], lat, h_lat)
    out_ctx = post(ys_ctx[0] + ys_ctx[1], ctx, h_ctx) if with_ctx else None
    return out_lat, out_ctx


def mlstm_chunked(q, k, v, li, lf, state):
    B, H, L, dk = q.shape
    dv = v.shape[-1]
    T = ML_CHUNK
    nc = L // T
    qc = q.reshape(B, H, nc, T, dk)
    kc = k.reshape(B, H, nc, T, dk)
    vc = v.reshape(B, H, nc, T, dv)
    lic = li.reshape(B, H, nc, T)
    b = jnp.cumsum(lf.reshape(B, H, nc, T), axis=-1)
    b_end = b[..., -1]
    g_end = b_end[..., None] - b + lic
    m_loc = jnp.max(g_end, axis=-1)
    w_end = jnp.exp(g_end - m_loc[..., None])
    C_loc = jnp.einsum('bhcsk,bhcsv->bhckv', kc * w_end[..., None], vc)
    n_loc = jnp.einsum('bhcs,bhcsk->bhck', w_end, kc)

    def step(carry, inp):
        C, n, m = carry
        Cl, nl, ml, be = inp
        m_new = jnp.maximum(be + m, ml)
        a = jnp.exp(be + m - m_new)
        e = jnp.exp(ml - m_new)
        new = (a[..., None, None] * C + e[..., None, None] * Cl,
               a[..., None] * n + e[..., None] * nl, m_new)
        return new, (C, n, m)

    xs = tuple(jnp.moveaxis(t, 2, 0) for t in (C_loc, n_loc, m_loc, b_end))
    final, starts = lax.scan(step, state, xs)
    C0, n0, m0 = (jnp.moveaxis(t, 0, 2) for t in starts)
    tri = jnp.tril(jnp.ones((T, T), dtype=bool))
    d_log = jnp.where(tri, b[..., :, None] - b[..., None, :] + lic[..., None, :], -jnp.inf)
    inter = b + m0[..., None]
    m_t = jnp.maximum(jnp.max(d_log, axis=-1), inter)
    p = jnp.exp(d_log - m_t[..., None]) * jnp.einsum('bhctk,bhcsk->bhcts', qc, kc)
    e_inter = jnp.exp(inter - m_t)
    num = (jnp.einsum('bhcts,bhcsv->bhctv', p, vc)
           + e_inter[..., None] * jnp.einsum('bhctk,bhckv->bhctv', qc, C0))
    den = jnp.sum(p, axis=-1) + e_inter * jnp.einsum('bhctk,bhck->bhct', qc, n0)
    h = num / jnp.maximum(jnp.abs(den), jnp.exp(-m_t))[..., None]
    return h.reshape(B, H, L, dv), final


def mlstm_mixer(h_lat, h_ctx, w_in, b_in, norm_g, w_out, with_ctx):
    H, dk, dv = ML_HEADS, ML_QK, ML_V
    f32 = jnp.float32
    cuts = [H * dk, 2 * H * dk, 2 * H * dk + H * dv, 2 * H * dk + H * dv + D_MODEL]

    def prep(h):
        B, n, _ = h.shape
        q, k, v, o, gt = jnp.split(h @ w_in + b_in, cuts, axis=-1)
        heads = lambda t, e: jnp.transpose(t.reshape(B, n, H, e), (0, 2, 1, 3)).astype(f32)
        gt = jnp.transpose(gt.astype(f32).reshape(B, n, 2, 2, H), (2, 3, 0, 4, 1))
        return (heads(q, dk), heads(k, dk) * (dk ** -0.5), heads(v, dv),
                jax.nn.sigmoid(o), gt[:, 0], jax.nn.log_sigmoid(gt[:, 1]))

    def post(hsum, o):
        B, _, n, _ = hsum.shape
        hn = hsum * lax.rsqrt(jnp.mean(hsum * hsum, axis=-1, keepdims=True) + NORM_EPS)
        hn = jnp.transpose(hn, (0, 2, 1, 3)).reshape(B, n, D_MODEL)
        return (hn * norm_g * o).astype(o.dtype) @ w_out

    lat, ctx = prep(h_lat), prep(h_ctx)
    B = h_lat.shape[0]
    state0 = (jnp.zeros((B, H, dk, dv), f32), jnp.zeros((B, H, dk), f32), jnp.zeros((B, H), f32))
    hs_lat, hs_ctx = [], []
    for d in range(2):
        fl = (lambda t: t) if d == 0 else (lambda t: jnp.flip(t, axis=2))
        hc, s_ctx = mlstm_chunked(fl(ctx[0]), fl(ctx[1]), fl(ctx[2]), fl(ctx[4][d]), fl(ctx[5][d]), state0)
        hl, _ = mlstm_chunked(fl(lat[0]), fl(lat[1]), fl(lat[2]), fl(lat[4][d]), fl(lat[5][d]), s_ctx)
        hs_lat.append(fl(hl))
        hs_ctx.append(fl(hc))
    out_lat = post(hs_lat[0] + hs_lat[1], lat[3])
    out_ctx = post(hs_ctx[0] + hs_ctx[1], ctx[3]) if with_ctx else None
    return out_lat, out_ctx


def axial_rope(n, dim):
    rows = n // GRID_W
    row = jnp.repeat(jnp.arange(rows), GRID_W).astype(jnp.float32)
    col = jnp.tile(jnp.arange(GRID_W), rows).astype(jnp.float32)
    nf = dim // 4
    inv = jnp.power(ROPE_BASE, -jnp.arange(nf, dtype=jnp.float32) / nf)
    ang = jnp.concatenate([row[:, None] * inv, col[:, None] * inv], axis=-1)
    return jnp.cos(ang), jnp.sin(ang)


def apply_rope(x, cos, sin):
    half = x.shape[-1] // 2
    x1, x2 = x[..., :half], x[..., half:]
    cos = cos[None, :, None, None, :].astype(x.dtype)
    sin = sin[None, :, None, None, :].astype(x.dtype)
    return jnp.concatenate([x1 * cos - x2 * sin, x1 * sin + x2 * cos], axis=-1)


def diff_attention_mixer(h_lat, h_ctx, w_qkv, lam, norm_g, w_out, lambda_init, with_ctx):
    H, d = DA_HEADS, DA_HEAD
    f32 = jnp.float32
    B, S, _ = h_lat.shape

    def proj(h):
        n = h.shape[1]
        q, k, v = jnp.split(h @ w_qkv, 3, axis=-1)
        return q.reshape(B, n, H, 2, d), k.reshape(B, n, H, 2, d), v.reshape(B, n, H, 2 * d)

    ql, kl, vl = proj(h_lat)
    qc, kc, vc = proj(h_ctx)
    cos, sin = axial_rope(S, d)
    ql, kl = apply_rope(ql, cos, sin), apply_rope(kl, cos, sin)
    lam = lam.astype(f32)
    lam_full = jnp.exp(jnp.sum(lam[0] * lam[1])) - jnp.exp(jnp.sum(lam[2] * lam[3])) + lambda_init

    def attend(q, k, v):
        s = jnp.einsum('bqhmd,bkhmd->bhmqk', q, k).astype(f32) * (d ** -0.5)
        p = jax.nn.softmax(s, axis=-1)
        p = p[:, :, 0] - lam_full * p[:, :, 1]
        return jnp.einsum('bhqk,bkhe->bqhe', p.astype(v.dtype), v)

    def post(o):
        n = o.shape[1]
        of = o.astype(f32)
        of = of * lax.rsqrt(jnp.mean(of * of, axis=-1, keepdims=True) + 1e-5) * norm_g * (1.0 - lambda_init)
        return of.astype(o.dtype).reshape(B, n, D_MODEL) @ w_out

    k_all = jnp.concatenate([kl, kc], axis=1)
    v_all = jnp.concatenate([vl, vc], axis=1)
    nb = S // DA_QBLOCK
    qb = jnp.transpose(ql.reshape(B, nb, DA_QBLOCK, H, 2, d), (1, 0, 2, 3, 4, 5))
    ol = lax.map(lambda q: attend(q, k_all, v_all), qb)
    ol = jnp.transpose(ol, (1, 0, 2, 3, 4)).reshape(B, S, H, 2 * d)
    out_lat = post(ol)
    out_ctx = post(attend(qc, kc, vc)) if with_ctx else None
    return out_lat, out_ctx


def setup_inputs(seed: int = 0) -> dict:
    key = jax.random.key(seed)
    keys = iter(jax.random.split(key, 48))
    f32 = jnp.float32
    D, F = D_MODEL, FFN_HIDDEN
    s = D ** -0.5

    def nrm(shape, scale):
        return jax.random.normal(next(keys), shape, f32) * scale

    def unif(shape, lo, hi):
        return jax.random.uniform(next(keys), shape, f32, lo, hi)

    ml_b_main = nrm((N_B, ML_IN - 4 * ML_HEADS), 0.02)
    ml_b_i = nrm((N_B, 2, ML_HEADS), 0.1)
    ml_b_f = unif((N_B, 2, ML_HEADS), 3.0, 6.0)
    ml_b_gate = jnp.stack([ml_b_i, ml_b_f], axis=2).reshape(N_B, 4 * ML_HEADS)

    return {
        "x": nrm((BATCH, SEQ, D), 1.0),
        "c": nrm((BATCH, D), 1.0),
        "ctx": nrm((BATCH, CTX_LEN, D), 1.0),
        "c_ctx": nrm((D,), 1.0),
        "ada_w": nrm((DEPTH, D, 6 * D), 0.5 * s),
        "ada_b": nrm((DEPTH, 6 * D), 0.02),
        "norm1_g": 1.0 + nrm((DEPTH, D), 0.02),
        "norm2_g": 1.0 + nrm((DEPTH, D), 0.02),
        "ffn_w_in": nrm((DEPTH, D, 2 * F), s),
        "ffn_conv_w": nrm((DEPTH, CONV_W, F), CONV_W ** -0.5),
        "ffn_conv_b": nrm((DEPTH, F), 0.02),
        "ffn_w_out": nrm((DEPTH, F, D), F ** -0.5),
        "ra_mix": unif((N_A, 6, D), 0.0, 1.0),
        "ra_w_rkv": nrm((N_A, 3, D, D), s),
        "ra_w0": unif((N_A, 2, D), -6.0, -1.0),
        "ra_w1": nrm((N_A, 2, D, RW_DECAY_LORA), 0.1 * s),
        "ra_w2": nrm((N_A, 2, RW_DECAY_LORA, D), 0.1 * RW_DECAY_LORA ** -0.5),
        "ra_a0": nrm((N_A, 2, D), 0.1),
        "ra_a1": nrm((N_A, 2, D, RW_AAA_LORA), 0.1 * s),
        "ra_a2": nrm((N_A, 2, RW_AAA_LORA, D), RW_AAA_LORA ** -0.5),
        "ra_g1": nrm((N_A, D, RW_GATE_LORA), s),
        "ra_g2": nrm((N_A, RW_GATE_LORA, D), RW_GATE_LORA ** -0.5),
        "ra_k_k": 0.85 + nrm((N_A, D), 0.02),
        "ra_k_a": 1.0 + nrm((N_A, D), 0.02),
        "ra_r_k": nrm((N_A, RW_HEADS, RW_HEAD), 0.1),
        "ra_lnx_g": 1.0 + nrm((N_A, D), 0.02),
        "ra_lnx_b": nrm((N_A, D), 0.02),
        "ra_w_out": nrm((N_A, D, D), s),
        "ml_w_in": nrm((N_B, D, ML_IN), s),
        "ml_b_in": jnp.concatenate([ml_b_main, ml_b_gate], axis=-1),
        "ml_norm_g": 1.0 + nrm((N_B, D), 0.02),
        "ml_w_out": nrm((N_B, D, D), s),
        "da_w_qkv": nrm((N_C, D, 3 * D), s),
        "da_lambda": nrm((N_C, 4, DA_HEAD), 0.1),
        "da_norm_g": 1.0 + nrm((N_C, 2 * DA_HEAD), 0.02),
        "da_w_out": nrm((N_C, D, D), s),
        "final_g": 1.0 + nrm((D,), 0.02),
    }


def reference(x, c, ctx, c_ctx, ada_w, ada_b, norm1_g, norm2_g, ffn_w_in, ffn_conv_w,
              ffn_conv_b, ffn_w_out, ra_mix, ra_w_rkv, ra_w0, ra_w1, ra_w2, ra_a0, ra_a1,
              ra_a2, ra_g1, ra_g2, ra_k_k, ra_k_a, ra_r_k, ra_lnx_g, ra_lnx_b, ra_w_out,
              ml_w_in, ml_b_in, ml_norm_g, ml_w_out, da_w_qkv, da_lambda, da_norm_g,
              da_w_out, final_g):
    for i in range(DEPTH):
        last = i == DEPTH - 1
        kind, j = i % N_MIXERS, i // N_MIXERS
        sh1, sc1, g1, sh2, sc2, g2 = [t[:, None, :] for t in
                                      jnp.split(jax.nn.silu(c) @ ada_w[i] + ada_b[i], 6, axis=-1)]
        csh1, csc1, cg1, csh2, csc2, cg2 = jnp.split(jax.nn.silu(c_ctx) @ ada_w[i] + ada_b[i], 6, axis=-1)
        h_lat = rms_norm(x, norm1_g[i]) * (1.0 + sc1) + sh1
        h_ctx = rms_norm(ctx, norm1_g[i]) * (1.0 + csc1) + csh1
        if kind == 0:
            o_lat, o_ctx = rwkv7_mixer(h_lat, h_ctx, ra_mix[j], ra_w_rkv[j], ra_w0[j], ra_w1[j],
                                       ra_w2[j], ra_a0[j], ra_a1[j], ra_a2[j], ra_g1[j], ra_g2[j],
                                       ra_k_k[j], ra_k_a[j], ra_r_k[j], ra_lnx_g[j], ra_lnx_b[j],
                                       ra_w_out[j], not last)
        elif kind == 1:
            o_lat, o_ctx = mlstm_mixer(h_lat, h_ctx, ml_w_in[j], ml_b_in[j], ml_norm_g[j],
                                       ml_w_out[j], not last)
        else:
            lambda_init = 0.8 - 0.6 * math.exp(-0.3 * i)
            o_lat, o_ctx = diff_attention_mixer(h_lat, h_ctx, da_w_qkv[j], da_lambda[j], da_norm_g[j],
                                                da_w_out[j], lambda_init, not last)
        x = x + g1 * o_lat
        x = x + g2 * conv_ffn(rms_norm(x, norm2_g[i]) * (1.0 + sc2) + sh2,
                              ffn_w_in[i], ffn_conv_w[i], ffn_conv_b[i], ffn_w_out[i])
        if not last:
            ctx = ctx + cg1 * o_ctx
            ctx = ctx + cg2 * conv_ffn(rms_norm(ctx, norm2_g[i]) * (1.0 + csc2) + csh2,
                                       ffn_w_in[i], ffn_conv_w[i], ffn_conv_b[i], ffn_w_out[i])
    return rms_norm(x, final_g)
```

```python
import math
import numpy as np
import concourse.bass as bass
import concourse.mybir as mybir
from concourse.bass_utils import run_bass_kernel_spmd

F32 = mybir.dt.float32
ALU = mybir.AluOpType
AF = mybir.ActivationFunctionType
AX = mybir.AxisListType

D = 1024
SEQ = 8192
CTX = 256
NTOK = CTX + SEQ
NT = NTOK // 128
DEPTH = 4
FF = 2816
NFC = FF // 128
EPS = 1e-6
HTC = NTOK + 4


def col_of(tok):
    return tok + 1 if tok < CTX else tok + 3


class Buf:
    __slots__ = ("name", "t", "wr", "rd", "multi")

    def __init__(self, name, t=None, multi=False):
        self.name = name
        self.t = t
        self.wr = {}
        self.rd = {}
        self.multi = multi

    def __getitem__(self, idx):
        return self.t[idx]

    def ap(self):
        return self.t.ap()


class Prog:
    NDMA = 56

    def __init__(self, nc):
        self.nc = nc
        self.E = {"pe": nc.tensor, "dve": nc.vector, "act": nc.scalar,
                  "pool": nc.gpsimd, "sp": nc.sync}
        self.sem = {e: nc.alloc_semaphore("s_" + e) for e in self.E}
        self.cnt = {e: 0 for e in self.E}
        self.seen = {e: {} for e in self.E}
        self.dsem = [nc.alloc_semaphore("d%d" % i) for i in range(self.NDMA)]
        self.duse = [0] * self.NDMA
        self.dnext = 0
        self.ninst = 0
        self.out_tokens = []
        self.dump = set()
        self.pend = {e: ([], []) for e in self.E}

    def sb(self, name, shape, dt=F32):
        return Buf(name, self.nc.alloc_sbuf_tensor(name, list(shape), dt))

    def ps(self, name, shape, dt=F32):
        return Buf(name, self.nc.alloc_psum_tensor(name, list(shape), dt))

    def dram(self, name, shape, dt=F32, kind=None):
        if kind is None:
            kind = "ExternalOutput" if name in self.dump else "Internal"
        return Buf(name, self.nc.dram_tensor(name, list(shape), dt, kind=kind), multi=True)

    def _semh(self, key):
        return self.sem[key] if isinstance(key, str) else self.dsem[key]

    def _need(self, eng, key, val, need):
        if key == eng and eng == "pe":
            return
        if self.seen[eng].get(key, 0) >= val:
            return
        if need.get(key, 0) < val:
            need[key] = val

    def _deps(self, eng, reads, writes):
        need = {}
        for b in reads:
            for k, v in b.wr.items():
                self._need(eng, k, v, need)
        for b in writes:
            if not b.multi:
                for k, v in b.wr.items():
                    self._need(eng, k, v, need)
            for k, v in b.rd.items():
                self._need(eng, k, v, need)
        return need

    def _emit_waits(self, eng, need, keep_last=True):
        items = list(need.items())
        last = None
        if keep_last and items:
            last = items.pop()
        for k, v in items:
            self.E[eng].wait_ge(self._semh(k), v)
            self.seen[eng][k] = v
            self.ninst += 1
        if last is not None:
            self.seen[eng][last[0]] = last[1]
        return last

    def _commit(self, key, val, reads, writes):
        for b in writes:
            if b.multi:
                b.wr[key] = max(b.wr.get(key, 0), val)
            else:
                b.wr = {key: val}
                b.rd = {}
        for b in reads:
            if (not b.multi) and any(b is w for w in writes):
                continue
            b.rd[key] = max(b.rd.get(key, 0), val)

    def op(self, eng, fn, reads=(), writes=(), inc=True):
        last = self._emit_waits(eng, self._deps(eng, reads, writes))
        ins = fn(self.E[eng])
        if last is not None:
            ins._wait_ge(self._semh(last[0]), last[1])
        self.ninst += 1
        pr, pw = self.pend[eng]
        if inc:
            ins.then_inc(self.sem[eng], 1)
            self.cnt[eng] += 1
            if pr or pw:
                reads = list(reads) + [b for b in pr if not any(b is r for r in reads)]
                writes = list(writes) + [b for b in pw if not any(b is w for w in writes)]
                self.pend[eng] = ([], [])
            self._commit(eng, self.cnt[eng], reads, writes)
        else:
            for b in reads:
                if not any(b is r for r in pr):
                    pr.append(b)
            for b in writes:
                if not any(b is w for w in pw):
                    pw.append(b)
        return ins

    def dma(self, q, out, in_, reads=(), writes=(), is_output=False, **kw):
        need = self._deps(q, reads, writes)
        j = self.dnext
        self.dnext = (self.dnext + 1) % self.NDMA
        if self.duse[j]:
            self._need(q, j, 16 * self.duse[j], need)
        last = self._emit_waits(q, need)
        self.duse[j] += 1
        ins = self.E[q].dma_start(out=out, in_=in_, **kw)
        if last is not None:
            ins._wait_ge(self._semh(last[0]), last[1])
        ins.then_inc(self.dsem[j], 16)
        self.ninst += 1
        self._commit(j, 16 * self.duse[j], reads, writes)
        if is_output:
            self.out_tokens.append((j, 16 * self.duse[j]))

    def finish(self, eng="sp"):
        need = {}
        for key, val in self.out_tokens:
            self._need(eng, key, val, need)
        for e in self.E:
            if e != eng and self.cnt[e]:
                self._need(eng, e, self.cnt[e], need)
        for j in range(self.NDMA):
            if self.duse[j]:
                self._need(eng, j, 16 * self.duse[j], need)
        self._emit_waits(eng, need, keep_last=False)


INPUT_SHAPES = [
    ("x", [SEQ, D]), ("c", [D]), ("ctx", [CTX, D]), ("c_ctx", [D]),
    ("ada_w", [4, D, 6 * D]), ("ada_b", [4, 6 * D]), ("norm1_g", [4, D]), ("norm2_g", [4, D]),
    ("ffn_w_in", [4, D, 2 * FF]), ("ffn_conv_w", [4, 3, FF]), ("ffn_conv_b", [4, FF]),
    ("ffn_w_out", [4, FF, D]),
    ("ra_mix", [2, 6, D]), ("ra_w_rkv", [2, 3, D, D]), ("ra_w0", [2, 2, D]),
    ("ra_w1", [2, 2, D, 64]), ("ra_w2", [2, 2, 64, D]), ("ra_a0", [2, 2, D]),
    ("ra_a1", [2, 2, D, 64]), ("ra_a2", [2, 2, 64, D]), ("ra_g1", [2, D, 160]),
    ("ra_g2", [2, 160, D]), ("ra_k_k", [2, D]), ("ra_k_a", [2, D]), ("ra_r_k", [2, 16, 64]),
    ("ra_lnx_g", [2, D]), ("ra_lnx_b", [2, D]), ("ra_w_out", [2, D, D]),
    ("ml_w_in", [1, D, 3088]), ("ml_b_in", [1, 3088]), ("ml_norm_g", [1, D]), ("ml_w_out", [1, D, D]),
    ("da_w_qkv", [1, D, 3 * D]), ("da_lambda", [1, 4, 64]), ("da_norm_g", [1, 128]),
    ("da_w_out", [1, D, D]), ("final_g", [D]),
]


class K:
    def __init__(self, cfg):
        self.cfg = cfg
        self.seq = cfg.get("seq", SEQ)
        self.ntok = CTX + self.seq
        self.nt = self.ntok // 128
        self.nblk = self.ntok // 256
        self.htc = self.ntok + 4
        nc = bass.Bass("TRN2", target_bir_lowering=False)
        self.nc = nc
        P = Prog(nc)
        P.dump = set(cfg.get("dump", []))
        self.P = P
        self.I = {}
        for name, shape in INPUT_SHAPES:
            if name == "x":
                shape = [self.seq, D]
            self.I[name] = P.dram(name, shape, F32, kind="ExternalInput")
        self.out = P.dram("out", [self.seq, D], F32, kind="ExternalOutput")
        self.X = [P.dram("XA", [self.ntok, D]), P.dram("XB", [self.ntok, D])]
        self.HT = P.dram("HT", [D, self.htc])
        self.T4 = [P.sb("t4_%d" % i, [128, 1024]) for i in range(30)]
        self.S = {}
        self.MOD = [P.sb("modl", [128, 6 * D]), P.sb("modc", [128, 6 * D])]
        self.PS = [P.ps("ps%d" % i, [128, 512]) for i in range(8)]
        self.sm = {}
        self.consts()

    def small(self, name, shape, dt=F32):
        if name not in self.sm:
            self.sm[name] = self.P.sb("sm_" + name, shape, dt)
        return self.sm[name]

    def col_of(self, tok):
        return tok + 1 if tok < CTX else tok + 3

    def consts(self):
        P = self.P
        self.ident = P.sb("ident", [128, 128])
        self.ones = P.sb("ones", [128, 128])
        self.lo = P.sb("lo", [128, 128])
        self.up = P.sb("up", [128, 128])
        self.los = P.sb("los", [128, 128])
        self.ups = P.sb("ups", [128, 128])
        self.zero = P.sb("zero", [128, 64])

        def mask(t, pat, cm, op):
            P.op("pool", lambda e: e.memset(t[:], 1.0), [], [t])
            P.op("pool", lambda e: e.affine_select(out=t[:], in_=t[:], pattern=[[pat, 128]],
                                                    compare_op=op, fill=0.0, base=0,
                                                    channel_multiplier=cm), [t], [t])
        mask(self.lo, -1, 1, ALU.is_ge)
        mask(self.up, 1, -1, ALU.is_ge)
        mask(self.los, -1, 1, ALU.is_gt)
        mask(self.ups, 1, -1, ALU.is_gt)
        P.op("pool", lambda e: e.memset(self.ones[:], 1.0), [], [self.ones])
        P.op("pool", lambda e: e.memset(self.zero[:], 0.0), [], [self.zero])
        P.op("dve", lambda e: e.tensor_tensor(out=self.ident[:], in0=self.lo[:], in1=self.up[:],
                                              op=ALU.mult), [self.lo, self.up], [self.ident])

    def bcast_load(self, dst, dst_ap, src_buf, src_ap, q="sp"):
        self.P.dma(q, dst_ap, src_ap.partition_broadcast(128), reads=[src_buf], writes=[dst])

    def transpose_to(self, src, src_aps, dst, dst_aps, psa, evac="act"):
        P = self.P
        for i, sap in enumerate(src_aps):
            P.op("pe", lambda e, i=i, sap=sap: e.transpose(psa[:, i * 128:(i + 1) * 128], sap, self.ident[:]),
                 [src, self.ident], [psa])
        for (dap, c0, c1) in dst_aps:
            if evac == "act":
                P.op("act", lambda e, dap=dap, c0=c0, c1=c1: e.activation(out=dap, in_=psa[:, c0:c1], func=AF.Copy),
                     [psa], [dst])
            else:
                P.op("dve", lambda e, dap=dap, c0=c0, c1=c1: e.tensor_copy(out=dap, in_=psa[:, c0:c1]),
                     [psa], [dst])

    def load_cols(self, dst, dst_ap, src_buf, src_ap, n):
        P = self.P
        st = self.small("lc_stage", [128, 128])
        ps = self.PS[0]
        P.dma("sp", st[0:n, :], src_ap, reads=[src_buf], writes=[st])
        P.op("pe", lambda e: e.transpose(ps[:, 0:n], st[0:n, :], self.ident[0:n, 0:n]), [st, self.ident], [ps])
        P.op("dve", lambda e: e.tensor_copy(out=dst_ap, in_=ps[:, 0:n]), [ps], [dst])

    def init(self):
        P = self.P
        X0 = self.X[0]
        P.dma("sp", X0.ap()[0:CTX, :], self.I["ctx"].ap(), reads=[self.I["ctx"]], writes=[X0])
        nrow = self.seq
        step = 2048
        for r0 in range(0, nrow, step):
            r1 = min(nrow, r0 + step)
            P.dma("pool" if (r0 // step) % 2 else "sp", X0.ap()[CTX + r0:CTX + r1, :],
                  self.I["x"].ap()[r0:r1, :], reads=[self.I["x"]], writes=[X0])
        htv = self.HT.ap().rearrange("(kc p) c -> p kc c", p=128)
        for c in (0, CTX + 1, CTX + 2, self.htc - 1):
            P.dma("sp", htv[:, :, c:c + 1], self.zero[:, 0:8].unsqueeze(2), reads=[self.zero], writes=[self.HT],
                  allow_slow_non_contiguous=True)
        self.CB = []
        for s, nm in enumerate(("c", "c_ctx")):
            cc = self.small("cc%d" % s, [128, 8])
            self.load_cols(cc, cc[:], self.I[nm], self.I[nm].ap().rearrange("(kc p) -> kc p", p=128), 8)
            P.op("act", lambda e, cc=cc: e.activation(out=cc[:], in_=cc[:], func=AF.Silu), [cc], [cc])
            cb = P.sb("cb%d" % s, [128, 1024])
            for kc in range(8):
                P.op("dve", lambda e, kc=kc, cb=cb, cc=cc: e.tensor_scalar(
                    out=cb[:, kc * 128:(kc + 1) * 128], in0=self.ones[:], scalar1=cc[:, kc:kc + 1], scalar2=None,
                    op0=ALU.mult), [self.ones, cc], [cb])
            self.CB.append(cb)

    def adaln(self, i):
        P = self.P
        aw = self.I["ada_w"].ap()[i].rearrange("(kc p) n -> p kc n", p=128)
        for blk in range(12):
            wts = [self.T4[4 + (blk % 2) * 4 + q] for q in range(4)]
            for q in range(4):
                P.dma("sp" if q % 2 == 0 else "pool", wts[q][:].rearrange("p (kc n) -> p kc n", n=512),
                      aw[:, 2 * q:2 * q + 2, blk * 512:(blk + 1) * 512], reads=[self.I["ada_w"]], writes=[wts[q]])
            bt = self.T4[blk % 2]
            self.bcast_load(bt, bt[:, 0:512], self.I["ada_b"], self.I["ada_b"].ap()[i, blk * 512:(blk + 1) * 512])
            for s in range(2):
                ps = self.PS[(blk % 2) * 2 + s]
                for kc in range(8):
                    wt = wts[kc // 2]
                    P.op("pe", lambda e, kc=kc, s=s, ps=ps, wt=wt: e.matmul(
                        ps[:, :], lhsT=self.CB[s][:, kc * 128:(kc + 1) * 128],
                        rhs=wt[:, (kc % 2) * 512:(kc % 2 + 1) * 512],
                        start=(kc == 0), stop=(kc == 7)), [self.CB[s], wt], [ps], inc=(kc == 7))
                P.op("dve", lambda e, s=s, ps=ps, bt=bt, blk=blk: e.tensor_tensor(
                    out=self.MOD[s][:, blk * 512:(blk + 1) * 512], in0=ps[:, :], in1=bt[:, 0:512], op=ALU.add),
                    [ps, bt], [self.MOD[s]])
        for (nm, off) in (("norm1_g", 1), ("norm2_g", 4)):
            gt = self.T4[2]
            self.bcast_load(gt, gt[:], self.I[nm], self.I[nm].ap()[i])
            for s in range(2):
                m = self.MOD[s]
                P.op("dve", lambda e, m=m, gt=gt, off=off: e.scalar_tensor_tensor(
                    out=m[:, off * D:(off + 1) * D], in0=m[:, off * D:(off + 1) * D], scalar=1.0, in1=gt[:],
                    op0=ALU.add, op1=ALU.mult), [m, gt], [m])

    def rms_rstd(self, xt, junk, ss, n, eps):
        P = self.P
        P.op("act", lambda e: e.activation(out=junk[:], in_=xt[:], func=AF.Square, accum_out=ss[:, 0:1]),
             [xt], [junk, ss])
        P.op("dve", lambda e: e.tensor_scalar(out=ss[:, 0:1], in0=ss[:, 0:1], scalar1=1.0 / n, scalar2=eps,
                                              op0=ALU.mult, op1=ALU.add), [ss], [ss])
        P.op("act", lambda e: e.activation(out=ss[:, 0:1], in_=ss[:, 0:1], func=AF.Sqrt), [ss], [ss])
        P.op("dve", lambda e: e.reciprocal(out=ss[:, 0:1], in_=ss[:, 0:1]), [ss], [ss])

    def norm_to_HT(self, Xin, goff, shoff):
        P = self.P
        htv = self.HT.ap().rearrange("(kc p) c -> p kc c", p=128)
        for t in range(self.nt):
            par = t % 2
            xt, h, hT = self.T4[par * 3], self.T4[par * 3 + 1], self.T4[par * 3 + 2]
            ss = self.small("nss%d" % par, [128, 1])
            m = self.MOD[1 if t < 2 else 0]
            P.dma("sp", xt[:], Xin.ap()[t * 128:(t + 1) * 128, :], reads=[Xin], writes=[xt])
            self.rms_rstd(xt, h, ss, D, EPS)
            P.op("dve", lambda e, xt=xt, h=h, ss=ss, m=m: e.scalar_tensor_tensor(
                out=h[:], in0=xt[:], scalar=ss[:, 0:1], in1=m[:, goff * D:(goff + 1) * D],
                op0=ALU.mult, op1=ALU.mult), [xt, ss, m], [h])
            P.op("dve", lambda e, h=h, m=m: e.tensor_tensor(
                out=h[:], in0=h[:], in1=m[:, shoff * D:(shoff + 1) * D], op=ALU.add), [h, m], [h])
            for hf in range(2):
                psa = self.PS[par * 2 + hf]
                self.transpose_to(h, [h[:, (hf * 4 + q) * 128:(hf * 4 + q + 1) * 128] for q in range(4)],
                                  hT, [(hT[:, hf * 512:(hf + 1) * 512], 0, 512)], psa,
                                  evac="act" if hf == 0 else "dve")
            c0 = self.col_of(t * 128)
            P.dma("pool", htv[:, :, c0:c0 + 128], hT[:].rearrange("p (kc c) -> p kc c", c=128),
                  reads=[hT], writes=[self.HT])

    def load_hTe(self, j, tiles):
        htv = self.HT.ap().rearrange("(kc p) c -> p kc c", p=128)
        c0 = self.col_of(j * 256)
        views = []
        for q, (k0, k1) in enumerate(((0, 3), (3, 6), (6, 8))):
            v = tiles[q][:, 0:(k1 - k0) * 258].rearrange("p (kc c) -> p kc c", c=258)
            self.P.dma("sp" if q != 1 else "pool", v, htv[:, k0:k1, c0 - 1:c0 + 257], reads=[self.HT], writes=[tiles[q]])
            views.append(v)

        def hv(kc):
            return tiles[kc // 3], views[kc // 3][:, kc % 3, :]
        return hv

    def ffn(self, i, Xin, Xout):
        P = self.P
        self.norm_to_HT(Xin, 4, 3)
        cw = self.small("ffn_cw", [128, 3, NFC])
        cbias = self.small("ffn_cb", [128, NFC])
        self.load_cols(cw, cw[:].rearrange("p j fc -> p (j fc)"), self.I["ffn_conv_w"],
                       self.I["ffn_conv_w"].ap()[i].rearrange("j (fc p) -> (j fc) p", p=128), 3 * NFC)
        self.load_cols(cbias, cbias[:], self.I["ffn_conv_b"],
                       self.I["ffn_conv_b"].ap()[i].rearrange("(fc p) -> fc p", p=128), NFC)
        win = self.I["ffn_w_in"].ap()[i].rearrange("(kc p) n -> p kc n", p=128)
        wout = self.I["ffn_w_out"].ap()[i]
        last_ctx = (i == DEPTH - 1)
        for j in range(self.nblk):
            if j == 0 and last_ctx:
                continue
            m = self.MOD[1 if j == 0 else 0]
            hv = self.load_hTe(j, self.T4[18 + (j % 2) * 3:21 + (j % 2) * 3])
            po = [self.PS[4], self.PS[5], self.PS[6], self.PS[7]]
            for fc in range(NFC):
                par = fc % 2
                wg, wv_, wo = self.T4[6 + par * 3], self.T4[7 + par * 3], self.T4[8 + par * 3]
                wgv = wg[:].rearrange("p (kc f) -> p kc f", f=128)
                wvv = wv_[:].rearrange("p (kc f) -> p kc f", f=128)
                P.dma("sp", wgv, win[:, :, FF + fc * 128:FF + (fc + 1) * 128], reads=[self.I["ffn_w_in"]], writes=[wg])
                P.dma("pool", wvv, win[:, :, fc * 128:(fc + 1) * 128], reads=[self.I["ffn_w_in"]], writes=[wv_])
                P.dma("sp", wo[:], wout[fc * 128:(fc + 1) * 128, :], reads=[self.I["ffn_w_out"]], writes=[wo])
                pg, pv = self.PS[par * 2], self.PS[par * 2 + 1]
                for kc in range(8):
                    hb, ha = hv(kc)
                    P.op("pe", lambda e, kc=kc, pg=pg, wgv=wgv, ha=ha: e.matmul(
                        pg[:, 0:258], lhsT=wgv[:, kc, :], rhs=ha, start=(kc == 0), stop=(kc == 7)),
                        [wg, hb], [pg], inc=(kc == 7))
                for kc in range(8):
                    hb, ha = hv(kc)
                    P.op("pe", lambda e, kc=kc, pv=pv, wvv=wvv, ha=ha: e.matmul(
                        pv[:, 0:256], lhsT=wvv[:, kc, :], rhs=ha[:, 1:257], start=(kc == 0), stop=(kc == 7)),
                        [wv_, hb], [pv], inc=(kc == 7))
                tt = self.T4[12 + par]
                P.op("act", lambda e, tt=tt, pg=pg, fc=fc: e.activation(
                    out=tt[:, 0:256], in_=pg[:, 1:257], func=AF.Identity, scale=cw[:, 1, fc:fc + 1],
                    bias=cbias[:, fc:fc + 1]), [pg, cw, cbias], [tt])
                P.op("dve", lambda e, tt=tt, pg=pg, fc=fc: e.scalar_tensor_tensor(
                    out=tt[:, 0:256], in0=pg[:, 0:256], scalar=cw[:, 0, fc:fc + 1], in1=tt[:, 0:256],
                    op0=ALU.mult, op1=ALU.add), [pg, cw, tt], [tt])
                P.op("dve", lambda e, tt=tt, pg=pg, fc=fc: e.scalar_tensor_tensor(
                    out=tt[:, 0:256], in0=pg[:, 2:258], scalar=cw[:, 2, fc:fc + 1], in1=tt[:, 0:256],
                    op0=ALU.mult, op1=ALU.add), [pg, cw, tt], [tt])
                P.op("act", lambda e, tt=tt: e.activation(out=tt[:, 256:512], in_=tt[:, 0:256],
                                                          func=AF.Gelu_apprx_tanh), [tt], [tt])
                P.op("dve", lambda e, tt=tt, pv=pv: e.tensor_tensor(
                    out=tt[:, 512:768], in0=tt[:, 256:512], in1=pv[:, 0:256], op=ALU.mult), [tt, pv], [tt])
                for mt in range(2):
                    for hf in range(2):
                        pb = po[mt * 2 + hf]
                        P.op("pe", lambda e, tt=tt, wo=wo, pb=pb, mt=mt, hf=hf, fc=fc: e.matmul(
                            pb[:, :], lhsT=tt[:, 512 + mt * 128:512 + (mt + 1) * 128],
                            rhs=wo[:, hf * 512:(hf + 1) * 512], start=(fc == 0), stop=(fc == NFC - 1)),
                            [tt, wo], [pb], inc=(fc == NFC - 1))
            for mt in range(2):
                r0 = j * 256 + mt * 128
                xt, xn = self.T4[14 + mt * 2], self.T4[15 + mt * 2]
                P.dma("sp", xt[:], Xin.ap()[r0:r0 + 128, :], reads=[Xin], writes=[xt])
                for hf in range(2):
                    pb = po[mt * 2 + hf]
                    P.op("dve", lambda e, xn=xn, pb=pb, hf=hf, m=m: e.tensor_tensor(
                        out=xn[:, hf * 512:(hf + 1) * 512], in0=pb[:, :],
                        in1=m[:, 5 * D + hf * 512:5 * D + (hf + 1) * 512], op=ALU.mult), [pb, m], [xn])
                P.op("pool", lambda e, xn=xn, xt=xt: e.tensor_tensor(
                    out=xn[:], in0=xn[:], in1=xt[:], op=ALU.add), [xn, xt], [xn])
                P.dma("pool", Xout.ap()[r0:r0 + 128, :], xn[:], reads=[xn], writes=[Xout])

    def final(self, Xin):
        P = self.P
        g = self.T4[6]
        self.bcast_load(g, g[:], self.I["final_g"], self.I["final_g"].ap())
        for t in range(2, self.nt):
            par = t % 2
            xt, h = self.T4[par * 3], self.T4[par * 3 + 1]
            ss = self.small("nss%d" % par, [128, 1])
            P.dma("sp", xt[:], Xin.ap()[t * 128:(t + 1) * 128, :], reads=[Xin], writes=[xt])
            self.rms_rstd(xt, h, ss, D, EPS)
            P.op("dve", lambda e, xt=xt, h=h, ss=ss: e.scalar_tensor_tensor(
                out=h[:], in0=xt[:], scalar=ss[:, 0:1], in1=g[:], op0=ALU.mult, op1=ALU.mult), [xt, ss, g], [h])
            P.dma("pool", self.out.ap()[(t - 2) * 128:(t - 1) * 128, :], h[:], reads=[h], writes=[self.out],
                  is_output=True)


def build(cfg):
    k = K(cfg)
    k.init()
    cur = 0
    for (i, kind) in cfg.get("layers", [(0, "ra"), (1, "ml"), (2, "da"), (3, "ra")]):
        k.adaln(i)
        if kind == "ra":
            k.rwkv(i, k.X[cur], k.X[1 - cur]); cur = 1 - cur
        elif kind == "ml":
            k.mlstm(i, k.X[cur], k.X[1 - cur]); cur = 1 - cur
        elif kind == "da":
            k.diffattn(i, k.X[cur], k.X[1 - cur]); cur = 1 - cur
        if cfg.get("ffn", True):
            k.ffn(i, k.X[cur], k.X[1 - cur]); cur = 1 - cur
    k.final(k.X[cur])
    k.P.finish()
    return k


def scratch(self, name, cols=D):
    if name not in self.S:
        self.S[name] = self.P.dram("S_" + name, [self.ntok, cols])
    return self.S[name]


def load_wres(self, wbuf, wap, tiles):
    for kc in range(8):
        self.P.dma("sp" if kc % 2 == 0 else "pool", tiles[kc][:], wap[kc * 128:(kc + 1) * 128, :],
                   reads=[wbuf], writes=[tiles[kc]])


def proj_residual(self, zT_list, wres, Xin, Xout, t, par):
    P = self.P
    m = self.MOD[1 if t < 2 else 0]
    xt, xn = self.T4[26 + par * 2], self.T4[27 + par * 2]
    P.dma("sp", xt[:], Xin.ap()[t * 128:(t + 1) * 128, :], reads=[Xin], writes=[xt])
    for hf in range(2):
        pb = self.PS[6 + hf]
        for kc in range(8):
            zb, za = zT_list[kc]
            P.op("pe", lambda e, kc=kc, za=za, pb=pb, hf=hf: e.matmul(
                pb[:, :], lhsT=za, rhs=wres[kc][:, hf * 512:(hf + 1) * 512], start=(kc == 0), stop=(kc == 7)),
                [zb, wres[kc]], [pb], inc=(kc == 7))
        P.op("dve", lambda e, xn=xn, pb=pb, hf=hf, m=m: e.tensor_tensor(
            out=xn[:, hf * 512:(hf + 1) * 512], in0=pb[:, :], in1=m[:, 2 * D + hf * 512:2 * D + (hf + 1) * 512],
            op=ALU.mult), [pb, m], [xn])
    P.op("pool", lambda e, xn=xn, xt=xt: e.tensor_tensor(out=xn[:], in0=xn[:], in1=xt[:], op=ALU.add), [xn, xt], [xn])
    P.dma("pool", Xout.ap()[t * 128:(t + 1) * 128, :], xn[:], reads=[xn], writes=[Xout])


K.scratch = scratch
K.load_wres = load_wres
K.proj_residual = proj_residual


CDEC = math.exp(-0.5)


def rw_xs(self, hv, jm, mixc, xs_tiles, xx_tiles):
    P = self.P
    views = []
    for kc in range(8):
        hb, ha = hv(kc)
        xt = xs_tiles[kc // 4]
        xv = xt[:, (kc % 4) * 256:(kc % 4 + 1) * 256]
        xxb = xx_tiles[kc // 4]
        xxv = xxb[:, (kc % 4) * 256:(kc % 4 + 1) * 256]
        eng = "dve"
        P.op(eng, lambda e, xv=xv, xxv=xxv, ha=ha, kc=kc: e.scalar_tensor_tensor(
            out=xv, in0=xxv, scalar=mixc[:, jm * 8 + kc:jm * 8 + kc + 1], in1=ha[:, 1:257],
            op0=ALU.mult, op1=ALU.add), [xxb, mixc, hb], [xt])
        views.append((xt, xv))
    return views


def rw_xx(self, hv, xx_tiles):
    P = self.P
    for kc in range(8):
        hb, ha = hv(kc)
        xxb = xx_tiles[kc // 4]
        xxv = xxb[:, (kc % 4) * 256:(kc % 4 + 1) * 256]
        eng = "dve"
        P.op(eng, lambda e, xxv=xxv, ha=ha: e.tensor_tensor(out=xxv, in0=ha[:, 0:256], in1=ha[:, 2:258], op=ALU.add),
             [hb], [xxb])
        P.op(eng, lambda e, xxv=xxv, ha=ha: e.scalar_tensor_tensor(
            out=xxv, in0=xxv, scalar=0.5, in1=ha[:, 1:257], op0=ALU.mult, op1=ALU.subtract), [xxb, hb], [xxb])


def rwkv_pre(self, i, Xin):
    P = self.P
    j = i // 3
    I = self.I
    self.norm_to_HT(Xin, 1, 0)
    mixc = self.small("ra_mixc", [128, 48])
    self.load_cols(mixc, mixc[:], I["ra_mix"], I["ra_mix"].ap()[j].rearrange("m (kc p) -> (m kc) p", p=128), 48)
    T4 = self.T4
    wres = T4[0:8]
    SR, SV, SK, SKK = self.scratch("r"), self.scratch("v"), self.scratch("k"), self.scratch("kk")
    SG_ = self.scratch("g")
    SSG = [self.scratch("sg0"), self.scratch("sg1")]
    SKD = [self.scratch("kd0"), self.scratch("kd1")]
    SBA = [self.scratch("ba0"), self.scratch("ba1")]
    SRKS = self.scratch("rks", 16)

    def block_setup(jb, jm):
        hv = self.load_hTe(jb, T4[8 + (jb % 2) * 3:11 + (jb % 2) * 3])
        xx_tiles = T4[14:16]
        self.rw_xx(hv, xx_tiles)
        xs_tiles = T4[16 + (jb % 2) * 2:18 + (jb % 2) * 2]
        return self.rw_xs(hv, jm, mixc, xs_tiles, xx_tiles)

    kkb = T4[24]
    for (jm, proj, dst) in ((0, 0, SR), (2, 2, SV), (1, 1, SK)):
        self.load_wres(I["ra_w_rkv"], I["ra_w_rkv"].ap()[j, proj], wres)
        if proj == 1:
            self.bcast_load(kkb, kkb[:], I["ra_k_k"], I["ra_k_k"].ap()[j])
        for jb in range(self.nblk):
            xs = block_setup(jb, jm)
            for mt in range(2):
                ot = T4[20 + mt]
                for hf in range(2):
                    pb = self.PS[(mt * 2 + hf) % 4]
                    for kc in range(8):
                        xb, xv = xs[kc]
                        P.op("pe", lambda e, kc=kc, xv=xv, pb=pb, hf=hf, mt=mt: e.matmul(
                            pb[:, :], lhsT=xv[:, mt * 128:(mt + 1) * 128], rhs=wres[kc][:, hf * 512:(hf + 1) * 512],
                            start=(kc == 0), stop=(kc == 7)), [xb, wres[kc]], [pb], inc=(kc == 7))
                    P.op("act", lambda e, ot=ot, pb=pb, hf=hf: e.activation(
                        out=ot[:, hf * 512:(hf + 1) * 512], in_=pb[:, :], func=AF.Copy), [pb], [ot])
                r0 = jb * 256 + mt * 128
                P.dma("pool", dst.ap()[r0:r0 + 128, :], ot[:], reads=[ot], writes=[dst])
                if proj == 1:
                    kr, sq = T4[22], T4[23]
                    ssh = self.small("kk_ss", [128, 16])
                    P.op("dve", lambda e, kr=kr, ot=ot: e.tensor_tensor(out=kr[:], in0=ot[:], in1=kkb[:], op=ALU.mult),
                         [ot, kkb], [kr])
                    P.op("act", lambda e, kr=kr, sq=sq: e.activation(out=sq[:], in_=kr[:], func=AF.Square), [kr], [sq])
                    P.op("dve", lambda e, sq=sq: e.tensor_reduce(
                        out=ssh[:], in_=sq[:].rearrange("p (h k) -> p h k", k=64), axis=AX.X, op=ALU.add), [sq], [ssh])
                    P.op("act", lambda e: e.activation(out=ssh[:], in_=ssh[:], func=AF.Sqrt), [ssh], [ssh])
                    P.op("dve", lambda e: e.tensor_scalar_max(out=ssh[:], in0=ssh[:], scalar1=1e-12), [ssh], [ssh])
                    P.op("dve", lambda e: e.reciprocal(out=ssh[:], in_=ssh[:]), [ssh], [ssh])
                    P.op("dve", lambda e, kr=kr: e.tensor_tensor(
                        out=kr[:].rearrange("p (h k) -> p h k", k=64), in0=kr[:].rearrange("p (h k) -> p h k", k=64),
                        in1=ssh[:].unsqueeze(2).to_broadcast([128, 16, 64]), op=ALU.mult), [kr, ssh], [kr])
                    P.dma("sp", SKK.ap()[r0:r0 + 128, :], kr[:], reads=[kr], writes=[SKK])

    kab, rkb = T4[24], T4[25]
    for (jm, nm1, nm2, nm0, is_w) in ((3, "ra_w1", "ra_w2", "ra_w0", True), (4, "ra_a1", "ra_a2", "ra_a0", False)):
        l1, l2 = T4[0], T4[1]
        b0 = [T4[2], T4[3]]
        for z in range(2):
            P.dma("sp", l1[:].rearrange("p (kc z l) -> p kc z l", z=2, l=64)[:, :, z, :],
                  I[nm1].ap()[j, z].rearrange("(kc p) l -> p kc l", p=128), reads=[I[nm1]], writes=[l1])
        P.dma("sp", l2[:], I[nm2].ap()[j].rearrange("z l n -> (z l) n"), reads=[I[nm2]], writes=[l2])
        for d in range(2):
            self.bcast_load(b0[d], b0[d][:], I[nm0], I[nm0].ap()[j, d])
        if not is_w:
            self.bcast_load(kab, kab[:], I["ra_k_a"], I["ra_k_a"].ap()[j])
            self.bcast_load(rkb, rkb[:], I["ra_r_k"], I["ra_r_k"].ap()[j].rearrange("h k -> (h k)"))
        for jb in range(self.nblk):
            xs = block_setup(jb, jm)
            pl = self.PS[4]
            for kc in range(8):
                xb, xv = xs[kc]
                P.op("pe", lambda e, kc=kc, xv=xv: e.matmul(
                    pl[:, 0:256], lhsT=l1[:, kc * 128:(kc + 1) * 128], rhs=xv, start=(kc == 0), stop=(kc == 7)),
                    [l1, xb], [pl], inc=(kc == 7))
            t1 = T4[4]
            P.op("act", lambda e: e.activation(out=t1[:, 0:256], in_=pl[:, 0:256],
                                               func=AF.Tanh if is_w else AF.Copy), [pl], [t1])
            for mt in range(2):
                r0 = jb * 256 + mt * 128
                if not is_w:
                    kt, kkt, rt = T4[5], T4[6], T4[7]
                    P.dma("sp", kt[:], SK.ap()[r0:r0 + 128, :], reads=[SK], writes=[kt])
                    P.dma("pool", kkt[:], SKK.ap()[r0:r0 + 128, :], reads=[SKK], writes=[kkt])
                    P.dma("sp", rt[:], SR.ap()[r0:r0 + 128, :], reads=[SR], writes=[rt])
                    kds = T4[26]
                for d in range(2):
                    ot = T4[20 + d]
                    for hf in range(2):
                        pb = self.PS[d * 2 + hf]
                        P.op("pe", lambda e, d=d, hf=hf, mt=mt, pb=pb: e.matmul(
                            pb[:, :], lhsT=t1[64 * d:64 * d + 64, mt * 128:(mt + 1) * 128],
                            rhs=l2[64 * d:64 * d + 64, hf * 512:(hf + 1) * 512], start=True, stop=True),
                            [t1, l2], [pb])
                        P.op("dve", lambda e, d=d, hf=hf, pb=pb, ot=ot: e.tensor_tensor(
                            out=ot[:, hf * 512:(hf + 1) * 512], in0=pb[:, :], in1=b0[d][:, hf * 512:(hf + 1) * 512],
                            op=ALU.add), [pb, b0[d]], [ot])
                    P.op("act", lambda e, ot=ot: e.activation(out=ot[:], in_=ot[:], func=AF.Sigmoid), [ot], [ot])
                    if is_w:
                        P.dma("pool", SSG[d].ap()[r0:r0 + 128, :], ot[:], reads=[ot], writes=[SSG[d]])
                    else:
                        kd, ba = T4[22 + d], T4[24 + 4 + d]
                        P.op("dve", lambda e, kd=kd, ot=ot: e.scalar_tensor_tensor(
                            out=kd[:], in0=ot[:], scalar=-1.0, in1=kab[:], op0=ALU.add, op1=ALU.mult), [ot, kab], [kd])
                        P.op("dve", lambda e, kd=kd: e.scalar_tensor_tensor(
                            out=kd[:], in0=kd[:], scalar=1.0, in1=kt[:], op0=ALU.add, op1=ALU.mult), [kd, kt], [kd])
                        P.op("pool", lambda e, ba=ba, ot=ot: e.tensor_tensor(out=ba[:], in0=ot[:], in1=kkt[:], op=ALU.mult),
                             [ot, kkt], [ba])
                        P.dma("sp", SKD[d].ap()[r0:r0 + 128, :], kd[:], reads=[kd], writes=[SKD[d]])
                        P.dma("pool", SBA[d].ap()[r0:r0 + 128, :], ba[:], reads=[ba], writes=[SBA[d]])
                if not is_w:
                    kd0, kd1 = T4[22], T4[23]
                    rks = self.small("rks_t", [128, 16])
                    P.op("dve", lambda e: e.tensor_tensor(out=kds[:], in0=kd0[:], in1=kd1[:], op=ALU.add), [kd0, kd1], [kds])
                    P.op("dve", lambda e: e.tensor_tensor(out=kds[:], in0=kds[:], in1=rt[:], op=ALU.mult), [kds, rt], [kds])
                    P.op("dve", lambda e: e.tensor_tensor(out=kds[:], in0=kds[:], in1=rkb[:], op=ALU.mult), [kds, rkb], [kds])
                    P.op("dve", lambda e: e.tensor_reduce(
                        out=rks[:], in_=kds[:].rearrange("p (h k) -> p h k", k=64), axis=AX.X, op=ALU.add), [kds], [rks])
                    P.dma("sp", SRKS.ap()[r0:r0 + 128, :], rks[:], reads=[rks], writes=[SRKS])

    g1a, g1b, g2a, g2b = T4[0], T4[1], T4[2], T4[3]
    g1v = I["ra_g1"].ap()[j].rearrange("(kc p) l -> p kc l", p=128)
    P.dma("sp", g1a[:].rearrange("p (kc l) -> p kc l", l=128), g1v[:, :, 0:128], reads=[I["ra_g1"]], writes=[g1a])
    P.dma("sp", g1b[:, 0:256].rearrange("p (kc l) -> p kc l", l=32), g1v[:, :, 128:160], reads=[I["ra_g1"]], writes=[g1b])
    P.dma("sp", g2a[:], I["ra_g2"].ap()[j, 0:128, :], reads=[I["ra_g2"]], writes=[g2a])
    P.dma("sp", g2b[0:32, :], I["ra_g2"].ap()[j, 128:160, :], reads=[I["ra_g2"]], writes=[g2b])
    for jb in range(self.nblk):
        xs = block_setup(jb, 5)
        pa, pb2 = self.PS[4], self.PS[5]
        for kc in range(8):
            xb, xv = xs[kc]
            P.op("pe", lambda e, kc=kc, xv=xv: e.matmul(
                pa[:, 0:256], lhsT=g1a[:, kc * 128:(kc + 1) * 128], rhs=xv, start=(kc == 0), stop=(kc == 7)),
                [g1a, xb], [pa], inc=(kc == 7))
        for kc in range(8):
            xb, xv = xs[kc]
            P.op("pe", lambda e, kc=kc, xv=xv: e.matmul(
                pb2[0:32, 0:256], lhsT=g1b[:, kc * 32:(kc + 1) * 32], rhs=xv, start=(kc == 0), stop=(kc == 7)),
                [g1b, xb], [pb2], inc=(kc == 7))
        ta, tb = T4[4], T4[5]
        P.op("act", lambda e: e.activation(out=ta[:, 0:256], in_=pa[:, 0:256], func=AF.Sigmoid), [pa], [ta])
        P.op("act", lambda e: e.activation(out=tb[0:32, 0:256], in_=pb2[0:32, 0:256], func=AF.Sigmoid), [pb2], [tb])
        for mt in range(2):
            r0 = jb * 256 + mt * 128
            ot = T4[20 + mt]
            for hf in range(2):
                pb = self.PS[mt * 2 + hf]
                P.op("pe", lambda e, hf=hf, mt=mt, pb=pb: e.matmul(
                    pb[:, :], lhsT=ta[:, mt * 128:(mt + 1) * 128], rhs=g2a[:, hf * 512:(hf + 1) * 512],
                    start=True, stop=False), [ta, g2a], [pb], inc=False)
                P.op("pe", lambda e, hf=hf, mt=mt, pb=pb: e.matmul(
                    pb[:, :], lhsT=tb[0:32, mt * 128:(mt + 1) * 128], rhs=g2b[0:32, hf * 512:(hf + 1) * 512],
                    start=False, stop=True), [tb, g2b], [pb])
                P.op("act", lambda e, ot=ot, pb=pb, hf=hf: e.activation(
                    out=ot[:, hf * 512:(hf + 1) * 512], in_=pb[:, :], func=AF.Copy), [pb], [ot])
            P.dma("pool", SG_.ap()[r0:r0 + 128, :], ot[:], reads=[ot], writes=[SG_])


K.rw_xs = rw_xs
K.rw_xx = rw_xx
K.rwkv_pre = rwkv_pre


def rwkv_consts(self):
    if hasattr(self, "upC"):
        return
    P = self.P
    self.upC = P.sb("upC", [128, 128]); self.loC = P.sb("loC", [128, 128]); self.onesC = P.sb("onesC", [128, 128])
    for dst, src in ((self.upC, self.up), (self.loC, self.lo), (self.onesC, self.ones)):
        P.op("dve", lambda e, dst=dst, src=src: e.tensor_scalar(out=dst[:], in0=src[:], scalar1=-CDEC, scalar2=None,
                                                                 op0=ALU.mult), [src], [dst])
    self.MK1 = []; self.MK3 = []
    for d in range(2):
        strict_T, incl_T, strict_N = (self.ups, self.up, self.los) if d == 0 else (self.los, self.lo, self.ups)
        m1 = P.sb("mk1_%d" % d, [128, 512]); m3 = P.sb("mk3_%d" % d, [128, 512])
        for q in range(4):
            src = strict_T if q % 2 == 0 else incl_T
            P.op("dve", lambda e, m1=m1, q=q, src=src: e.tensor_copy(out=m1[:, q * 128:(q + 1) * 128], in_=src[:]), [src], [m1])
            P.op("dve", lambda e, m3=m3, q=q: e.tensor_copy(out=m3[:, q * 128:(q + 1) * 128], in_=strict_N[:]), [strict_N], [m3])
        self.MK1.append(m1); self.MK3.append(m3)


def rwkv_scan(self, d):
    P = self.P
    T4 = self.T4
    PS = self.PS
    SR, SV, SKK = self.scratch("r"), self.scratch("v"), self.scratch("kk")
    SSG, SKD, SBA = self.scratch("sg%d" % d), self.scratch("kd%d" % d), self.scratch("ba%d" % d)
    SY = self.scratch("y%d" % d)
    triC = self.upC if d == 0 else self.loC
    mk1, mk3 = self.MK1[d], self.MK3[d]
    ST = T4[29]
    stv = ST[:, 0:512].rearrange("p (q v) -> p q v", v=64)
    P.op("pool", lambda e: e.memset(ST[:], 0.0), [], [ST])
    order = [0, 1] + list(range(2, self.nt)) if d == 0 else [1, 0] + list(range(self.nt - 1, 1, -1))
    for c in order:
        r0 = c * 128
        Lr, Lv, Lkk, Lkd, Lba, Lsg = T4[0:6]
        for (dst, src, q) in ((Lsg, SSG, "sp"), (Lkk, SKK, "pool"), (Lba, SBA, "sp"), (Lkd, SKD, "pool"),
                              (Lr, SR, "sp"), (Lv, SV, "pool")):
            P.dma(q, dst[:], src.ap()[r0:r0 + 128, :], reads=[src], writes=[dst])
        for hf in range(2):
            P.op("pe", lambda e, hf=hf: e.matmul(PS[hf][:, :], lhsT=triC[:], rhs=Lsg[:, hf * 512:(hf + 1) * 512],
                                                 start=True, stop=True), [triC, Lsg], [PS[hf]])
            P.op("pe", lambda e, hf=hf: e.matmul(PS[2 + hf][:, :], lhsT=self.onesC[:], rhs=Lsg[:, hf * 512:(hf + 1) * 512],
                                                 start=True, stop=True), [self.onesC, Lsg], [PS[2 + hf]])
        gC = self.small("rw_gC", [128, 8])
        for p in range(8):
            P.op("pe", lambda e, p=p: e.matmul(PS[5][:, p:p + 1], lhsT=Lsg[:, p * 128:(p + 1) * 128],
                                               rhs=self.onesC[:, 0:1], start=True, stop=True),
                 [Lsg, self.onesC], [PS[5]], inc=(p == 7))
        P.op("act", lambda e: e.activation(out=gC[:], in_=PS[5][:, 0:8], func=AF.Exp), [PS[5]], [gC])
        lg, ex = T4[6], T4[7]
        Bt_, Kt_, Rt_, At_, Bh, Kh = T4[8:14]
        for hf in range(2):
            P.op("act", lambda e, hf=hf: e.activation(out=lg[:, hf * 512:(hf + 1) * 512], in_=PS[hf][:, :], func=AF.Copy),
                 [PS[hf]], [lg])
        P.op("act", lambda e: e.activation(out=ex[:], in_=lg[:], func=AF.Exp, scale=-1.0), [lg], [ex])
        P.op("dve", lambda e: e.tensor_tensor(out=Bt_[:], in0=Lba[:], in1=ex[:], op=ALU.mult), [Lba, ex], [Bt_])
        P.op("pool", lambda e: e.tensor_tensor(out=Kt_[:], in0=Lkd[:], in1=ex[:], op=ALU.mult), [Lkd, ex], [Kt_])
        P.op("act", lambda e: e.activation(out=ex[:], in_=lg[:], func=AF.Exp), [lg], [ex])
        P.op("dve", lambda e: e.tensor_tensor(out=Rt_[:], in0=Lr[:], in1=ex[:], op=ALU.mult), [Lr, ex], [Rt_])
        P.op("dve", lambda e: e.scalar_tensor_tensor(out=ex[:], in0=Lsg[:], scalar=CDEC, in1=lg[:], op0=ALU.mult,
                                                     op1=ALU.add), [Lsg, lg, Rt_], [ex])
        P.op("act", lambda e: e.activation(out=ex[:], in_=ex[:], func=AF.Exp), [ex], [ex])
        P.op("dve", lambda e: e.scalar_tensor_tensor(out=At_[:], in0=Lkk[:], scalar=-1.0, in1=ex[:], op0=ALU.mult,
                                                     op1=ALU.mult), [Lkk, ex], [At_])
        for hf in range(2):
            P.op("dve", lambda e, hf=hf: e.tensor_tensor(out=ex[:, hf * 512:(hf + 1) * 512], in0=PS[2 + hf][:, :],
                                                         in1=lg[:, hf * 512:(hf + 1) * 512], op=ALU.subtract),
                 [PS[2 + hf], lg, At_], [ex])
        P.op("act", lambda e: e.activation(out=ex[:], in_=ex[:], func=AF.Exp), [ex], [ex])
        P.op("dve", lambda e: e.tensor_tensor(out=Bh[:], in0=Lba[:], in1=ex[:], op=ALU.mult), [Lba, ex], [Bh])
        P.op("pool", lambda e: e.tensor_tensor(out=Kh[:], in0=Lkd[:], in1=ex[:], op=ALU.mult), [Lkd, ex], [Kh])
        AR0, AR1, BT, KT = T4[14], T4[15], T4[16], T4[17]
        bank = 4
        for (src, kind) in ((At_, 0), (Rt_, 1), (Bt_, 2), (Kt_, 3)):
            for hf in range(2):
                psa = PS[4 + (bank % 4)]; bank += 1
                if kind < 2:
                    art = AR0 if hf == 0 else AR1
                    dst = art
                    dap = art[:].rearrange("p (q a t) -> p q a t", a=2, t=128)[:, :, kind, :]
                else:
                    dst = BT if kind == 2 else KT
                    dap = dst[:, hf * 512:(hf + 1) * 512].rearrange("p (q t) -> p q t", t=128)
                for q in range(4):
                    P.op("pe", lambda e, q=q, psa=psa, src=src, hf=hf: e.transpose(
                        psa[:, q * 128:(q + 1) * 128], src[:, (hf * 4 + q) * 128:(hf * 4 + q + 1) * 128], self.ident[:]),
                        [src, self.ident], [psa], inc=(q == 3))
                eng = "act" if hf == 0 else "dve"
                if eng == "act":
                    P.op("act", lambda e, dap=dap, psa=psa: e.activation(
                        out=dap, in_=psa[:, :].rearrange("p (q t) -> p q t", t=128), func=AF.Copy), [psa], [dst])
                else:
                    P.op("dve", lambda e, dap=dap, psa=psa: e.tensor_copy(
                        out=dap, in_=psa[:, :].rearrange("p (q t) -> p q t", t=128)), [psa], [dst])

        def ar(p, a, rows):
            art = AR0 if p < 4 else AR1
            return art, art[:].rearrange("p (q a t) -> p q a t", a=2, t=128)[rows, p % 4, a, :]

        def ar2(p, rows):
            art = AR0 if p < 4 else AR1
            return art, art[rows, (p % 4) * 256:(p % 4 + 1) * 256]

        Yt = T4[18]
        for g in range(4):
            CH = [T4[19], T4[20]]
            MR, MA = T4[21], T4[22]
            chv = [t[:].rearrange("p (x h t) -> p x h t", x=2, t=128) for t in CH]
            mrv = MR[:].rearrange("p (x h t) -> p x h t", x=2, t=128)
            makv = MA[:, 0:512].rearrange("p (h t) -> p h t", t=128)
            Wt = MA[:, 512:768]
            for pp in range(2):
                p = g * 2 + pp
                pm1, pm2 = PS[pp], PS[2 + pp]
                for jj in range(2):
                    rows = slice(64 * jj, 64 * jj + 64)
                    ab, a2 = ar2(p, rows)
                    bt_ap = BT[rows, p * 128:(p + 1) * 128]
                    kt_ap = KT[rows, p * 128:(p + 1) * 128]
                    P.op("pe", lambda e, pm1=pm1, jj=jj, bt_ap=bt_ap, a2=a2: e.matmul(
                        pm1[:, jj * 256:(jj + 1) * 256], lhsT=bt_ap, rhs=a2, start=True, stop=True),
                        [BT, ab], [pm1], inc=(jj == 1))
                    P.op("pe", lambda e, pm2=pm2, jj=jj, kt_ap=kt_ap, a2=a2: e.matmul(
                        pm2[:, jj * 256:(jj + 1) * 256], lhsT=kt_ap, rhs=a2, start=True, stop=True),
                        [KT, ab], [pm2], inc=(jj == 1))
                    ab0, a0 = ar(p, 0, rows)
                    hh = pp * 2 + jj
                    P.op("pe", lambda e, hh=hh, a0=a0, bt_ap=bt_ap: e.matmul(
                        PS[4][:, hh * 128:(hh + 1) * 128], lhsT=a0, rhs=bt_ap, start=True, stop=True),
                        [ab0, BT], [PS[4]], inc=(hh == 3))
                pm1v = pm1[:, :].rearrange("p (h a t) -> p h a t", a=2, t=128)
                pm2v = pm2[:, :].rearrange("p (h a t) -> p h a t", a=2, t=128)
                mkv = mk1[:].rearrange("p (h a t) -> p h a t", a=2, t=128)
                hs = slice(pp * 2, pp * 2 + 2)
                P.op("dve", lambda e, pm1v=pm1v, hs=hs: e.tensor_tensor(
                    out=chv[0][:, 1, hs, :], in0=pm1v[:, :, 0, :], in1=mkv[:, :, 0, :], op=ALU.mult), [pm1, mk1], [CH[0]])
                P.op("dve", lambda e, pm1v=pm1v, hs=hs: e.tensor_tensor(
                    out=mrv[:, 0, hs, :], in0=pm1v[:, :, 1, :], in1=mkv[:, :, 1, :], op=ALU.mult), [pm1, mk1], [MR])
                P.op("dve", lambda e, pm2v=pm2v, hs=hs: e.tensor_tensor(
                    out=makv[:, hs, :], in0=pm2v[:, :, 0, :], in1=mkv[:, :, 0, :], op=ALU.mult), [pm2, mk1], [MA])
                P.op("dve", lambda e, pm2v=pm2v, hs=hs: e.tensor_tensor(
                    out=mrv[:, 1, hs, :], in0=pm2v[:, :, 1, :], in1=mkv[:, :, 1, :], op=ALU.mult), [pm2, mk1], [MR])
            P.op("dve", lambda e: e.tensor_tensor(out=chv[0][:, 0, :, :].rearrange("p h t -> p (h t)"),
                                                  in0=PS[4][:, :], in1=mk3[:], op=ALU.mult), [PS[4], mk3], [CH[0]])
            for hh in range(4):
                p = g * 2 + hh // 2
                rows = slice(64 * (hh % 2), 64 * (hh % 2) + 64)
                h = g * 4 + hh
                ab0, a0 = ar(p, 0, rows)
                P.op("pe", lambda e, hh=hh, a0=a0, rows=rows, p=p: e.matmul(
                    PS[7][:, hh * 64:(hh + 1) * 64], lhsT=a0, rhs=stv[rows, p, :], start=True, stop=False),
                    [ab0, ST], [PS[7]], inc=False)
                P.op("pe", lambda e, hh=hh, h=h: e.matmul(
                    PS[7][:, hh * 64:(hh + 1) * 64], lhsT=makv[:, hh, :], rhs=Lv[:, h * 64:(h + 1) * 64],
                    start=False, stop=True), [MA, Lv], [PS[7]], inc=(hh == 3))
            P.op("act", lambda e: e.activation(out=Wt, in_=PS[7][:, 0:256], func=AF.Copy), [PS[7]], [MA])
            for rd in range(7):
                cur, nxt = chv[rd % 2], chv[(rd + 1) % 2]
                cb, nb = CH[rd % 2], CH[(rd + 1) % 2]
                for hh in range(4):
                    P.op("pe", lambda e, hh=hh, cur=cur: e.matmul(
                        PS[7][:, hh * 64:(hh + 1) * 64], lhsT=cur[:, 1, hh, :], rhs=Wt[:, hh * 64:(hh + 1) * 64],
                        start=True, stop=True), [cb, MA], [PS[7]], inc=(hh == 3))
                P.op("dve", lambda e: e.tensor_tensor(out=Wt, in0=Wt, in1=PS[7][:, 0:256], op=ALU.add), [MA, PS[7]], [MA])
                if rd < 6:
                    for hh in range(4):
                        P.op("pe", lambda e, hh=hh, cur=cur: e.matmul(
                            PS[5][:, hh * 128:(hh + 1) * 128], lhsT=cur[:, 1, hh, :], rhs=cur[:, 0, hh, :],
                            start=True, stop=True), [cb], [PS[5]], inc=(hh == 3))
                    for hh in range(4):
                        P.op("pe", lambda e, hh=hh, cur=cur: e.matmul(
                            PS[6][:, hh * 128:(hh + 1) * 128], lhsT=cur[:, 0, hh, :], rhs=cur[:, 1, hh, :],
                            start=True, stop=True), [cb], [PS[6]], inc=(hh == 3))
                    P.op("act", lambda e, nxt=nxt: e.activation(
                        out=nxt[:, 0, :, :].rearrange("p h t -> p (h t)"), in_=PS[5][:, :], func=AF.Copy), [PS[5]], [nb])
                    P.op("dve", lambda e, nxt=nxt: e.tensor_copy(
                        out=nxt[:, 1, :, :].rearrange("p h t -> p (h t)"), in_=PS[6][:, :]), [PS[6]], [nb])
            for hh in range(4):
                p = g * 2 + hh // 2
                rows = slice(64 * (hh % 2), 64 * (hh % 2) + 64)
                h = g * 4 + hh
                ab1, a1 = ar(p, 1, rows)
                P.op("pe", lambda e, hh=hh, a1=a1, rows=rows, p=p: e.matmul(
                    PS[4][:, hh * 64:(hh + 1) * 64], lhsT=a1, rhs=stv[rows, p, :], start=True, stop=False),
                    [ab1, ST], [PS[4]], inc=False)
                P.op("pe", lambda e, hh=hh: e.matmul(
                    PS[4][:, hh * 64:(hh + 1) * 64], lhsT=mrv[:, 0, hh, :], rhs=Wt[:, hh * 64:(hh + 1) * 64],
                    start=False, stop=False), [MR, MA], [PS[4]], inc=False)
                P.op("pe", lambda e, hh=hh, h=h: e.matmul(
                    PS[4][:, hh * 64:(hh + 1) * 64], lhsT=mrv[:, 1, hh, :], rhs=Lv[:, h * 64:(h + 1) * 64],
                    start=False, stop=True), [MR, Lv], [PS[4]], inc=(hh == 3))
            P.op("act", lambda e, g=g: e.activation(out=Yt[:, g * 256:(g + 1) * 256], in_=PS[4][:, 0:256], func=AF.Copy),
                 [PS[4]], [Yt])
            for pp in range(2):
                p = g * 2 + pp
                P.op("pe", lambda e, pp=pp, p=p: e.matmul(
                    PS[0][:, pp * 128:(pp + 1) * 128], lhsT=Bh[:, p * 128:(p + 1) * 128],
                    rhs=Wt[:, pp * 128:(pp + 1) * 128], start=True, stop=False), [Bh, MA], [PS[0]], inc=False)
                P.op("pe", lambda e, pp=pp, p=p: e.matmul(
                    PS[0][:, pp * 128:(pp + 1) * 128], lhsT=Kh[:, p * 128:(p + 1) * 128],
                    rhs=Lv[:, p * 128:(p + 1) * 128], start=False, stop=True), [Kh, Lv], [PS[0]], inc=(pp == 1))
            for pp in range(2):
                p = g * 2 + pp
                for jj in range(2):
                    rows = slice(64 * jj, 64 * jj + 64)
                    P.op("dve", lambda e, rows=rows, p=p, pp=pp, jj=jj: e.scalar_tensor_tensor(
                        out=stv[rows, p, :], in0=stv[rows, p, :], scalar=gC[rows, p:p + 1],
                        in1=PS[0][rows, pp * 128 + jj * 64:pp * 128 + jj * 64 + 64], op0=ALU.mult, op1=ALU.add),
                        [ST, gC, PS[0]], [ST])
        P.dma("sp", SY.ap()[r0:r0 + 128, :], Yt[:], reads=[Yt], writes=[SY])


def rwkv_post(self, i, Xin, Xout):
    P = self.P
    j = i // 3
    I = self.I
    T4 = self.T4
    wres = T4[0:8]
    self.load_wres(I["ra_w_out"], I["ra_w_out"].ap()[j], wres)
    lng, lnb = T4[8], T4[9]
    self.bcast_load(lng, lng[:], I["ra_lnx_g"], I["ra_lnx_g"].ap()[j])
    self.bcast_load(lnb, lnb[:], I["ra_lnx_b"], I["ra_lnx_b"].ap()[j])
    SY0, SY1, SV, SG_, SRKS = (self.scratch("y0"), self.scratch("y1"), self.scratch("v"), self.scratch("g"),
                               self.scratch("rks", 16))
    t_start = 2 if i == DEPTH - 1 else 0
    for t in range(t_start, self.nt):
        par = t % 2
        r0 = t * 128
        y, y1, vt, gt_, sq = T4[10 + par * 5:15 + par * 5]
        rks = self.small("post_rks%d" % par, [128, 16])
        mu = self.small("post_mu%d" % par, [128, 16])
        var = self.small("post_var%d" % par, [128, 16])
        P.dma("sp", y[:], SY0.ap()[r0:r0 + 128, :], reads=[SY0], writes=[y])
        P.dma("pool", y1[:], SY1.ap()[r0:r0 + 128, :], reads=[SY1], writes=[y1])
        P.dma("sp", vt[:], SV.ap()[r0:r0 + 128, :], reads=[SV], writes=[vt])
        P.dma("pool", gt_[:], SG_.ap()[r0:r0 + 128, :], reads=[SG_], writes=[gt_])
        P.dma("sp", rks[:], SRKS.ap()[r0:r0 + 128, :], reads=[SRKS], writes=[rks])
        v3 = lambda tl: tl[:].rearrange("p (h k) -> p h k", k=64)
        bc = lambda sm_: sm_[:].unsqueeze(2).to_broadcast([128, 16, 64])
        P.op("dve", lambda e: e.tensor_tensor(out=y[:], in0=y[:], in1=y1[:], op=ALU.add), [y, y1], [y])
        P.op("dve", lambda e: e.tensor_reduce(out=mu[:], in_=v3(y), axis=AX.X, op=ALU.add), [y], [mu])
        P.op("dve", lambda e: e.tensor_scalar(out=mu[:], in0=mu[:], scalar1=1.0 / 64, scalar2=None, op0=ALU.mult), [mu], [mu])
        P.op("dve", lambda e: e.tensor_tensor(out=v3(y), in0=v3(y), in1=bc(mu), op=ALU.subtract), [y, mu], [y])
        P.op("act", lambda e: e.activation(out=sq[:], in_=y[:], func=AF.Square), [y], [sq])
        P.op("dve", lambda e: e.tensor_reduce(out=var[:], in_=v3(sq), axis=AX.X, op=ALU.add), [sq], [var])
        P.op("dve", lambda e: e.tensor_scalar(out=var[:], in0=var[:], scalar1=1.0 / 64, scalar2=6.4e-4, op0=ALU.mult,
                                              op1=ALU.add), [var], [var])
        P.op("act", lambda e: e.activation(out=var[:], in_=var[:], func=AF.Sqrt), [var], [var])
        P.op("dve", lambda e: e.reciprocal(out=var[:], in_=var[:]), [var], [var])
        P.op("dve", lambda e: e.tensor_tensor(out=v3(y), in0=v3(y), in1=bc(var), op=ALU.mult), [y, var], [y])
        P.op("pool", lambda e: e.tensor_tensor(out=y[:], in0=y[:], in1=lng[:], op=ALU.mult), [y, lng], [y])
        P.op("pool", lambda e: e.tensor_tensor(out=y[:], in0=y[:], in1=lnb[:], op=ALU.add), [y, lnb], [y])
        P.op("dve", lambda e: e.tensor_tensor(out=v3(vt), in0=v3(vt), in1=bc(rks), op=ALU.mult), [vt, rks], [vt])
        P.op("dve", lambda e: e.tensor_tensor(out=y[:], in0=y[:], in1=vt[:], op=ALU.add), [y, vt], [y])
        P.op("dve", lambda e: e.tensor_tensor(out=y[:], in0=y[:], in1=gt_[:], op=ALU.mult), [y, gt_], [y])
        zT = T4[20 + par]
        for hf in range(2):
            self.transpose_to(y, [y[:, (hf * 4 + q) * 128:(hf * 4 + q + 1) * 128] for q in range(4)],
                              zT, [(zT[:, hf * 512:(hf + 1) * 512], 0, 512)], self.PS[par * 2 + hf],
                              evac="act" if hf == 0 else "dve")
        self.proj_residual([(zT, zT[:, kc * 128:(kc + 1) * 128]) for kc in range(8)], wres, Xin, Xout, t, par)


def rwkv(self, i, Xin, Xout):
    ph = self.cfg.get("ra_phases", "pre,scan0,scan1,post")
    self.rwkv_consts()
    if "pre" in ph:
        self.rwkv_pre(i, Xin)
    for d in range(2):
        if "scan%d" % d in ph:
            self.rwkv_scan(d)
    if "post" in ph:
        self.rwkv_post(i, Xin, Xout)


K.rwkv_consts = rwkv_consts
K.rwkv_scan = rwkv_scan
K.rwkv_post = rwkv_post
K.rwkv = rwkv


def tm_pass(self, wbuf, wap2d, col0, ncols, bias_buf, bias_ap, dst, post=None):
    P = self.P
    T4 = self.T4
    wres = T4[0:8]
    for kc in range(8):
        P.dma("sp" if kc % 2 == 0 else "pool", wres[kc][:, 0:ncols], wap2d[kc * 128:(kc + 1) * 128, col0:col0 + ncols],
              reads=[wbuf], writes=[wres[kc]])
    bt = None
    if bias_buf is not None:
        bt = T4[24]
        self.bcast_load(bt, bt[:, 0:ncols], bias_buf, bias_ap)
    nb = (ncols + 511) // 512
    for jb in range(self.nblk):
        hv = self.load_hTe(jb, T4[8 + (jb % 2) * 3:11 + (jb % 2) * 3])
        for mt in range(2):
            ot = T4[20 + mt]
            for b in range(nb):
                w = min(512, ncols - b * 512)
                pb = self.PS[(mt * 2 + b) % 4]
                for kc in range(8):
                    hb, ha = hv(kc)
                    P.op("pe", lambda e, kc=kc, ha=ha, pb=pb, b=b, w=w, mt=mt: e.matmul(
                        pb[:, 0:w], lhsT=ha[:, 1 + mt * 128:1 + (mt + 1) * 128], rhs=wres[kc][:, b * 512:b * 512 + w],
                        start=(kc == 0), stop=(kc == 7)), [hb, wres[kc]], [pb], inc=(kc == 7))
                if bt is not None:
                    P.op("dve", lambda e, ot=ot, pb=pb, b=b, w=w: e.tensor_tensor(
                        out=ot[:, b * 512:b * 512 + w], in0=pb[:, 0:w], in1=bt[:, b * 512:b * 512 + w], op=ALU.add),
                        [pb, bt], [ot])
                else:
                    P.op("act", lambda e, ot=ot, pb=pb, b=b, w=w: e.activation(
                        out=ot[:, b * 512:b * 512 + w], in_=pb[:, 0:w], func=AF.Copy), [pb], [ot])
            if post is not None:
                post(ot, jb, mt)
            r0 = jb * 256 + mt * 128
            if dst is not None:
                P.dma("pool", dst.ap()[r0:r0 + 128, 0:ncols], ot[:, 0:ncols], reads=[ot], writes=[dst])


def fm_pass(self, wbuf, wap2d, col0, nchunks, bias_cols, scales, dsts):
    P = self.P
    T4 = self.T4
    wres = T4[0:8]
    ncols = nchunks * 128
    for kc in range(8):
        P.dma("sp" if kc % 2 == 0 else "pool", wres[kc][:, 0:ncols], wap2d[kc * 128:(kc + 1) * 128, col0:col0 + ncols],
              reads=[wbuf], writes=[wres[kc]])
    for jb in range(self.nblk):
        hv = self.load_hTe(jb, T4[8 + (jb % 2) * 3:11 + (jb % 2) * 3])
        for hc in range(nchunks):
            pb = self.PS[hc % 4]
            for kc in range(8):
                hb, ha = hv(kc)
                P.op("pe", lambda e, kc=kc, ha=ha, pb=pb, hc=hc: e.matmul(
                    pb[:, 0:256], lhsT=wres[kc][:, hc * 128:(hc + 1) * 128], rhs=ha[:, 1:257],
                    start=(kc == 0), stop=(kc == 7)), [hb, wres[kc]], [pb], inc=(kc == 7))
            ot = T4[20 + hc % 4]
            if bias_cols is not None:
                P.op("act", lambda e, ot=ot, pb=pb, hc=hc: e.activation(
                    out=ot[:, 0:256], in_=pb[:, 0:256], func=AF.Identity, scale=scales[hc],
                    bias=bias_cols[:, hc:hc + 1]), [pb, bias_cols], [ot])
            else:
                P.op("act", lambda e, ot=ot, pb=pb, hc=hc: e.activation(
                    out=ot[:, 0:256], in_=pb[:, 0:256], func=AF.Copy), [pb], [ot])
            dbuf, row0 = dsts[hc]
            P.dma("pool", dbuf.ap()[row0:row0 + 128, jb * 256:(jb + 1) * 256], ot[:, 0:256], reads=[ot], writes=[dbuf])


K.tm_pass = tm_pass
K.fm_pass = fm_pass


def fscratch(self, name, rows):
    if name not in self.S:
        self.S[name] = self.P.dram("S_" + name, [rows, self.ntok])
    return self.S[name]


def mlstm_pre(self, i, Xin):
    P = self.P
    I = self.I
    self.norm_to_HT(Xin, 1, 0)
    w = I["ml_w_in"].ap()[0]
    b = I["ml_b_in"]
    QT, KT = self.fscratch("mlqT", 512), self.fscratch("mlkT", 512)
    MLA, MLB, MLC = self.scratch("mla"), self.scratch("mlb"), self.scratch("mlc")
    bc = self.small("ml_bc", [128, 8])
    self.load_cols(bc, bc[:], b, b.ap()[0, 0:1024].rearrange("(hc p) -> hc p", p=128), 8)
    sk = 128 ** -0.5
    P.op("dve", lambda e: e.tensor_scalar(out=bc[:, 4:8], in0=bc[:, 4:8], scalar1=sk, scalar2=None, op0=ALU.mult), [bc], [bc])
    self.fm_pass(I["ml_w_in"], w, 0, 8, bc, [1.0] * 4 + [sk] * 4,
                 [(QT, h * 128) for h in range(4)] + [(KT, h * 128) for h in range(4)])

    def scale_k(ot, jb, mt):
        P.op("dve", lambda e: e.tensor_scalar(out=ot[:, 0:512], in0=ot[:, 0:512], scalar1=sk, scalar2=None, op0=ALU.mult),
             [ot], [ot])
    self.tm_pass(I["ml_w_in"], w, 512, 1024, b, b.ap()[0, 512:1536], MLA, post=scale_k)
    self.tm_pass(I["ml_w_in"], w, 1536, 1024, b, b.ap()[0, 1536:2560], MLB)
    self.tm_pass(I["ml_w_in"], w, 2560, 528, b, b.ap()[0, 2560:3088], MLC)


def mlstm_scan(self, d):
    P = self.P
    T4 = self.T4
    PS = self.PS
    QT, KT = self.fscratch("mlqT", 512), self.fscratch("mlkT", 512)
    MLA, MLB, MLC = self.scratch("mla"), self.scratch("mlb"), self.scratch("mlc")
    HS = self.scratch("mlh%d" % d)
    tri = self.up if d == 0 else self.lo
    ecol_i = 127 if d == 0 else 0
    CN = [T4[28], T4[29]]
    V1 = [T4[26], T4[27]]
    cnv = [t[:, 0:514].rearrange("p (h v) -> p h v", v=257) for t in CN]
    v1v = [t[:, 0:514].rearrange("p (h v) -> p h v", v=257) for t in V1]
    for t in CN:
        P.op("pool", lambda e, t=t: e.memset(t[:], 0.0), [], [t])
    for t in V1:
        P.op("pool", lambda e, t=t: e.memset(t[:], 1.0), [], [t])
    order = [0, 1] + list(range(2, self.nt)) if d == 0 else [1, 0] + list(range(self.nt - 1, 1, -1))
    for c in order:
        r0 = c * 128
        qk = T4[0]
        qkv = qk[:].rearrange("p (x h t) -> p x h t", x=2, t=128)
        la, lb = T4[1], T4[2]
        gts = self.small("ml_g", [128, 16])
        lfa = self.small("ml_lf", [128, 16])
        P.dma("sp", qkv[:, 0, :, :], QT.ap().rearrange("(h p) c -> p h c", p=128)[:, :, r0:r0 + 128], reads=[QT], writes=[qk])
        P.dma("pool", qkv[:, 1, :, :], KT.ap().rearrange("(h p) c -> p h c", p=128)[:, :, r0:r0 + 128], reads=[KT], writes=[qk])
        P.dma("sp", la[:], MLA.ap()[r0:r0 + 128, :], reads=[MLA], writes=[la])
        P.dma("pool", lb[:, 0:512], MLB.ap()[r0:r0 + 128, 0:512], reads=[MLB], writes=[lb])
        P.dma("sp", gts[:], MLC.ap()[r0:r0 + 128, 512:528], reads=[MLC], writes=[gts])
        P.op("act", lambda e: e.activation(out=lfa[:], in_=gts[:], func=AF.Sigmoid), [gts], [lfa])
        P.op("act", lambda e: e.activation(out=lfa[:], in_=lfa[:], func=AF.Ln), [lfa], [lfa])
        P.op("dve", lambda e: e.tensor_copy(out=v1v[0][:, :, 0:256], in_=la[:, 512:1024].rearrange("p (h v) -> p h v", v=256)),
             [la], [V1[0]])
        P.op("dve", lambda e: e.tensor_copy(out=v1v[1][:, :, 0:256], in_=lb[:, 0:512].rearrange("p (h v) -> p h v", v=256)),
             [lb], [V1[1]])
        hout = T4[3]
        for h in range(4):
            par = h % 2
            lfcol = lfa[:, d * 8 + 4 + h:d * 8 + 5 + h]
            licol = gts[:, d * 8 + h:d * 8 + h + 1]
            LFB, Dm, EB, PT, QS, KH = T4[4 + par * 6:10 + par * 6]
            sm = self.small("ml_sm%d" % par, [128, 4])
            psb, psq, pso, pss = PS[par * 4:par * 4 + 4]
            P.op("dve", lambda e: e.tensor_scalar(out=LFB[:, 0:128], in0=self.ones[:], scalar1=lfcol, scalar2=None,
                                                  op0=ALU.mult), [self.ones, lfa], [LFB])
            P.op("pe", lambda e: e.matmul(psb[:, 0:128], lhsT=LFB[:, 0:128], rhs=tri[:], start=True, stop=True),
                 [LFB, tri], [psb], inc=False)
            P.op("pe", lambda e: e.matmul(psb[:, 128:129], lhsT=tri[:], rhs=lfcol, start=True, stop=True),
                 [tri, lfa], [psb])
            P.op("dve", lambda e: e.tensor_tensor(out=sm[:, 0:1], in0=licol, in1=psb[:, 128:129], op=ALU.subtract),
                 [gts, psb], [sm])
            P.op("act", lambda e: e.activation(out=Dm[:, 0:128], in_=psb[:, 0:128], func=AF.Exp, bias=sm[:, 0:1]),
                 [psb, sm], [Dm])
            P.op("dve", lambda e: e.tensor_tensor(out=Dm[:, 0:128], in0=Dm[:, 0:128], in1=tri[:], op=ALU.mult), [Dm, tri], [Dm])
            P.op("act", lambda e: e.activation(out=EB[:, 0:128], in_=psb[:, 0:128], func=AF.Exp), [psb], [EB])
            P.op("pe", lambda e: e.matmul(psq[:, 0:128], lhsT=qkv[:, 1, h, :], rhs=qkv[:, 0, h, :], start=True, stop=True),
                 [qk], [psq])
            P.op("dve", lambda e: e.tensor_tensor(out=PT[:, 0:128], in0=Dm[:, 0:128], in1=psq[:, 0:128], op=ALU.mult),
                 [Dm, psq], [PT])
            P.op("pool", lambda e: e.tensor_tensor(out=QS[:, 0:128], in0=qkv[:, 0, h, :], in1=EB[:, 0:128], op=ALU.mult),
                 [qk, EB], [QS])
            vv = v1v[h // 2][:, h % 2, :]
            cv = cnv[h // 2][:, h % 2, :]
            P.op("pe", lambda e: e.matmul(pso[:, 0:257], lhsT=PT[:, 0:128], rhs=vv, start=True, stop=False),
                 [PT, V1[h // 2]], [pso], inc=False)
            P.op("pe", lambda e: e.matmul(pso[:, 0:257], lhsT=QS[:, 0:128], rhs=cv, start=False, stop=True),
                 [QS, CN[h // 2]], [pso])
            P.op("dve", lambda e: e.tensor_copy(out=sm[:, 3:4], in_=pso[:, 256:257]), [pso], [sm])
            P.op("dve", lambda e: e.scalar_tensor_tensor(out=sm[:, 1:2], in0=sm[:, 3:4], scalar=-1.0,
                                                         in1=sm[:, 3:4], op0=ALU.mult, op1=ALU.max), [sm], [sm])
            P.op("dve", lambda e: e.tensor_scalar_max(out=sm[:, 1:2], in0=sm[:, 1:2], scalar1=1.0), [sm], [sm])
            P.op("dve", lambda e: e.reciprocal(out=sm[:, 1:2], in_=sm[:, 1:2]), [sm], [sm])
            P.op("dve", lambda e: e.tensor_scalar(out=hout[:, h * 256:(h + 1) * 256], in0=pso[:, 0:256], scalar1=sm[:, 1:2],
                                                  scalar2=None, op0=ALU.mult), [pso, sm], [hout])
            P.op("act", lambda e: e.activation(out=sm[:, 2:3], in_=psb[:, ecol_i:ecol_i + 1], func=AF.Exp, bias=sm[:, 0:1]),
                 [psb, sm], [sm])
            P.op("dve", lambda e: e.tensor_scalar(out=KH[:, 0:128], in0=la[:, h * 128:(h + 1) * 128], scalar1=sm[:, 2:3],
                                                  scalar2=None, op0=ALU.mult), [la, sm], [KH])
            P.op("pe", lambda e: e.matmul(pss[:, 0:257], lhsT=KH[:, 0:128], rhs=vv, start=True, stop=True),
                 [KH, V1[h // 2]], [pss])
            P.op("dve", lambda e: e.scalar_tensor_tensor(out=cv, in0=cv, scalar=EB[:, ecol_i:ecol_i + 1], in1=pss[:, 0:257],
                                                         op0=ALU.mult, op1=ALU.add), [CN[h // 2], EB, pss], [CN[h // 2]])
        P.dma("sp", HS.ap()[r0:r0 + 128, :], hout[:], reads=[hout], writes=[HS])


def mlstm_post(self, i, Xin, Xout):
    P = self.P
    I = self.I
    T4 = self.T4
    wres = T4[0:8]
    self.load_wres(I["ml_w_out"], I["ml_w_out"].ap()[0], wres)
    ng = T4[8]
    self.bcast_load(ng, ng[:], I["ml_norm_g"], I["ml_norm_g"].ap()[0])
    HS0, HS1, MLB, MLC = self.scratch("mlh0"), self.scratch("mlh1"), self.scratch("mlb"), self.scratch("mlc")
    t_start = 2 if i == DEPTH - 1 else 0
    for t in range(t_start, self.nt):
        par = t % 2
        r0 = t * 128
        y, y1, ot, sq = T4[10 + par * 4:14 + par * 4]
        ss = self.small("mlp_ss%d" % par, [128, 4])
        P.dma("sp", y[:], HS0.ap()[r0:r0 + 128, :], reads=[HS0], writes=[y])
        P.dma("pool", y1[:], HS1.ap()[r0:r0 + 128, :], reads=[HS1], writes=[y1])
        P.dma("sp", ot[:, 0:512], MLB.ap()[r0:r0 + 128, 512:1024], reads=[MLB], writes=[ot])
        P.dma("pool", ot[:, 512:1024], MLC.ap()[r0:r0 + 128, 0:512], reads=[MLC], writes=[ot])
        v3 = lambda tl: tl[:].rearrange("p (h k) -> p h k", k=256)
        P.op("dve", lambda e: e.tensor_tensor(out=y[:], in0=y[:], in1=y1[:], op=ALU.add), [y, y1], [y])
        P.op("act", lambda e: e.activation(out=sq[:], in_=y[:], func=AF.Square), [y], [sq])
        P.op("dve", lambda e: e.tensor_reduce(out=ss[:], in_=v3(sq), axis=AX.X, op=ALU.add), [sq], [ss])
        P.op("dve", lambda e: e.tensor_scalar(out=ss[:], in0=ss[:], scalar1=1.0 / 256, scalar2=EPS, op0=ALU.mult,
                                              op1=ALU.add), [ss], [ss])
        P.op("act", lambda e: e.activation(out=ss[:], in_=ss[:], func=AF.Sqrt), [ss], [ss])
        P.op("dve", lambda e: e.reciprocal(out=ss[:], in_=ss[:]), [ss], [ss])
        P.op("dve", lambda e: e.tensor_tensor(out=v3(y), in0=v3(y), in1=ss[:].unsqueeze(2).to_broadcast([128, 4, 256]),
                                              op=ALU.mult), [y, ss], [y])
        P.op("act", lambda e: e.activation(out=ot[:], in_=ot[:], func=AF.Sigmoid), [ot], [ot])
        P.op("pool", lambda e: e.tensor_tensor(out=y[:], in0=y[:], in1=ng[:], op=ALU.mult), [y, ng], [y])
        P.op("dve", lambda e: e.tensor_tensor(out=y[:], in0=y[:], in1=ot[:], op=ALU.mult), [y, ot], [y])
        zT = T4[20 + par]
        for hf in range(2):
            self.transpose_to(y, [y[:, (hf * 4 + q) * 128:(hf * 4 + q + 1) * 128] for q in range(4)],
                              zT, [(zT[:, hf * 512:(hf + 1) * 512], 0, 512)], self.PS[par * 2 + hf],
                              evac="act" if hf == 0 else "dve")
        self.proj_residual([(zT, zT[:, kc * 128:(kc + 1) * 128]) for kc in range(8)], wres, Xin, Xout, t, par)


def mlstm(self, i, Xin, Xout):
    self.mlstm_pre(i, Xin)
    for d in range(2):
        self.mlstm_scan(d)
    self.mlstm_post(i, Xin, Xout)


K.fscratch = fscratch
K.mlstm_pre = mlstm_pre
K.mlstm_scan = mlstm_scan
K.mlstm_post = mlstm_post
K.mlstm = mlstm


I32 = mybir.dt.int32


def da_rope_tables(self):
    P = self.P
    T4 = self.T4
    nlt = self.seq // 128
    pi_ = self.small("rp_pi", [128, 1], I32); ti = self.small("rp_ti", [128, 1], I32)
    colv = self.small("rp_col", [128, 1]); rowp = self.small("rp_row", [128, 1])
    ji = self.small("rp_ji", [128, 16], I32); inv = self.small("rp_inv", [128, 16]); angc = self.small("rp_angc", [128, 16])
    taui = self.small("rp_taui", [128, 64], I32); rowf = self.small("rp_rowf", [128, 64])
    P.op("pool", lambda e: e.iota(pi_[:], [[0, 1]], base=0, channel_multiplier=1), [], [pi_])
    P.op("pool", lambda e: e.iota(ji[:], [[1, 16]], base=0, channel_multiplier=0), [], [ji])
    P.op("pool", lambda e: e.iota(taui[:], [[2, 64]], base=0, channel_multiplier=0), [], [taui])
    P.op("dve", lambda e: e.tensor_single_scalar(out=ti[:], in_=pi_[:], scalar=63, op=ALU.bitwise_and), [pi_], [ti])
    P.op("dve", lambda e: e.tensor_copy(out=colv[:], in_=ti[:]), [ti], [colv])
    P.op("dve", lambda e: e.tensor_single_scalar(out=ti[:], in_=pi_[:], scalar=6, op=ALU.arith_shift_right), [pi_, colv], [ti])
    P.op("dve", lambda e: e.tensor_copy(out=rowp[:], in_=ti[:]), [ti], [rowp])
    P.op("dve", lambda e: e.tensor_copy(out=inv[:], in_=ji[:]), [ji], [inv])
    P.op("act", lambda e: e.activation(out=inv[:], in_=inv[:], func=AF.Exp, scale=-math.log(10000.0) / 16), [inv], [inv])
    P.op("dve", lambda e: e.tensor_copy(out=rowf[:], in_=taui[:]), [taui], [rowf])
    P.op("dve", lambda e: e.tensor_scalar(out=rowf[:], in0=rowf[:], scalar1=rowp[:, 0:1], scalar2=None, op0=ALU.add),
         [rowf, rowp], [rowf])
    P.op("dve", lambda e: e.tensor_scalar(out=angc[:], in0=inv[:], scalar1=colv[:, 0:1], scalar2=None, op0=ALU.mult),
         [inv, colv], [angc])
    self.COS = [T4[25], T4[26]]
    self.SIN = [T4[27], T4[28]]
    ang, kf, tmp = T4[14], T4[15], T4[16]
    ki = self.P.sb("rp_ki", [128, 1024], I32)
    for half in range(2):
        n = min(32, nlt - half * 32)
        if n <= 0:
            break
        w = n * 32
        av = ang[:, 0:w].rearrange("p (t j) -> p t j", j=32)
        for tt in range(n):
            P.op("dve", lambda e, tt=tt: e.tensor_scalar(
                out=av[:, tt, 0:16], in0=inv[:], scalar1=rowf[:, half * 32 + tt:half * 32 + tt + 1], scalar2=None,
                op0=ALU.mult), [inv, rowf], [ang])
        P.op("dve", lambda e: e.tensor_copy(out=av[:, :, 16:32], in_=angc[:].unsqueeze(1).to_broadcast([128, n, 16])),
             [angc], [ang])
        for (dst, shift) in ((self.SIN[half], 0.0), (self.COS[half], math.pi / 2)):
            P.op("dve", lambda e: e.tensor_scalar(out=tmp[:, 0:w], in0=ang[:, 0:w], scalar1=shift, scalar2=None, op0=ALU.add),
                 [ang], [tmp])
            P.op("dve", lambda e: e.tensor_scalar(out=kf[:, 0:w], in0=tmp[:, 0:w], scalar1=1.0 / (2 * math.pi), scalar2=None,
                                                  op0=ALU.mult), [tmp], [kf])
            P.op("dve", lambda e: e.tensor_copy(out=ki[:, 0:w], in_=kf[:, 0:w]), [kf], [ki])
            P.op("dve", lambda e: e.tensor_copy(out=kf[:, 0:w], in_=ki[:, 0:w]), [ki], [kf])
            P.op("dve", lambda e: e.scalar_tensor_tensor(out=tmp[:, 0:w], in0=kf[:, 0:w], scalar=-2 * math.pi,
                                                         in1=tmp[:, 0:w], op0=ALU.mult, op1=ALU.add), [kf, tmp], [tmp])
            P.op("dve", lambda e: e.tensor_scalar(out=kf[:, 0:w], in0=tmp[:, 0:w], scalar1=math.pi, scalar2=None,
                                                  op0=ALU.is_gt), [tmp], [kf])
            P.op("dve", lambda e: e.scalar_tensor_tensor(out=tmp[:, 0:w], in0=kf[:, 0:w], scalar=-2 * math.pi,
                                                         in1=tmp[:, 0:w], op0=ALU.mult, op1=ALU.add), [kf, tmp], [tmp])
            P.op("act", lambda e, dst=dst: e.activation(out=dst[:, 0:w], in_=tmp[:, 0:w], func=AF.Sin), [tmp], [dst])


def da_pre(self, i, Xin):
    P = self.P
    I = self.I
    T4 = self.T4
    self.norm_to_HT(Xin, 1, 0)
    self.da_rope_tables()
    w = I["da_w_qkv"].ap()[0]
    QT, KT = self.fscratch("daqT", 1024), self.fscratch("dakT", 1024)
    V = self.scratch("dav")

    def mk_post(dstT):
        def post(ot, jb, mt):
            tau = jb * 2 + mt - 2
            src = ot
            if tau >= 0:
                t1, t2, rt = T4[14], T4[15], T4[16]
                ov = ot[:].rearrange("p (h x j) -> p h x j", x=2, j=32)
                rv = rt[:].rearrange("p (h x j) -> p h x j", x=2, j=32)
                t1v = t1[:, 0:512].rearrange("p (h j) -> p h j", j=32)
                t2v = t2[:, 0:512].rearrange("p (h j) -> p h j", j=32)
                cs = self.COS[tau // 32][:, (tau % 32) * 32:(tau % 32 + 1) * 32].unsqueeze(1).to_broadcast([128, 16, 32])
                sn = self.SIN[tau // 32][:, (tau % 32) * 32:(tau % 32 + 1) * 32].unsqueeze(1).to_broadcast([128, 16, 32])
                cb, sb_ = self.COS[tau // 32], self.SIN[tau // 32]
                P.op("dve", lambda e: e.tensor_tensor(out=t1v, in0=ov[:, :, 0, :], in1=cs, op=ALU.mult), [ot, cb], [t1])
                P.op("pool", lambda e: e.tensor_tensor(out=t2v, in0=ov[:, :, 1, :], in1=sn, op=ALU.mult), [ot, sb_], [t2])
                P.op("dve", lambda e: e.tensor_tensor(out=rv[:, :, 0, :], in0=t1v, in1=t2v, op=ALU.subtract), [t1, t2], [rt])
                P.op("dve", lambda e: e.tensor_tensor(out=t1v, in0=ov[:, :, 0, :], in1=sn, op=ALU.mult), [ot, sb_, rt], [t1])
                P.op("pool", lambda e: e.tensor_tensor(out=t2v, in0=ov[:, :, 1, :], in1=cs, op=ALU.mult), [ot, cb, rt], [t2])
                P.op("dve", lambda e: e.tensor_tensor(out=rv[:, :, 1, :], in0=t1v, in1=t2v, op=ALU.add), [t1, t2], [rt])
                src = rt
            zT = T4[17 + mt]
            for hf in range(2):
                self.transpose_to(src, [src[:, (hf * 4 + q) * 128:(hf * 4 + q + 1) * 128] for q in range(4)],
                                  zT, [(zT[:, hf * 512:(hf + 1) * 512], 0, 512)], self.PS[4 + mt * 2 + hf],
                                  evac="act" if hf == 0 else "dve")
            r0 = jb * 256 + mt * 128
            P.dma("sp", dstT.ap().rearrange("(h p) c -> p h c", p=128)[:, :, r0:r0 + 128],
                  zT[:].rearrange("p (h c) -> p h c", c=128), reads=[zT], writes=[dstT])
        return post
    self.tm_pass(I["da_w_qkv"], w, 0, 1024, None, None, None, post=mk_post(QT))
    self.tm_pass(I["da_w_qkv"], w, 1024, 1024, None, None, None, post=mk_post(KT))
    self.tm_pass(I["da_w_qkv"], w, 2048, 1024, None, None, V)


def da_attn(self, i):
    P = self.P
    I = self.I
    T4 = self.T4
    PS = self.PS
    QT, KT, OT = self.fscratch("daqT", 1024), self.fscratch("dakT", 1024), self.fscratch("daoT", 1024)
    V = self.scratch("dav")
    lambda_init = 0.8 - 0.6 * math.exp(-0.3 * i)
    lt = T4[29]
    self.bcast_load(lt, lt[:, 0:256], I["da_lambda"], I["da_lambda"].ap()[0].rearrange("a k -> (a k)"))
    lam = self.small("da_lam", [128, 4])
    P.op("dve", lambda e: e.tensor_tensor(out=lt[:, 256:320], in0=lt[:, 0:64], in1=lt[:, 64:128], op=ALU.mult), [lt], [lt])
    P.op("dve", lambda e: e.tensor_tensor(out=lt[:, 320:384], in0=lt[:, 128:192], in1=lt[:, 192:256], op=ALU.mult), [lt], [lt])
    P.op("dve", lambda e: e.tensor_reduce(out=lam[:, 0:2], in_=lt[:, 256:384].rearrange("p (a k) -> p a k", k=64),
                                          axis=AX.X, op=ALU.add), [lt], [lam])
    P.op("act", lambda e: e.activation(out=lam[:, 0:2], in_=lam[:, 0:2], func=AF.Exp), [lam], [lam])
    P.op("dve", lambda e: e.tensor_tensor(out=lam[:, 2:3], in0=lam[:, 1:2], in1=lam[:, 0:1], op=ALU.subtract), [lam], [lam])
    P.op("dve", lambda e: e.tensor_scalar(out=lam[:, 2:3], in0=lam[:, 2:3], scalar1=-lambda_init, scalar2=None, op0=ALU.add),
         [lam], [lam])
    ngc = self.small("da_ngc", [128, 1])
    self.load_cols(ngc, ngc[:], I["da_norm_g"], I["da_norm_g"].ap()[0:1, :], 1)
    P.op("dve", lambda e: e.tensor_scalar(out=ngc[:], in0=ngc[:], scalar1=1.0 - lambda_init, scalar2=None, op0=ALU.mult),
         [ngc], [ngc])
    nkt = self.nt
    nkT = (self.ntok + 1023) // 1024
    for h in range(8):
        kres = T4[0:nkT]
        vres = T4[9:9 + nkT]
        for q in range(nkT):
            c0 = q * 1024
            wd = min(1024, self.ntok - c0)
            P.dma("sp", kres[q][:, 0:wd], KT.ap()[h * 128:(h + 1) * 128, c0:c0 + wd], reads=[KT], writes=[kres[q]])
            nk = wd // 128
            P.dma("pool", vres[q][:, 0:wd].rearrange("p (kt e) -> p kt e", e=128),
                  V.ap()[c0:c0 + wd, h * 128:(h + 1) * 128].rearrange("(kt p) e -> p kt e", p=128),
                  reads=[V], writes=[vres[q]])
        blocks = [(0, 256, [0, 1])] + [(CTX + b * 512, 512, list(range(nkt))) for b in range(self.seq // 512)]
        for bi, (q0, qn, kts) in enumerate(blocks):
            qt = T4[18 + bi % 2]
            P.dma("sp", qt[:, 0:qn], QT.ap()[h * 128:(h + 1) * 128, q0:q0 + qn], reads=[QT], writes=[qt])
            for ki_, kt in enumerate(kts):
                kq, ko = kt // 8, (kt % 8) * 128
                for m in range(2):
                    psc = PS[4 + (2 * ki_ + m) % 4]
                    et = T4[20 + (2 * ki_ + m) % 4]
                    rows = slice(64 * m, 64 * m + 64)
                    P.op("pe", lambda e: e.matmul(psc[:, 0:qn], lhsT=kres[kq][rows, ko:ko + 128], rhs=qt[rows, 0:qn],
                                                  start=True, stop=True), [kres[kq], qt], [psc])
                    P.op("act", lambda e: e.activation(out=et[:, 0:qn], in_=psc[:, 0:qn], func=AF.Exp, scale=0.125),
                         [psc], [et])
                    first, last = (ki_ == 0), (ki_ == len(kts) - 1)
                    P.op("pe", lambda e: e.matmul(PS[2 * m][:, 0:qn], lhsT=vres[kq][:, ko:ko + 128], rhs=et[:, 0:qn],
                                                  start=first, stop=last), [vres[kq], et], [PS[2 * m]], inc=False)
                    P.op("pe", lambda e: e.matmul(PS[2 * m + 1][:, 0:qn], lhsT=self.ones[:], rhs=et[:, 0:qn],
                                                  start=first, stop=last), [self.ones, et], [PS[2 * m + 1]])
            o0, o1, df = T4[24], T4[25], T4[26]
            for m, ob in ((0, o0), (1, o1)):
                P.op("dve", lambda e: e.reciprocal(out=ob[:, 0:qn], in_=PS[2 * m + 1][:, 0:qn]), [PS[2 * m + 1]], [ob])
                P.op("dve", lambda e: e.tensor_tensor(out=ob[:, 512:512 + qn], in0=PS[2 * m][:, 0:qn], in1=ob[:, 0:qn],
                                                      op=ALU.mult), [PS[2 * m], ob], [ob])
            P.op("dve", lambda e: e.scalar_tensor_tensor(out=df[:, 0:qn], in0=o1[:, 512:512 + qn], scalar=lam[:, 2:3],
                                                         in1=o0[:, 512:512 + qn], op0=ALU.mult, op1=ALU.add),
                 [o0, o1, lam], [df])
            P.op("act", lambda e: e.activation(out=df[:, 512:512 + qn], in_=df[:, 0:qn], func=AF.Square), [df], [df])
            P.op("pe", lambda e: e.matmul(PS[4][:, 0:qn], lhsT=self.ones[:], rhs=df[:, 512:512 + qn], start=True, stop=True),
                 [self.ones, df], [PS[4]])
            rs = T4[27]
            P.op("dve", lambda e: e.tensor_scalar(out=rs[:, 0:qn], in0=PS[4][:, 0:qn], scalar1=1.0 / 128, scalar2=1e-5,
                                                  op0=ALU.mult, op1=ALU.add), [PS[4]], [rs])
            P.op("act", lambda e: e.activation(out=rs[:, 0:qn], in_=rs[:, 0:qn], func=AF.Sqrt), [rs], [rs])
            P.op("dve", lambda e: e.reciprocal(out=rs[:, 0:qn], in_=rs[:, 0:qn]), [rs], [rs])
            P.op("dve", lambda e: e.scalar_tensor_tensor(out=rs[:, 512:512 + qn], in0=df[:, 0:qn], scalar=ngc[:, 0:1],
                                                         in1=rs[:, 0:qn], op0=ALU.mult, op1=ALU.mult), [df, ngc, rs], [rs])
            P.dma("pool", OT.ap()[h * 128:(h + 1) * 128, q0:q0 + qn], rs[:, 512:512 + qn], reads=[rs], writes=[OT])


def da_post(self, i, Xin, Xout):
    P = self.P
    I = self.I
    T4 = self.T4
    wres = T4[0:8]
    self.load_wres(I["da_w_out"], I["da_w_out"].ap()[0], wres)
    OT = self.fscratch("daoT", 1024)
    t_start = 2 if i == DEPTH - 1 else 0
    for t in range(t_start, self.nt):
        par = t % 2
        zT = T4[20 + par]
        P.dma("sp", zT[:].rearrange("p (h c) -> p h c", c=128),
              OT.ap().rearrange("(h p) c -> p h c", p=128)[:, :, t * 128:(t + 1) * 128], reads=[OT], writes=[zT])
        self.proj_residual([(zT, zT[:, kc * 128:(kc + 1) * 128]) for kc in range(8)], wres, Xin, Xout, t, par)


def diffattn(self, i, Xin, Xout):
    self.da_pre(i, Xin)
    self.da_attn(i)
    self.da_post(i, Xin, Xout)


K.da_rope_tables = da_rope_tables
K.da_pre = da_pre
K.da_attn = da_attn
K.da_post = da_post
K.diffattn = diffattn


N_CORES = 8
_CACHE = {}


def kernel(**inputs):
    if "k" not in _CACHE:
        _CACHE["k"] = build({})
    k = _CACHE["k"]
    maps = []
    for c in range(N_CORES):
        b = c % 2
        m = {}
        for name, _ in INPUT_SHAPES:
            a = np.asarray(inputs[name])
            if name in ("x", "c", "ctx"):
                a = a[b]
            m[name] = np.ascontiguousarray(a, dtype=np.float32)
        maps.append(m)
    res = run_bass_kernel_spmd(k.nc, maps, core_ids=list(range(N_CORES)))
    out = np.stack([np.asarray(res.results[b]["out"]) for b in range(2)], axis=0)
    return out.astype(np.float32)
```

```python
import math
import numpy as np
import concourse.bass as bass
import concourse.mybir as mybir
from concourse.bass_utils import run_bass_kernel_spmd

F32 = mybir.dt.float32
BF16 = mybir.dt.bfloat16
ALU = mybir.AluOpType
AF = mybir.ActivationFunctionType
AX = mybir.AxisListType

D = 1024
SEQ = 8192
CTX = 256
NTOK = CTX + SEQ
NT = NTOK // 128
DEPTH = 4
FF = 2816
NFC = FF // 128
EPS = 1e-6
HTC = NTOK + 4


def bf(tile):
    return tile[:].bitcast(BF16)


def col_of(tok):
    return tok + 1 if tok < CTX else tok + 3


class Buf:
    __slots__ = ("name", "t", "wr", "rd", "multi")

    def __init__(self, name, t=None, multi=False):
        self.name = name
        self.t = t
        self.wr = {}
        self.rd = {}
        self.multi = multi

    def __getitem__(self, idx):
        return self.t[idx]

    def ap(self):
        return self.t.ap()


class Prog:
    NDMA = 56

    def __init__(self, nc):
        self.nc = nc
        self.E = {"pe": nc.tensor, "dve": nc.vector, "act": nc.scalar,
                  "pool": nc.gpsimd, "sp": nc.sync}
        self.sem = {e: nc.alloc_semaphore("s_" + e) for e in self.E}
        self.cnt = {e: 0 for e in self.E}
        self.seen = {e: {} for e in self.E}
        self.dsem = [nc.alloc_semaphore("d%d" % i) for i in range(self.NDMA)]
        self.duse = [0] * self.NDMA
        self.dnext = 0
        self.ninst = 0
        self.out_tokens = []
        self.dump = set()
        self.pend = {e: ([], []) for e in self.E}

    def sb(self, name, shape, dt=F32):
        return Buf(name, self.nc.alloc_sbuf_tensor(name, list(shape), dt))

    def ps(self, name, shape, dt=F32):
        return Buf(name, self.nc.alloc_psum_tensor(name, list(shape), dt))

    def dram(self, name, shape, dt=F32, kind=None):
        if kind is None:
            kind = "ExternalOutput" if name in self.dump else "Internal"
        return Buf(name, self.nc.dram_tensor(name, list(shape), dt, kind=kind), multi=True)

    def _semh(self, key):
        return self.sem[key] if isinstance(key, str) else self.dsem[key]

    def _need(self, eng, key, val, need):
        if key == eng and eng == "pe":
            return
        if self.seen[eng].get(key, 0) >= val:
            return
        if need.get(key, 0) < val:
            need[key] = val

    def _deps(self, eng, reads, writes):
        need = {}
        for b in reads:
            for k, v in b.wr.items():
                self._need(eng, k, v, need)
        for b in writes:
            if not b.multi:
                for k, v in b.wr.items():
                    self._need(eng, k, v, need)
            for k, v in b.rd.items():
                self._need(eng, k, v, need)
        return need

    def _emit_waits(self, eng, need, keep_last=True):
        items = list(need.items())
        last = None
        if keep_last and items:
            last = items.pop()
        for k, v in items:
            self.E[eng].wait_ge(self._semh(k), v)
            self.seen[eng][k] = v
            self.ninst += 1
        if last is not None:
            self.seen[eng][last[0]] = last[1]
        return last

    def _commit(self, key, val, reads, writes):
        for b in writes:
            if b.multi:
                b.wr[key] = max(b.wr.get(key, 0), val)
            else:
                b.wr = {key: val}
                b.rd = {}
        for b in reads:
            if (not b.multi) and any(b is w for w in writes):
                continue
            b.rd[key] = max(b.rd.get(key, 0), val)

    def op(self, eng, fn, reads=(), writes=(), inc=True):
        last = self._emit_waits(eng, self._deps(eng, reads, writes))
        ins = fn(self.E[eng])
        if last is not None:
            ins._wait_ge(self._semh(last[0]), last[1])
        self.ninst += 1
        pr, pw = self.pend[eng]
        if inc:
            ins.then_inc(self.sem[eng], 1)
            self.cnt[eng] += 1
            if pr or pw:
                reads = list(reads) + [b for b in pr if not any(b is r for r in reads)]
                writes = list(writes) + [b for b in pw if not any(b is w for w in writes)]
                self.pend[eng] = ([], [])
            self._commit(eng, self.cnt[eng], reads, writes)
        else:
            for b in reads:
                if not any(b is r for r in pr):
                    pr.append(b)
            for b in writes:
                if not any(b is w for w in pw):
                    pw.append(b)
        return ins

    def dma(self, q, out, in_, reads=(), writes=(), is_output=False, **kw):
        q = "sp"
        need = self._deps(q, reads, writes)
        j = self.dnext
        self.dnext = (self.dnext + 1) % self.NDMA
        if self.duse[j]:
            self._need(q, j, 16 * self.duse[j], need)
        last = self._emit_waits(q, need)
        self.duse[j] += 1
        ins = self.E[q].dma_start(out=out, in_=in_, **kw)
        if last is not None:
            ins._wait_ge(self._semh(last[0]), last[1])
        ins.then_inc(self.dsem[j], 16)
        self.ninst += 1
        self._commit(j, 16 * self.duse[j], reads, writes)
        if is_output:
            self.out_tokens.append((j, 16 * self.duse[j]))

    def finish(self, eng="sp"):
        need = {}
        for key, val in self.out_tokens:
            self._need(eng, key, val, need)
        for e in self.E:
            if e != eng and self.cnt[e]:
                self._need(eng, e, self.cnt[e], need)
        for j in range(self.NDMA):
            if self.duse[j]:
                self._need(eng, j, 16 * self.duse[j], need)
        self._emit_waits(eng, need, keep_last=False)


INPUT_SHAPES = [
    ("x", [SEQ, D]), ("c", [D]), ("ctx", [CTX, D]), ("c_ctx", [D]),
    ("ada_w", [4, D, 6 * D]), ("ada_b", [4, 6 * D]), ("norm1_g", [4, D]), ("norm2_g", [4, D]),
    ("ffn_w_in", [4, D, 2 * FF]), ("ffn_conv_w", [4, 3, FF]), ("ffn_conv_b", [4, FF]),
    ("ffn_w_out", [4, FF, D]),
    ("ra_mix", [2, 6, D]), ("ra_w_rkv", [2, 3, D, D]), ("ra_w0", [2, 2, D]),
    ("ra_w1", [2, 2, D, 64]), ("ra_w2", [2, 2, 64, D]), ("ra_a0", [2, 2, D]),
    ("ra_a1", [2, 2, D, 64]), ("ra_a2", [2, 2, 64, D]), ("ra_g1", [2, D, 160]),
    ("ra_g2", [2, 160, D]), ("ra_k_k", [2, D]), ("ra_k_a", [2, D]), ("ra_r_k", [2, 16, 64]),
    ("ra_lnx_g", [2, D]), ("ra_lnx_b", [2, D]), ("ra_w_out", [2, D, D]),
    ("ml_w_in", [1, D, 3088]), ("ml_b_in", [1, 3088]), ("ml_norm_g", [1, D]), ("ml_w_out", [1, D, D]),
    ("da_w_qkv", [1, D, 3 * D]), ("da_lambda", [1, 4, 64]), ("da_norm_g", [1, 128]),
    ("da_w_out", [1, D, D]), ("final_g", [D]),
]


class K:
    def __init__(self, cfg):
        self.cfg = cfg
        self.seq = cfg.get("seq", SEQ)
        self.ntok = CTX + self.seq
        self.nt = self.ntok // 128
        self.nblk = self.ntok // 256
        self.htc = self.ntok + 4
        nc = bass.Bass("TRN2", target_bir_lowering=False)
        self.nc = nc
        P = Prog(nc)
        P.dump = set(cfg.get("dump", []))
        self.P = P
        self.I = {}
        for name, shape in INPUT_SHAPES:
            if name == "x":
                shape = [self.seq, D]
            self.I[name] = P.dram(name, shape, F32, kind="ExternalInput")
        self.out = P.dram("out", [self.seq, D], F32, kind="ExternalOutput")
        self.X = [P.dram("XA", [self.ntok, D]), P.dram("XB", [self.ntok, D])]
        self.HT = P.dram("HT", [D, self.htc], BF16)
        self.T4 = [P.sb("t4_%d" % i, [128, 1024]) for i in range(30)]
        self.S = {}
        self.MOD = [P.sb("modl", [128, 6 * D]), P.sb("modc", [128, 6 * D])]
        self.PS = [P.ps("ps%d" % i, [128, 512]) for i in range(8)]
        self.sm = {}
        self.consts()

    def small(self, name, shape, dt=F32):
        if name not in self.sm:
            self.sm[name] = self.P.sb("sm_" + name, shape, dt)
        return self.sm[name]

    def col_of(self, tok):
        return tok + 1 if tok < CTX else tok + 3

    def consts(self):
        P = self.P
        self.ident = P.sb("ident", [128, 128])
        self.ones = P.sb("ones", [128, 128])
        self.lo = P.sb("lo", [128, 128])
        self.up = P.sb("up", [128, 128])
        self.los = P.sb("los", [128, 128])
        self.ups = P.sb("ups", [128, 128])
        self.zero = P.sb("zero", [128, 64])

        def mask(t, pat, cm, op):
            P.op("pool", lambda e: e.memset(t[:], 1.0), [], [t])
            P.op("pool", lambda e: e.affine_select(out=t[:], in_=t[:], pattern=[[pat, 128]],
                                                    compare_op=op, fill=0.0, base=0,
                                                    channel_multiplier=cm), [t], [t])
        mask(self.lo, -1, 1, ALU.is_ge)
        mask(self.up, 1, -1, ALU.is_ge)
        mask(self.los, -1, 1, ALU.is_gt)
        mask(self.ups, 1, -1, ALU.is_gt)
        P.op("pool", lambda e: e.memset(self.ones[:], 1.0), [], [self.ones])
        self.ones_bf = P.sb("ones_bf", [128, 128], BF16)
        P.op("pool", lambda e: e.memset(self.ones_bf[:], 1.0), [], [self.ones_bf])
        P.op("pool", lambda e: e.memset(self.zero[:], 0.0), [], [self.zero])
        P.op("dve", lambda e: e.tensor_tensor(out=self.ident[:], in0=self.lo[:], in1=self.up[:],
                                              op=ALU.mult), [self.lo, self.up], [self.ident])

    def bcast_load(self, dst, dst_ap, src_buf, src_ap, q="sp"):
        self.P.dma(q, dst_ap, src_ap.partition_broadcast(128), reads=[src_buf], writes=[dst])

    def transpose_to(self, src, src_aps, dst, dst_aps, psa, evac="act"):
        P = self.P
        for i, sap in enumerate(src_aps):
            P.op("pe", lambda e, i=i, sap=sap: e.transpose(psa[:, i * 128:(i + 1) * 128], sap, self.ident[:]),
                 [src, self.ident], [psa])
        for (dap, c0, c1) in dst_aps:
            if evac == "act":
                P.op("act", lambda e, dap=dap, c0=c0, c1=c1: e.activation(out=dap, in_=psa[:, c0:c1], func=AF.Copy),
                     [psa], [dst])
            else:
                P.op("dve", lambda e, dap=dap, c0=c0, c1=c1: e.tensor_copy(out=dap, in_=psa[:, c0:c1]),
                     [psa], [dst])

    def load_cols(self, dst, dst_ap, src_buf, src_ap, n):
        P = self.P
        st = self.small("lc_stage", [128, 128])
        ps = self.PS[0]
        P.dma("sp", st[0:n, :], src_ap, reads=[src_buf], writes=[st])
        P.op("pe", lambda e: e.transpose(ps[:, 0:n], st[0:n, :], self.ident[0:n, 0:n]), [st, self.ident], [ps])
        P.op("dve", lambda e: e.tensor_copy(out=dst_ap, in_=ps[:, 0:n]), [ps], [dst])

    def init(self):
        P = self.P
        X0 = self.X[0]
        P.dma("sp", X0.ap()[0:CTX, :], self.I["ctx"].ap(), reads=[self.I["ctx"]], writes=[X0])
        nrow = self.seq
        step = 2048
        for r0 in range(0, nrow, step):
            r1 = min(nrow, r0 + step)
            P.dma("pool" if (r0 // step) % 2 else "sp", X0.ap()[CTX + r0:CTX + r1, :],
                  self.I["x"].ap()[r0:r1, :], reads=[self.I["x"]], writes=[X0])
        htv = self.HT.ap().rearrange("(kc p) c -> p kc c", p=128)
        for c in (0, CTX + 1, CTX + 2, self.htc - 1):
            P.dma("sp", htv[:, :, c:c + 1], self.zero[:].bitcast(BF16)[:, 0:8].unsqueeze(2), reads=[self.zero], writes=[self.HT],
                  allow_slow_non_contiguous=True)
        self.CB = []
        for s, nm in enumerate(("c", "c_ctx")):
            cc = self.small("cc%d" % s, [128, 8])
            self.load_cols(cc, cc[:], self.I[nm], self.I[nm].ap().rearrange("(kc p) -> kc p", p=128), 8)
            P.op("act", lambda e, cc=cc: e.activation(out=cc[:], in_=cc[:], func=AF.Silu), [cc], [cc])
            cb = P.sb("cb%d" % s, [128, 1024])
            for kc in range(8):
                P.op("dve", lambda e, kc=kc, cb=cb, cc=cc: e.tensor_scalar(
                    out=cb[:, kc * 128:(kc + 1) * 128], in0=self.ones[:], scalar1=cc[:, kc:kc + 1], scalar2=None,
                    op0=ALU.mult), [self.ones, cc], [cb])
            self.CB.append(cb)

    def adaln(self, i):
        P = self.P
        aw = self.I["ada_w"].ap()[i].rearrange("(kc p) n -> p kc n", p=128)
        for blk in range(12):
            wts = [self.T4[4 + (blk % 2) * 4 + q] for q in range(4)]
            for q in range(4):
                P.dma("sp" if q % 2 == 0 else "pool", wts[q][:].rearrange("p (kc n) -> p kc n", n=512),
                      aw[:, 2 * q:2 * q + 2, blk * 512:(blk + 1) * 512], reads=[self.I["ada_w"]], writes=[wts[q]])
            bt = self.T4[blk % 2]
            self.bcast_load(bt, bt[:, 0:512], self.I["ada_b"], self.I["ada_b"].ap()[i, blk * 512:(blk + 1) * 512])
            for s in range(2):
                ps = self.PS[(blk % 2) * 2 + s]
                for kc in range(8):
                    wt = wts[kc // 2]
                    P.op("pe", lambda e, kc=kc, s=s, ps=ps, wt=wt: e.matmul(
                        ps[:, :], lhsT=self.CB[s][:, kc * 128:(kc + 1) * 128],
                        rhs=wt[:, (kc % 2) * 512:(kc % 2 + 1) * 512],
                        start=(kc == 0), stop=(kc == 7)), [self.CB[s], wt], [ps], inc=(kc == 7))
                P.op("dve", lambda e, s=s, ps=ps, bt=bt, blk=blk: e.tensor_tensor(
                    out=self.MOD[s][:, blk * 512:(blk + 1) * 512], in0=ps[:, :], in1=bt[:, 0:512], op=ALU.add),
                    [ps, bt], [self.MOD[s]])
        for (nm, off) in (("norm1_g", 1), ("norm2_g", 4)):
            gt = self.T4[2]
            self.bcast_load(gt, gt[:], self.I[nm], self.I[nm].ap()[i])
            for s in range(2):
                m = self.MOD[s]
                P.op("dve", lambda e, m=m, gt=gt, off=off: e.scalar_tensor_tensor(
                    out=m[:, off * D:(off + 1) * D], in0=m[:, off * D:(off + 1) * D], scalar=1.0, in1=gt[:],
                    op0=ALU.add, op1=ALU.mult), [m, gt], [m])

    def rms_rstd(self, xt, junk, ss, n, eps):
        P = self.P
        P.op("act", lambda e: e.activation(out=junk[:], in_=xt[:], func=AF.Square, accum_out=ss[:, 0:1]),
             [xt], [junk, ss])
        P.op("dve", lambda e: e.tensor_scalar(out=ss[:, 0:1], in0=ss[:, 0:1], scalar1=1.0 / n, scalar2=eps,
                                              op0=ALU.mult, op1=ALU.add), [ss], [ss])
        P.op("act", lambda e: e.activation(out=ss[:, 0:1], in_=ss[:, 0:1], func=AF.Sqrt), [ss], [ss])
        P.op("dve", lambda e: e.reciprocal(out=ss[:, 0:1], in_=ss[:, 0:1]), [ss], [ss])

    def norm_to_HT(self, Xin, goff, shoff):
        P = self.P
        htv = self.HT.ap().rearrange("(kc p) c -> p kc c", p=128)
        for t in range(self.nt):
            par = t % 2
            xt, h, hT = self.T4[par * 3], self.T4[par * 3 + 1], self.T4[par * 3 + 2]
            ss = self.small("nss%d" % par, [128, 1])
            m = self.MOD[1 if t < 2 else 0]
            P.dma("sp", xt[:], Xin.ap()[t * 128:(t + 1) * 128, :], reads=[Xin], writes=[xt])
            self.rms_rstd(xt, h, ss, D, EPS)
            P.op("dve", lambda e, xt=xt, h=h, ss=ss, m=m: e.scalar_tensor_tensor(
                out=h[:], in0=xt[:], scalar=ss[:, 0:1], in1=m[:, goff * D:(goff + 1) * D],
                op0=ALU.mult, op1=ALU.mult), [xt, ss, m], [h])
            P.op("dve", lambda e, h=h, m=m: e.tensor_tensor(
                out=h[:], in0=h[:], in1=m[:, shoff * D:(shoff + 1) * D], op=ALU.add), [h, m], [h])
            for hf in range(2):
                psa = self.PS[par * 2 + hf]
                self.transpose_to(h, [h[:, (hf * 4 + q) * 128:(hf * 4 + q + 1) * 128] for q in range(4)],
                                  hT, [(bf(hT)[:, hf * 512:(hf + 1) * 512], 0, 512)], psa,
                                  evac="act" if hf == 0 else "dve")
            c0 = self.col_of(t * 128)
            P.dma("pool", htv[:, :, c0:c0 + 128], bf(hT)[:, 0:1024].rearrange("p (kc c) -> p kc c", c=128),
                  reads=[hT], writes=[self.HT])

    def load_hTe(self, j, tiles):
        htv = self.HT.ap().rearrange("(kc p) c -> p kc c", p=128)
        c0 = self.col_of(j * 256)
        views = []
        for q, (k0, k1) in enumerate(((0, 3), (3, 6), (6, 8))):
            v = bf(tiles[q])[:, 0:(k1 - k0) * 258].rearrange("p (kc c) -> p kc c", c=258)
            self.P.dma("sp" if q != 1 else "pool", v, htv[:, k0:k1, c0 - 1:c0 + 257], reads=[self.HT], writes=[tiles[q]])
            views.append(v)

        def hv(kc):
            return tiles[kc // 3], views[kc // 3][:, kc % 3, :]
        return hv

    def cast_w(self, key, src_buf, src_ap, R, C):
        P = self.P
        name = "WB_" + key
        if name not in self.S:
            self.S[name] = P.dram(name, [R, C], BF16)
        wb = self.S[name]
        n = 0
        for r0 in range(0, R, 128):
            rr = min(128, R - r0)
            for c0 in range(0, C, 1024):
                cc = min(1024, C - c0)
                st, ob = self.T4[14 + (n % 2) * 2], self.T4[15 + (n % 2) * 2]
                P.dma("sp", st[0:rr, 0:cc], src_ap[r0:r0 + rr, c0:c0 + cc], reads=[src_buf], writes=[st])
                k3 = n % 3
                if k3 == 0:
                    P.op("dve", lambda e: e.tensor_copy(out=bf(ob)[0:rr, 0:cc], in_=st[0:rr, 0:cc]), [st], [ob])
                elif k3 == 1:
                    P.op("act", lambda e: e.activation(out=bf(ob)[0:rr, 0:cc], in_=st[0:rr, 0:cc], func=AF.Copy), [st], [ob])
                else:
                    P.op("pool", lambda e: e.tensor_copy(out=bf(ob)[0:rr, 0:cc], in_=st[0:rr, 0:cc]), [st], [ob])
                P.dma("pool", wb.ap()[r0:r0 + rr, c0:c0 + cc], bf(ob)[0:rr, 0:cc], reads=[ob], writes=[wb])
                n += 1
        return wb

    def ffn(self, i, Xin, Xout):
        P = self.P
        self.norm_to_HT(Xin, 4, 3)
        cw = self.small("ffn_cw", [128, 3, NFC])
        cbias = self.small("ffn_cb", [128, NFC])
        self.load_cols(cw, cw[:].rearrange("p j fc -> p (j fc)"), self.I["ffn_conv_w"],
                       self.I["ffn_conv_w"].ap()[i].rearrange("j (fc p) -> (j fc) p", p=128), 3 * NFC)
        self.load_cols(cbias, cbias[:], self.I["ffn_conv_b"],
                       self.I["ffn_conv_b"].ap()[i].rearrange("(fc p) -> fc p", p=128), NFC)
        winb = self.cast_w("ffn_win", self.I["ffn_w_in"], self.I["ffn_w_in"].ap()[i], D, 2 * FF)
        woutb = self.cast_w("ffn_wout", self.I["ffn_w_out"], self.I["ffn_w_out"].ap()[i], FF, D)
        win = winb.ap().rearrange("(kc p) n -> p kc n", p=128)
        wout = woutb.ap()
        last_ctx = (i == DEPTH - 1)
        for j in range(self.nblk):
            if j == 0 and last_ctx:
                continue
            m = self.MOD[1 if j == 0 else 0]
            hv = self.load_hTe(j, self.T4[18 + (j % 2) * 3:21 + (j % 2) * 3])
            po = [self.PS[4], self.PS[5], self.PS[6], self.PS[7]]
            for fc in range(NFC):
                par = fc % 2
                wg, wv_, wo = self.T4[6 + par * 3], self.T4[7 + par * 3], self.T4[8 + par * 3]
                wgv = bf(wg)[:, 0:1024].rearrange("p (kc f) -> p kc f", f=128)
                wvv = bf(wv_)[:, 0:1024].rearrange("p (kc f) -> p kc f", f=128)
                P.dma("sp", wgv, win[:, :, FF + fc * 128:FF + (fc + 1) * 128], reads=[winb], writes=[wg])
                P.dma("pool", wvv, win[:, :, fc * 128:(fc + 1) * 128], reads=[winb], writes=[wv_])
                P.dma("sp", bf(wo)[:, 0:1024], wout[fc * 128:(fc + 1) * 128, :], reads=[woutb], writes=[wo])
                pg, pv = self.PS[par * 2], self.PS[par * 2 + 1]
                for kc in range(8):
                    hb, ha = hv(kc)
                    P.op("pe", lambda e, kc=kc, pg=pg, wgv=wgv, ha=ha: e.matmul(
                        pg[:, 0:258], lhsT=wgv[:, kc, :], rhs=ha, start=(kc == 0), stop=(kc == 7)),
                        [wg, hb], [pg], inc=(kc == 7))
                for kc in range(8):
                    hb, ha = hv(kc)
                    P.op("pe", lambda e, kc=kc, pv=pv, wvv=wvv, ha=ha: e.matmul(
                        pv[:, 0:256], lhsT=wvv[:, kc, :], rhs=ha[:, 1:257], start=(kc == 0), stop=(kc == 7)),
                        [wv_, hb], [pv], inc=(kc == 7))
                tt = self.T4[12 + par]
                P.op("act", lambda e, tt=tt, pg=pg, fc=fc: e.activation(
                    out=tt[:, 0:256], in_=pg[:, 1:257], func=AF.Identity, scale=cw[:, 1, fc:fc + 1],
                    bias=cbias[:, fc:fc + 1]), [pg, cw, cbias], [tt])
                P.op("dve", lambda e, tt=tt, pg=pg, fc=fc: e.scalar_tensor_tensor(
                    out=tt[:, 0:256], in0=pg[:, 0:256], scalar=cw[:, 0, fc:fc + 1], in1=tt[:, 0:256],
                    op0=ALU.mult, op1=ALU.add), [pg, cw, tt], [tt])
                P.op("dve", lambda e, tt=tt, pg=pg, fc=fc: e.scalar_tensor_tensor(
                    out=tt[:, 0:256], in0=pg[:, 2:258], scalar=cw[:, 2, fc:fc + 1], in1=tt[:, 0:256],
                    op0=ALU.mult, op1=ALU.add), [pg, cw, tt], [tt])
                P.op("act", lambda e, tt=tt: e.activation(out=tt[:, 256:512], in_=tt[:, 0:256],
                                                          func=AF.Gelu_apprx_tanh), [tt], [tt])
                P.op("dve", lambda e, tt=tt, pv=pv: e.tensor_tensor(
                    out=bf(tt)[:, 1536:1792], in0=tt[:, 256:512], in1=pv[:, 0:256], op=ALU.mult), [tt, pv], [tt])
                for mt in range(2):
                    for hf in range(2):
                        pb = po[mt * 2 + hf]
                        P.op("pe", lambda e, tt=tt, wo=wo, pb=pb, mt=mt, hf=hf, fc=fc: e.matmul(
                            pb[:, :], lhsT=bf(tt)[:, 1536 + mt * 128:1536 + (mt + 1) * 128],
                            rhs=bf(wo)[:, hf * 512:(hf + 1) * 512], start=(fc == 0), stop=(fc == NFC - 1)),
                            [tt, wo], [pb], inc=(fc == NFC - 1))
            for mt in range(2):
                r0 = j * 256 + mt * 128
                xt, xn = self.T4[14 + mt * 2], self.T4[15 + mt * 2]
                P.dma("sp", xt[:], Xin.ap()[r0:r0 + 128, :], reads=[Xin], writes=[xt])
                for hf in range(2):
                    pb = po[mt * 2 + hf]
                    P.op("dve", lambda e, xn=xn, pb=pb, hf=hf, m=m: e.tensor_tensor(
                        out=xn[:, hf * 512:(hf + 1) * 512], in0=pb[:, :],
                        in1=m[:, 5 * D + hf * 512:5 * D + (hf + 1) * 512], op=ALU.mult), [pb, m], [xn])
                P.op("pool", lambda e, xn=xn, xt=xt: e.tensor_tensor(
                    out=xn[:], in0=xn[:], in1=xt[:], op=ALU.add), [xn, xt], [xn])
                P.dma("pool", Xout.ap()[r0:r0 + 128, :], xn[:], reads=[xn], writes=[Xout])

    def final(self, Xin):
        P = self.P
        g = self.T4[6]
        self.bcast_load(g, g[:], self.I["final_g"], self.I["final_g"].ap())
        for t in range(2, self.nt):
            par = t % 2
            xt, h = self.T4[par * 3], self.T4[par * 3 + 1]
            ss = self.small("nss%d" % par, [128, 1])
            P.dma("sp", xt[:], Xin.ap()[t * 128:(t + 1) * 128, :], reads=[Xin], writes=[xt])
            self.rms_rstd(xt, h, ss, D, EPS)
            P.op("dve", lambda e, xt=xt, h=h, ss=ss: e.scalar_tensor_tensor(
                out=h[:], in0=xt[:], scalar=ss[:, 0:1], in1=g[:], op0=ALU.mult, op1=ALU.mult), [xt, ss, g], [h])
            P.dma("pool", self.out.ap()[(t - 2) * 128:(t - 1) * 128, :], h[:], reads=[h], writes=[self.out],
                  is_output=True)


def build(cfg):
    k = K(cfg)
    k.init()
    cur = 0
    for (i, kind) in cfg.get("layers", [(0, "ra"), (1, "ml"), (2, "da"), (3, "ra")]):
        k.adaln(i)
        if kind == "ra":
            k.rwkv(i, k.X[cur], k.X[1 - cur]); cur = 1 - cur
        elif kind == "ml":
            k.mlstm(i, k.X[cur], k.X[1 - cur]); cur = 1 - cur
        elif kind == "da":
            k.diffattn(i, k.X[cur], k.X[1 - cur]); cur = 1 - cur
        if cfg.get("ffn", True):
            k.ffn(i, k.X[cur], k.X[1 - cur]); cur = 1 - cur
    k.final(k.X[cur])
    k.P.finish()
    return k


def scratch(self, name, cols=D, dt=F32):
    if name not in self.S:
        self.S[name] = self.P.dram("S_" + name, [self.ntok, cols], dt)
    return self.S[name]


def load_wres(self, wbuf, wap, tiles, ncols=D):
    wb = self.cast_w("wres", wbuf, wap, D, ncols) if ncols == D else self.cast_w("wres%d" % ncols, wbuf, wap, D, ncols)
    for kc in range(8):
        self.P.dma("sp" if kc % 2 == 0 else "pool", bf(tiles[kc])[:, 0:ncols], wb.ap()[kc * 128:(kc + 1) * 128, :],
                   reads=[wb], writes=[tiles[kc]])


def proj_residual(self, zT_list, wres, Xin, Xout, t, par):
    P = self.P
    m = self.MOD[1 if t < 2 else 0]
    xt, xn = self.T4[26 + par * 2], self.T4[27 + par * 2]
    P.dma("sp", xt[:], Xin.ap()[t * 128:(t + 1) * 128, :], reads=[Xin], writes=[xt])
    for hf in range(2):
        pb = self.PS[6 + hf]
        for kc in range(8):
            zb, za = zT_list[kc]
            P.op("pe", lambda e, kc=kc, za=za, pb=pb, hf=hf: e.matmul(
                pb[:, :], lhsT=za, rhs=bf(wres[kc])[:, hf * 512:(hf + 1) * 512], start=(kc == 0), stop=(kc == 7)),
                [zb, wres[kc]], [pb], inc=(kc == 7))
        P.op("dve", lambda e, xn=xn, pb=pb, hf=hf, m=m: e.tensor_tensor(
            out=xn[:, hf * 512:(hf + 1) * 512], in0=pb[:, :], in1=m[:, 2 * D + hf * 512:2 * D + (hf + 1) * 512],
            op=ALU.mult), [pb, m], [xn])
    P.op("pool", lambda e, xn=xn, xt=xt: e.tensor_tensor(out=xn[:], in0=xn[:], in1=xt[:], op=ALU.add), [xn, xt], [xn])
    P.dma("pool", Xout.ap()[t * 128:(t + 1) * 128, :], xn[:], reads=[xn], writes=[Xout])


K.scratch = scratch
K.load_wres = load_wres
K.proj_residual = proj_residual


CDEC = math.exp(-0.5)


def rw_xs(self, hv, jm, mixc, xs_tiles, xx_tiles):
    P = self.P
    views = []
    for kc in range(8):
        hb, ha = hv(kc)
        xt = xs_tiles[kc // 4]
        xv = bf(xt)[:, (kc % 4) * 256:(kc % 4 + 1) * 256]
        xxb = xx_tiles[kc // 4]
        xxv = xxb[:, (kc % 4) * 256:(kc % 4 + 1) * 256]
        eng = "dve"
        P.op(eng, lambda e, xv=xv, xxv=xxv, ha=ha, kc=kc: e.scalar_tensor_tensor(
            out=xv, in0=xxv, scalar=mixc[:, jm * 8 + kc:jm * 8 + kc + 1], in1=ha[:, 1:257],
            op0=ALU.mult, op1=ALU.add), [xxb, mixc, hb], [xt])
        views.append((xt, xv))
    return views


def rw_xx(self, hv, xx_tiles):
    P = self.P
    for kc in range(8):
        hb, ha = hv(kc)
        xxb = xx_tiles[kc // 4]
        xxv = xxb[:, (kc % 4) * 256:(kc % 4 + 1) * 256]
        eng = "dve"
        P.op(eng, lambda e, xxv=xxv, ha=ha: e.tensor_tensor(out=xxv, in0=ha[:, 0:256], in1=ha[:, 2:258], op=ALU.add),
             [hb], [xxb])
        P.op(eng, lambda e, xxv=xxv, ha=ha: e.scalar_tensor_tensor(
            out=xxv, in0=xxv, scalar=0.5, in1=ha[:, 1:257], op0=ALU.mult, op1=ALU.subtract), [xxb, hb], [xxb])


def rwkv_pre(self, i, Xin):
    P = self.P
    j = i // 3
    I = self.I
    self.norm_to_HT(Xin, 1, 0)
    mixc = self.small("ra_mixc", [128, 48])
    self.load_cols(mixc, mixc[:], I["ra_mix"], I["ra_mix"].ap()[j].rearrange("m (kc p) -> (m kc) p", p=128), 48)
    T4 = self.T4
    wres = T4[0:8]
    SR, SV, SK, SKK = self.scratch("r"), self.scratch("v"), self.scratch("k"), self.scratch("kk")
    SG_ = self.scratch("g")
    SSG = [self.scratch("sg0"), self.scratch("sg1")]
    SKD = [self.scratch("kd0"), self.scratch("kd1")]
    SBA = [self.scratch("ba0"), self.scratch("ba1")]
    SRKS = self.scratch("rks", 16)

    def block_setup(jb, jm):
        hv = self.load_hTe(jb, T4[8 + (jb % 2) * 3:11 + (jb % 2) * 3])
        xx_tiles = T4[14:16]
        self.rw_xx(hv, xx_tiles)
        xs_tiles = T4[16 + (jb % 2) * 2:18 + (jb % 2) * 2]
        return self.rw_xs(hv, jm, mixc, xs_tiles, xx_tiles)

    kkb = T4[24]
    for (jm, proj, dst) in ((0, 0, SR), (2, 2, SV), (1, 1, SK)):
        self.load_wres(I["ra_w_rkv"], I["ra_w_rkv"].ap()[j, proj], wres)
        if proj == 1:
            self.bcast_load(kkb, kkb[:], I["ra_k_k"], I["ra_k_k"].ap()[j])
        for jb in range(self.nblk):
            xs = block_setup(jb, jm)
            for mt in range(2):
                ot = T4[20 + mt]
                for hf in range(2):
                    pb = self.PS[(mt * 2 + hf) % 4]
                    for kc in range(8):
                        xb, xv = xs[kc]
                        P.op("pe", lambda e, kc=kc, xv=xv, pb=pb, hf=hf, mt=mt: e.matmul(
                            pb[:, :], lhsT=xv[:, mt * 128:(mt + 1) * 128], rhs=bf(wres[kc])[:, hf * 512:(hf + 1) * 512],
                            start=(kc == 0), stop=(kc == 7)), [xb, wres[kc]], [pb], inc=(kc == 7))
                    P.op("act", lambda e, ot=ot, pb=pb, hf=hf: e.activation(
                        out=ot[:, hf * 512:(hf + 1) * 512], in_=pb[:, :], func=AF.Copy), [pb], [ot])
                r0 = jb * 256 + mt * 128
                P.dma("pool", dst.ap()[r0:r0 + 128, :], ot[:], reads=[ot], writes=[dst])
                if proj == 1:
                    kr, sq = T4[22], T4[23]
                    ssh = self.small("kk_ss", [128, 16])
                    P.op("dve", lambda e, kr=kr, ot=ot: e.tensor_tensor(out=kr[:], in0=ot[:], in1=kkb[:], op=ALU.mult),
                         [ot, kkb], [kr])
                    P.op("act", lambda e, kr=kr, sq=sq: e.activation(out=sq[:], in_=kr[:], func=AF.Square), [kr], [sq])
                    P.op("dve", lambda e, sq=sq: e.tensor_reduce(
                        out=ssh[:], in_=sq[:].rearrange("p (h k) -> p h k", k=64), axis=AX.X, op=ALU.add), [sq], [ssh])
                    P.op("act", lambda e: e.activation(out=ssh[:], in_=ssh[:], func=AF.Sqrt), [ssh], [ssh])
                    P.op("dve", lambda e: e.tensor_scalar_max(out=ssh[:], in0=ssh[:], scalar1=1e-12), [ssh], [ssh])
                    P.op("dve", lambda e: e.reciprocal(out=ssh[:], in_=ssh[:]), [ssh], [ssh])
                    P.op("dve", lambda e, kr=kr: e.tensor_tensor(
                        out=kr[:].rearrange("p (h k) -> p h k", k=64), in0=kr[:].rearrange("p (h k) -> p h k", k=64),
                        in1=ssh[:].unsqueeze(2).to_broadcast([128, 16, 64]), op=ALU.mult), [kr, ssh], [kr])
                    P.dma("sp", SKK.ap()[r0:r0 + 128, :], kr[:], reads=[kr], writes=[SKK])

    kab, rkb = T4[24], T4[25]
    for (jm, nm1, nm2, nm0, is_w) in ((3, "ra_w1", "ra_w2", "ra_w0", True), (4, "ra_a1", "ra_a2", "ra_a0", False)):
        l1, l2 = T4[0], T4[1]
        b0 = [T4[2], T4[3]]
        for z in range(2):
            wb1 = self.cast_w("lora1_%d" % z, I[nm1], I[nm1].ap()[j, z], D, 64)
            P.dma("sp", bf(l1)[:, 0:1024].rearrange("p (kc z l) -> p kc z l", z=2, l=64)[:, :, z, :],
                  wb1.ap().rearrange("(kc p) l -> p kc l", p=128), reads=[wb1], writes=[l1])
        P.dma("sp", l2[:], I[nm2].ap()[j].rearrange("z l n -> (z l) n"), reads=[I[nm2]], writes=[l2])
        for d in range(2):
            self.bcast_load(b0[d], b0[d][:], I[nm0], I[nm0].ap()[j, d])
        if not is_w:
            self.bcast_load(kab, kab[:], I["ra_k_a"], I["ra_k_a"].ap()[j])
            self.bcast_load(rkb, rkb[:], I["ra_r_k"], I["ra_r_k"].ap()[j].rearrange("h k -> (h k)"))
        for jb in range(self.nblk):
            xs = block_setup(jb, jm)
            pl = self.PS[4]
            for kc in range(8):
                xb, xv = xs[kc]
                P.op("pe", lambda e, kc=kc, xv=xv: e.matmul(
                    pl[:, 0:256], lhsT=bf(l1)[:, kc * 128:(kc + 1) * 128], rhs=xv, start=(kc == 0), stop=(kc == 7)),
                    [l1, xb], [pl], inc=(kc == 7))
            t1 = T4[4]
            P.op("act", lambda e: e.activation(out=t1[:, 0:256], in_=pl[:, 0:256],
                                               func=AF.Tanh if is_w else AF.Copy), [pl], [t1])
            for mt in range(2):
                r0 = jb * 256 + mt * 128
                if not is_w:
                    kt, kkt, rt = T4[5], T4[6], T4[7]
                    P.dma("sp", kt[:], SK.ap()[r0:r0 + 128, :], reads=[SK], writes=[kt])
                    P.dma("pool", kkt[:], SKK.ap()[r0:r0 + 128, :], reads=[SKK], writes=[kkt])
                    P.dma("sp", rt[:], SR.ap()[r0:r0 + 128, :], reads=[SR], writes=[rt])
                    kds = T4[26]
                for d in range(2):
                    ot = T4[20 + d]
                    for hf in range(2):
                        pb = self.PS[d * 2 + hf]
                        P.op("pe", lambda e, d=d, hf=hf, mt=mt, pb=pb: e.matmul(
                            pb[:, :], lhsT=t1[64 * d:64 * d + 64, mt * 128:(mt + 1) * 128],
                            rhs=l2[64 * d:64 * d + 64, hf * 512:(hf + 1) * 512], start=True, stop=True),
                            [t1, l2], [pb])
                        P.op("dve", lambda e, d=d, hf=hf, pb=pb, ot=ot: e.tensor_tensor(
                            out=ot[:, hf * 512:(hf + 1) * 512], in0=pb[:, :], in1=b0[d][:, hf * 512:(hf + 1) * 512],
                            op=ALU.add), [pb, b0[d]], [ot])
                    P.op("act", lambda e, ot=ot: e.activation(out=ot[:], in_=ot[:], func=AF.Sigmoid), [ot], [ot])
                    if is_w:
                        P.dma("pool", SSG[d].ap()[r0:r0 + 128, :], ot[:], reads=[ot], writes=[SSG[d]])
                    else:
                        kd, ba = T4[22 + d], T4[24 + 4 + d]
                        P.op("dve", lambda e, kd=kd, ot=ot: e.scalar_tensor_tensor(
                            out=kd[:], in0=ot[:], scalar=-1.0, in1=kab[:], op0=ALU.add, op1=ALU.mult), [ot, kab], [kd])
                        P.op("dve", lambda e, kd=kd: e.scalar_tensor_tensor(
                            out=kd[:], in0=kd[:], scalar=1.0, in1=kt[:], op0=ALU.add, op1=ALU.mult), [kd, kt], [kd])
                        P.op("pool", lambda e, ba=ba, ot=ot: e.tensor_tensor(out=ba[:], in0=ot[:], in1=kkt[:], op=ALU.mult),
                             [ot, kkt], [ba])
                        P.dma("sp", SKD[d].ap()[r0:r0 + 128, :], kd[:], reads=[kd], writes=[SKD[d]])
                        P.dma("pool", SBA[d].ap()[r0:r0 + 128, :], ba[:], reads=[ba], writes=[SBA[d]])
                if not is_w:
                    kd0, kd1 = T4[22], T4[23]
                    rks = self.small("rks_t", [128, 16])
                    P.op("dve", lambda e: e.tensor_tensor(out=kds[:], in0=kd0[:], in1=kd1[:], op=ALU.add), [kd0, kd1], [kds])
                    P.op("dve", lambda e: e.tensor_tensor(out=kds[:], in0=kds[:], in1=rt[:], op=ALU.mult), [kds, rt], [kds])
                    P.op("dve", lambda e: e.tensor_tensor(out=kds[:], in0=kds[:], in1=rkb[:], op=ALU.mult), [kds, rkb], [kds])
                    P.op("dve", lambda e: e.tensor_reduce(
                        out=rks[:], in_=kds[:].rearrange("p (h k) -> p h k", k=64), axis=AX.X, op=ALU.add), [kds], [rks])
                    P.dma("sp", SRKS.ap()[r0:r0 + 128, :], rks[:], reads=[rks], writes=[SRKS])

    g1a, g1b, g2a, g2b = T4[0], T4[1], T4[2], T4[3]
    wbg = self.cast_w("g1", I["ra_g1"], I["ra_g1"].ap()[j], D, 160)
    g1v = wbg.ap().rearrange("(kc p) l -> p kc l", p=128)
    P.dma("sp", bf(g1a)[:, 0:1024].rearrange("p (kc l) -> p kc l", l=128), g1v[:, :, 0:128], reads=[wbg], writes=[g1a])
    P.dma("sp", bf(g1b)[:, 0:256].rearrange("p (kc l) -> p kc l", l=32), g1v[:, :, 128:160], reads=[wbg], writes=[g1b])
    P.dma("sp", g2a[:], I["ra_g2"].ap()[j, 0:128, :], reads=[I["ra_g2"]], writes=[g2a])
    P.dma("sp", g2b[0:32, :], I["ra_g2"].ap()[j, 128:160, :], reads=[I["ra_g2"]], writes=[g2b])
    for jb in range(self.nblk):
        xs = block_setup(jb, 5)
        pa, pb2 = self.PS[4], self.PS[5]
        for kc in range(8):
            xb, xv = xs[kc]
            P.op("pe", lambda e, kc=kc, xv=xv: e.matmul(
                pa[:, 0:256], lhsT=bf(g1a)[:, kc * 128:(kc + 1) * 128], rhs=xv, start=(kc == 0), stop=(kc == 7)),
                [g1a, xb], [pa], inc=(kc == 7))
        for kc in range(8):
            xb, xv = xs[kc]
            P.op("pe", lambda e, kc=kc, xv=xv: e.matmul(
                pb2[0:32, 0:256], lhsT=bf(g1b)[:, kc * 32:(kc + 1) * 32], rhs=xv, start=(kc == 0), stop=(kc == 7)),
                [g1b, xb], [pb2], inc=(kc == 7))
        ta, tb = T4[4], T4[5]
        P.op("act", lambda e: e.activation(out=ta[:, 0:256], in_=pa[:, 0:256], func=AF.Sigmoid), [pa], [ta])
        P.op("act", lambda e: e.activation(out=tb[0:32, 0:256], in_=pb2[0:32, 0:256], func=AF.Sigmoid), [pb2], [tb])
        for mt in range(2):
            r0 = jb * 256 + mt * 128
            ot = T4[20 + mt]
            for hf in range(2):
                pb = self.PS[mt * 2 + hf]
                P.op("pe", lambda e, hf=hf, mt=mt, pb=pb: e.matmul(
                    pb[:, :], lhsT=ta[:, mt * 128:(mt + 1) * 128], rhs=g2a[:, hf * 512:(hf + 1) * 512],
                    start=True, stop=False), [ta, g2a], [pb], inc=False)
                P.op("pe", lambda e, hf=hf, mt=mt, pb=pb: e.matmul(
                    pb[:, :], lhsT=tb[0:32, mt * 128:(mt + 1) * 128], rhs=g2b[0:32, hf * 512:(hf + 1) * 512],
                    start=False, stop=True), [tb, g2b], [pb])
                P.op("act", lambda e, ot=ot, pb=pb, hf=hf: e.activation(
                    out=ot[:, hf * 512:(hf + 1) * 512], in_=pb[:, :], func=AF.Copy), [pb], [ot])
            P.dma("pool", SG_.ap()[r0:r0 + 128, :], ot[:], reads=[ot], writes=[SG_])


K.rw_xs = rw_xs
K.rw_xx = rw_xx
K.rwkv_pre = rwkv_pre


def rwkv_consts(self):
    if hasattr(self, "upC"):
        return
    P = self.P
    self.upC = P.sb("upC", [128, 128]); self.loC = P.sb("loC", [128, 128]); self.onesC = P.sb("onesC", [128, 128])
    for dst, src in ((self.upC, self.up), (self.loC, self.lo), (self.onesC, self.ones)):
        P.op("dve", lambda e, dst=dst, src=src: e.tensor_scalar(out=dst[:], in0=src[:], scalar1=-CDEC, scalar2=None,
                                                                 op0=ALU.mult), [src], [dst])
    self.MK1 = []; self.MK3 = []
    for d in range(2):
        strict_T, incl_T, strict_N = (self.ups, self.up, self.los) if d == 0 else (self.los, self.lo, self.ups)
        m1 = P.sb("mk1_%d" % d, [128, 512]); m3 = P.sb("mk3_%d" % d, [128, 512])
        for q in range(4):
            src = strict_T if q % 2 == 0 else incl_T
            P.op("dve", lambda e, m1=m1, q=q, src=src: e.tensor_copy(out=m1[:, q * 128:(q + 1) * 128], in_=src[:]), [src], [m1])
            P.op("dve", lambda e, m3=m3, q=q: e.tensor_copy(out=m3[:, q * 128:(q + 1) * 128], in_=strict_N[:]), [strict_N], [m3])
        self.MK1.append(m1); self.MK3.append(m3)


def rwkv_scan(self, d):
    P = self.P
    T4 = self.T4
    PS = self.PS
    SR, SV, SKK = self.scratch("r"), self.scratch("v"), self.scratch("kk")
    SSG, SKD, SBA = self.scratch("sg%d" % d), self.scratch("kd%d" % d), self.scratch("ba%d" % d)
    SY = self.scratch("y%d" % d)
    triC = self.upC if d == 0 else self.loC
    mk1, mk3 = self.MK1[d], self.MK3[d]
    ST = T4[29]
    stv = ST[:, 0:512].rearrange("p (q v) -> p q v", v=64)
    P.op("pool", lambda e: e.memset(ST[:], 0.0), [], [ST])
    order = [0, 1] + list(range(2, self.nt)) if d == 0 else [1, 0] + list(range(self.nt - 1, 1, -1))
    for c in order:
        r0 = c * 128
        Lr, Lv, Lkk, Lkd, Lba, Lsg = T4[0:6]
        for (dst, src, q) in ((Lsg, SSG, "sp"), (Lkk, SKK, "pool"), (Lba, SBA, "sp"), (Lkd, SKD, "pool"),
                              (Lr, SR, "sp"), (Lv, SV, "pool")):
            P.dma(q, dst[:], src.ap()[r0:r0 + 128, :], reads=[src], writes=[dst])
        for hf in range(2):
            P.op("pe", lambda e, hf=hf: e.matmul(PS[hf][:, :], lhsT=triC[:], rhs=Lsg[:, hf * 512:(hf + 1) * 512],
                                                 start=True, stop=True), [triC, Lsg], [PS[hf]])
            P.op("pe", lambda e, hf=hf: e.matmul(PS[2 + hf][:, :], lhsT=self.onesC[:], rhs=Lsg[:, hf * 512:(hf + 1) * 512],
                                                 start=True, stop=True), [self.onesC, Lsg], [PS[2 + hf]])
        gC = self.small("rw_gC", [128, 8])
        for p in range(8):
            P.op("pe", lambda e, p=p: e.matmul(PS[5][:, p:p + 1], lhsT=Lsg[:, p * 128:(p + 1) * 128],
                                               rhs=self.onesC[:, 0:1], start=True, stop=True),
                 [Lsg, self.onesC], [PS[5]], inc=(p == 7))
        P.op("act", lambda e: e.activation(out=gC[:], in_=PS[5][:, 0:8], func=AF.Exp), [PS[5]], [gC])
        lg, ex = T4[6], T4[7]
        Bt_, Kt_, Rt_, At_, Bh, Kh = T4[8:14]
        for hf in range(2):
            P.op("act", lambda e, hf=hf: e.activation(out=lg[:, hf * 512:(hf + 1) * 512], in_=PS[hf][:, :], func=AF.Copy),
                 [PS[hf]], [lg])
        P.op("act", lambda e: e.activation(out=ex[:], in_=lg[:], func=AF.Exp, scale=-1.0), [lg], [ex])
        P.op("dve", lambda e: e.tensor_tensor(out=Bt_[:], in0=Lba[:], in1=ex[:], op=ALU.mult), [Lba, ex], [Bt_])
        P.op("pool", lambda e: e.tensor_tensor(out=Kt_[:], in0=Lkd[:], in1=ex[:], op=ALU.mult), [Lkd, ex], [Kt_])
        P.op("act", lambda e: e.activation(out=ex[:], in_=lg[:], func=AF.Exp), [lg], [ex])
        P.op("dve", lambda e: e.tensor_tensor(out=Rt_[:], in0=Lr[:], in1=ex[:], op=ALU.mult), [Lr, ex], [Rt_])
        P.op("dve", lambda e: e.scalar_tensor_tensor(out=ex[:], in0=Lsg[:], scalar=CDEC, in1=lg[:], op0=ALU.mult,
                                                     op1=ALU.add), [Lsg, lg, Rt_], [ex])
        P.op("act", lambda e: e.activation(out=ex[:], in_=ex[:], func=AF.Exp), [ex], [ex])
        P.op("dve", lambda e: e.scalar_tensor_tensor(out=At_[:], in0=Lkk[:], scalar=-1.0, in1=ex[:], op0=ALU.mult,
                                                     op1=ALU.mult), [Lkk, ex], [At_])
        for hf in range(2):
            P.op("dve", lambda e, hf=hf: e.tensor_tensor(out=ex[:, hf * 512:(hf + 1) * 512], in0=PS[2 + hf][:, :],
                                                         in1=lg[:, hf * 512:(hf + 1) * 512], op=ALU.subtract),
                 [PS[2 + hf], lg, At_], [ex])
        P.op("act", lambda e: e.activation(out=ex[:], in_=ex[:], func=AF.Exp), [ex], [ex])
        P.op("dve", lambda e: e.tensor_tensor(out=Bh[:], in0=Lba[:], in1=ex[:], op=ALU.mult), [Lba, ex], [Bh])
        P.op("pool", lambda e: e.tensor_tensor(out=Kh[:], in0=Lkd[:], in1=ex[:], op=ALU.mult), [Lkd, ex], [Kh])
        AR0, AR1, BT, KT = T4[14], T4[15], T4[16], T4[17]
        bank = 4
        for (src, kind) in ((At_, 0), (Rt_, 1), (Bt_, 2), (Kt_, 3)):
            for hf in range(2):
                psa = PS[4 + (bank % 4)]; bank += 1
                if kind < 2:
                    art = AR0 if hf == 0 else AR1
                    dst = art
                    dap = art[:].rearrange("p (q a t) -> p q a t", a=2, t=128)[:, :, kind, :]
                else:
                    dst = BT if kind == 2 else KT
                    dap = dst[:, hf * 512:(hf + 1) * 512].rearrange("p (q t) -> p q t", t=128)
                for q in range(4):
                    P.op("pe", lambda e, q=q, psa=psa, src=src, hf=hf: e.transpose(
                        psa[:, q * 128:(q + 1) * 128], src[:, (hf * 4 + q) * 128:(hf * 4 + q + 1) * 128], self.ident[:]),
                        [src, self.ident], [psa], inc=(q == 3))
                eng = "act" if hf == 0 else "dve"
                if eng == "act":
                    P.op("act", lambda e, dap=dap, psa=psa: e.activation(
                        out=dap, in_=psa[:, :].rearrange("p (q t) -> p q t", t=128), func=AF.Copy), [psa], [dst])
                else:
                    P.op("dve", lambda e, dap=dap, psa=psa: e.tensor_copy(
                        out=dap, in_=psa[:, :].rearrange("p (q t) -> p q t", t=128)), [psa], [dst])

        def ar(p, a, rows):
            art = AR0 if p < 4 else AR1
            return art, art[:].rearrange("p (q a t) -> p q a t", a=2, t=128)[rows, p % 4, a, :]

        def ar2(p, rows):
            art = AR0 if p < 4 else AR1
            return art, art[rows, (p % 4) * 256:(p % 4 + 1) * 256]

        Yt = T4[18]
        for g in range(4):
            CH = [T4[19], T4[20]]
            MR, MA = T4[21], T4[22]
            chv = [t[:].rearrange("p (x h t) -> p x h t", x=2, t=128) for t in CH]
            mrv = MR[:].rearrange("p (x h t) -> p x h t", x=2, t=128)
            makv = MA[:, 0:512].rearrange("p (h t) -> p h t", t=128)
            Wt = MA[:, 512:768]
            for pp in range(2):
                p = g * 2 + pp
                pm1, pm2 = PS[pp], PS[2 + pp]
                for jj in range(2):
                    rows = slice(64 * jj, 64 * jj + 64)
                    ab, a2 = ar2(p, rows)
                    bt_ap = BT[rows, p * 128:(p + 1) * 128]
                    kt_ap = KT[rows, p * 128:(p + 1) * 128]
                    P.op("pe", lambda e, pm1=pm1, jj=jj, bt_ap=bt_ap, a2=a2: e.matmul(
                        pm1[:, jj * 256:(jj + 1) * 256], lhsT=bt_ap, rhs=a2, start=True, stop=True),
                        [BT, ab], [pm1], inc=(jj == 1))
                    P.op("pe", lambda e, pm2=pm2, jj=jj, kt_ap=kt_ap, a2=a2: e.matmul(
                        pm2[:, jj * 256:(jj + 1) * 256], lhsT=kt_ap, rhs=a2, start=True, stop=True),
                        [KT, ab], [pm2], inc=(jj == 1))
                    ab0, a0 = ar(p, 0, rows)
                    hh = pp * 2 + jj
                    P.op("pe", lambda e, hh=hh, a0=a0, bt_ap=bt_ap: e.matmul(
                        PS[4][:, hh * 128:(hh + 1) * 128], lhsT=a0, rhs=bt_ap, start=True, stop=True),
                        [ab0, BT], [PS[4]], inc=(hh == 3))
                pm1v = pm1[:, :].rearrange("p (h a t) -> p h a t", a=2, t=128)
                pm2v = pm2[:, :].rearrange("p (h a t) -> p h a t", a=2, t=128)
                mkv = mk1[:].rearrange("p (h a t) -> p h a t", a=2, t=128)
                hs = slice(pp * 2, pp * 2 + 2)
                P.op("dve", lambda e, pm1v=pm1v, hs=hs: e.tensor_tensor(
                    out=chv[0][:, 1, hs, :], in0=pm1v[:, :, 0, :], in1=mkv[:, :, 0, :], op=ALU.mult), [pm1, mk1], [CH[0]])
                P.op("dve", lambda e, pm1v=pm1v, hs=hs: e.tensor_tensor(
                    out=mrv[:, 0, hs, :], in0=pm1v[:, :, 1, :], in1=mkv[:, :, 1, :], op=ALU.mult), [pm1, mk1], [MR])
                P.op("dve", lambda e, pm2v=pm2v, hs=hs: e.tensor_tensor(
                    out=makv[:, hs, :], in0=pm2v[:, :, 0, :], in1=mkv[:, :, 0, :], op=ALU.mult), [pm2, mk1], [MA])
                P.op("dve", lambda e, pm2v=pm2v, hs=hs: e.tensor_tensor(
                    out=mrv[:, 1, hs, :], in0=pm2v[:, :, 1, :], in1=mkv[:, :, 1, :], op=ALU.mult), [pm2, mk1], [MR])
            P.op("dve", lambda e: e.tensor_tensor(out=chv[0][:, 0, :, :].rearrange("p h t -> p (h t)"),
                                                  in0=PS[4][:, :], in1=mk3[:], op=ALU.mult), [PS[4], mk3], [CH[0]])
            for hh in range(4):
                p = g * 2 + hh // 2
                rows = slice(64 * (hh % 2), 64 * (hh % 2) + 64)
                h = g * 4 + hh
                ab0, a0 = ar(p, 0, rows)
                P.op("pe", lambda e, hh=hh, a0=a0, rows=rows, p=p: e.matmul(
                    PS[7][:, hh * 64:(hh + 1) * 64], lhsT=a0, rhs=stv[rows, p, :], start=True, stop=False),
                    [ab0, ST], [PS[7]], inc=False)
                P.op("pe", lambda e, hh=hh, h=h: e.matmul(
                    PS[7][:, hh * 64:(hh + 1) * 64], lhsT=makv[:, hh, :], rhs=Lv[:, h * 64:(h + 1) * 64],
                    start=False, stop=True), [MA, Lv], [PS[7]], inc=(hh == 3))
            P.op("act", lambda e: e.activation(out=Wt, in_=PS[7][:, 0:256], func=AF.Copy), [PS[7]], [MA])
            for rd in range(7):
                cur, nxt = chv[rd % 2], chv[(rd + 1) % 2]
                cb, nb = CH[rd % 2], CH[(rd + 1) % 2]
                for hh in range(4):
                    P.op("pe", lambda e, hh=hh, cur=cur: e.matmul(
                        PS[7][:, hh * 64:(hh + 1) * 64], lhsT=cur[:, 1, hh, :], rhs=Wt[:, hh * 64:(hh + 1) * 64],
                        start=True, stop=True), [cb, MA], [PS[7]], inc=(hh == 3))
                P.op("dve", lambda e: e.tensor_tensor(out=Wt, in0=Wt, in1=PS[7][:, 0:256], op=ALU.add), [MA, PS[7]], [MA])
                if rd < 6:
                    for hh in range(4):
                        P.op("pe", lambda e, hh=hh, cur=cur: e.matmul(
                            PS[5][:, hh * 128:(hh + 1) * 128], lhsT=cur[:, 1, hh, :], rhs=cur[:, 0, hh, :],
                            start=True, stop=True), [cb], [PS[5]], inc=(hh == 3))
                    for hh in range(4):
                        P.op("pe", lambda e, hh=hh, cur=cur: e.matmul(
                            PS[6][:, hh * 128:(hh + 1) * 128], lhsT=cur[:, 0, hh, :], rhs=cur[:, 1, hh, :],
                            start=True, stop=True), [cb], [PS[6]], inc=(hh == 3))
                    P.op("act", lambda e, nxt=nxt: e.activation(
                        out=nxt[:, 0, :, :].rearrange("p h t -> p (h t)"), in_=PS[5][:, :], func=AF.Copy), [PS[5]], [nb])
                    P.op("dve", lambda e, nxt=nxt: e.tensor_copy(
                        out=nxt[:, 1, :, :].rearrange("p h t -> p (h t)"), in_=PS[6][:, :]), [PS[6]], [nb])
            for hh in range(4):
                p = g * 2 + hh // 2
                rows = slice(64 * (hh % 2), 64 * (hh % 2) + 64)
                h = g * 4 + hh
                ab1, a1 = ar(p, 1, rows)
                P.op("pe", lambda e, hh=hh, a1=a1, rows=rows, p=p: e.matmul(
                    PS[4][:, hh * 64:(hh + 1) * 64], lhsT=a1, rhs=stv[rows, p, :], start=True, stop=False),
                    [ab1, ST], [PS[4]], inc=False)
                P.op("pe", lambda e, hh=hh: e.matmul(
                    PS[4][:, hh * 64:(hh + 1) * 64], lhsT=mrv[:, 0, hh, :], rhs=Wt[:, hh * 64:(hh + 1) * 64],
                    start=False, stop=False), [MR, MA], [PS[4]], inc=False)
                P.op("pe", lambda e, hh=hh, h=h: e.matmul(
                    PS[4][:, hh * 64:(hh + 1) * 64], lhsT=mrv[:, 1, hh, :], rhs=Lv[:, h * 64:(h + 1) * 64],
                    start=False, stop=True), [MR, Lv], [PS[4]], inc=(hh == 3))
            P.op("act", lambda e, g=g: e.activation(out=Yt[:, g * 256:(g + 1) * 256], in_=PS[4][:, 0:256], func=AF.Copy),
                 [PS[4]], [Yt])
            for pp in range(2):
                p = g * 2 + pp
                P.op("pe", lambda e, pp=pp, p=p: e.matmul(
                    PS[0][:, pp * 128:(pp + 1) * 128], lhsT=Bh[:, p * 128:(p + 1) * 128],
                    rhs=Wt[:, pp * 128:(pp + 1) * 128], start=True, stop=False), [Bh, MA], [PS[0]], inc=False)
                P.op("pe", lambda e, pp=pp, p=p: e.matmul(
                    PS[0][:, pp * 128:(pp + 1) * 128], lhsT=Kh[:, p * 128:(p + 1) * 128],
                    rhs=Lv[:, p * 128:(p + 1) * 128], start=False, stop=True), [Kh, Lv], [PS[0]], inc=(pp == 1))
            for pp in range(2):
                p = g * 2 + pp
                for jj in range(2):
                    rows = slice(64 * jj, 64 * jj + 64)
                    P.op("dve", lambda e, rows=rows, p=p, pp=pp, jj=jj: e.scalar_tensor_tensor(
                        out=stv[rows, p, :], in0=stv[rows, p, :], scalar=gC[rows, p:p + 1],
                        in1=PS[0][rows, pp * 128 + jj * 64:pp * 128 + jj * 64 + 64], op0=ALU.mult, op1=ALU.add),
                        [ST, gC, PS[0]], [ST])
        P.dma("sp", SY.ap()[r0:r0 + 128, :], Yt[:], reads=[Yt], writes=[SY])


def rwkv_post(self, i, Xin, Xout):
    P = self.P
    j = i // 3
    I = self.I
    T4 = self.T4
    wres = T4[0:8]
    self.load_wres(I["ra_w_out"], I["ra_w_out"].ap()[j], wres)
    lng, lnb = T4[8], T4[9]
    self.bcast_load(lng, lng[:], I["ra_lnx_g"], I["ra_lnx_g"].ap()[j])
    self.bcast_load(lnb, lnb[:], I["ra_lnx_b"], I["ra_lnx_b"].ap()[j])
    SY0, SY1, SV, SG_, SRKS = (self.scratch("y0"), self.scratch("y1"), self.scratch("v"), self.scratch("g"),
                               self.scratch("rks", 16))
    t_start = 2 if i == DEPTH - 1 else 0
    for t in range(t_start, self.nt):
        par = t % 2
        r0 = t * 128
        y, y1, vt, gt_, sq = T4[10 + par * 5:15 + par * 5]
        rks = self.small("post_rks%d" % par, [128, 16])
        mu = self.small("post_mu%d" % par, [128, 16])
        var = self.small("post_var%d" % par, [128, 16])
        P.dma("sp", y[:], SY0.ap()[r0:r0 + 128, :], reads=[SY0], writes=[y])
        P.dma("pool", y1[:], SY1.ap()[r0:r0 + 128, :], reads=[SY1], writes=[y1])
        P.dma("sp", vt[:], SV.ap()[r0:r0 + 128, :], reads=[SV], writes=[vt])
        P.dma("pool", gt_[:], SG_.ap()[r0:r0 + 128, :], reads=[SG_], writes=[gt_])
        P.dma("sp", rks[:], SRKS.ap()[r0:r0 + 128, :], reads=[SRKS], writes=[rks])
        v3 = lambda tl: tl[:].rearrange("p (h k) -> p h k", k=64)
        bc = lambda sm_: sm_[:].unsqueeze(2).to_broadcast([128, 16, 64])
        P.op("dve", lambda e: e.tensor_tensor(out=y[:], in0=y[:], in1=y1[:], op=ALU.add), [y, y1], [y])
        P.op("dve", lambda e: e.tensor_reduce(out=mu[:], in_=v3(y), axis=AX.X, op=ALU.add), [y], [mu])
        P.op("dve", lambda e: e.tensor_scalar(out=mu[:], in0=mu[:], scalar1=1.0 / 64, scalar2=None, op0=ALU.mult), [mu], [mu])
        P.op("dve", lambda e: e.tensor_tensor(out=v3(y), in0=v3(y), in1=bc(mu), op=ALU.subtract), [y, mu], [y])
        P.op("act", lambda e: e.activation(out=sq[:], in_=y[:], func=AF.Square), [y], [sq])
        P.op("dve", lambda e: e.tensor_reduce(out=var[:], in_=v3(sq), axis=AX.X, op=ALU.add), [sq], [var])
        P.op("dve", lambda e: e.tensor_scalar(out=var[:], in0=var[:], scalar1=1.0 / 64, scalar2=6.4e-4, op0=ALU.mult,
                                              op1=ALU.add), [var], [var])
        P.op("act", lambda e: e.activation(out=var[:], in_=var[:], func=AF.Sqrt), [var], [var])
        P.op("dve", lambda e: e.reciprocal(out=var[:], in_=var[:]), [var], [var])
        P.op("dve", lambda e: e.tensor_tensor(out=v3(y), in0=v3(y), in1=bc(var), op=ALU.mult), [y, var], [y])
        P.op("pool", lambda e: e.tensor_tensor(out=y[:], in0=y[:], in1=lng[:], op=ALU.mult), [y, lng], [y])
        P.op("pool", lambda e: e.tensor_tensor(out=y[:], in0=y[:], in1=lnb[:], op=ALU.add), [y, lnb], [y])
        P.op("dve", lambda e: e.tensor_tensor(out=v3(vt), in0=v3(vt), in1=bc(rks), op=ALU.mult), [vt, rks], [vt])
        P.op("dve", lambda e: e.tensor_tensor(out=y[:], in0=y[:], in1=vt[:], op=ALU.add), [y, vt], [y])
        P.op("dve", lambda e: e.tensor_tensor(out=y[:], in0=y[:], in1=gt_[:], op=ALU.mult), [y, gt_], [y])
        zT = T4[20 + par]
        for hf in range(2):
            self.transpose_to(y, [y[:, (hf * 4 + q) * 128:(hf * 4 + q + 1) * 128] for q in range(4)],
                              zT, [(bf(zT)[:, hf * 512:(hf + 1) * 512], 0, 512)], self.PS[par * 2 + hf],
                              evac="act" if hf == 0 else "dve")
        self.proj_residual([(zT, bf(zT)[:, kc * 128:(kc + 1) * 128]) for kc in range(8)], wres, Xin, Xout, t, par)


def rwkv(self, i, Xin, Xout):
    ph = self.cfg.get("ra_phases", "pre,scan0,scan1,post")
    self.rwkv_consts()
    if "pre" in ph:
        self.rwkv_pre(i, Xin)
    for d in range(2):
        if "scan%d" % d in ph:
            self.rwkv_scan(d)
    if "post" in ph:
        self.rwkv_post(i, Xin, Xout)


K.rwkv_consts = rwkv_consts
K.rwkv_scan = rwkv_scan
K.rwkv_post = rwkv_post
K.rwkv = rwkv


def tm_pass(self, wbuf, wap2d, col0, ncols, bias_buf, bias_ap, dst, post=None, out_bf=False):
    P = self.P
    T4 = self.T4
    wres = T4[0:8]
    self.load_wres(wbuf, wap2d[:, col0:col0 + ncols], wres, ncols)
    bt = None
    if bias_buf is not None:
        bt = T4[24]
        self.bcast_load(bt, bt[:, 0:ncols], bias_buf, bias_ap)
    nb = (ncols + 511) // 512
    for jb in range(self.nblk):
        hv = self.load_hTe(jb, T4[8 + (jb % 2) * 3:11 + (jb % 2) * 3])
        for mt in range(2):
            ot = T4[20 + mt]
            for b in range(nb):
                w = min(512, ncols - b * 512)
                pb = self.PS[(mt * 2 + b) % 4]
                for kc in range(8):
                    hb, ha = hv(kc)
                    P.op("pe", lambda e, kc=kc, ha=ha, pb=pb, b=b, w=w, mt=mt: e.matmul(
                        pb[:, 0:w], lhsT=ha[:, 1 + mt * 128:1 + (mt + 1) * 128], rhs=bf(wres[kc])[:, b * 512:b * 512 + w],
                        start=(kc == 0), stop=(kc == 7)), [hb, wres[kc]], [pb], inc=(kc == 7))
                if bt is not None:
                    P.op("dve", lambda e, ot=ot, pb=pb, b=b, w=w: e.tensor_tensor(
                        out=ot[:, b * 512:b * 512 + w], in0=pb[:, 0:w], in1=bt[:, b * 512:b * 512 + w], op=ALU.add),
                        [pb, bt], [ot])
                else:
                    oap = bf(ot)[:, b * 512:b * 512 + w] if out_bf else ot[:, b * 512:b * 512 + w]
                    P.op("act", lambda e, oap=oap, pb=pb, w=w: e.activation(
                        out=oap, in_=pb[:, 0:w], func=AF.Copy), [pb], [ot])
            if post is not None:
                post(ot, jb, mt)
            r0 = jb * 256 + mt * 128
            if dst is not None:
                src = bf(ot)[:, 0:ncols] if out_bf else ot[:, 0:ncols]
                P.dma("pool", dst.ap()[r0:r0 + 128, 0:ncols], src, reads=[ot], writes=[dst])


def fm_pass(self, wbuf, wap2d, col0, nchunks, bias_cols, scales, dsts):
    P = self.P
    T4 = self.T4
    wres = T4[0:8]
    ncols = nchunks * 128
    self.load_wres(wbuf, wap2d[:, col0:col0 + ncols], wres, ncols)
    for jb in range(self.nblk):
        hv = self.load_hTe(jb, T4[8 + (jb % 2) * 3:11 + (jb % 2) * 3])
        for hc in range(nchunks):
            pb = self.PS[hc % 4]
            for kc in range(8):
                hb, ha = hv(kc)
                P.op("pe", lambda e, kc=kc, ha=ha, pb=pb, hc=hc: e.matmul(
                    pb[:, 0:256], lhsT=bf(wres[kc])[:, hc * 128:(hc + 1) * 128], rhs=ha[:, 1:257],
                    start=(kc == 0), stop=(kc == 7)), [hb, wres[kc]], [pb], inc=(kc == 7))
            ot = T4[20 + hc % 4]
            if bias_cols is not None:
                P.op("act", lambda e, ot=ot, pb=pb, hc=hc: e.activation(
                    out=ot[:, 0:256], in_=pb[:, 0:256], func=AF.Identity, scale=scales[hc],
                    bias=bias_cols[:, hc:hc + 1]), [pb, bias_cols], [ot])
            else:
                P.op("act", lambda e, ot=ot, pb=pb, hc=hc: e.activation(
                    out=ot[:, 0:256], in_=pb[:, 0:256], func=AF.Copy), [pb], [ot])
            dbuf, row0 = dsts[hc]
            P.dma("pool", dbuf.ap()[row0:row0 + 128, jb * 256:(jb + 1) * 256], ot[:, 0:256], reads=[ot], writes=[dbuf])


K.tm_pass = tm_pass
K.fm_pass = fm_pass


def fscratch(self, name, rows, dt=F32):
    if name not in self.S:
        self.S[name] = self.P.dram("S_" + name, [rows, self.ntok], dt)
    return self.S[name]


def mlstm_pre(self, i, Xin):
    P = self.P
    I = self.I
    self.norm_to_HT(Xin, 1, 0)
    w = I["ml_w_in"].ap()[0]
    b = I["ml_b_in"]
    QT, KT = self.fscratch("mlqT", 512), self.fscratch("mlkT", 512)
    MLA, MLB, MLC = self.scratch("mla"), self.scratch("mlb"), self.scratch("mlc")
    bc = self.small("ml_bc", [128, 8])
    self.load_cols(bc, bc[:], b, b.ap()[0, 0:1024].rearrange("(hc p) -> hc p", p=128), 8)
    sk = 128 ** -0.5
    P.op("dve", lambda e: e.tensor_scalar(out=bc[:, 4:8], in0=bc[:, 4:8], scalar1=sk, scalar2=None, op0=ALU.mult), [bc], [bc])
    self.fm_pass(I["ml_w_in"], w, 0, 8, bc, [1.0] * 4 + [sk] * 4,
                 [(QT, h * 128) for h in range(4)] + [(KT, h * 128) for h in range(4)])

    def scale_k(ot, jb, mt):
        P.op("dve", lambda e: e.tensor_scalar(out=ot[:, 0:512], in0=ot[:, 0:512], scalar1=sk, scalar2=None, op0=ALU.mult),
             [ot], [ot])
    self.tm_pass(I["ml_w_in"], w, 512, 1024, b, b.ap()[0, 512:1536], MLA, post=scale_k)
    self.tm_pass(I["ml_w_in"], w, 1536, 1024, b, b.ap()[0, 1536:2560], MLB)
    self.tm_pass(I["ml_w_in"], w, 2560, 528, b, b.ap()[0, 2560:3088], MLC)


def mlstm_scan(self, d):
    P = self.P
    T4 = self.T4
    PS = self.PS
    QT, KT = self.fscratch("mlqT", 512), self.fscratch("mlkT", 512)
    MLA, MLB, MLC = self.scratch("mla"), self.scratch("mlb"), self.scratch("mlc")
    HS = self.scratch("mlh%d" % d)
    tri = self.up if d == 0 else self.lo
    ecol_i = 127 if d == 0 else 0
    CN = [T4[28], T4[29]]
    V1 = [T4[26], T4[27]]
    cnv = [t[:, 0:514].rearrange("p (h v) -> p h v", v=257) for t in CN]
    v1v = [t[:, 0:514].rearrange("p (h v) -> p h v", v=257) for t in V1]
    for t in CN:
        P.op("pool", lambda e, t=t: e.memset(t[:], 0.0), [], [t])
    for t in V1:
        P.op("pool", lambda e, t=t: e.memset(t[:], 1.0), [], [t])
    order = [0, 1] + list(range(2, self.nt)) if d == 0 else [1, 0] + list(range(self.nt - 1, 1, -1))
    for c in order:
        r0 = c * 128
        qk = T4[0]
        qkv = qk[:].rearrange("p (x h t) -> p x h t", x=2, t=128)
        la, lb = T4[1], T4[2]
        gts = self.small("ml_g", [128, 16])
        lfa = self.small("ml_lf", [128, 16])
        P.dma("sp", qkv[:, 0, :, :], QT.ap().rearrange("(h p) c -> p h c", p=128)[:, :, r0:r0 + 128], reads=[QT], writes=[qk])
        P.dma("pool", qkv[:, 1, :, :], KT.ap().rearrange("(h p) c -> p h c", p=128)[:, :, r0:r0 + 128], reads=[KT], writes=[qk])
        P.dma("sp", la[:], MLA.ap()[r0:r0 + 128, :], reads=[MLA], writes=[la])
        P.dma("pool", lb[:, 0:512], MLB.ap()[r0:r0 + 128, 0:512], reads=[MLB], writes=[lb])
        P.dma("sp", gts[:], MLC.ap()[r0:r0 + 128, 512:528], reads=[MLC], writes=[gts])
        P.op("act", lambda e: e.activation(out=lfa[:], in_=gts[:], func=AF.Sigmoid), [gts], [lfa])
        P.op("act", lambda e: e.activation(out=lfa[:], in_=lfa[:], func=AF.Ln), [lfa], [lfa])
        P.op("dve", lambda e: e.tensor_copy(out=v1v[0][:, :, 0:256], in_=la[:, 512:1024].rearrange("p (h v) -> p h v", v=256)),
             [la], [V1[0]])
        P.op("dve", lambda e: e.tensor_copy(out=v1v[1][:, :, 0:256], in_=lb[:, 0:512].rearrange("p (h v) -> p h v", v=256)),
             [lb], [V1[1]])
        hout = T4[3]
        for h in range(4):
            par = h % 2
            lfcol = lfa[:, d * 8 + 4 + h:d * 8 + 5 + h]
            licol = gts[:, d * 8 + h:d * 8 + h + 1]
            LFB, Dm, EB, PT, QS, KH = T4[4 + par * 6:10 + par * 6]
            sm = self.small("ml_sm%d" % par, [128, 4])
            psb, psq, pso, pss = PS[par * 4:par * 4 + 4]
            P.op("dve", lambda e: e.tensor_scalar(out=LFB[:, 0:128], in0=self.ones[:], scalar1=lfcol, scalar2=None,
                                                  op0=ALU.mult), [self.ones, lfa], [LFB])
            P.op("pe", lambda e: e.matmul(psb[:, 0:128], lhsT=LFB[:, 0:128], rhs=tri[:], start=True, stop=True),
                 [LFB, tri], [psb], inc=False)
            P.op("pe", lambda e: e.matmul(psb[:, 128:129], lhsT=tri[:], rhs=lfcol, start=True, stop=True),
                 [tri, lfa], [psb])
            P.op("dve", lambda e: e.tensor_tensor(out=sm[:, 0:1], in0=licol, in1=psb[:, 128:129], op=ALU.subtract),
                 [gts, psb], [sm])
            P.op("act", lambda e: e.activation(out=Dm[:, 0:128], in_=psb[:, 0:128], func=AF.Exp, bias=sm[:, 0:1]),
                 [psb, sm], [Dm])
            P.op("dve", lambda e: e.tensor_tensor(out=Dm[:, 0:128], in0=Dm[:, 0:128], in1=tri[:], op=ALU.mult), [Dm, tri], [Dm])
            P.op("act", lambda e: e.activation(out=EB[:, 0:128], in_=psb[:, 0:128], func=AF.Exp), [psb], [EB])
            P.op("pe", lambda e: e.matmul(psq[:, 0:128], lhsT=qkv[:, 1, h, :], rhs=qkv[:, 0, h, :], start=True, stop=True),
                 [qk], [psq])
            P.op("dve", lambda e: e.tensor_tensor(out=PT[:, 0:128], in0=Dm[:, 0:128], in1=psq[:, 0:128], op=ALU.mult),
                 [Dm, psq], [PT])
            P.op("pool", lambda e: e.tensor_tensor(out=QS[:, 0:128], in0=qkv[:, 0, h, :], in1=EB[:, 0:128], op=ALU.mult),
                 [qk, EB], [QS])
            vv = v1v[h // 2][:, h % 2, :]
            cv = cnv[h // 2][:, h % 2, :]
            P.op("pe", lambda e: e.matmul(pso[:, 0:257], lhsT=PT[:, 0:128], rhs=vv, start=True, stop=False),
                 [PT, V1[h // 2]], [pso], inc=False)
            P.op("pe", lambda e: e.matmul(pso[:, 0:257], lhsT=QS[:, 0:128], rhs=cv, start=False, stop=True),
                 [QS, CN[h // 2]], [pso])
            P.op("dve", lambda e: e.tensor_copy(out=sm[:, 3:4], in_=pso[:, 256:257]), [pso], [sm])
            P.op("dve", lambda e: e.scalar_tensor_tensor(out=sm[:, 1:2], in0=sm[:, 3:4], scalar=-1.0,
                                                         in1=sm[:, 3:4], op0=ALU.mult, op1=ALU.max), [sm], [sm])
            P.op("dve", lambda e: e.tensor_scalar_max(out=sm[:, 1:2], in0=sm[:, 1:2], scalar1=1.0), [sm], [sm])
            P.op("dve", lambda e: e.reciprocal(out=sm[:, 1:2], in_=sm[:, 1:2]), [sm], [sm])
            P.op("dve", lambda e: e.tensor_scalar(out=hout[:, h * 256:(h + 1) * 256], in0=pso[:, 0:256], scalar1=sm[:, 1:2],
                                                  scalar2=None, op0=ALU.mult), [pso, sm], [hout])
            P.op("act", lambda e: e.activation(out=sm[:, 2:3], in_=psb[:, ecol_i:ecol_i + 1], func=AF.Exp, bias=sm[:, 0:1]),
                 [psb, sm], [sm])
            P.op("dve", lambda e: e.tensor_scalar(out=KH[:, 0:128], in0=la[:, h * 128:(h + 1) * 128], scalar1=sm[:, 2:3],
                                                  scalar2=None, op0=ALU.mult), [la, sm], [KH])
            P.op("pe", lambda e: e.matmul(pss[:, 0:257], lhsT=KH[:, 0:128], rhs=vv, start=True, stop=True),
                 [KH, V1[h // 2]], [pss])
            P.op("dve", lambda e: e.scalar_tensor_tensor(out=cv, in0=cv, scalar=EB[:, ecol_i:ecol_i + 1], in1=pss[:, 0:257],
                                                         op0=ALU.mult, op1=ALU.add), [CN[h // 2], EB, pss], [CN[h // 2]])
        P.dma("sp", HS.ap()[r0:r0 + 128, :], hout[:], reads=[hout], writes=[HS])


def mlstm_post(self, i, Xin, Xout):
    P = self.P
    I = self.I
    T4 = self.T4
    wres = T4[0:8]
    self.load_wres(I["ml_w_out"], I["ml_w_out"].ap()[0], wres)
    ng = T4[8]
    self.bcast_load(ng, ng[:], I["ml_norm_g"], I["ml_norm_g"].ap()[0])
    HS0, HS1, MLB, MLC = self.scratch("mlh0"), self.scratch("mlh1"), self.scratch("mlb"), self.scratch("mlc")
    t_start = 2 if i == DEPTH - 1 else 0
    for t in range(t_start, self.nt):
        par = t % 2
        r0 = t * 128
        y, y1, ot, sq = T4[10 + par * 4:14 + par * 4]
        ss = self.small("mlp_ss%d" % par, [128, 4])
        P.dma("sp", y[:], HS0.ap()[r0:r0 + 128, :], reads=[HS0], writes=[y])
        P.dma("pool", y1[:], HS1.ap()[r0:r0 + 128, :], reads=[HS1], writes=[y1])
        P.dma("sp", ot[:, 0:512], MLB.ap()[r0:r0 + 128, 512:1024], reads=[MLB], writes=[ot])
        P.dma("pool", ot[:, 512:1024], MLC.ap()[r0:r0 + 128, 0:512], reads=[MLC], writes=[ot])
        v3 = lambda tl: tl[:].rearrange("p (h k) -> p h k", k=256)
        P.op("dve", lambda e: e.tensor_tensor(out=y[:], in0=y[:], in1=y1[:], op=ALU.add), [y, y1], [y])
        P.op("act", lambda e: e.activation(out=sq[:], in_=y[:], func=AF.Square), [y], [sq])
        P.op("dve", lambda e: e.tensor_reduce(out=ss[:], in_=v3(sq), axis=AX.X, op=ALU.add), [sq], [ss])
        P.op("dve", lambda e: e.tensor_scalar(out=ss[:], in0=ss[:], scalar1=1.0 / 256, scalar2=EPS, op0=ALU.mult,
                                              op1=ALU.add), [ss], [ss])
        P.op("act", lambda e: e.activation(out=ss[:], in_=ss[:], func=AF.Sqrt), [ss], [ss])
        P.op("dve", lambda e: e.reciprocal(out=ss[:], in_=ss[:]), [ss], [ss])
        P.op("dve", lambda e: e.tensor_tensor(out=v3(y), in0=v3(y), in1=ss[:].unsqueeze(2).to_broadcast([128, 4, 256]),
                                              op=ALU.mult), [y, ss], [y])
        P.op("act", lambda e: e.activation(out=ot[:], in_=ot[:], func=AF.Sigmoid), [ot], [ot])
        P.op("pool", lambda e: e.tensor_tensor(out=y[:], in0=y[:], in1=ng[:], op=ALU.mult), [y, ng], [y])
        P.op("dve", lambda e: e.tensor_tensor(out=y[:], in0=y[:], in1=ot[:], op=ALU.mult), [y, ot], [y])
        zT = T4[20 + par]
        for hf in range(2):
            self.transpose_to(y, [y[:, (hf * 4 + q) * 128:(hf * 4 + q + 1) * 128] for q in range(4)],
                              zT, [(bf(zT)[:, hf * 512:(hf + 1) * 512], 0, 512)], self.PS[par * 2 + hf],
                              evac="act" if hf == 0 else "dve")
        self.proj_residual([(zT, bf(zT)[:, kc * 128:(kc + 1) * 128]) for kc in range(8)], wres, Xin, Xout, t, par)


def mlstm(self, i, Xin, Xout):
    self.mlstm_pre(i, Xin)
    for d in range(2):
        self.mlstm_scan(d)
    self.mlstm_post(i, Xin, Xout)


K.fscratch = fscratch
K.mlstm_pre = mlstm_pre
K.mlstm_scan = mlstm_scan
K.mlstm_post = mlstm_post
K.mlstm = mlstm


I32 = mybir.dt.int32


def da_rope_tables(self):
    P = self.P
    T4 = self.T4
    nlt = self.seq // 128
    pi_ = self.small("rp_pi", [128, 1], I32); ti = self.small("rp_ti", [128, 1], I32)
    colv = self.small("rp_col", [128, 1]); rowp = self.small("rp_row", [128, 1])
    ji = self.small("rp_ji", [128, 16], I32); inv = self.small("rp_inv", [128, 16]); angc = self.small("rp_angc", [128, 16])
    taui = self.small("rp_taui", [128, 64], I32); rowf = self.small("rp_rowf", [128, 64])
    P.op("pool", lambda e: e.iota(pi_[:], [[0, 1]], base=0, channel_multiplier=1), [], [pi_])
    P.op("pool", lambda e: e.iota(ji[:], [[1, 16]], base=0, channel_multiplier=0), [], [ji])
    P.op("pool", lambda e: e.iota(taui[:], [[2, 64]], base=0, channel_multiplier=0), [], [taui])
    P.op("dve", lambda e: e.tensor_single_scalar(out=ti[:], in_=pi_[:], scalar=63, op=ALU.bitwise_and), [pi_], [ti])
    P.op("dve", lambda e: e.tensor_copy(out=colv[:], in_=ti[:]), [ti], [colv])
    P.op("dve", lambda e: e.tensor_single_scalar(out=ti[:], in_=pi_[:], scalar=6, op=ALU.arith_shift_right), [pi_, colv], [ti])
    P.op("dve", lambda e: e.tensor_copy(out=rowp[:], in_=ti[:]), [ti], [rowp])
    P.op("dve", lambda e: e.tensor_copy(out=inv[:], in_=ji[:]), [ji], [inv])
    P.op("act", lambda e: e.activation(out=inv[:], in_=inv[:], func=AF.Exp, scale=-math.log(10000.0) / 16), [inv], [inv])
    P.op("dve", lambda e: e.tensor_copy(out=rowf[:], in_=taui[:]), [taui], [rowf])
    P.op("dve", lambda e: e.tensor_scalar(out=rowf[:], in0=rowf[:], scalar1=rowp[:, 0:1], scalar2=None, op0=ALU.add),
         [rowf, rowp], [rowf])
    P.op("dve", lambda e: e.tensor_scalar(out=angc[:], in0=inv[:], scalar1=colv[:, 0:1], scalar2=None, op0=ALU.mult),
         [inv, colv], [angc])
    self.COS = [T4[25], T4[26]]
    self.SIN = [T4[27], T4[28]]
    ang, kf, tmp = T4[14], T4[15], T4[16]
    ki = self.P.sb("rp_ki", [128, 1024], I32)
    for half in range(2):
        n = min(32, nlt - half * 32)
        if n <= 0:
            break
        w = n * 32
        av = ang[:, 0:w].rearrange("p (t j) -> p t j", j=32)
        for tt in range(n):
            P.op("dve", lambda e, tt=tt: e.tensor_scalar(
                out=av[:, tt, 0:16], in0=inv[:], scalar1=rowf[:, half * 32 + tt:half * 32 + tt + 1], scalar2=None,
                op0=ALU.mult), [inv, rowf], [ang])
        P.op("dve", lambda e: e.tensor_copy(out=av[:, :, 16:32], in_=angc[:].unsqueeze(1).to_broadcast([128, n, 16])),
             [angc], [ang])
        for (dst, shift) in ((self.SIN[half], 0.0), (self.COS[half], math.pi / 2)):
            P.op("dve", lambda e: e.tensor_scalar(out=tmp[:, 0:w], in0=ang[:, 0:w], scalar1=shift, scalar2=None, op0=ALU.add),
                 [ang], [tmp])
            P.op("dve", lambda e: e.tensor_scalar(out=kf[:, 0:w], in0=tmp[:, 0:w], scalar1=1.0 / (2 * math.pi), scalar2=None,
                                                  op0=ALU.mult), [tmp], [kf])
            P.op("dve", lambda e: e.tensor_copy(out=ki[:, 0:w], in_=kf[:, 0:w]), [kf], [ki])
            P.op("dve", lambda e: e.tensor_copy(out=kf[:, 0:w], in_=ki[:, 0:w]), [ki], [kf])
            P.op("dve", lambda e: e.scalar_tensor_tensor(out=tmp[:, 0:w], in0=kf[:, 0:w], scalar=-2 * math.pi,
                                                         in1=tmp[:, 0:w], op0=ALU.mult, op1=ALU.add), [kf, tmp], [tmp])
            P.op("dve", lambda e: e.tensor_scalar(out=kf[:, 0:w], in0=tmp[:, 0:w], scalar1=math.pi, scalar2=None,
                                                  op0=ALU.is_gt), [tmp], [kf])
            P.op("dve", lambda e: e.scalar_tensor_tensor(out=tmp[:, 0:w], in0=kf[:, 0:w], scalar=-2 * math.pi,
                                                         in1=tmp[:, 0:w], op0=ALU.mult, op1=ALU.add), [kf, tmp], [tmp])
            P.op("act", lambda e, dst=dst: e.activation(out=dst[:, 0:w], in_=tmp[:, 0:w], func=AF.Sin), [tmp], [dst])


def da_pre(self, i, Xin):
    P = self.P
    I = self.I
    T4 = self.T4
    self.norm_to_HT(Xin, 1, 0)
    self.da_rope_tables()
    w = I["da_w_qkv"].ap()[0]
    QT, KT = self.fscratch("daqT", 1024, BF16), self.fscratch("dakT", 1024, BF16)
    V = self.scratch("dav", D, BF16)

    def mk_post(dstT):
        def post(ot, jb, mt):
            tau = jb * 2 + mt - 2
            src = ot
            if tau >= 0:
                t1, t2, rt = T4[14], T4[15], T4[16]
                ov = ot[:].rearrange("p (h x j) -> p h x j", x=2, j=32)
                rv = rt[:].rearrange("p (h x j) -> p h x j", x=2, j=32)
                t1v = t1[:, 0:512].rearrange("p (h j) -> p h j", j=32)
                t2v = t2[:, 0:512].rearrange("p (h j) -> p h j", j=32)
                cs = self.COS[tau // 32][:, (tau % 32) * 32:(tau % 32 + 1) * 32].unsqueeze(1).to_broadcast([128, 16, 32])
                sn = self.SIN[tau // 32][:, (tau % 32) * 32:(tau % 32 + 1) * 32].unsqueeze(1).to_broadcast([128, 16, 32])
                cb, sb_ = self.COS[tau // 32], self.SIN[tau // 32]
                P.op("dve", lambda e: e.tensor_tensor(out=t1v, in0=ov[:, :, 0, :], in1=cs, op=ALU.mult), [ot, cb], [t1])
                P.op("pool", lambda e: e.tensor_tensor(out=t2v, in0=ov[:, :, 1, :], in1=sn, op=ALU.mult), [ot, sb_], [t2])
                P.op("dve", lambda e: e.tensor_tensor(out=rv[:, :, 0, :], in0=t1v, in1=t2v, op=ALU.subtract), [t1, t2], [rt])
                P.op("dve", lambda e: e.tensor_tensor(out=t1v, in0=ov[:, :, 0, :], in1=sn, op=ALU.mult), [ot, sb_, rt], [t1])
                P.op("pool", lambda e: e.tensor_tensor(out=t2v, in0=ov[:, :, 1, :], in1=cs, op=ALU.mult), [ot, cb, rt], [t2])
                P.op("dve", lambda e: e.tensor_tensor(out=rv[:, :, 1, :], in0=t1v, in1=t2v, op=ALU.add), [t1, t2], [rt])
                src = rt
            zT = T4[17 + mt]
            for hf in range(2):
                self.transpose_to(src, [src[:, (hf * 4 + q) * 128:(hf * 4 + q + 1) * 128] for q in range(4)],
                                  zT, [(bf(zT)[:, hf * 512:(hf + 1) * 512], 0, 512)], self.PS[4 + mt * 2 + hf],
                                  evac="act" if hf == 0 else "dve")
            r0 = jb * 256 + mt * 128
            P.dma("sp", dstT.ap().rearrange("(h p) c -> p h c", p=128)[:, :, r0:r0 + 128],
                  bf(zT)[:, 0:1024].rearrange("p (h c) -> p h c", c=128), reads=[zT], writes=[dstT])
        return post
    self.tm_pass(I["da_w_qkv"], w, 0, 1024, None, None, None, post=mk_post(QT))
    self.tm_pass(I["da_w_qkv"], w, 1024, 1024, None, None, None, post=mk_post(KT))
    self.tm_pass(I["da_w_qkv"], w, 2048, 1024, None, None, V, out_bf=True)


def da_attn(self, i):
    P = self.P
    I = self.I
    T4 = self.T4
    PS = self.PS
    QT, KT, OT = self.fscratch("daqT", 1024, BF16), self.fscratch("dakT", 1024, BF16), self.fscratch("daoT", 1024, BF16)
    V = self.scratch("dav", D, BF16)
    lambda_init = 0.8 - 0.6 * math.exp(-0.3 * i)
    lt = T4[29]
    self.bcast_load(lt, lt[:, 0:256], I["da_lambda"], I["da_lambda"].ap()[0].rearrange("a k -> (a k)"))
    lam = self.small("da_lam", [128, 4])
    P.op("dve", lambda e: e.tensor_tensor(out=lt[:, 256:320], in0=lt[:, 0:64], in1=lt[:, 64:128], op=ALU.mult), [lt], [lt])
    P.op("dve", lambda e: e.tensor_tensor(out=lt[:, 320:384], in0=lt[:, 128:192], in1=lt[:, 192:256], op=ALU.mult), [lt], [lt])
    P.op("dve", lambda e: e.tensor_reduce(out=lam[:, 0:2], in_=lt[:, 256:384].rearrange("p (a k) -> p a k", k=64),
                                          axis=AX.X, op=ALU.add), [lt], [lam])
    P.op("act", lambda e: e.activation(out=lam[:, 0:2], in_=lam[:, 0:2], func=AF.Exp), [lam], [lam])
    P.op("dve", lambda e: e.tensor_tensor(out=lam[:, 2:3], in0=lam[:, 1:2], in1=lam[:, 0:1], op=ALU.subtract), [lam], [lam])
    P.op("dve", lambda e: e.tensor_scalar(out=lam[:, 2:3], in0=lam[:, 2:3], scalar1=-lambda_init, scalar2=None, op0=ALU.add),
         [lam], [lam])
    ngc = self.small("da_ngc", [128, 1])
    self.load_cols(ngc, ngc[:], I["da_norm_g"], I["da_norm_g"].ap()[0:1, :], 1)
    P.op("dve", lambda e: e.tensor_scalar(out=ngc[:], in0=ngc[:], scalar1=1.0 - lambda_init, scalar2=None, op0=ALU.mult),
         [ngc], [ngc])
    nkt = self.nt
    nkT = (self.ntok + 1023) // 1024
    for h in range(8):
        kres = T4[0:nkT]
        vres = T4[9:9 + nkT]
        for q in range(nkT):
            c0 = q * 1024
            wd = min(1024, self.ntok - c0)
            P.dma("sp", bf(kres[q])[:, 0:wd], KT.ap()[h * 128:(h + 1) * 128, c0:c0 + wd], reads=[KT], writes=[kres[q]])
            nk = wd // 128
            P.dma("pool", bf(vres[q])[:, 0:wd].rearrange("p (kt e) -> p kt e", e=128),
                  V.ap()[c0:c0 + wd, h * 128:(h + 1) * 128].rearrange("(kt p) e -> p kt e", p=128),
                  reads=[V], writes=[vres[q]])
        blocks = [(0, 256, [0, 1])] + [(CTX + b * 512, 512, list(range(nkt))) for b in range(self.seq // 512)]
        for bi, (q0, qn, kts) in enumerate(blocks):
            qt = T4[18 + bi % 2]
            P.dma("sp", bf(qt)[:, 0:qn], QT.ap()[h * 128:(h + 1) * 128, q0:q0 + qn], reads=[QT], writes=[qt])
            for ki_, kt in enumerate(kts):
                kq, ko = kt // 8, (kt % 8) * 128
                for m in range(2):
                    psc = PS[4 + (2 * ki_ + m) % 4]
                    et = T4[20 + (2 * ki_ + m) % 4]
                    rows = slice(64 * m, 64 * m + 64)
                    P.op("pe", lambda e: e.matmul(psc[:, 0:qn], lhsT=bf(kres[kq])[rows, ko:ko + 128], rhs=bf(qt)[rows, 0:qn],
                                                  start=True, stop=True), [kres[kq], qt], [psc])
                    P.op("act", lambda e: e.activation(out=bf(et)[:, 0:qn], in_=psc[:, 0:qn], func=AF.Exp, scale=0.125),
                         [psc], [et])
                    first, last = (ki_ == 0), (ki_ == len(kts) - 1)
                    P.op("pe", lambda e: e.matmul(PS[2 * m][:, 0:qn], lhsT=bf(vres[kq])[:, ko:ko + 128], rhs=bf(et)[:, 0:qn],
                                                  start=first, stop=last), [vres[kq], et], [PS[2 * m]], inc=False)
                    P.op("pe", lambda e: e.matmul(PS[2 * m + 1][:, 0:qn], lhsT=self.ones_bf[:], rhs=bf(et)[:, 0:qn],
                                                  start=first, stop=last), [self.ones_bf, et], [PS[2 * m + 1]])
            o0, o1, df = T4[24], T4[25], T4[26]
            for m, ob in ((0, o0), (1, o1)):
                P.op("dve", lambda e: e.reciprocal(out=ob[:, 0:qn], in_=PS[2 * m + 1][:, 0:qn]), [PS[2 * m + 1]], [ob])
                P.op("dve", lambda e: e.tensor_tensor(out=ob[:, 512:512 + qn], in0=PS[2 * m][:, 0:qn], in1=ob[:, 0:qn],
                                                      op=ALU.mult), [PS[2 * m], ob], [ob])
            P.op("dve", lambda e: e.scalar_tensor_tensor(out=df[:, 0:qn], in0=o1[:, 512:512 + qn], scalar=lam[:, 2:3],
                                                         in1=o0[:, 512:512 + qn], op0=ALU.mult, op1=ALU.add),
                 [o0, o1, lam], [df])
            P.op("act", lambda e: e.activation(out=df[:, 512:512 + qn], in_=df[:, 0:qn], func=AF.Square), [df], [df])
            P.op("pe", lambda e: e.matmul(PS[4][:, 0:qn], lhsT=self.ones[:], rhs=df[:, 512:512 + qn], start=True, stop=True),
                 [self.ones, df], [PS[4]])
            rs = T4[27]
            P.op("dve", lambda e: e.tensor_scalar(out=rs[:, 0:qn], in0=PS[4][:, 0:qn], scalar1=1.0 / 128, scalar2=1e-5,
                                                  op0=ALU.mult, op1=ALU.add), [PS[4]], [rs])
            P.op("act", lambda e: e.activation(out=rs[:, 0:qn], in_=rs[:, 0:qn], func=AF.Sqrt), [rs], [rs])
            P.op("dve", lambda e: e.reciprocal(out=rs[:, 0:qn], in_=rs[:, 0:qn]), [rs], [rs])
            P.op("dve", lambda e: e.scalar_tensor_tensor(out=bf(rs)[:, 1024:1024 + qn], in0=df[:, 0:qn], scalar=ngc[:, 0:1],
                                                         in1=rs[:, 0:qn], op0=ALU.mult, op1=ALU.mult), [df, ngc, rs], [rs])
            P.dma("pool", OT.ap()[h * 128:(h + 1) * 128, q0:q0 + qn], bf(rs)[:, 1024:1024 + qn], reads=[rs], writes=[OT])


def da_post(self, i, Xin, Xout):
    P = self.P
    I = self.I
    T4 = self.T4
    wres = T4[0:8]
    self.load_wres(I["da_w_out"], I["da_w_out"].ap()[0], wres)
    OT = self.fscratch("daoT", 1024, BF16)
    t_start = 2 if i == DEPTH - 1 else 0
    for t in range(t_start, self.nt):
        par = t % 2
        zT = T4[20 + par]
        P.dma("sp", bf(zT)[:, 0:1024].rearrange("p (h c) -> p h c", c=128),
              OT.ap().rearrange("(h p) c -> p h c", p=128)[:, :, t * 128:(t + 1) * 128], reads=[OT], writes=[zT])
        self.proj_residual([(zT, bf(zT)[:, kc * 128:(kc + 1) * 128]) for kc in range(8)], wres, Xin, Xout, t, par)


def diffattn(self, i, Xin, Xout):
    self.da_pre(i, Xin)
    self.da_attn(i)
    self.da_post(i, Xin, Xout)


K.da_rope_tables = da_rope_tables
K.da_pre = da_pre
K.da_attn = da_attn
K.da_post = da_post
K.diffattn = diffattn


N_CORES = 8
_CACHE = {}


def kernel(**inputs):
    if "k" not in _CACHE:
        _CACHE["k"] = build({})
    k = _CACHE["k"]
    maps = []
    for c in range(N_CORES):
        b = c % 2
        m = {}
        for name, _ in INPUT_SHAPES:
            a = np.asarray(inputs[name])
            if name in ("x", "c", "ctx"):
                a = a[b]
            m[name] = np.ascontiguousarray(a, dtype=np.float32)
        maps.append(m)
    res = run_bass_kernel_spmd(k.nc, maps, core_ids=list(range(N_CORES)))
    out = np.stack([np.asarray(res.results[b]["out"]) for b in range(2)], axis=0)
    return out.astype(np.float32)
```

```python
import math
import numpy as np
import concourse.bass as bass
import concourse.mybir as mybir
from concourse.bass_utils import run_bass_kernel_spmd

F32 = mybir.dt.float32
BF16 = mybir.dt.bfloat16
ALU = mybir.AluOpType
AF = mybir.ActivationFunctionType
AX = mybir.AxisListType

D = 1024
SEQ = 8192
CTX = 256
NTOK = CTX + SEQ
NT = NTOK // 128
DEPTH = 4
FF = 2816
NFC = FF // 128
EPS = 1e-6
HTC = NTOK + 4


def bf(tile):
    return tile[:].bitcast(BF16)


def col_of(tok):
    return tok + 1 if tok < CTX else tok + 3


class Buf:
    __slots__ = ("name", "t", "wr", "rd", "multi")

    def __init__(self, name, t=None, multi=False):
        self.name = name
        self.t = t
        self.wr = {}
        self.rd = {}
        self.multi = multi

    def __getitem__(self, idx):
        return self.t[idx]

    def ap(self):
        return self.t.ap()


class Prog:
    NDMA = 56

    def __init__(self, nc):
        self.nc = nc
        self.E = {"pe": nc.tensor, "dve": nc.vector, "act": nc.scalar,
                  "pool": nc.gpsimd, "sp": nc.sync}
        self.sem = {e: nc.alloc_semaphore("s_" + e) for e in self.E}
        self.cnt = {e: 0 for e in self.E}
        self.seen = {e: {} for e in self.E}
        self.dsem = [nc.alloc_semaphore("d%d" % i) for i in range(self.NDMA)]
        self.duse = [0] * self.NDMA
        self.dnext = 0
        self.ninst = 0
        self.out_tokens = []
        self.dump = set()
        self.pend = {e: ([], []) for e in self.E}

    def sb(self, name, shape, dt=F32):
        return Buf(name, self.nc.alloc_sbuf_tensor(name, list(shape), dt))

    def ps(self, name, shape, dt=F32):
        return Buf(name, self.nc.alloc_psum_tensor(name, list(shape), dt))

    def dram(self, name, shape, dt=F32, kind=None):
        if kind is None:
            kind = "ExternalOutput" if name in self.dump else "Internal"
        return Buf(name, self.nc.dram_tensor(name, list(shape), dt, kind=kind), multi=True)

    def _semh(self, key):
        return self.sem[key] if isinstance(key, str) else self.dsem[key]

    def _need(self, eng, key, val, need):
        if key == eng and eng == "pe":
            return
        if self.seen[eng].get(key, 0) >= val:
            return
        if need.get(key, 0) < val:
            need[key] = val

    def _deps(self, eng, reads, writes):
        need = {}
        for b in reads:
            for k, v in b.wr.items():
                self._need(eng, k, v, need)
        for b in writes:
            if not b.multi:
                for k, v in b.wr.items():
                    self._need(eng, k, v, need)
            for k, v in b.rd.items():
                self._need(eng, k, v, need)
        return need

    def _emit_waits(self, eng, need, keep_last=True):
        items = list(need.items())
        last = None
        if keep_last and items:
            last = items.pop()
        for k, v in items:
            self.E[eng].wait_ge(self._semh(k), v)
            self.seen[eng][k] = v
            self.ninst += 1
        if last is not None:
            self.seen[eng][last[0]] = last[1]
        return last

    def _commit(self, key, val, reads, writes):
        for b in writes:
            if b.multi:
                b.wr[key] = max(b.wr.get(key, 0), val)
            else:
                b.wr = {key: val}
                b.rd = {}
        for b in reads:
            if (not b.multi) and any(b is w for w in writes):
                continue
            b.rd[key] = max(b.rd.get(key, 0), val)

    def op(self, eng, fn, reads=(), writes=(), inc=True):
        last = self._emit_waits(eng, self._deps(eng, reads, writes))
        ins = fn(self.E[eng])
        if last is not None:
            ins._wait_ge(self._semh(last[0]), last[1])
        self.ninst += 1
        pr, pw = self.pend[eng]
        if inc:
            ins.then_inc(self.sem[eng], 1)
            self.cnt[eng] += 1
            if pr or pw:
                reads = list(reads) + [b for b in pr if not any(b is r for r in reads)]
                writes = list(writes) + [b for b in pw if not any(b is w for w in writes)]
                self.pend[eng] = ([], [])
            self._commit(eng, self.cnt[eng], reads, writes)
        else:
            for b in reads:
                if not any(b is r for r in pr):
                    pr.append(b)
            for b in writes:
                if not any(b is w for w in pw):
                    pw.append(b)
        return ins

    def dma(self, q, out, in_, reads=(), writes=(), is_output=False, **kw):
        q = "sp"
        need = self._deps(q, reads, writes)
        j = self.dnext
        self.dnext = (self.dnext + 1) % self.NDMA
        if self.duse[j]:
            self._need(q, j, 16 * self.duse[j], need)
        last = self._emit_waits(q, need)
        self.duse[j] += 1
        ins = self.E[q].dma_start(out=out, in_=in_, **kw)
        if last is not None:
            ins._wait_ge(self._semh(last[0]), last[1])
        ins.then_inc(self.dsem[j], 16)
        self.ninst += 1
        self._commit(j, 16 * self.duse[j], reads, writes)
        if is_output:
            self.out_tokens.append((j, 16 * self.duse[j]))

    def finish(self, eng="sp"):
        need = {}
        for key, val in self.out_tokens:
            self._need(eng, key, val, need)
        for e in self.E:
            if e != eng and self.cnt[e]:
                self._need(eng, e, self.cnt[e], need)
        for j in range(self.NDMA):
            if self.duse[j]:
                self._need(eng, j, 16 * self.duse[j], need)
        self._emit_waits(eng, need, keep_last=False)


INPUT_SHAPES = [
    ("x", [SEQ, D]), ("c", [D]), ("ctx", [CTX, D]), ("c_ctx", [D]),
    ("ada_w", [4, D, 6 * D]), ("ada_b", [4, 6 * D]), ("norm1_g", [4, D]), ("norm2_g", [4, D]),
    ("ffn_w_in", [4, D, 2 * FF]), ("ffn_conv_w", [4, 3, FF]), ("ffn_conv_b", [4, FF]),
    ("ffn_w_out", [4, FF, D]),
    ("ra_mix", [2, 6, D]), ("ra_w_rkv", [2, 3, D, D]), ("ra_w0", [2, 2, D]),
    ("ra_w1", [2, 2, D, 64]), ("ra_w2", [2, 2, 64, D]), ("ra_a0", [2, 2, D]),
    ("ra_a1", [2, 2, D, 64]), ("ra_a2", [2, 2, 64, D]), ("ra_g1", [2, D, 160]),
    ("ra_g2", [2, 160, D]), ("ra_k_k", [2, D]), ("ra_k_a", [2, D]), ("ra_r_k", [2, 16, 64]),
    ("ra_lnx_g", [2, D]), ("ra_lnx_b", [2, D]), ("ra_w_out", [2, D, D]),
    ("ml_w_in", [1, D, 3088]), ("ml_b_in", [1, 3088]), ("ml_norm_g", [1, D]), ("ml_w_out", [1, D, D]),
    ("da_w_qkv", [1, D, 3 * D]), ("da_lambda", [1, 4, 64]), ("da_norm_g", [1, 128]),
    ("da_w_out", [1, D, D]), ("final_g", [D]),
]


class K:
    def __init__(self, cfg):
        self.cfg = cfg
        self.seq = cfg.get("seq", SEQ)
        self.ntok = CTX + self.seq
        self.nt = self.ntok // 128
        self.nblk = self.ntok // 256
        self.htc = self.ntok + 4
        nc = bass.Bass("TRN2", target_bir_lowering=False)
        self.nc = nc
        P = Prog(nc)
        P.dump = set(cfg.get("dump", []))
        self.P = P
        self.I = {}
        for name, shape in INPUT_SHAPES:
            if name == "x":
                shape = [self.seq, D]
            self.I[name] = P.dram(name, shape, F32, kind="ExternalInput")
        self.out = P.dram("out", [self.seq, D], F32, kind="ExternalOutput")
        self.X = [P.dram("XA", [self.ntok, D]), P.dram("XB", [self.ntok, D])]
        self.HT = P.dram("HT", [D, self.htc], BF16)
        self.T4 = [P.sb("t4_%d" % i, [128, 1024]) for i in range(30)]
        self.S = {}
        self.MOD = [P.sb("modl", [128, 6 * D]), P.sb("modc", [128, 6 * D])]
        self.PS = [P.ps("ps%d" % i, [128, 512]) for i in range(8)]
        self.sm = {}
        self.consts()

    def small(self, name, shape, dt=F32):
        if name not in self.sm:
            self.sm[name] = self.P.sb("sm_" + name, shape, dt)
        return self.sm[name]

    def col_of(self, tok):
        return tok + 1 if tok < CTX else tok + 3

    def consts(self):
        P = self.P
        self.ident = P.sb("ident", [128, 128])
        self.ones = P.sb("ones", [128, 128])
        self.lo = P.sb("lo", [128, 128])
        self.up = P.sb("up", [128, 128])
        self.los = P.sb("los", [128, 128])
        self.ups = P.sb("ups", [128, 128])
        self.zero = P.sb("zero", [128, 64])

        def mask(t, pat, cm, op):
            P.op("pool", lambda e: e.memset(t[:], 1.0), [], [t])
            P.op("pool", lambda e: e.affine_select(out=t[:], in_=t[:], pattern=[[pat, 128]],
                                                    compare_op=op, fill=0.0, base=0,
                                                    channel_multiplier=cm), [t], [t])
        mask(self.lo, -1, 1, ALU.is_ge)
        mask(self.up, 1, -1, ALU.is_ge)
        mask(self.los, -1, 1, ALU.is_gt)
        mask(self.ups, 1, -1, ALU.is_gt)
        P.op("pool", lambda e: e.memset(self.ones[:], 1.0), [], [self.ones])
        self.ones_bf = P.sb("ones_bf", [128, 128], BF16)
        P.op("pool", lambda e: e.memset(self.ones_bf[:], 1.0), [], [self.ones_bf])
        P.op("pool", lambda e: e.memset(self.zero[:], 0.0), [], [self.zero])
        P.op("dve", lambda e: e.tensor_tensor(out=self.ident[:], in0=self.lo[:], in1=self.up[:],
                                              op=ALU.mult), [self.lo, self.up], [self.ident])

    def bcast_load(self, dst, dst_ap, src_buf, src_ap, q="sp"):
        self.P.dma(q, dst_ap, src_ap.partition_broadcast(128), reads=[src_buf], writes=[dst])

    def transpose_to(self, src, src_aps, dst, dst_aps, psa, evac="act"):
        P = self.P
        for i, sap in enumerate(src_aps):
            P.op("pe", lambda e, i=i, sap=sap: e.transpose(psa[:, i * 128:(i + 1) * 128], sap, self.ident[:]),
                 [src, self.ident], [psa])
        for (dap, c0, c1) in dst_aps:
            if evac == "act":
                P.op("act", lambda e, dap=dap, c0=c0, c1=c1: e.activation(out=dap, in_=psa[:, c0:c1], func=AF.Copy),
                     [psa], [dst])
            else:
                P.op("dve", lambda e, dap=dap, c0=c0, c1=c1: e.tensor_copy(out=dap, in_=psa[:, c0:c1]),
                     [psa], [dst])

    def load_cols(self, dst, dst_ap, src_buf, src_ap, n):
        P = self.P
        st = self.small("lc_stage", [128, 128])
        ps = self.PS[0]
        P.dma("sp", st[0:n, :], src_ap, reads=[src_buf], writes=[st])
        P.op("pe", lambda e: e.transpose(ps[:, 0:n], st[0:n, :], self.ident[0:n, 0:n]), [st, self.ident], [ps])
        P.op("dve", lambda e: e.tensor_copy(out=dst_ap, in_=ps[:, 0:n]), [ps], [dst])

    def init(self):
        P = self.P
        X0 = self.X[0]
        P.dma("sp", X0.ap()[0:CTX, :], self.I["ctx"].ap(), reads=[self.I["ctx"]], writes=[X0])
        nrow = self.seq
        step = 2048
        for r0 in range(0, nrow, step):
            r1 = min(nrow, r0 + step)
            P.dma("pool" if (r0 // step) % 2 else "sp", X0.ap()[CTX + r0:CTX + r1, :],
                  self.I["x"].ap()[r0:r1, :], reads=[self.I["x"]], writes=[X0])
        htv = self.HT.ap().rearrange("(kc p) c -> p kc c", p=128)
        for c in (0, CTX + 1, CTX + 2, self.htc - 1):
            P.dma("sp", htv[:, :, c:c + 1], self.zero[:].bitcast(BF16)[:, 0:8].unsqueeze(2), reads=[self.zero], writes=[self.HT],
                  allow_slow_non_contiguous=True)
        self.CB = []
        for s, nm in enumerate(("c", "c_ctx")):
            cc = self.small("cc%d" % s, [128, 8])
            self.load_cols(cc, cc[:], self.I[nm], self.I[nm].ap().rearrange("(kc p) -> kc p", p=128), 8)
            P.op("act", lambda e, cc=cc: e.activation(out=cc[:], in_=cc[:], func=AF.Silu), [cc], [cc])
            cb = P.sb("cb%d" % s, [128, 1024])
            for kc in range(8):
                P.op("dve", lambda e, kc=kc, cb=cb, cc=cc: e.tensor_scalar(
                    out=cb[:, kc * 128:(kc + 1) * 128], in0=self.ones[:], scalar1=cc[:, kc:kc + 1], scalar2=None,
                    op0=ALU.mult), [self.ones, cc], [cb])
            self.CB.append(cb)

    def adaln(self, i):
        P = self.P
        aw = self.I["ada_w"].ap()[i].rearrange("(kc p) n -> p kc n", p=128)
        for blk in range(12):
            wts = [self.T4[4 + (blk % 2) * 4 + q] for q in range(4)]
            for q in range(4):
                P.dma("sp" if q % 2 == 0 else "pool", wts[q][:].rearrange("p (kc n) -> p kc n", n=512),
                      aw[:, 2 * q:2 * q + 2, blk * 512:(blk + 1) * 512], reads=[self.I["ada_w"]], writes=[wts[q]])
            bt = self.T4[blk % 2]
            self.bcast_load(bt, bt[:, 0:512], self.I["ada_b"], self.I["ada_b"].ap()[i, blk * 512:(blk + 1) * 512])
            for s in range(2):
                ps = self.PS[(blk % 2) * 2 + s]
                for kc in range(8):
                    wt = wts[kc // 2]
                    P.op("pe", lambda e, kc=kc, s=s, ps=ps, wt=wt: e.matmul(
                        ps[:, :], lhsT=self.CB[s][:, kc * 128:(kc + 1) * 128],
                        rhs=wt[:, (kc % 2) * 512:(kc % 2 + 1) * 512],
                        start=(kc == 0), stop=(kc == 7)), [self.CB[s], wt], [ps], inc=(kc == 7))
                P.op("dve", lambda e, s=s, ps=ps, bt=bt, blk=blk: e.tensor_tensor(
                    out=self.MOD[s][:, blk * 512:(blk + 1) * 512], in0=ps[:, :], in1=bt[:, 0:512], op=ALU.add),
                    [ps, bt], [self.MOD[s]])
        for (nm, off) in (("norm1_g", 1), ("norm2_g", 4)):
            gt = self.T4[2]
            self.bcast_load(gt, gt[:], self.I[nm], self.I[nm].ap()[i])
            for s in range(2):
                m = self.MOD[s]
                P.op("dve", lambda e, m=m, gt=gt, off=off: e.scalar_tensor_tensor(
                    out=m[:, off * D:(off + 1) * D], in0=m[:, off * D:(off + 1) * D], scalar=1.0, in1=gt[:],
                    op0=ALU.add, op1=ALU.mult), [m, gt], [m])

    def rms_rstd(self, xt, junk, ss, n, eps):
        P = self.P
        P.op("act", lambda e: e.activation(out=junk[:], in_=xt[:], func=AF.Square, accum_out=ss[:, 0:1]),
             [xt], [junk, ss])
        P.op("dve", lambda e: e.tensor_scalar(out=ss[:, 0:1], in0=ss[:, 0:1], scalar1=1.0 / n, scalar2=eps,
                                              op0=ALU.mult, op1=ALU.add), [ss], [ss])
        P.op("act", lambda e: e.activation(out=ss[:, 0:1], in_=ss[:, 0:1], func=AF.Sqrt), [ss], [ss])
        P.op("dve", lambda e: e.reciprocal(out=ss[:, 0:1], in_=ss[:, 0:1]), [ss], [ss])

    def norm_to_HT(self, Xin, goff, shoff):
        P = self.P
        htv = self.HT.ap().rearrange("(kc p) c -> p kc c", p=128)
        for t in range(self.nt):
            par = t % 2
            xt, h, hT = self.T4[par * 3], self.T4[par * 3 + 1], self.T4[par * 3 + 2]
            ss = self.small("nss%d" % par, [128, 1])
            m = self.MOD[1 if t < 2 else 0]
            P.dma("sp", xt[:], Xin.ap()[t * 128:(t + 1) * 128, :], reads=[Xin], writes=[xt])
            self.rms_rstd(xt, h, ss, D, EPS)
            P.op("dve", lambda e, xt=xt, h=h, ss=ss, m=m: e.scalar_tensor_tensor(
                out=h[:], in0=xt[:], scalar=ss[:, 0:1], in1=m[:, goff * D:(goff + 1) * D],
                op0=ALU.mult, op1=ALU.mult), [xt, ss, m], [h])
            P.op("dve", lambda e, h=h, m=m: e.tensor_tensor(
                out=h[:], in0=h[:], in1=m[:, shoff * D:(shoff + 1) * D], op=ALU.add), [h, m], [h])
            for hf in range(2):
                psa = self.PS[par * 2 + hf]
                self.transpose_to(h, [h[:, (hf * 4 + q) * 128:(hf * 4 + q + 1) * 128] for q in range(4)],
                                  hT, [(bf(hT)[:, hf * 512:(hf + 1) * 512], 0, 512)], psa,
                                  evac="act" if hf == 0 else "dve")
            c0 = self.col_of(t * 128)
            P.dma("pool", htv[:, :, c0:c0 + 128], bf(hT)[:, 0:1024].rearrange("p (kc c) -> p kc c", c=128),
                  reads=[hT], writes=[self.HT])

    def load_hTe(self, j, tiles):
        htv = self.HT.ap().rearrange("(kc p) c -> p kc c", p=128)
        c0 = self.col_of(j * 256)
        views = []
        for q, (k0, k1) in enumerate(((0, 3), (3, 6), (6, 8))):
            v = bf(tiles[q])[:, 0:(k1 - k0) * 258].rearrange("p (kc c) -> p kc c", c=258)
            self.P.dma("sp" if q != 1 else "pool", v, htv[:, k0:k1, c0 - 1:c0 + 257], reads=[self.HT], writes=[tiles[q]])
            views.append(v)

        def hv(kc):
            return tiles[kc // 3], views[kc // 3][:, kc % 3, :]
        return hv

    def cast_w(self, key, src_buf, src_ap, R, C):
        P = self.P
        name = "WB_" + key
        if name not in self.S:
            self.S[name] = P.dram(name, [R, C], BF16)
        wb = self.S[name]
        n = 0
        for r0 in range(0, R, 128):
            rr = min(128, R - r0)
            for c0 in range(0, C, 1024):
                cc = min(1024, C - c0)
                st, ob = self.T4[14 + (n % 2) * 2], self.T4[15 + (n % 2) * 2]
                P.dma("sp", st[0:rr, 0:cc], src_ap[r0:r0 + rr, c0:c0 + cc], reads=[src_buf], writes=[st])
                k3 = n % 3
                if k3 == 0:
                    P.op("dve", lambda e: e.tensor_copy(out=bf(ob)[0:rr, 0:cc], in_=st[0:rr, 0:cc]), [st], [ob])
                elif k3 == 1:
                    P.op("act", lambda e: e.activation(out=bf(ob)[0:rr, 0:cc], in_=st[0:rr, 0:cc], func=AF.Copy), [st], [ob])
                else:
                    P.op("pool", lambda e: e.tensor_copy(out=bf(ob)[0:rr, 0:cc], in_=st[0:rr, 0:cc]), [st], [ob])
                P.dma("pool", wb.ap()[r0:r0 + rr, c0:c0 + cc], bf(ob)[0:rr, 0:cc], reads=[ob], writes=[wb])
                n += 1
        return wb

    def ffn(self, i, Xin, Xout):
        P = self.P
        self.norm_to_HT(Xin, 4, 3)
        cw = self.small("ffn_cw", [128, 3, NFC])
        cbias = self.small("ffn_cb", [128, NFC])
        self.load_cols(cw, cw[:].rearrange("p j fc -> p (j fc)"), self.I["ffn_conv_w"],
                       self.I["ffn_conv_w"].ap()[i].rearrange("j (fc p) -> (j fc) p", p=128), 3 * NFC)
        self.load_cols(cbias, cbias[:], self.I["ffn_conv_b"],
                       self.I["ffn_conv_b"].ap()[i].rearrange("(fc p) -> fc p", p=128), NFC)
        winb = self.cast_w("ffn_win", self.I["ffn_w_in"], self.I["ffn_w_in"].ap()[i], D, 2 * FF)
        woutb = self.cast_w("ffn_wout", self.I["ffn_w_out"], self.I["ffn_w_out"].ap()[i], FF, D)
        win = winb.ap().rearrange("(kc p) n -> p kc n", p=128)
        wout = woutb.ap()
        last_ctx = (i == DEPTH - 1)
        for j in range(self.nblk):
            if j == 0 and last_ctx:
                continue
            m = self.MOD[1 if j == 0 else 0]
            hv = self.load_hTe(j, self.T4[18 + (j % 2) * 3:21 + (j % 2) * 3])
            po = [self.PS[4], self.PS[5], self.PS[6], self.PS[7]]
            for fc in range(NFC):
                par = fc % 2
                wg, wv_, wo = self.T4[6 + par * 3], self.T4[7 + par * 3], self.T4[8 + par * 3]
                wgv = bf(wg)[:, 0:1024].rearrange("p (kc f) -> p kc f", f=128)
                wvv = bf(wv_)[:, 0:1024].rearrange("p (kc f) -> p kc f", f=128)
                P.dma("sp", wgv, win[:, :, FF + fc * 128:FF + (fc + 1) * 128], reads=[winb], writes=[wg])
                P.dma("pool", wvv, win[:, :, fc * 128:(fc + 1) * 128], reads=[winb], writes=[wv_])
                P.dma("sp", bf(wo)[:, 0:1024], wout[fc * 128:(fc + 1) * 128, :], reads=[woutb], writes=[wo])
                pg, pv = self.PS[par * 2], self.PS[par * 2 + 1]
                for kc in range(8):
                    hb, ha = hv(kc)
                    P.op("pe", lambda e, kc=kc, pg=pg, wgv=wgv, ha=ha: e.matmul(
                        pg[:, 0:258], lhsT=wgv[:, kc, :], rhs=ha, start=(kc == 0), stop=(kc == 7)),
                        [wg, hb], [pg], inc=(kc == 7))
                for kc in range(8):
                    hb, ha = hv(kc)
                    P.op("pe", lambda e, kc=kc, pv=pv, wvv=wvv, ha=ha: e.matmul(
                        pv[:, 0:256], lhsT=wvv[:, kc, :], rhs=ha[:, 1:257], start=(kc == 0), stop=(kc == 7)),
                        [wv_, hb], [pv], inc=(kc == 7))
                tt = self.T4[12 + par]
                P.op("act", lambda e, tt=tt, pg=pg, fc=fc: e.activation(
                    out=tt[:, 0:256], in_=pg[:, 1:257], func=AF.Identity, scale=cw[:, 1, fc:fc + 1],
                    bias=cbias[:, fc:fc + 1]), [pg, cw, cbias], [tt])
                P.op("dve", lambda e, tt=tt, pg=pg, fc=fc: e.scalar_tensor_tensor(
                    out=tt[:, 0:256], in0=pg[:, 0:256], scalar=cw[:, 0, fc:fc + 1], in1=tt[:, 0:256],
                    op0=ALU.mult, op1=ALU.add), [pg, cw, tt], [tt])
                P.op("dve", lambda e, tt=tt, pg=pg, fc=fc: e.scalar_tensor_tensor(
                    out=tt[:, 0:256], in0=pg[:, 2:258], scalar=cw[:, 2, fc:fc + 1], in1=tt[:, 0:256],
                    op0=ALU.mult, op1=ALU.add), [pg, cw, tt], [tt])
                P.op("act", lambda e, tt=tt: e.activation(out=tt[:, 256:512], in_=tt[:, 0:256],
                                                          func=AF.Gelu_apprx_tanh), [tt], [tt])
                P.op("dve", lambda e, tt=tt, pv=pv: e.tensor_tensor(
                    out=bf(tt)[:, 1536:1792], in0=tt[:, 256:512], in1=pv[:, 0:256], op=ALU.mult), [tt, pv], [tt])
                for mt in range(2):
                    for hf in range(2):
                        pb = po[mt * 2 + hf]
                        P.op("pe", lambda e, tt=tt, wo=wo, pb=pb, mt=mt, hf=hf, fc=fc: e.matmul(
                            pb[:, :], lhsT=bf(tt)[:, 1536 + mt * 128:1536 + (mt + 1) * 128],
                            rhs=bf(wo)[:, hf * 512:(hf + 1) * 512], start=(fc == 0), stop=(fc == NFC - 1)),
                            [tt, wo], [pb], inc=(fc == NFC - 1))
            for mt in range(2):
                r0 = j * 256 + mt * 128
                xt, xn = self.T4[14 + mt * 2], self.T4[15 + mt * 2]
                P.dma("sp", xt[:], Xin.ap()[r0:r0 + 128, :], reads=[Xin], writes=[xt])
                for hf in range(2):
                    pb = po[mt * 2 + hf]
                    P.op("dve", lambda e, xn=xn, pb=pb, hf=hf, m=m: e.tensor_tensor(
                        out=xn[:, hf * 512:(hf + 1) * 512], in0=pb[:, :],
                        in1=m[:, 5 * D + hf * 512:5 * D + (hf + 1) * 512], op=ALU.mult), [pb, m], [xn])
                P.op("pool", lambda e, xn=xn, xt=xt: e.tensor_tensor(
                    out=xn[:], in0=xn[:], in1=xt[:], op=ALU.add), [xn, xt], [xn])
                P.dma("pool", Xout.ap()[r0:r0 + 128, :], xn[:], reads=[xn], writes=[Xout])

    def final(self, Xin):
        P = self.P
        g = self.T4[6]
        self.bcast_load(g, g[:], self.I["final_g"], self.I["final_g"].ap())
        for t in range(2, self.nt):
            par = t % 2
            xt, h = self.T4[par * 3], self.T4[par * 3 + 1]
            ss = self.small("nss%d" % par, [128, 1])
            P.dma("sp", xt[:], Xin.ap()[t * 128:(t + 1) * 128, :], reads=[Xin], writes=[xt])
            self.rms_rstd(xt, h, ss, D, EPS)
            P.op("dve", lambda e, xt=xt, h=h, ss=ss: e.scalar_tensor_tensor(
                out=h[:], in0=xt[:], scalar=ss[:, 0:1], in1=g[:], op0=ALU.mult, op1=ALU.mult), [xt, ss, g], [h])
            P.dma("pool", self.out.ap()[(t - 2) * 128:(t - 1) * 128, :], h[:], reads=[h], writes=[self.out],
                  is_output=True)


def build(cfg):
    k = K(cfg)
    k.init()
    cur = 0
    for (i, kind) in cfg.get("layers", [(0, "ra"), (1, "ml"), (2, "da"), (3, "ra")]):
        k.adaln(i)
        if kind == "ra":
            k.rwkv(i, k.X[cur], k.X[1 - cur]); cur = 1 - cur
        elif kind == "ml":
            k.mlstm(i, k.X[cur], k.X[1 - cur]); cur = 1 - cur
        elif kind == "da":
            k.diffattn(i, k.X[cur], k.X[1 - cur]); cur = 1 - cur
        if cfg.get("ffn", True):
            k.ffn(i, k.X[cur], k.X[1 - cur]); cur = 1 - cur
    k.final(k.X[cur])
    k.P.finish()
    return k


def scratch(self, name, cols=D, dt=F32):
    if name not in self.S:
        self.S[name] = self.P.dram("S_" + name, [self.ntok, cols], dt)
    return self.S[name]


def load_wres(self, wbuf, wap, tiles, ncols=D):
    wb = self.cast_w("wres", wbuf, wap, D, ncols) if ncols == D else self.cast_w("wres%d" % ncols, wbuf, wap, D, ncols)
    for kc in range(8):
        self.P.dma("sp" if kc % 2 == 0 else "pool", bf(tiles[kc])[:, 0:ncols], wb.ap()[kc * 128:(kc + 1) * 128, :],
                   reads=[wb], writes=[tiles[kc]])


def proj_residual(self, zT_list, wres, Xin, Xout, t, par):
    P = self.P
    m = self.MOD[1 if t < 2 else 0]
    xt, xn = self.T4[26 + par * 2], self.T4[27 + par * 2]
    P.dma("sp", xt[:], Xin.ap()[t * 128:(t + 1) * 128, :], reads=[Xin], writes=[xt])
    for hf in range(2):
        pb = self.PS[6 + hf]
        for kc in range(8):
            zb, za = zT_list[kc]
            P.op("pe", lambda e, kc=kc, za=za, pb=pb, hf=hf: e.matmul(
                pb[:, :], lhsT=za, rhs=bf(wres[kc])[:, hf * 512:(hf + 1) * 512], start=(kc == 0), stop=(kc == 7)),
                [zb, wres[kc]], [pb], inc=(kc == 7))
        P.op("dve", lambda e, xn=xn, pb=pb, hf=hf, m=m: e.tensor_tensor(
            out=xn[:, hf * 512:(hf + 1) * 512], in0=pb[:, :], in1=m[:, 2 * D + hf * 512:2 * D + (hf + 1) * 512],
            op=ALU.mult), [pb, m], [xn])
    P.op("pool", lambda e, xn=xn, xt=xt: e.tensor_tensor(out=xn[:], in0=xn[:], in1=xt[:], op=ALU.add), [xn, xt], [xn])
    P.dma("pool", Xout.ap()[t * 128:(t + 1) * 128, :], xn[:], reads=[xn], writes=[Xout])


K.scratch = scratch
K.load_wres = load_wres
K.proj_residual = proj_residual


CDEC = math.exp(-0.5)


def interleave(gens):
    gens = list(gens)
    while gens:
        for g in list(gens):
            try:
                next(g)
            except StopIteration:
                gens.remove(g)


def rw_xs(self, hv, jm, mixc, xs_tiles, xx_tiles):
    P = self.P
    views = []
    for kc in range(8):
        hb, ha = hv(kc)
        xt = xs_tiles[kc // 4]
        xv = bf(xt)[:, (kc % 4) * 256:(kc % 4 + 1) * 256]
        xxb = xx_tiles[kc // 4]
        xxv = xxb[:, (kc % 4) * 256:(kc % 4 + 1) * 256]
        eng = "dve"
        P.op(eng, lambda e, xv=xv, xxv=xxv, ha=ha, kc=kc: e.scalar_tensor_tensor(
            out=xv, in0=xxv, scalar=mixc[:, jm * 8 + kc:jm * 8 + kc + 1], in1=ha[:, 1:257],
            op0=ALU.mult, op1=ALU.add), [xxb, mixc, hb], [xt])
        views.append((xt, xv))
    return views


def rw_xx(self, hv, xx_tiles):
    P = self.P
    for kc in range(8):
        hb, ha = hv(kc)
        xxb = xx_tiles[kc // 4]
        xxv = xxb[:, (kc % 4) * 256:(kc % 4 + 1) * 256]
        eng = "dve"
        P.op(eng, lambda e, xxv=xxv, ha=ha: e.tensor_tensor(out=xxv, in0=ha[:, 0:256], in1=ha[:, 2:258], op=ALU.add),
             [hb], [xxb])
        P.op(eng, lambda e, xxv=xxv, ha=ha: e.scalar_tensor_tensor(
            out=xxv, in0=xxv, scalar=0.5, in1=ha[:, 1:257], op0=ALU.mult, op1=ALU.subtract), [xxb, hb], [xxb])


def rwkv_pre(self, i, Xin):
    P = self.P
    j = i // 3
    I = self.I
    self.norm_to_HT(Xin, 1, 0)
    mixc = self.small("ra_mixc", [128, 48])
    self.load_cols(mixc, mixc[:], I["ra_mix"], I["ra_mix"].ap()[j].rearrange("m (kc p) -> (m kc) p", p=128), 48)
    T4 = self.T4
    wres = T4[0:8]
    SR, SV, SK, SKK = self.scratch("r"), self.scratch("v"), self.scratch("k"), self.scratch("kk")
    SG_ = self.scratch("g")
    SSG = [self.scratch("sg0"), self.scratch("sg1")]
    SKD = [self.scratch("kd0"), self.scratch("kd1")]
    SBA = [self.scratch("ba0"), self.scratch("ba1")]
    SRKS = self.scratch("rks", 16)

    def block_setup(jb, jm):
        hv = self.load_hTe(jb, T4[8 + (jb % 2) * 3:11 + (jb % 2) * 3])
        xx_tiles = T4[14:16]
        self.rw_xx(hv, xx_tiles)
        xs_tiles = T4[16 + (jb % 2) * 2:18 + (jb % 2) * 2]
        return self.rw_xs(hv, jm, mixc, xs_tiles, xx_tiles)

    kkb = T4[24]
    for (jm, proj, dst) in ((0, 0, SR), (2, 2, SV), (1, 1, SK)):
        self.load_wres(I["ra_w_rkv"], I["ra_w_rkv"].ap()[j, proj], wres)
        if proj == 1:
            self.bcast_load(kkb, kkb[:], I["ra_k_k"], I["ra_k_k"].ap()[j])
        for jb in range(self.nblk):
            xs = block_setup(jb, jm)
            for mt in range(2):
                ot = T4[20 + mt]
                for hf in range(2):
                    pb = self.PS[(mt * 2 + hf) % 4]
                    for kc in range(8):
                        xb, xv = xs[kc]
                        P.op("pe", lambda e, kc=kc, xv=xv, pb=pb, hf=hf, mt=mt: e.matmul(
                            pb[:, :], lhsT=xv[:, mt * 128:(mt + 1) * 128], rhs=bf(wres[kc])[:, hf * 512:(hf + 1) * 512],
                            start=(kc == 0), stop=(kc == 7)), [xb, wres[kc]], [pb], inc=(kc == 7))
                    P.op("act", lambda e, ot=ot, pb=pb, hf=hf: e.activation(
                        out=ot[:, hf * 512:(hf + 1) * 512], in_=pb[:, :], func=AF.Copy), [pb], [ot])
                r0 = jb * 256 + mt * 128
                P.dma("pool", dst.ap()[r0:r0 + 128, :], ot[:], reads=[ot], writes=[dst])
                if proj == 1:
                    kr, sq = T4[22], T4[23]
                    ssh = self.small("kk_ss", [128, 16])
                    P.op("dve", lambda e, kr=kr, ot=ot: e.tensor_tensor(out=kr[:], in0=ot[:], in1=kkb[:], op=ALU.mult),
                         [ot, kkb], [kr])
                    P.op("act", lambda e, kr=kr, sq=sq: e.activation(out=sq[:], in_=kr[:], func=AF.Square), [kr], [sq])
                    P.op("dve", lambda e, sq=sq: e.tensor_reduce(
                        out=ssh[:], in_=sq[:].rearrange("p (h k) -> p h k", k=64), axis=AX.X, op=ALU.add), [sq], [ssh])
                    P.op("act", lambda e: e.activation(out=ssh[:], in_=ssh[:], func=AF.Sqrt), [ssh], [ssh])
                    P.op("dve", lambda e: e.tensor_scalar_max(out=ssh[:], in0=ssh[:], scalar1=1e-12), [ssh], [ssh])
                    P.op("dve", lambda e: e.reciprocal(out=ssh[:], in_=ssh[:]), [ssh], [ssh])
                    P.op("dve", lambda e, kr=kr: e.tensor_tensor(
                        out=kr[:].rearrange("p (h k) -> p h k", k=64), in0=kr[:].rearrange("p (h k) -> p h k", k=64),
                        in1=ssh[:].unsqueeze(2).to_broadcast([128, 16, 64]), op=ALU.mult), [kr, ssh], [kr])
                    P.dma("sp", SKK.ap()[r0:r0 + 128, :], kr[:], reads=[kr], writes=[SKK])

    kab, rkb = T4[24], T4[25]
    for (jm, nm1, nm2, nm0, is_w) in ((3, "ra_w1", "ra_w2", "ra_w0", True), (4, "ra_a1", "ra_a2", "ra_a0", False)):
        l1, l2 = T4[0], T4[1]
        b0 = [T4[2], T4[3]]
        for z in range(2):
            wb1 = self.cast_w("lora1_%d" % z, I[nm1], I[nm1].ap()[j, z], D, 64)
            P.dma("sp", bf(l1)[:, 0:1024].rearrange("p (kc z l) -> p kc z l", z=2, l=64)[:, :, z, :],
                  wb1.ap().rearrange("(kc p) l -> p kc l", p=128), reads=[wb1], writes=[l1])
        P.dma("sp", l2[:], I[nm2].ap()[j].rearrange("z l n -> (z l) n"), reads=[I[nm2]], writes=[l2])
        for d in range(2):
            self.bcast_load(b0[d], b0[d][:], I[nm0], I[nm0].ap()[j, d])
        if not is_w:
            self.bcast_load(kab, kab[:], I["ra_k_a"], I["ra_k_a"].ap()[j])
            self.bcast_load(rkb, rkb[:], I["ra_r_k"], I["ra_r_k"].ap()[j].rearrange("h k -> (h k)"))
        for jb in range(self.nblk):
            xs = block_setup(jb, jm)
            pl = self.PS[4]
            for kc in range(8):
                xb, xv = xs[kc]
                P.op("pe", lambda e, kc=kc, xv=xv: e.matmul(
                    pl[:, 0:256], lhsT=bf(l1)[:, kc * 128:(kc + 1) * 128], rhs=xv, start=(kc == 0), stop=(kc == 7)),
                    [l1, xb], [pl], inc=(kc == 7))
            t1 = T4[4]
            P.op("act", lambda e: e.activation(out=t1[:, 0:256], in_=pl[:, 0:256],
                                               func=AF.Tanh if is_w else AF.Copy), [pl], [t1])
            for mt in range(2):
                r0 = jb * 256 + mt * 128
                if not is_w:
                    kt, kkt, rt = T4[5], T4[6], T4[7]
                    P.dma("sp", kt[:], SK.ap()[r0:r0 + 128, :], reads=[SK], writes=[kt])
                    P.dma("pool", kkt[:], SKK.ap()[r0:r0 + 128, :], reads=[SKK], writes=[kkt])
                    P.dma("sp", rt[:], SR.ap()[r0:r0 + 128, :], reads=[SR], writes=[rt])
                    kds = T4[26]
                for d in range(2):
                    ot = T4[20 + d]
                    for hf in range(2):
                        pb = self.PS[d * 2 + hf]
                        P.op("pe", lambda e, d=d, hf=hf, mt=mt, pb=pb: e.matmul(
                            pb[:, :], lhsT=t1[64 * d:64 * d + 64, mt * 128:(mt + 1) * 128],
                            rhs=l2[64 * d:64 * d + 64, hf * 512:(hf + 1) * 512], start=True, stop=True),
                            [t1, l2], [pb])
                        P.op("dve", lambda e, d=d, hf=hf, pb=pb, ot=ot: e.tensor_tensor(
                            out=ot[:, hf * 512:(hf + 1) * 512], in0=pb[:, :], in1=b0[d][:, hf * 512:(hf + 1) * 512],
                            op=ALU.add), [pb, b0[d]], [ot])
                    P.op("act", lambda e, ot=ot: e.activation(out=ot[:], in_=ot[:], func=AF.Sigmoid), [ot], [ot])
                    if is_w:
                        P.dma("pool", SSG[d].ap()[r0:r0 + 128, :], ot[:], reads=[ot], writes=[SSG[d]])
                    else:
                        kd, ba = T4[22 + d], T4[24 + 4 + d]
                        P.op("dve", lambda e, kd=kd, ot=ot: e.scalar_tensor_tensor(
                            out=kd[:], in0=ot[:], scalar=-1.0, in1=kab[:], op0=ALU.add, op1=ALU.mult), [ot, kab], [kd])
                        P.op("dve", lambda e, kd=kd: e.scalar_tensor_tensor(
                            out=kd[:], in0=kd[:], scalar=1.0, in1=kt[:], op0=ALU.add, op1=ALU.mult), [kd, kt], [kd])
                        P.op("pool", lambda e, ba=ba, ot=ot: e.tensor_tensor(out=ba[:], in0=ot[:], in1=kkt[:], op=ALU.mult),
                             [ot, kkt], [ba])
                        P.dma("sp", SKD[d].ap()[r0:r0 + 128, :], kd[:], reads=[kd], writes=[SKD[d]])
                        P.dma("pool", SBA[d].ap()[r0:r0 + 128, :], ba[:], reads=[ba], writes=[SBA[d]])
                if not is_w:
                    kd0, kd1 = T4[22], T4[23]
                    rks = self.small("rks_t", [128, 16])
                    P.op("dve", lambda e: e.tensor_tensor(out=kds[:], in0=kd0[:], in1=kd1[:], op=ALU.add), [kd0, kd1], [kds])
                    P.op("dve", lambda e: e.tensor_tensor(out=kds[:], in0=kds[:], in1=rt[:], op=ALU.mult), [kds, rt], [kds])
                    P.op("dve", lambda e: e.tensor_tensor(out=kds[:], in0=kds[:], in1=rkb[:], op=ALU.mult), [kds, rkb], [kds])
                    P.op("dve", lambda e: e.tensor_reduce(
                        out=rks[:], in_=kds[:].rearrange("p (h k) -> p h k", k=64), axis=AX.X, op=ALU.add), [kds], [rks])
                    P.dma("sp", SRKS.ap()[r0:r0 + 128, :], rks[:], reads=[rks], writes=[SRKS])

    g1a, g1b, g2a, g2b = T4[0], T4[1], T4[2], T4[3]
    wbg = self.cast_w("g1", I["ra_g1"], I["ra_g1"].ap()[j], D, 160)
    g1v = wbg.ap().rearrange("(kc p) l -> p kc l", p=128)
    P.dma("sp", bf(g1a)[:, 0:1024].rearrange("p (kc l) -> p kc l", l=128), g1v[:, :, 0:128], reads=[wbg], writes=[g1a])
    P.dma("sp", bf(g1b)[:, 0:256].rearrange("p (kc l) -> p kc l", l=32), g1v[:, :, 128:160], reads=[wbg], writes=[g1b])
    P.dma("sp", g2a[:], I["ra_g2"].ap()[j, 0:128, :], reads=[I["ra_g2"]], writes=[g2a])
    P.dma("sp", g2b[0:32, :], I["ra_g2"].ap()[j, 128:160, :], reads=[I["ra_g2"]], writes=[g2b])
    for jb in range(self.nblk):
        xs = block_setup(jb, 5)
        pa, pb2 = self.PS[4], self.PS[5]
        for kc in range(8):
            xb, xv = xs[kc]
            P.op("pe", lambda e, kc=kc, xv=xv: e.matmul(
                pa[:, 0:256], lhsT=bf(g1a)[:, kc * 128:(kc + 1) * 128], rhs=xv, start=(kc == 0), stop=(kc == 7)),
                [g1a, xb], [pa], inc=(kc == 7))
        for kc in range(8):
            xb, xv = xs[kc]
            P.op("pe", lambda e, kc=kc, xv=xv: e.matmul(
                pb2[0:32, 0:256], lhsT=bf(g1b)[:, kc * 32:(kc + 1) * 32], rhs=xv, start=(kc == 0), stop=(kc == 7)),
                [g1b, xb], [pb2], inc=(kc == 7))
        ta, tb = T4[4], T4[5]
        P.op("act", lambda e: e.activation(out=ta[:, 0:256], in_=pa[:, 0:256], func=AF.Sigmoid), [pa], [ta])
        P.op("act", lambda e: e.activation(out=tb[0:32, 0:256], in_=pb2[0:32, 0:256], func=AF.Sigmoid), [pb2], [tb])
        for mt in range(2):
            r0 = jb * 256 + mt * 128
            ot = T4[20 + mt]
            for hf in range(2):
                pb = self.PS[mt * 2 + hf]
                P.op("pe", lambda e, hf=hf, mt=mt, pb=pb: e.matmul(
                    pb[:, :], lhsT=ta[:, mt * 128:(mt + 1) * 128], rhs=g2a[:, hf * 512:(hf + 1) * 512],
                    start=True, stop=False), [ta, g2a], [pb], inc=False)
                P.op("pe", lambda e, hf=hf, mt=mt, pb=pb: e.matmul(
                    pb[:, :], lhsT=tb[0:32, mt * 128:(mt + 1) * 128], rhs=g2b[0:32, hf * 512:(hf + 1) * 512],
                    start=False, stop=True), [tb, g2b], [pb])
                P.op("act", lambda e, ot=ot, pb=pb, hf=hf: e.activation(
                    out=ot[:, hf * 512:(hf + 1) * 512], in_=pb[:, :], func=AF.Copy), [pb], [ot])
            P.dma("pool", SG_.ap()[r0:r0 + 128, :], ot[:], reads=[ot], writes=[SG_])


K.rw_xs = rw_xs
K.rw_xx = rw_xx
K.rwkv_pre = rwkv_pre


def rwkv_consts(self):
    if hasattr(self, "upC"):
        return
    P = self.P
    self.upC = P.sb("upC", [128, 128]); self.loC = P.sb("loC", [128, 128]); self.onesC = P.sb("onesC", [128, 128])
    for dst, src in ((self.upC, self.up), (self.loC, self.lo), (self.onesC, self.ones)):
        P.op("dve", lambda e, dst=dst, src=src: e.tensor_scalar(out=dst[:], in0=src[:], scalar1=-CDEC, scalar2=None,
                                                                 op0=ALU.mult), [src], [dst])
    self.MK1 = []; self.MK3 = []
    for d in range(2):
        strict_T, incl_T, strict_N = (self.ups, self.up, self.los) if d == 0 else (self.los, self.lo, self.ups)
        m1 = P.sb("mk1_%d" % d, [128, 512]); m3 = P.sb("mk3_%d" % d, [128, 512])
        for q in range(4):
            src = strict_T if q % 2 == 0 else incl_T
            P.op("dve", lambda e, m1=m1, q=q, src=src: e.tensor_copy(out=m1[:, q * 128:(q + 1) * 128], in_=src[:]), [src], [m1])
            P.op("dve", lambda e, m3=m3, q=q: e.tensor_copy(out=m3[:, q * 128:(q + 1) * 128], in_=strict_N[:]), [strict_N], [m3])
        self.MK1.append(m1); self.MK3.append(m3)


def rwkv_scan(self, d):
    P = self.P
    T4 = self.T4
    PS = self.PS
    SR, SV, SKK = self.scratch("r"), self.scratch("v"), self.scratch("kk")
    SSG, SKD, SBA = self.scratch("sg%d" % d), self.scratch("kd%d" % d), self.scratch("ba%d" % d)
    SY = self.scratch("y%d" % d)
    triC = self.upC if d == 0 else self.loC
    mk1, mk3 = self.MK1[d], self.MK3[d]
    ST = T4[29]
    stv = ST[:, 0:512].rearrange("p (q v) -> p q v", v=64)
    P.op("pool", lambda e: e.memset(ST[:], 0.0), [], [ST])
    order = [0, 1] + list(range(2, self.nt)) if d == 0 else [1, 0] + list(range(self.nt - 1, 1, -1))
    for c in order:
        r0 = c * 128
        Lr, Lv, Lkk, Lkd, Lba, Lsg = T4[0:6]
        for (dst, src, q) in ((Lsg, SSG, "sp"), (Lkk, SKK, "pool"), (Lba, SBA, "sp"), (Lkd, SKD, "pool"),
                              (Lr, SR, "sp"), (Lv, SV, "pool")):
            P.dma(q, dst[:], src.ap()[r0:r0 + 128, :], reads=[src], writes=[dst])
        for hf in range(2):
            P.op("pe", lambda e, hf=hf: e.matmul(PS[hf][:, :], lhsT=triC[:], rhs=Lsg[:, hf * 512:(hf + 1) * 512],
                                                 start=True, stop=True), [triC, Lsg], [PS[hf]])
            P.op("pe", lambda e, hf=hf: e.matmul(PS[2 + hf][:, :], lhsT=self.onesC[:], rhs=Lsg[:, hf * 512:(hf + 1) * 512],
                                                 start=True, stop=True), [self.onesC, Lsg], [PS[2 + hf]])
        gC = self.small("rw_gC", [128, 8])
        for p in range(8):
            P.op("pe", lambda e, p=p: e.matmul(PS[5][:, p:p + 1], lhsT=Lsg[:, p * 128:(p + 1) * 128],
                                               rhs=self.onesC[:, 0:1], start=True, stop=True),
                 [Lsg, self.onesC], [PS[5]], inc=(p == 7))
        P.op("act", lambda e: e.activation(out=gC[:], in_=PS[5][:, 0:8], func=AF.Exp), [PS[5]], [gC])
        lg, ex = T4[6], T4[7]
        Bt_, Kt_, Rt_, At_, Bh, Kh = T4[8:14]
        for hf in range(2):
            P.op("act", lambda e, hf=hf: e.activation(out=lg[:, hf * 512:(hf + 1) * 512], in_=PS[hf][:, :], func=AF.Copy),
                 [PS[hf]], [lg])
        P.op("act", lambda e: e.activation(out=ex[:], in_=lg[:], func=AF.Exp, scale=-1.0), [lg], [ex])
        P.op("dve", lambda e: e.tensor_tensor(out=Bt_[:], in0=Lba[:], in1=ex[:], op=ALU.mult), [Lba, ex], [Bt_])
        P.op("pool", lambda e: e.tensor_tensor(out=Kt_[:], in0=Lkd[:], in1=ex[:], op=ALU.mult), [Lkd, ex], [Kt_])
        P.op("act", lambda e: e.activation(out=ex[:], in_=lg[:], func=AF.Exp), [lg], [ex])
        P.op("dve", lambda e: e.tensor_tensor(out=Rt_[:], in0=Lr[:], in1=ex[:], op=ALU.mult), [Lr, ex], [Rt_])
        P.op("dve", lambda e: e.scalar_tensor_tensor(out=ex[:], in0=Lsg[:], scalar=CDEC, in1=lg[:], op0=ALU.mult,
                                                     op1=ALU.add), [Lsg, lg, Rt_], [ex])
        P.op("act", lambda e: e.activation(out=ex[:], in_=ex[:], func=AF.Exp), [ex], [ex])
        P.op("dve", lambda e: e.scalar_tensor_tensor(out=At_[:], in0=Lkk[:], scalar=-1.0, in1=ex[:], op0=ALU.mult,
                                                     op1=ALU.mult), [Lkk, ex], [At_])
        for hf in range(2):
            P.op("dve", lambda e, hf=hf: e.tensor_tensor(out=ex[:, hf * 512:(hf + 1) * 512], in0=PS[2 + hf][:, :],
                                                         in1=lg[:, hf * 512:(hf + 1) * 512], op=ALU.subtract),
                 [PS[2 + hf], lg, At_], [ex])
        P.op("act", lambda e: e.activation(out=ex[:], in_=ex[:], func=AF.Exp), [ex], [ex])
        P.op("dve", lambda e: e.tensor_tensor(out=Bh[:], in0=Lba[:], in1=ex[:], op=ALU.mult), [Lba, ex], [Bh])
        P.op("pool", lambda e: e.tensor_tensor(out=Kh[:], in0=Lkd[:], in1=ex[:], op=ALU.mult), [Lkd, ex], [Kh])
        AR0, AR1, BT, KT = T4[14], T4[15], T4[16], T4[17]
        bank = 4
        for (src, kind) in ((At_, 0), (Rt_, 1), (Bt_, 2), (Kt_, 3)):
            for hf in range(2):
                psa = PS[4 + (bank % 4)]; bank += 1
                if kind < 2:
                    art = AR0 if hf == 0 else AR1
                    dst = art
                    dap = art[:].rearrange("p (q a t) -> p q a t", a=2, t=128)[:, :, kind, :]
                else:
                    dst = BT if kind == 2 else KT
                    dap = dst[:, hf * 512:(hf + 1) * 512].rearrange("p (q t) -> p q t", t=128)
                for q in range(4):
                    P.op("pe", lambda e, q=q, psa=psa, src=src, hf=hf: e.transpose(
                        psa[:, q * 128:(q + 1) * 128], src[:, (hf * 4 + q) * 128:(hf * 4 + q + 1) * 128], self.ident[:]),
                        [src, self.ident], [psa], inc=(q == 3))
                eng = "act" if hf == 0 else "dve"
                if eng == "act":
                    P.op("act", lambda e, dap=dap, psa=psa: e.activation(
                        out=dap, in_=psa[:, :].rearrange("p (q t) -> p q t", t=128), func=AF.Copy), [psa], [dst])
                else:
                    P.op("dve", lambda e, dap=dap, psa=psa: e.tensor_copy(
                        out=dap, in_=psa[:, :].rearrange("p (q t) -> p q t", t=128)), [psa], [dst])

        def ar(p, a, rows):
            art = AR0 if p < 4 else AR1
            return art, art[:].rearrange("p (q a t) -> p q a t", a=2, t=128)[rows, p % 4, a, :]

        def ar2(p, rows):
            art = AR0 if p < 4 else AR1
            return art, art[rows, (p % 4) * 256:(p % 4 + 1) * 256]

        Yt = T4[18]
        for g in range(4):
            CH = [T4[19], T4[20]]
            MR, MA = T4[21], T4[22]
            chv = [t[:].rearrange("p (x h t) -> p x h t", x=2, t=128) for t in CH]
            mrv = MR[:].rearrange("p (x h t) -> p x h t", x=2, t=128)
            makv = MA[:, 0:512].rearrange("p (h t) -> p h t", t=128)
            Wt = MA[:, 512:768]
            for pp in range(2):
                p = g * 2 + pp
                pm1, pm2 = PS[pp], PS[2 + pp]
                for jj in range(2):
                    rows = slice(64 * jj, 64 * jj + 64)
                    ab, a2 = ar2(p, rows)
                    bt_ap = BT[rows, p * 128:(p + 1) * 128]
                    kt_ap = KT[rows, p * 128:(p + 1) * 128]
                    P.op("pe", lambda e, pm1=pm1, jj=jj, bt_ap=bt_ap, a2=a2: e.matmul(
                        pm1[:, jj * 256:(jj + 1) * 256], lhsT=bt_ap, rhs=a2, start=True, stop=True),
                        [BT, ab], [pm1], inc=(jj == 1))
                    P.op("pe", lambda e, pm2=pm2, jj=jj, kt_ap=kt_ap, a2=a2: e.matmul(
                        pm2[:, jj * 256:(jj + 1) * 256], lhsT=kt_ap, rhs=a2, start=True, stop=True),
                        [KT, ab], [pm2], inc=(jj == 1))
                    ab0, a0 = ar(p, 0, rows)
                    hh = pp * 2 + jj
                    P.op("pe", lambda e, hh=hh, a0=a0, bt_ap=bt_ap: e.matmul(
                        PS[4][:, hh * 128:(hh + 1) * 128], lhsT=a0, rhs=bt_ap, start=True, stop=True),
                        [ab0, BT], [PS[4]], inc=(hh == 3))
                pm1v = pm1[:, :].rearrange("p (h a t) -> p h a t", a=2, t=128)
                pm2v = pm2[:, :].rearrange("p (h a t) -> p h a t", a=2, t=128)
                mkv = mk1[:].rearrange("p (h a t) -> p h a t", a=2, t=128)
                hs = slice(pp * 2, pp * 2 + 2)
                P.op("dve", lambda e, pm1v=pm1v, hs=hs: e.tensor_tensor(
                    out=chv[0][:, 1, hs, :], in0=pm1v[:, :, 0, :], in1=mkv[:, :, 0, :], op=ALU.mult), [pm1, mk1], [CH[0]])
                P.op("dve", lambda e, pm1v=pm1v, hs=hs: e.tensor_tensor(
                    out=mrv[:, 0, hs, :], in0=pm1v[:, :, 1, :], in1=mkv[:, :, 1, :], op=ALU.mult), [pm1, mk1], [MR])
                P.op("dve", lambda e, pm2v=pm2v, hs=hs: e.tensor_tensor(
                    out=makv[:, hs, :], in0=pm2v[:, :, 0, :], in1=mkv[:, :, 0, :], op=ALU.mult), [pm2, mk1], [MA])
                P.op("dve", lambda e, pm2v=pm2v, hs=hs: e.tensor_tensor(
                    out=mrv[:, 1, hs, :], in0=pm2v[:, :, 1, :], in1=mkv[:, :, 1, :], op=ALU.mult), [pm2, mk1], [MR])
            P.op("dve", lambda e: e.tensor_tensor(out=chv[0][:, 0, :, :].rearrange("p h t -> p (h t)"),
                                                  in0=PS[4][:, :], in1=mk3[:], op=ALU.mult), [PS[4], mk3], [CH[0]])
            for hh in range(4):
                p = g * 2 + hh // 2
                rows = slice(64 * (hh % 2), 64 * (hh % 2) + 64)
                h = g * 4 + hh
                ab0, a0 = ar(p, 0, rows)
                P.op("pe", lambda e, hh=hh, a0=a0, rows=rows, p=p: e.matmul(
                    PS[7][:, hh * 64:(hh + 1) * 64], lhsT=a0, rhs=stv[rows, p, :], start=True, stop=False),
                    [ab0, ST], [PS[7]], inc=False)
                P.op("pe", lambda e, hh=hh, h=h: e.matmul(
                    PS[7][:, hh * 64:(hh + 1) * 64], lhsT=makv[:, hh, :], rhs=Lv[:, h * 64:(h + 1) * 64],
                    start=False, stop=True), [MA, Lv], [PS[7]], inc=(hh == 3))
            P.op("act", lambda e: e.activation(out=Wt, in_=PS[7][:, 0:256], func=AF.Copy), [PS[7]], [MA])
            for rd in range(7):
                cur, nxt = chv[rd % 2], chv[(rd + 1) % 2]
                cb, nb = CH[rd % 2], CH[(rd + 1) % 2]
                for hh in range(4):
                    P.op("pe", lambda e, hh=hh, cur=cur: e.matmul(
                        PS[7][:, hh * 64:(hh + 1) * 64], lhsT=cur[:, 1, hh, :], rhs=Wt[:, hh * 64:(hh + 1) * 64],
                        start=True, stop=True), [cb, MA], [PS[7]], inc=(hh == 3))
                P.op("dve", lambda e: e.tensor_tensor(out=Wt, in0=Wt, in1=PS[7][:, 0:256], op=ALU.add), [MA, PS[7]], [MA])
                if rd < 6:
                    for hh in range(4):
                        P.op("pe", lambda e, hh=hh, cur=cur: e.matmul(
                            PS[5][:, hh * 128:(hh + 1) * 128], lhsT=cur[:, 1, hh, :], rhs=cur[:, 0, hh, :],
                            start=True, stop=True), [cb], [PS[5]], inc=(hh == 3))
                    for hh in range(4):
                        P.op("pe", lambda e, hh=hh, cur=cur: e.matmul(
                            PS[6][:, hh * 128:(hh + 1) * 128], lhsT=cur[:, 0, hh, :], rhs=cur[:, 1, hh, :],
                            start=True, stop=True), [cb], [PS[6]], inc=(hh == 3))
                    P.op("act", lambda e, nxt=nxt: e.activation(
                        out=nxt[:, 0, :, :].rearrange("p h t -> p (h t)"), in_=PS[5][:, :], func=AF.Copy), [PS[5]], [nb])
                    P.op("dve", lambda e, nxt=nxt: e.tensor_copy(
                        out=nxt[:, 1, :, :].rearrange("p h t -> p (h t)"), in_=PS[6][:, :]), [PS[6]], [nb])
            for hh in range(4):
                p = g * 2 + hh // 2
                rows = slice(64 * (hh % 2), 64 * (hh % 2) + 64)
                h = g * 4 + hh
                ab1, a1 = ar(p, 1, rows)
                P.op("pe", lambda e, hh=hh, a1=a1, rows=rows, p=p: e.matmul(
                    PS[4][:, hh * 64:(hh + 1) * 64], lhsT=a1, rhs=stv[rows, p, :], start=True, stop=False),
                    [ab1, ST], [PS[4]], inc=False)
                P.op("pe", lambda e, hh=hh: e.matmul(
                    PS[4][:, hh * 64:(hh + 1) * 64], lhsT=mrv[:, 0, hh, :], rhs=Wt[:, hh * 64:(hh + 1) * 64],
                    start=False, stop=False), [MR, MA], [PS[4]], inc=False)
                P.op("pe", lambda e, hh=hh, h=h: e.matmul(
                    PS[4][:, hh * 64:(hh + 1) * 64], lhsT=mrv[:, 1, hh, :], rhs=Lv[:, h * 64:(h + 1) * 64],
                    start=False, stop=True), [MR, Lv], [PS[4]], inc=(hh == 3))
            P.op("act", lambda e, g=g: e.activation(out=Yt[:, g * 256:(g + 1) * 256], in_=PS[4][:, 0:256], func=AF.Copy),
                 [PS[4]], [Yt])
            for pp in range(2):
                p = g * 2 + pp
                P.op("pe", lambda e, pp=pp, p=p: e.matmul(
                    PS[0][:, pp * 128:(pp + 1) * 128], lhsT=Bh[:, p * 128:(p + 1) * 128],
                    rhs=Wt[:, pp * 128:(pp + 1) * 128], start=True, stop=False), [Bh, MA], [PS[0]], inc=False)
                P.op("pe", lambda e, pp=pp, p=p: e.matmul(
                    PS[0][:, pp * 128:(pp + 1) * 128], lhsT=Kh[:, p * 128:(p + 1) * 128],
                    rhs=Lv[:, p * 128:(p + 1) * 128], start=False, stop=True), [Kh, Lv], [PS[0]], inc=(pp == 1))
            for pp in range(2):
                p = g * 2 + pp
                for jj in range(2):
                    rows = slice(64 * jj, 64 * jj + 64)
                    P.op("dve", lambda e, rows=rows, p=p, pp=pp, jj=jj: e.scalar_tensor_tensor(
                        out=stv[rows, p, :], in0=stv[rows, p, :], scalar=gC[rows, p:p + 1],
                        in1=PS[0][rows, pp * 128 + jj * 64:pp * 128 + jj * 64 + 64], op0=ALU.mult, op1=ALU.add),
                        [ST, gC, PS[0]], [ST])
        P.dma("sp", SY.ap()[r0:r0 + 128, :], Yt[:], reads=[Yt], writes=[SY])


def rwkv_post(self, i, Xin, Xout):
    P = self.P
    j = i // 3
    I = self.I
    T4 = self.T4
    wres = T4[0:8]
    self.load_wres(I["ra_w_out"], I["ra_w_out"].ap()[j], wres)
    lng, lnb = T4[8], T4[9]
    self.bcast_load(lng, lng[:], I["ra_lnx_g"], I["ra_lnx_g"].ap()[j])
    self.bcast_load(lnb, lnb[:], I["ra_lnx_b"], I["ra_lnx_b"].ap()[j])
    SY0, SY1, SV, SG_, SRKS = (self.scratch("y0"), self.scratch("y1"), self.scratch("v"), self.scratch("g"),
                               self.scratch("rks", 16))
    t_start = 2 if i == DEPTH - 1 else 0
    for t in range(t_start, self.nt):
        par = t % 2
        r0 = t * 128
        y, y1, vt, gt_, sq = T4[10 + par * 5:15 + par * 5]
        rks = self.small("post_rks%d" % par, [128, 16])
        mu = self.small("post_mu%d" % par, [128, 16])
        var = self.small("post_var%d" % par, [128, 16])
        P.dma("sp", y[:], SY0.ap()[r0:r0 + 128, :], reads=[SY0], writes=[y])
        P.dma("pool", y1[:], SY1.ap()[r0:r0 + 128, :], reads=[SY1], writes=[y1])
        P.dma("sp", vt[:], SV.ap()[r0:r0 + 128, :], reads=[SV], writes=[vt])
        P.dma("pool", gt_[:], SG_.ap()[r0:r0 + 128, :], reads=[SG_], writes=[gt_])
        P.dma("sp", rks[:], SRKS.ap()[r0:r0 + 128, :], reads=[SRKS], writes=[rks])
        v3 = lambda tl: tl[:].rearrange("p (h k) -> p h k", k=64)
        bc = lambda sm_: sm_[:].unsqueeze(2).to_broadcast([128, 16, 64])
        P.op("dve", lambda e: e.tensor_tensor(out=y[:], in0=y[:], in1=y1[:], op=ALU.add), [y, y1], [y])
        P.op("dve", lambda e: e.tensor_reduce(out=mu[:], in_=v3(y), axis=AX.X, op=ALU.add), [y], [mu])
        P.op("dve", lambda e: e.tensor_scalar(out=mu[:], in0=mu[:], scalar1=1.0 / 64, scalar2=None, op0=ALU.mult), [mu], [mu])
        P.op("dve", lambda e: e.tensor_tensor(out=v3(y), in0=v3(y), in1=bc(mu), op=ALU.subtract), [y, mu], [y])
        P.op("act", lambda e: e.activation(out=sq[:], in_=y[:], func=AF.Square), [y], [sq])
        P.op("dve", lambda e: e.tensor_reduce(out=var[:], in_=v3(sq), axis=AX.X, op=ALU.add), [sq], [var])
        P.op("dve", lambda e: e.tensor_scalar(out=var[:], in0=var[:], scalar1=1.0 / 64, scalar2=6.4e-4, op0=ALU.mult,
                                              op1=ALU.add), [var], [var])
        P.op("act", lambda e: e.activation(out=var[:], in_=var[:], func=AF.Sqrt), [var], [var])
        P.op("dve", lambda e: e.reciprocal(out=var[:], in_=var[:]), [var], [var])
        P.op("dve", lambda e: e.tensor_tensor(out=v3(y), in0=v3(y), in1=bc(var), op=ALU.mult), [y, var], [y])
        P.op("pool", lambda e: e.tensor_tensor(out=y[:], in0=y[:], in1=lng[:], op=ALU.mult), [y, lng], [y])
        P.op("pool", lambda e: e.tensor_tensor(out=y[:], in0=y[:], in1=lnb[:], op=ALU.add), [y, lnb], [y])
        P.op("dve", lambda e: e.tensor_tensor(out=v3(vt), in0=v3(vt), in1=bc(rks), op=ALU.mult), [vt, rks], [vt])
        P.op("dve", lambda e: e.tensor_tensor(out=y[:], in0=y[:], in1=vt[:], op=ALU.add), [y, vt], [y])
        P.op("dve", lambda e: e.tensor_tensor(out=y[:], in0=y[:], in1=gt_[:], op=ALU.mult), [y, gt_], [y])
        zT = T4[20 + par]
        for hf in range(2):
            self.transpose_to(y, [y[:, (hf * 4 + q) * 128:(hf * 4 + q + 1) * 128] for q in range(4)],
                              zT, [(bf(zT)[:, hf * 512:(hf + 1) * 512], 0, 512)], self.PS[par * 2 + hf],
                              evac="act" if hf == 0 else "dve")
        self.proj_residual([(zT, bf(zT)[:, kc * 128:(kc + 1) * 128]) for kc in range(8)], wres, Xin, Xout, t, par)


def rwkv(self, i, Xin, Xout):
    ph = self.cfg.get("ra_phases", "pre,scan0,scan1,post")
    self.rwkv_consts()
    if "pre" in ph:
        self.rwkv_pre(i, Xin)
    for d in range(2):
        if "scan%d" % d in ph:
            self.rwkv_scan(d)
    if "post" in ph:
        self.rwkv_post(i, Xin, Xout)


K.rwkv_consts = rwkv_consts
K.rwkv_scan = rwkv_scan
K.rwkv_post = rwkv_post
K.rwkv = rwkv


def tm_pass(self, wbuf, wap2d, col0, ncols, bias_buf, bias_ap, dst, post=None, out_bf=False):
    P = self.P
    T4 = self.T4
    wres = T4[0:8]
    self.load_wres(wbuf, wap2d[:, col0:col0 + ncols], wres, ncols)
    bt = None
    if bias_buf is not None:
        bt = T4[24]
        self.bcast_load(bt, bt[:, 0:ncols], bias_buf, bias_ap)
    nb = (ncols + 511) // 512
    for jb in range(self.nblk):
        hv = self.load_hTe(jb, T4[8 + (jb % 2) * 3:11 + (jb % 2) * 3])
        for mt in range(2):
            ot = T4[20 + mt]
            for b in range(nb):
                w = min(512, ncols - b * 512)
                pb = self.PS[(mt * 2 + b) % 4]
                for kc in range(8):
                    hb, ha = hv(kc)
                    P.op("pe", lambda e, kc=kc, ha=ha, pb=pb, b=b, w=w, mt=mt: e.matmul(
                        pb[:, 0:w], lhsT=ha[:, 1 + mt * 128:1 + (mt + 1) * 128], rhs=bf(wres[kc])[:, b * 512:b * 512 + w],
                        start=(kc == 0), stop=(kc == 7)), [hb, wres[kc]], [pb], inc=(kc == 7))
                if bt is not None:
                    P.op("dve", lambda e, ot=ot, pb=pb, b=b, w=w: e.tensor_tensor(
                        out=ot[:, b * 512:b * 512 + w], in0=pb[:, 0:w], in1=bt[:, b * 512:b * 512 + w], op=ALU.add),
                        [pb, bt], [ot])
                else:
                    oap = bf(ot)[:, b * 512:b * 512 + w] if out_bf else ot[:, b * 512:b * 512 + w]
                    P.op("act", lambda e, oap=oap, pb=pb, w=w: e.activation(
                        out=oap, in_=pb[:, 0:w], func=AF.Copy), [pb], [ot])
            if post is not None:
                post(ot, jb, mt)
            r0 = jb * 256 + mt * 128
            if dst is not None:
                src = bf(ot)[:, 0:ncols] if out_bf else ot[:, 0:ncols]
                P.dma("pool", dst.ap()[r0:r0 + 128, 0:ncols], src, reads=[ot], writes=[dst])


def fm_pass(self, wbuf, wap2d, col0, nchunks, bias_cols, scales, dsts):
    P = self.P
    T4 = self.T4
    wres = T4[0:8]
    ncols = nchunks * 128
    self.load_wres(wbuf, wap2d[:, col0:col0 + ncols], wres, ncols)
    for jb in range(self.nblk):
        hv = self.load_hTe(jb, T4[8 + (jb % 2) * 3:11 + (jb % 2) * 3])
        for hc in range(nchunks):
            pb = self.PS[hc % 4]
            for kc in range(8):
                hb, ha = hv(kc)
                P.op("pe", lambda e, kc=kc, ha=ha, pb=pb, hc=hc: e.matmul(
                    pb[:, 0:256], lhsT=bf(wres[kc])[:, hc * 128:(hc + 1) * 128], rhs=ha[:, 1:257],
                    start=(kc == 0), stop=(kc == 7)), [hb, wres[kc]], [pb], inc=(kc == 7))
            ot = T4[20 + hc % 4]
            if bias_cols is not None:
                P.op("act", lambda e, ot=ot, pb=pb, hc=hc: e.activation(
                    out=ot[:, 0:256], in_=pb[:, 0:256], func=AF.Identity, scale=scales[hc],
                    bias=bias_cols[:, hc:hc + 1]), [pb, bias_cols], [ot])
            else:
                P.op("act", lambda e, ot=ot, pb=pb, hc=hc: e.activation(
                    out=ot[:, 0:256], in_=pb[:, 0:256], func=AF.Copy), [pb], [ot])
            dbuf, row0 = dsts[hc]
            P.dma("pool", dbuf.ap()[row0:row0 + 128, jb * 256:(jb + 1) * 256], ot[:, 0:256], reads=[ot], writes=[dbuf])


K.tm_pass = tm_pass
K.fm_pass = fm_pass


def fscratch(self, name, rows, dt=F32):
    if name not in self.S:
        self.S[name] = self.P.dram("S_" + name, [rows, self.ntok], dt)
    return self.S[name]


def mlstm_pre(self, i, Xin):
    P = self.P
    I = self.I
    self.norm_to_HT(Xin, 1, 0)
    w = I["ml_w_in"].ap()[0]
    b = I["ml_b_in"]
    QT, KT = self.fscratch("mlqT", 512), self.fscratch("mlkT", 512)
    MLA, MLB, MLC = self.scratch("mla"), self.scratch("mlb"), self.scratch("mlc")
    bc = self.small("ml_bc", [128, 8])
    self.load_cols(bc, bc[:], b, b.ap()[0, 0:1024].rearrange("(hc p) -> hc p", p=128), 8)
    sk = 128 ** -0.5
    P.op("dve", lambda e: e.tensor_scalar(out=bc[:, 4:8], in0=bc[:, 4:8], scalar1=sk, scalar2=None, op0=ALU.mult), [bc], [bc])
    self.fm_pass(I["ml_w_in"], w, 0, 8, bc, [1.0] * 4 + [sk] * 4,
                 [(QT, h * 128) for h in range(4)] + [(KT, h * 128) for h in range(4)])

    def scale_k(ot, jb, mt):
        P.op("dve", lambda e: e.tensor_scalar(out=ot[:, 0:512], in0=ot[:, 0:512], scalar1=sk, scalar2=None, op0=ALU.mult),
             [ot], [ot])
    self.tm_pass(I["ml_w_in"], w, 512, 1024, b, b.ap()[0, 512:1536], MLA, post=scale_k)
    self.tm_pass(I["ml_w_in"], w, 1536, 1024, b, b.ap()[0, 1536:2560], MLB)
    self.tm_pass(I["ml_w_in"], w, 2560, 528, b, b.ap()[0, 2560:3088], MLC)


def mlstm_scan(self, d):
    P = self.P
    T4 = self.T4
    PS = self.PS
    QT, KT = self.fscratch("mlqT", 512), self.fscratch("mlkT", 512)
    MLA, MLB, MLC = self.scratch("mla"), self.scratch("mlb"), self.scratch("mlc")
    HS = self.scratch("mlh%d" % d)
    tri = self.up if d == 0 else self.lo
    ecol_i = 127 if d == 0 else 0
    CN = [T4[28], T4[29]]
    V1 = [T4[26], T4[27]]
    cnv = [t[:, 0:514].rearrange("p (h v) -> p h v", v=257) for t in CN]
    v1v = [t[:, 0:514].rearrange("p (h v) -> p h v", v=257) for t in V1]
    for t in CN:
        P.op("pool", lambda e, t=t: e.memset(t[:], 0.0), [], [t])
    for t in V1:
        P.op("pool", lambda e, t=t: e.memset(t[:], 1.0), [], [t])
    order = [0, 1] + list(range(2, self.nt)) if d == 0 else [1, 0] + list(range(self.nt - 1, 1, -1))
    for c in order:
        r0 = c * 128
        qk = T4[0]
        qkv = qk[:].rearrange("p (x h t) -> p x h t", x=2, t=128)
        la, lb = T4[1], T4[2]
        gts = self.small("ml_g", [128, 16])
        lfa = self.small("ml_lf", [128, 16])
        P.dma("sp", qkv[:, 0, :, :], QT.ap().rearrange("(h p) c -> p h c", p=128)[:, :, r0:r0 + 128], reads=[QT], writes=[qk])
        P.dma("pool", qkv[:, 1, :, :], KT.ap().rearrange("(h p) c -> p h c", p=128)[:, :, r0:r0 + 128], reads=[KT], writes=[qk])
        P.dma("sp", la[:], MLA.ap()[r0:r0 + 128, :], reads=[MLA], writes=[la])
        P.dma("pool", lb[:, 0:512], MLB.ap()[r0:r0 + 128, 0:512], reads=[MLB], writes=[lb])
        P.dma("sp", gts[:], MLC.ap()[r0:r0 + 128, 512:528], reads=[MLC], writes=[gts])
        P.op("act", lambda e: e.activation(out=lfa[:], in_=gts[:], func=AF.Sigmoid), [gts], [lfa])
        P.op("act", lambda e: e.activation(out=lfa[:], in_=lfa[:], func=AF.Ln), [lfa], [lfa])
        P.op("dve", lambda e: e.tensor_copy(out=v1v[0][:, :, 0:256], in_=la[:, 512:1024].rearrange("p (h v) -> p h v", v=256)),
             [la], [V1[0]])
        P.op("dve", lambda e: e.tensor_copy(out=v1v[1][:, :, 0:256], in_=lb[:, 0:512].rearrange("p (h v) -> p h v", v=256)),
             [lb], [V1[1]])
        hout = T4[3]

        def head_gen(h):
            par = h % 2
            lfcol = lfa[:, d * 8 + 4 + h:d * 8 + 5 + h]
            licol = gts[:, d * 8 + h:d * 8 + h + 1]
            LFB, Dm, EB, PT, QS, KH = T4[4 + par * 6:10 + par * 6]
            sm = self.small("ml_sm%d" % par, [128, 4])
            psb, psq, pso, pss = PS[par * 4:par * 4 + 4]
            P.op("dve", lambda e: e.tensor_scalar(out=LFB[:, 0:128], in0=self.ones[:], scalar1=lfcol, scalar2=None,
                                                  op0=ALU.mult), [self.ones, lfa], [LFB])
            P.op("pe", lambda e: e.matmul(psb[:, 0:128], lhsT=LFB[:, 0:128], rhs=tri[:], start=True, stop=True),
                 [LFB, tri], [psb], inc=False)
            P.op("pe", lambda e: e.matmul(psb[:, 128:129], lhsT=tri[:], rhs=lfcol, start=True, stop=True),
                 [tri, lfa], [psb])
            yield
            P.op("dve", lambda e: e.tensor_tensor(out=sm[:, 0:1], in0=licol, in1=psb[:, 128:129], op=ALU.subtract),
                 [gts, psb], [sm])
            P.op("act", lambda e: e.activation(out=Dm[:, 0:128], in_=psb[:, 0:128], func=AF.Exp, bias=sm[:, 0:1]),
                 [psb, sm], [Dm])
            yield
            P.op("dve", lambda e: e.tensor_tensor(out=Dm[:, 0:128], in0=Dm[:, 0:128], in1=tri[:], op=ALU.mult), [Dm, tri], [Dm])
            P.op("act", lambda e: e.activation(out=EB[:, 0:128], in_=psb[:, 0:128], func=AF.Exp), [psb], [EB])
            P.op("pe", lambda e: e.matmul(psq[:, 0:128], lhsT=qkv[:, 1, h, :], rhs=qkv[:, 0, h, :], start=True, stop=True),
                 [qk], [psq])
            yield
            P.op("dve", lambda e: e.tensor_tensor(out=PT[:, 0:128], in0=Dm[:, 0:128], in1=psq[:, 0:128], op=ALU.mult),
                 [Dm, psq], [PT])
            P.op("pool", lambda e: e.tensor_tensor(out=QS[:, 0:128], in0=qkv[:, 0, h, :], in1=EB[:, 0:128], op=ALU.mult),
                 [qk, EB], [QS])
            yield
            vv = v1v[h // 2][:, h % 2, :]
            cv = cnv[h // 2][:, h % 2, :]
            P.op("pe", lambda e: e.matmul(pso[:, 0:257], lhsT=PT[:, 0:128], rhs=vv, start=True, stop=False),
                 [PT, V1[h // 2]], [pso], inc=False)
            P.op("pe", lambda e: e.matmul(pso[:, 0:257], lhsT=QS[:, 0:128], rhs=cv, start=False, stop=True),
                 [QS, CN[h // 2]], [pso])
            yield
            P.op("dve", lambda e: e.tensor_copy(out=sm[:, 3:4], in_=pso[:, 256:257]), [pso], [sm])
            P.op("dve", lambda e: e.scalar_tensor_tensor(out=sm[:, 1:2], in0=sm[:, 3:4], scalar=-1.0,
                                                         in1=sm[:, 3:4], op0=ALU.mult, op1=ALU.max), [sm], [sm])
            P.op("dve", lambda e: e.tensor_scalar_max(out=sm[:, 1:2], in0=sm[:, 1:2], scalar1=1.0), [sm], [sm])
            P.op("dve", lambda e: e.reciprocal(out=sm[:, 1:2], in_=sm[:, 1:2]), [sm], [sm])
            P.op("dve", lambda e: e.tensor_scalar(out=hout[:, h * 256:(h + 1) * 256], in0=pso[:, 0:256], scalar1=sm[:, 1:2],
                                                  scalar2=None, op0=ALU.mult), [pso, sm], [hout])
            yield
            P.op("act", lambda e: e.activation(out=sm[:, 2:3], in_=psb[:, ecol_i:ecol_i + 1], func=AF.Exp, bias=sm[:, 0:1]),
                 [psb, sm], [sm])
            P.op("dve", lambda e: e.tensor_scalar(out=KH[:, 0:128], in0=la[:, h * 128:(h + 1) * 128], scalar1=sm[:, 2:3],
                                                  scalar2=None, op0=ALU.mult), [la, sm], [KH])
            P.op("pe", lambda e: e.matmul(pss[:, 0:257], lhsT=KH[:, 0:128], rhs=vv, start=True, stop=True),
                 [KH, V1[h // 2]], [pss])
            yield
            P.op("dve", lambda e: e.scalar_tensor_tensor(out=cv, in0=cv, scalar=EB[:, ecol_i:ecol_i + 1], in1=pss[:, 0:257],
                                                         op0=ALU.mult, op1=ALU.add), [CN[h // 2], EB, pss], [CN[h // 2]])
        interleave([head_gen(0), head_gen(1)])
        interleave([head_gen(2), head_gen(3)])
        P.dma("sp", HS.ap()[r0:r0 + 128, :], hout[:], reads=[hout], writes=[HS])


def mlstm_post(self, i, Xin, Xout):
    P = self.P
    I = self.I
    T4 = self.T4
    wres = T4[0:8]
    self.load_wres(I["ml_w_out"], I["ml_w_out"].ap()[0], wres)
    ng = T4[8]
    self.bcast_load(ng, ng[:], I["ml_norm_g"], I["ml_norm_g"].ap()[0])
    HS0, HS1, MLB, MLC = self.scratch("mlh0"), self.scratch("mlh1"), self.scratch("mlb"), self.scratch("mlc")
    t_start = 2 if i == DEPTH - 1 else 0
    for t in range(t_start, self.nt):
        par = t % 2
        r0 = t * 128
        y, y1, ot, sq = T4[10 + par * 4:14 + par * 4]
        ss = self.small("mlp_ss%d" % par, [128, 4])
        P.dma("sp", y[:], HS0.ap()[r0:r0 + 128, :], reads=[HS0], writes=[y])
        P.dma("pool", y1[:], HS1.ap()[r0:r0 + 128, :], reads=[HS1], writes=[y1])
        P.dma("sp", ot[:, 0:512], MLB.ap()[r0:r0 + 128, 512:1024], reads=[MLB], writes=[ot])
        P.dma("pool", ot[:, 512:1024], MLC.ap()[r0:r0 + 128, 0:512], reads=[MLC], writes=[ot])
        v3 = lambda tl: tl[:].rearrange("p (h k) -> p h k", k=256)
        P.op("dve", lambda e: e.tensor_tensor(out=y[:], in0=y[:], in1=y1[:], op=ALU.add), [y, y1], [y])
        P.op("act", lambda e: e.activation(out=sq[:], in_=y[:], func=AF.Square), [y], [sq])
        P.op("dve", lambda e: e.tensor_reduce(out=ss[:], in_=v3(sq), axis=AX.X, op=ALU.add), [sq], [ss])
        P.op("dve", lambda e: e.tensor_scalar(out=ss[:], in0=ss[:], scalar1=1.0 / 256, scalar2=EPS, op0=ALU.mult,
                                              op1=ALU.add), [ss], [ss])
        P.op("act", lambda e: e.activation(out=ss[:], in_=ss[:], func=AF.Sqrt), [ss], [ss])
        P.op("dve", lambda e: e.reciprocal(out=ss[:], in_=ss[:]), [ss], [ss])
        P.op("dve", lambda e: e.tensor_tensor(out=v3(y), in0=v3(y), in1=ss[:].unsqueeze(2).to_broadcast([128, 4, 256]),
                                              op=ALU.mult), [y, ss], [y])
        P.op("act", lambda e: e.activation(out=ot[:], in_=ot[:], func=AF.Sigmoid), [ot], [ot])
        P.op("pool", lambda e: e.tensor_tensor(out=y[:], in0=y[:], in1=ng[:], op=ALU.mult), [y, ng], [y])
        P.op("dve", lambda e: e.tensor_tensor(out=y[:], in0=y[:], in1=ot[:], op=ALU.mult), [y, ot], [y])
        zT = T4[20 + par]
        for hf in range(2):
            self.transpose_to(y, [y[:, (hf * 4 + q) * 128:(hf * 4 + q + 1) * 128] for q in range(4)],
                              zT, [(bf(zT)[:, hf * 512:(hf + 1) * 512], 0, 512)], self.PS[par * 2 + hf],
                              evac="act" if hf == 0 else "dve")
        self.proj_residual([(zT, bf(zT)[:, kc * 128:(kc + 1) * 128]) for kc in range(8)], wres, Xin, Xout, t, par)


def mlstm(self, i, Xin, Xout):
    self.mlstm_pre(i, Xin)
    for d in range(2):
        self.mlstm_scan(d)
    self.mlstm_post(i, Xin, Xout)


K.fscratch = fscratch
K.mlstm_pre = mlstm_pre
K.mlstm_scan = mlstm_scan
K.mlstm_post = mlstm_post
K.mlstm = mlstm


I32 = mybir.dt.int32


def da_rope_tables(self):
    P = self.P
    T4 = self.T4
    nlt = self.seq // 128
    pi_ = self.small("rp_pi", [128, 1], I32); ti = self.small("rp_ti", [128, 1], I32)
    colv = self.small("rp_col", [128, 1]); rowp = self.small("rp_row", [128, 1])
    ji = self.small("rp_ji", [128, 16], I32); inv = self.small("rp_inv", [128, 16]); angc = self.small("rp_angc", [128, 16])
    taui = self.small("rp_taui", [128, 64], I32); rowf = self.small("rp_rowf", [128, 64])
    P.op("pool", lambda e: e.iota(pi_[:], [[0, 1]], base=0, channel_multiplier=1), [], [pi_])
    P.op("pool", lambda e: e.iota(ji[:], [[1, 16]], base=0, channel_multiplier=0), [], [ji])
    P.op("pool", lambda e: e.iota(taui[:], [[2, 64]], base=0, channel_multiplier=0), [], [taui])
    P.op("dve", lambda e: e.tensor_single_scalar(out=ti[:], in_=pi_[:], scalar=63, op=ALU.bitwise_and), [pi_], [ti])
    P.op("dve", lambda e: e.tensor_copy(out=colv[:], in_=ti[:]), [ti], [colv])
    P.op("dve", lambda e: e.tensor_single_scalar(out=ti[:], in_=pi_[:], scalar=6, op=ALU.arith_shift_right), [pi_, colv], [ti])
    P.op("dve", lambda e: e.tensor_copy(out=rowp[:], in_=ti[:]), [ti], [rowp])
    P.op("dve", lambda e: e.tensor_copy(out=inv[:], in_=ji[:]), [ji], [inv])
    P.op("act", lambda e: e.activation(out=inv[:], in_=inv[:], func=AF.Exp, scale=-math.log(10000.0) / 16), [inv], [inv])
    P.op("dve", lambda e: e.tensor_copy(out=rowf[:], in_=taui[:]), [taui], [rowf])
    P.op("dve", lambda e: e.tensor_scalar(out=rowf[:], in0=rowf[:], scalar1=rowp[:, 0:1], scalar2=None, op0=ALU.add),
         [rowf, rowp], [rowf])
    P.op("dve", lambda e: e.tensor_scalar(out=angc[:], in0=inv[:], scalar1=colv[:, 0:1], scalar2=None, op0=ALU.mult),
         [inv, colv], [angc])
    self.COS = [T4[25], T4[26]]
    self.SIN = [T4[27], T4[28]]
    ang, kf, tmp = T4[14], T4[15], T4[16]
    ki = self.P.sb("rp_ki", [128, 1024], I32)
    for half in range(2):
        n = min(32, nlt - half * 32)
        if n <= 0:
            break
        w = n * 32
        av = ang[:, 0:w].rearrange("p (t j) -> p t j", j=32)
        for tt in range(n):
            P.op("dve", lambda e, tt=tt: e.tensor_scalar(
                out=av[:, tt, 0:16], in0=inv[:], scalar1=rowf[:, half * 32 + tt:half * 32 + tt + 1], scalar2=None,
                op0=ALU.mult), [inv, rowf], [ang])
        P.op("dve", lambda e: e.tensor_copy(out=av[:, :, 16:32], in_=angc[:].unsqueeze(1).to_broadcast([128, n, 16])),
             [angc], [ang])
        for (dst, shift) in ((self.SIN[half], 0.0), (self.COS[half], math.pi / 2)):
            P.op("dve", lambda e: e.tensor_scalar(out=tmp[:, 0:w], in0=ang[:, 0:w], scalar1=shift, scalar2=None, op0=ALU.add),
                 [ang], [tmp])
            P.op("dve", lambda e: e.tensor_scalar(out=kf[:, 0:w], in0=tmp[:, 0:w], scalar1=1.0 / (2 * math.pi), scalar2=None,
                                                  op0=ALU.mult), [tmp], [kf])
            P.op("dve", lambda e: e.tensor_copy(out=ki[:, 0:w], in_=kf[:, 0:w]), [kf], [ki])
            P.op("dve", lambda e: e.tensor_copy(out=kf[:, 0:w], in_=ki[:, 0:w]), [ki], [kf])
            P.op("dve", lambda e: e.scalar_tensor_tensor(out=tmp[:, 0:w], in0=kf[:, 0:w], scalar=-2 * math.pi,
                                                         in1=tmp[:, 0:w], op0=ALU.mult, op1=ALU.add), [kf, tmp], [tmp])
            P.op("dve", lambda e: e.tensor_scalar(out=kf[:, 0:w], in0=tmp[:, 0:w], scalar1=math.pi, scalar2=None,
                                                  op0=ALU.is_gt), [tmp], [kf])
            P.op("dve", lambda e: e.scalar_tensor_tensor(out=tmp[:, 0:w], in0=kf[:, 0:w], scalar=-2 * math.pi,
                                                         in1=tmp[:, 0:w], op0=ALU.mult, op1=ALU.add), [kf, tmp], [tmp])
            P.op("act", lambda e, dst=dst: e.activation(out=dst[:, 0:w], in_=tmp[:, 0:w], func=AF.Sin), [tmp], [dst])


def da_pre(self, i, Xin):
    P = self.P
    I = self.I
    T4 = self.T4
    self.norm_to_HT(Xin, 1, 0)
    self.da_rope_tables()
    w = I["da_w_qkv"].ap()[0]
    QT, KT = self.fscratch("daqT", 1024, BF16), self.fscratch("dakT", 1024, BF16)
    V = self.scratch("dav", D, BF16)

    def mk_post(dstT):
        def post(ot, jb, mt):
            tau = jb * 2 + mt - 2
            src = ot
            if tau >= 0:
                t1, t2, rt = T4[14], T4[15], T4[16]
                ov = ot[:].rearrange("p (h x j) -> p h x j", x=2, j=32)
                rv = rt[:].rearrange("p (h x j) -> p h x j", x=2, j=32)
                t1v = t1[:, 0:512].rearrange("p (h j) -> p h j", j=32)
                t2v = t2[:, 0:512].rearrange("p (h j) -> p h j", j=32)
                cs = self.COS[tau // 32][:, (tau % 32) * 32:(tau % 32 + 1) * 32].unsqueeze(1).to_broadcast([128, 16, 32])
                sn = self.SIN[tau // 32][:, (tau % 32) * 32:(tau % 32 + 1) * 32].unsqueeze(1).to_broadcast([128, 16, 32])
                cb, sb_ = self.COS[tau // 32], self.SIN[tau // 32]
                P.op("dve", lambda e: e.tensor_tensor(out=t1v, in0=ov[:, :, 0, :], in1=cs, op=ALU.mult), [ot, cb], [t1])
                P.op("pool", lambda e: e.tensor_tensor(out=t2v, in0=ov[:, :, 1, :], in1=sn, op=ALU.mult), [ot, sb_], [t2])
                P.op("dve", lambda e: e.tensor_tensor(out=rv[:, :, 0, :], in0=t1v, in1=t2v, op=ALU.subtract), [t1, t2], [rt])
                P.op("dve", lambda e: e.tensor_tensor(out=t1v, in0=ov[:, :, 0, :], in1=sn, op=ALU.mult), [ot, sb_, rt], [t1])
                P.op("pool", lambda e: e.tensor_tensor(out=t2v, in0=ov[:, :, 1, :], in1=cs, op=ALU.mult), [ot, cb, rt], [t2])
                P.op("dve", lambda e: e.tensor_tensor(out=rv[:, :, 1, :], in0=t1v, in1=t2v, op=ALU.add), [t1, t2], [rt])
                src = rt
            zT = T4[17 + mt]
            for hf in range(2):
                self.transpose_to(src, [src[:, (hf * 4 + q) * 128:(hf * 4 + q + 1) * 128] for q in range(4)],
                                  zT, [(bf(zT)[:, hf * 512:(hf + 1) * 512], 0, 512)], self.PS[4 + mt * 2 + hf],
                                  evac="act" if hf == 0 else "dve")
            r0 = jb * 256 + mt * 128
            P.dma("sp", dstT.ap().rearrange("(h p) c -> p h c", p=128)[:, :, r0:r0 + 128],
                  bf(zT)[:, 0:1024].rearrange("p (h c) -> p h c", c=128), reads=[zT], writes=[dstT])
        return post
    self.tm_pass(I["da_w_qkv"], w, 0, 1024, None, None, None, post=mk_post(QT))
    self.tm_pass(I["da_w_qkv"], w, 1024, 1024, None, None, None, post=mk_post(KT))
    self.tm_pass(I["da_w_qkv"], w, 2048, 1024, None, None, V, out_bf=True)


def da_attn(self, i):
    P = self.P
    I = self.I
    T4 = self.T4
    PS = self.PS
    QT, KT, OT = self.fscratch("daqT", 1024, BF16), self.fscratch("dakT", 1024, BF16), self.fscratch("daoT", 1024, BF16)
    V = self.scratch("dav", D, BF16)
    lambda_init = 0.8 - 0.6 * math.exp(-0.3 * i)
    lt = T4[29]
    self.bcast_load(lt, lt[:, 0:256], I["da_lambda"], I["da_lambda"].ap()[0].rearrange("a k -> (a k)"))
    lam = self.small("da_lam", [128, 4])
    P.op("dve", lambda e: e.tensor_tensor(out=lt[:, 256:320], in0=lt[:, 0:64], in1=lt[:, 64:128], op=ALU.mult), [lt], [lt])
    P.op("dve", lambda e: e.tensor_tensor(out=lt[:, 320:384], in0=lt[:, 128:192], in1=lt[:, 192:256], op=ALU.mult), [lt], [lt])
    P.op("dve", lambda e: e.tensor_reduce(out=lam[:, 0:2], in_=lt[:, 256:384].rearrange("p (a k) -> p a k", k=64),
                                          axis=AX.X, op=ALU.add), [lt], [lam])
    P.op("act", lambda e: e.activation(out=lam[:, 0:2], in_=lam[:, 0:2], func=AF.Exp), [lam], [lam])
    P.op("dve", lambda e: e.tensor_tensor(out=lam[:, 2:3], in0=lam[:, 1:2], in1=lam[:, 0:1], op=ALU.subtract), [lam], [lam])
    P.op("dve", lambda e: e.tensor_scalar(out=lam[:, 2:3], in0=lam[:, 2:3], scalar1=-lambda_init, scalar2=None, op0=ALU.add),
         [lam], [lam])
    ngc = self.small("da_ngc", [128, 1])
    self.load_cols(ngc, ngc[:], I["da_norm_g"], I["da_norm_g"].ap()[0:1, :], 1)
    P.op("dve", lambda e: e.tensor_scalar(out=ngc[:], in0=ngc[:], scalar1=1.0 - lambda_init, scalar2=None, op0=ALU.mult),
         [ngc], [ngc])
    nkt = self.nt
    nkT = (self.ntok + 1023) // 1024
    for h in range(8):
        kres = T4[0:nkT]
        vres = T4[9:9 + nkT]
        for q in range(nkT):
            c0 = q * 1024
            wd = min(1024, self.ntok - c0)
            P.dma("sp", bf(kres[q])[:, 0:wd], KT.ap()[h * 128:(h + 1) * 128, c0:c0 + wd], reads=[KT], writes=[kres[q]])
            nk = wd // 128
            P.dma("pool", bf(vres[q])[:, 0:wd].rearrange("p (kt e) -> p kt e", e=128),
                  V.ap()[c0:c0 + wd, h * 128:(h + 1) * 128].rearrange("(kt p) e -> p kt e", p=128),
                  reads=[V], writes=[vres[q]])
        blocks = [(0, 256, [0, 1])] + [(CTX + b * 512, 512, list(range(nkt))) for b in range(self.seq // 512)]
        for bi, (q0, qn, kts) in enumerate(blocks):
            qt = T4[18 + bi % 2]
            P.dma("sp", bf(qt)[:, 0:qn], QT.ap()[h * 128:(h + 1) * 128, q0:q0 + qn], reads=[QT], writes=[qt])
            for ki_, kt in enumerate(kts):
                kq, ko = kt // 8, (kt % 8) * 128
                for m in range(2):
                    psc = PS[4 + (2 * ki_ + m) % 4]
                    et = T4[20 + (2 * ki_ + m) % 4]
                    rows = slice(64 * m, 64 * m + 64)
                    P.op("pe", lambda e: e.matmul(psc[:, 0:qn], lhsT=bf(kres[kq])[rows, ko:ko + 128], rhs=bf(qt)[rows, 0:qn],
                                                  start=True, stop=True), [kres[kq], qt], [psc])
                    P.op("act", lambda e: e.activation(out=bf(et)[:, 0:qn], in_=psc[:, 0:qn], func=AF.Exp, scale=0.125),
                         [psc], [et])
                    first, last = (ki_ == 0), (ki_ == len(kts) - 1)
                    P.op("pe", lambda e: e.matmul(PS[2 * m][:, 0:qn], lhsT=bf(vres[kq])[:, ko:ko + 128], rhs=bf(et)[:, 0:qn],
                                                  start=first, stop=last), [vres[kq], et], [PS[2 * m]], inc=False)
                    P.op("pe", lambda e: e.matmul(PS[2 * m + 1][:, 0:qn], lhsT=self.ones_bf[:], rhs=bf(et)[:, 0:qn],
                                                  start=first, stop=last), [self.ones_bf, et], [PS[2 * m + 1]])
            o0, o1, df = T4[24], T4[25], T4[26]
            for m, ob in ((0, o0), (1, o1)):
                P.op("dve", lambda e: e.reciprocal(out=ob[:, 0:qn], in_=PS[2 * m + 1][:, 0:qn]), [PS[2 * m + 1]], [ob])
                P.op("dve", lambda e: e.tensor_tensor(out=ob[:, 512:512 + qn], in0=PS[2 * m][:, 0:qn], in1=ob[:, 0:qn],
                                                      op=ALU.mult), [PS[2 * m], ob], [ob])
            P.op("dve", lambda e: e.scalar_tensor_tensor(out=df[:, 0:qn], in0=o1[:, 512:512 + qn], scalar=lam[:, 2:3],
                                                         in1=o0[:, 512:512 + qn], op0=ALU.mult, op1=ALU.add),
                 [o0, o1, lam], [df])
            P.op("act", lambda e: e.activation(out=df[:, 512:512 + qn], in_=df[:, 0:qn], func=AF.Square), [df], [df])
            P.op("pe", lambda e: e.matmul(PS[4][:, 0:qn], lhsT=self.ones[:], rhs=df[:, 512:512 + qn], start=True, stop=True),
                 [self.ones, df], [PS[4]])
            rs = T4[27]
            P.op("dve", lambda e: e.tensor_scalar(out=rs[:, 0:qn], in0=PS[4][:, 0:qn], scalar1=1.0 / 128, scalar2=1e-5,
                                                  op0=ALU.mult, op1=ALU.add), [PS[4]], [rs])
            P.op("act", lambda e: e.activation(out=rs[:, 0:qn], in_=rs[:, 0:qn], func=AF.Sqrt), [rs], [rs])
            P.op("dve", lambda e: e.reciprocal(out=rs[:, 0:qn], in_=rs[:, 0:qn]), [rs], [rs])
            P.op("dve", lambda e: e.scalar_tensor_tensor(out=bf(rs)[:, 1024:1024 + qn], in0=df[:, 0:qn], scalar=ngc[:, 0:1],
                                                         in1=rs[:, 0:qn], op0=ALU.mult, op1=ALU.mult), [df, ngc, rs], [rs])
            P.dma("pool", OT.ap()[h * 128:(h + 1) * 128, q0:q0 + qn], bf(rs)[:, 1024:1024 + qn], reads=[rs], writes=[OT])


def da_post(self, i, Xin, Xout):
    P = self.P
    I = self.I
    T4 = self.T4
    wres = T4[0:8]
    self.load_wres(I["da_w_out"], I["da_w_out"].ap()[0], wres)
    OT = self.fscratch("daoT", 1024, BF16)
    t_start = 2 if i == DEPTH - 1 else 0
    for t in range(t_start, self.nt):
        par = t % 2
        zT = T4[20 + par]
        P.dma("sp", bf(zT)[:, 0:1024].rearrange("p (h c) -> p h c", c=128),
              OT.ap().rearrange("(h p) c -> p h c", p=128)[:, :, t * 128:(t + 1) * 128], reads=[OT], writes=[zT])
        self.proj_residual([(zT, bf(zT)[:, kc * 128:(kc + 1) * 128]) for kc in range(8)], wres, Xin, Xout, t, par)


def diffattn(self, i, Xin, Xout):
    self.da_pre(i, Xin)
    self.da_attn(i)
    self.da_post(i, Xin, Xout)


K.da_rope_tables = da_rope_tables
K.da_pre = da_pre
K.da_attn = da_attn
K.da_post = da_post
K.diffattn = diffattn


N_CORES = 8
_CACHE = {}


def kernel(**inputs):
    if "k" not in _CACHE:
        _CACHE["k"] = build({})
    k = _CACHE["k"]
    maps = []
    for c in range(N_CORES):
        b = c % 2
        m = {}
        for name, _ in INPUT_SHAPES:
            a = np.asarray(inputs[name])
            if name in ("x", "c", "ctx"):
                a = a[b]
            m[name] = np.ascontiguousarray(a, dtype=np.float32)
        maps.append(m)
    res = run_bass_kernel_spmd(k.nc, maps, core_ids=list(range(N_CORES)))
    out = np.stack([np.asarray(res.results[b]["out"]) for b in range(2)], axis=0)
    return out.astype(np.float32)
```

```python
import math
import numpy as np
import concourse.bass as bass
import concourse.mybir as mybir
from concourse.bass_utils import run_bass_kernel_spmd

F32 = mybir.dt.float32
BF16 = mybir.dt.bfloat16
ALU = mybir.AluOpType
AF = mybir.ActivationFunctionType
AX = mybir.AxisListType

D = 1024
SEQ = 8192
CTX = 256
NTOK = CTX + SEQ
NT = NTOK // 128
DEPTH = 4
FF = 2816
NFC = FF // 128
EPS = 1e-6
HTC = NTOK + 4


def bf(tile):
    return tile[:].bitcast(BF16)


def col_of(tok):
    return tok + 1 if tok < CTX else tok + 3


class Buf:
    __slots__ = ("name", "t", "wr", "rd", "multi")

    def __init__(self, name, t=None, multi=False):
        self.name = name
        self.t = t
        self.wr = {}
        self.rd = {}
        self.multi = multi

    def __getitem__(self, idx):
        return self.t[idx]

    def ap(self):
        return self.t.ap()


class Prog:
    NDMA = 56

    def __init__(self, nc):
        self.nc = nc
        self.E = {"pe": nc.tensor, "dve": nc.vector, "act": nc.scalar,
                  "pool": nc.gpsimd, "sp": nc.sync}
        self.sem = {e: nc.alloc_semaphore("s_" + e) for e in self.E}
        self.cnt = {e: 0 for e in self.E}
        self.seen = {e: {} for e in self.E}
        self.dsem = [nc.alloc_semaphore("d%d" % i) for i in range(self.NDMA)]
        self.duse = [0] * self.NDMA
        self.dnext = 0
        self.ninst = 0
        self.out_tokens = []
        self.dump = set()
        self.pend = {e: ([], []) for e in self.E}

    def sb(self, name, shape, dt=F32):
        return Buf(name, self.nc.alloc_sbuf_tensor(name, list(shape), dt))

    def ps(self, name, shape, dt=F32):
        return Buf(name, self.nc.alloc_psum_tensor(name, list(shape), dt))

    def dram(self, name, shape, dt=F32, kind=None):
        if kind is None:
            kind = "ExternalOutput" if name in self.dump else "Internal"
        return Buf(name, self.nc.dram_tensor(name, list(shape), dt, kind=kind), multi=True)

    def _semh(self, key):
        return self.sem[key] if isinstance(key, str) else self.dsem[key]

    def _need(self, eng, key, val, need):
        if key == eng and eng == "pe":
            return
        if self.seen[eng].get(key, 0) >= val:
            return
        if need.get(key, 0) < val:
            need[key] = val

    def _deps(self, eng, reads, writes):
        need = {}
        for b in reads:
            for k, v in b.wr.items():
                self._need(eng, k, v, need)
        for b in writes:
            if not b.multi:
                for k, v in b.wr.items():
                    self._need(eng, k, v, need)
            for k, v in b.rd.items():
                self._need(eng, k, v, need)
        return need

    def _emit_waits(self, eng, need, keep_last=True):
        items = list(need.items())
        last = None
        if keep_last and items:
            last = items.pop()
        for k, v in items:
            self.E[eng].wait_ge(self._semh(k), v)
            self.seen[eng][k] = v
            self.ninst += 1
        if last is not None:
            self.seen[eng][last[0]] = last[1]
        return last

    def _commit(self, key, val, reads, writes):
        for b in writes:
            if b.multi:
                b.wr[key] = max(b.wr.get(key, 0), val)
            else:
                b.wr = {key: val}
                b.rd = {}
        for b in reads:
            if (not b.multi) and any(b is w for w in writes):
                continue
            b.rd[key] = max(b.rd.get(key, 0), val)

    def op(self, eng, fn, reads=(), writes=(), inc=True):
        last = self._emit_waits(eng, self._deps(eng, reads, writes))
        ins = fn(self.E[eng])
        if last is not None:
            ins._wait_ge(self._semh(last[0]), last[1])
        self.ninst += 1
        pr, pw = self.pend[eng]
        if inc:
            ins.then_inc(self.sem[eng], 1)
            self.cnt[eng] += 1
            if pr or pw:
                reads = list(reads) + [b for b in pr if not any(b is r for r in reads)]
                writes = list(writes) + [b for b in pw if not any(b is w for w in writes)]
                self.pend[eng] = ([], [])
            self._commit(eng, self.cnt[eng], reads, writes)
        else:
            for b in reads:
                if not any(b is r for r in pr):
                    pr.append(b)
            for b in writes:
                if not any(b is w for w in pw):
                    pw.append(b)
        return ins

    def dma(self, q, out, in_, reads=(), writes=(), is_output=False, **kw):
        q = "act" if q == "pool" else "sp"
        need = self._deps(q, reads, writes)
        j = self.dnext
        self.dnext = (self.dnext + 1) % self.NDMA
        if self.duse[j]:
            self._need(q, j, 16 * self.duse[j], need)
        last = self._emit_waits(q, need)
        self.duse[j] += 1
        ins = self.E[q].dma_start(out=out, in_=in_, **kw)
        if last is not None:
            ins._wait_ge(self._semh(last[0]), last[1])
        ins.then_inc(self.dsem[j], 16)
        self.ninst += 1
        self._commit(j, 16 * self.duse[j], reads, writes)
        if is_output:
            self.out_tokens.append((j, 16 * self.duse[j]))

    def finish(self, eng="sp"):
        need = {}
        for key, val in self.out_tokens:
            self._need(eng, key, val, need)
        for e in self.E:
            if e != eng and self.cnt[e]:
                self._need(eng, e, self.cnt[e], need)
        for j in range(self.NDMA):
            if self.duse[j]:
                self._need(eng, j, 16 * self.duse[j], need)
        self._emit_waits(eng, need, keep_last=False)


INPUT_SHAPES = [
    ("x", [SEQ, D]), ("c", [D]), ("ctx", [CTX, D]), ("c_ctx", [D]),
    ("ada_w", [4, D, 6 * D]), ("ada_b", [4, 6 * D]), ("norm1_g", [4, D]), ("norm2_g", [4, D]),
    ("ffn_w_in", [4, D, 2 * FF]), ("ffn_conv_w", [4, 3, FF]), ("ffn_conv_b", [4, FF]),
    ("ffn_w_out", [4, FF, D]),
    ("ra_mix", [2, 6, D]), ("ra_w_rkv", [2, 3, D, D]), ("ra_w0", [2, 2, D]),
    ("ra_w1", [2, 2, D, 64]), ("ra_w2", [2, 2, 64, D]), ("ra_a0", [2, 2, D]),
    ("ra_a1", [2, 2, D, 64]), ("ra_a2", [2, 2, 64, D]), ("ra_g1", [2, D, 160]),
    ("ra_g2", [2, 160, D]), ("ra_k_k", [2, D]), ("ra_k_a", [2, D]), ("ra_r_k", [2, 16, 64]),
    ("ra_lnx_g", [2, D]), ("ra_lnx_b", [2, D]), ("ra_w_out", [2, D, D]),
    ("ml_w_in", [1, D, 3088]), ("ml_b_in", [1, 3088]), ("ml_norm_g", [1, D]), ("ml_w_out", [1, D, D]),
    ("da_w_qkv", [1, D, 3 * D]), ("da_lambda", [1, 4, 64]), ("da_norm_g", [1, 128]),
    ("da_w_out", [1, D, D]), ("final_g", [D]),
]


class K:
    def __init__(self, cfg):
        self.cfg = cfg
        self.seq = cfg.get("seq", SEQ)
        self.ntok = CTX + self.seq
        self.nt = self.ntok // 128
        self.nblk = self.ntok // 256
        self.htc = self.ntok + 4
        nc = bass.Bass("TRN2", target_bir_lowering=False)
        self.nc = nc
        P = Prog(nc)
        P.dump = set(cfg.get("dump", []))
        self.P = P
        self.I = {}
        for name, shape in INPUT_SHAPES:
            if name == "x":
                shape = [self.seq, D]
            self.I[name] = P.dram(name, shape, F32, kind="ExternalInput")
        self.out = P.dram("out", [self.seq, D], F32, kind="ExternalOutput")
        self.X = [P.dram("XA", [self.ntok, D]), P.dram("XB", [self.ntok, D])]
        self.HT = P.dram("HT", [D, self.htc], BF16)
        self.T4 = [P.sb("t4_%d" % i, [128, 1024]) for i in range(30)]
        self.S = {}
        self.MOD = [P.sb("modl", [128, 6 * D]), P.sb("modc", [128, 6 * D])]
        self.PS = [P.ps("ps%d" % i, [128, 512]) for i in range(8)]
        self.sm = {}
        self.consts()

    def small(self, name, shape, dt=F32):
        if name not in self.sm:
            self.sm[name] = self.P.sb("sm_" + name, shape, dt)
        return self.sm[name]

    def col_of(self, tok):
        return tok + 1 if tok < CTX else tok + 3

    def consts(self):
        P = self.P
        self.ident = P.sb("ident", [128, 128])
        self.ones = P.sb("ones", [128, 128])
        self.lo = P.sb("lo", [128, 128])
        self.up = P.sb("up", [128, 128])
        self.los = P.sb("los", [128, 128])
        self.ups = P.sb("ups", [128, 128])
        self.zero = P.sb("zero", [128, 64])

        def mask(t, pat, cm, op):
            P.op("pool", lambda e: e.memset(t[:], 1.0), [], [t])
            P.op("pool", lambda e: e.affine_select(out=t[:], in_=t[:], pattern=[[pat, 128]],
                                                    compare_op=op, fill=0.0, base=0,
                                                    channel_multiplier=cm), [t], [t])
        mask(self.lo, -1, 1, ALU.is_ge)
        mask(self.up, 1, -1, ALU.is_ge)
        mask(self.los, -1, 1, ALU.is_gt)
        mask(self.ups, 1, -1, ALU.is_gt)
        P.op("pool", lambda e: e.memset(self.ones[:], 1.0), [], [self.ones])
        self.ones_bf = P.sb("ones_bf", [128, 128], BF16)
        P.op("pool", lambda e: e.memset(self.ones_bf[:], 1.0), [], [self.ones_bf])
        P.op("pool", lambda e: e.memset(self.zero[:], 0.0), [], [self.zero])
        P.op("dve", lambda e: e.tensor_tensor(out=self.ident[:], in0=self.lo[:], in1=self.up[:],
                                              op=ALU.mult), [self.lo, self.up], [self.ident])

    def bcast_load(self, dst, dst_ap, src_buf, src_ap, q="sp"):
        self.P.dma(q, dst_ap, src_ap.partition_broadcast(128), reads=[src_buf], writes=[dst])

    def transpose_to(self, src, src_aps, dst, dst_aps, psa, evac="act"):
        P = self.P
        for i, sap in enumerate(src_aps):
            P.op("pe", lambda e, i=i, sap=sap: e.transpose(psa[:, i * 128:(i + 1) * 128], sap, self.ident[:]),
                 [src, self.ident], [psa])
        for (dap, c0, c1) in dst_aps:
            if evac == "act":
                P.op("act", lambda e, dap=dap, c0=c0, c1=c1: e.activation(out=dap, in_=psa[:, c0:c1], func=AF.Copy),
                     [psa], [dst])
            else:
                P.op("dve", lambda e, dap=dap, c0=c0, c1=c1: e.tensor_copy(out=dap, in_=psa[:, c0:c1]),
                     [psa], [dst])

    def load_cols(self, dst, dst_ap, src_buf, src_ap, n):
        P = self.P
        st = self.small("lc_stage", [128, 128])
        ps = self.PS[0]
        P.dma("sp", st[0:n, :], src_ap, reads=[src_buf], writes=[st])
        P.op("pe", lambda e: e.transpose(ps[:, 0:n], st[0:n, :], self.ident[0:n, 0:n]), [st, self.ident], [ps])
        P.op("dve", lambda e: e.tensor_copy(out=dst_ap, in_=ps[:, 0:n]), [ps], [dst])

    def init(self):
        P = self.P
        X0 = self.X[0]
        P.dma("sp", X0.ap()[0:CTX, :], self.I["ctx"].ap(), reads=[self.I["ctx"]], writes=[X0])
        nrow = self.seq
        step = 2048
        for r0 in range(0, nrow, step):
            r1 = min(nrow, r0 + step)
            P.dma("pool" if (r0 // step) % 2 else "sp", X0.ap()[CTX + r0:CTX + r1, :],
                  self.I["x"].ap()[r0:r1, :], reads=[self.I["x"]], writes=[X0])
        htv = self.HT.ap().rearrange("(kc p) c -> p kc c", p=128)
        for c in (0, CTX + 1, CTX + 2, self.htc - 1):
            P.dma("sp", htv[:, :, c:c + 1], self.zero[:].bitcast(BF16)[:, 0:8].unsqueeze(2), reads=[self.zero], writes=[self.HT],
                  allow_slow_non_contiguous=True)
        self.CB = []
        for s, nm in enumerate(("c", "c_ctx")):
            cc = self.small("cc%d" % s, [128, 8])
            self.load_cols(cc, cc[:], self.I[nm], self.I[nm].ap().rearrange("(kc p) -> kc p", p=128), 8)
            P.op("act", lambda e, cc=cc: e.activation(out=cc[:], in_=cc[:], func=AF.Silu), [cc], [cc])
            cb = P.sb("cb%d" % s, [128, 1024])
            for kc in range(8):
                P.op("dve", lambda e, kc=kc, cb=cb, cc=cc: e.tensor_scalar(
                    out=cb[:, kc * 128:(kc + 1) * 128], in0=self.ones[:], scalar1=cc[:, kc:kc + 1], scalar2=None,
                    op0=ALU.mult), [self.ones, cc], [cb])
            self.CB.append(cb)

    def adaln(self, i):
        P = self.P
        aw = self.I["ada_w"].ap()[i].rearrange("(kc p) n -> p kc n", p=128)
        for blk in range(12):
            wts = [self.T4[4 + (blk % 2) * 4 + q] for q in range(4)]
            for q in range(4):
                P.dma("sp" if q % 2 == 0 else "pool", wts[q][:].rearrange("p (kc n) -> p kc n", n=512),
                      aw[:, 2 * q:2 * q + 2, blk * 512:(blk + 1) * 512], reads=[self.I["ada_w"]], writes=[wts[q]])
            bt = self.T4[blk % 2]
            self.bcast_load(bt, bt[:, 0:512], self.I["ada_b"], self.I["ada_b"].ap()[i, blk * 512:(blk + 1) * 512])
            for s in range(2):
                ps = self.PS[(blk % 2) * 2 + s]
                for kc in range(8):
                    wt = wts[kc // 2]
                    P.op("pe", lambda e, kc=kc, s=s, ps=ps, wt=wt: e.matmul(
                        ps[:, :], lhsT=self.CB[s][:, kc * 128:(kc + 1) * 128],
                        rhs=wt[:, (kc % 2) * 512:(kc % 2 + 1) * 512],
                        start=(kc == 0), stop=(kc == 7)), [self.CB[s], wt], [ps], inc=(kc == 7))
                P.op("dve", lambda e, s=s, ps=ps, bt=bt, blk=blk: e.tensor_tensor(
                    out=self.MOD[s][:, blk * 512:(blk + 1) * 512], in0=ps[:, :], in1=bt[:, 0:512], op=ALU.add),
                    [ps, bt], [self.MOD[s]])
        for (nm, off) in (("norm1_g", 1), ("norm2_g", 4)):
            gt = self.T4[2]
            self.bcast_load(gt, gt[:], self.I[nm], self.I[nm].ap()[i])
            for s in range(2):
                m = self.MOD[s]
                P.op("dve", lambda e, m=m, gt=gt, off=off: e.scalar_tensor_tensor(
                    out=m[:, off * D:(off + 1) * D], in0=m[:, off * D:(off + 1) * D], scalar=1.0, in1=gt[:],
                    op0=ALU.add, op1=ALU.mult), [m, gt], [m])

    def rms_rstd(self, xt, junk, ss, n, eps):
        P = self.P
        P.op("act", lambda e: e.activation(out=junk[:], in_=xt[:], func=AF.Square, accum_out=ss[:, 0:1]),
             [xt], [junk, ss])
        P.op("dve", lambda e: e.tensor_scalar(out=ss[:, 0:1], in0=ss[:, 0:1], scalar1=1.0 / n, scalar2=eps,
                                              op0=ALU.mult, op1=ALU.add), [ss], [ss])
        P.op("act", lambda e: e.activation(out=ss[:, 0:1], in_=ss[:, 0:1], func=AF.Sqrt), [ss], [ss])
        P.op("dve", lambda e: e.reciprocal(out=ss[:, 0:1], in_=ss[:, 0:1]), [ss], [ss])

    def norm_to_HT(self, Xin, goff, shoff):
        P = self.P
        htv = self.HT.ap().rearrange("(kc p) c -> p kc c", p=128)
        for t in range(self.nt):
            par = t % 2
            xt, h, hT = self.T4[par * 3], self.T4[par * 3 + 1], self.T4[par * 3 + 2]
            ss = self.small("nss%d" % par, [128, 1])
            m = self.MOD[1 if t < 2 else 0]
            P.dma("sp", xt[:], Xin.ap()[t * 128:(t + 1) * 128, :], reads=[Xin], writes=[xt])
            self.rms_rstd(xt, h, ss, D, EPS)
            P.op("dve", lambda e, xt=xt, h=h, ss=ss, m=m: e.scalar_tensor_tensor(
                out=h[:], in0=xt[:], scalar=ss[:, 0:1], in1=m[:, goff * D:(goff + 1) * D],
                op0=ALU.mult, op1=ALU.mult), [xt, ss, m], [h])
            P.op("dve", lambda e, h=h, m=m: e.tensor_tensor(
                out=h[:], in0=h[:], in1=m[:, shoff * D:(shoff + 1) * D], op=ALU.add), [h, m], [h])
            for hf in range(2):
                psa = self.PS[par * 2 + hf]
                self.transpose_to(h, [h[:, (hf * 4 + q) * 128:(hf * 4 + q + 1) * 128] for q in range(4)],
                                  hT, [(bf(hT)[:, hf * 512:(hf + 1) * 512], 0, 512)], psa,
                                  evac="act" if hf == 0 else "dve")
            c0 = self.col_of(t * 128)
            P.dma("pool", htv[:, :, c0:c0 + 128], bf(hT)[:, 0:1024].rearrange("p (kc c) -> p kc c", c=128),
                  reads=[hT], writes=[self.HT])

    def load_hTe(self, j, tiles):
        htv = self.HT.ap().rearrange("(kc p) c -> p kc c", p=128)
        c0 = self.col_of(j * 256)
        views = []
        for q, (k0, k1) in enumerate(((0, 3), (3, 6), (6, 8))):
            v = bf(tiles[q])[:, 0:(k1 - k0) * 258].rearrange("p (kc c) -> p kc c", c=258)
            self.P.dma("sp" if q != 1 else "pool", v, htv[:, k0:k1, c0 - 1:c0 + 257], reads=[self.HT], writes=[tiles[q]])
            views.append(v)

        def hv(kc):
            return tiles[kc // 3], views[kc // 3][:, kc % 3, :]
        return hv

    def cast_w(self, key, src_buf, src_ap, R, C):
        P = self.P
        name = "WB_" + key
        if name not in self.S:
            self.S[name] = P.dram(name, [R, C], BF16)
        wb = self.S[name]
        n = 0
        for r0 in range(0, R, 128):
            rr = min(128, R - r0)
            for c0 in range(0, C, 1024):
                cc = min(1024, C - c0)
                st, ob = self.T4[14 + (n % 2) * 2], self.T4[15 + (n % 2) * 2]
                P.dma("sp", st[0:rr, 0:cc], src_ap[r0:r0 + rr, c0:c0 + cc], reads=[src_buf], writes=[st])
                k3 = n % 3
                if k3 == 0:
                    P.op("dve", lambda e: e.tensor_copy(out=bf(ob)[0:rr, 0:cc], in_=st[0:rr, 0:cc]), [st], [ob])
                elif k3 == 1:
                    P.op("act", lambda e: e.activation(out=bf(ob)[0:rr, 0:cc], in_=st[0:rr, 0:cc], func=AF.Copy), [st], [ob])
                else:
                    P.op("pool", lambda e: e.tensor_copy(out=bf(ob)[0:rr, 0:cc], in_=st[0:rr, 0:cc]), [st], [ob])
                P.dma("pool", wb.ap()[r0:r0 + rr, c0:c0 + cc], bf(ob)[0:rr, 0:cc], reads=[ob], writes=[wb])
                n += 1
        return wb

    def ffn(self, i, Xin, Xout):
        P = self.P
        self.norm_to_HT(Xin, 4, 3)
        cw = self.small("ffn_cw", [128, 3, NFC])
        cbias = self.small("ffn_cb", [128, NFC])
        self.load_cols(cw, cw[:].rearrange("p j fc -> p (j fc)"), self.I["ffn_conv_w"],
                       self.I["ffn_conv_w"].ap()[i].rearrange("j (fc p) -> (j fc) p", p=128), 3 * NFC)
        self.load_cols(cbias, cbias[:], self.I["ffn_conv_b"],
                       self.I["ffn_conv_b"].ap()[i].rearrange("(fc p) -> fc p", p=128), NFC)
        winb = self.cast_w("ffn_win", self.I["ffn_w_in"], self.I["ffn_w_in"].ap()[i], D, 2 * FF)
        woutb = self.cast_w("ffn_wout", self.I["ffn_w_out"], self.I["ffn_w_out"].ap()[i], FF, D)
        win = winb.ap().rearrange("(kc p) n -> p kc n", p=128)
        wout = woutb.ap()
        last_ctx = (i == DEPTH - 1)
        for j in range(self.nblk):
            if j == 0 and last_ctx:
                continue
            m = self.MOD[1 if j == 0 else 0]
            hv = self.load_hTe(j, self.T4[18 + (j % 2) * 3:21 + (j % 2) * 3])
            po = [self.PS[4], self.PS[5], self.PS[6], self.PS[7]]
            for fc in range(NFC):
                par = fc % 2
                wg, wv_, wo = self.T4[6 + par * 3], self.T4[7 + par * 3], self.T4[8 + par * 3]
                wgv = bf(wg)[:, 0:1024].rearrange("p (kc f) -> p kc f", f=128)
                wvv = bf(wv_)[:, 0:1024].rearrange("p (kc f) -> p kc f", f=128)
                P.dma("sp", wgv, win[:, :, FF + fc * 128:FF + (fc + 1) * 128], reads=[winb], writes=[wg])
                P.dma("pool", wvv, win[:, :, fc * 128:(fc + 1) * 128], reads=[winb], writes=[wv_])
                P.dma("sp", bf(wo)[:, 0:1024], wout[fc * 128:(fc + 1) * 128, :], reads=[woutb], writes=[wo])
                pg, pv = self.PS[par * 2], self.PS[par * 2 + 1]
                for kc in range(8):
                    hb, ha = hv(kc)
                    P.op("pe", lambda e, kc=kc, pg=pg, wgv=wgv, ha=ha: e.matmul(
                        pg[:, 0:258], lhsT=wgv[:, kc, :], rhs=ha, start=(kc == 0), stop=(kc == 7)),
                        [wg, hb], [pg], inc=(kc == 7))
                for kc in range(8):
                    hb, ha = hv(kc)
                    P.op("pe", lambda e, kc=kc, pv=pv, wvv=wvv, ha=ha: e.matmul(
                        pv[:, 0:256], lhsT=wvv[:, kc, :], rhs=ha[:, 1:257], start=(kc == 0), stop=(kc == 7)),
                        [wv_, hb], [pv], inc=(kc == 7))
                tt = self.T4[12 + par]
                P.op("act", lambda e, tt=tt, pg=pg, fc=fc: e.activation(
                    out=tt[:, 0:256], in_=pg[:, 1:257], func=AF.Identity, scale=cw[:, 1, fc:fc + 1],
                    bias=cbias[:, fc:fc + 1]), [pg, cw, cbias], [tt])
                P.op("dve", lambda e, tt=tt, pg=pg, fc=fc: e.scalar_tensor_tensor(
                    out=tt[:, 0:256], in0=pg[:, 0:256], scalar=cw[:, 0, fc:fc + 1], in1=tt[:, 0:256],
                    op0=ALU.mult, op1=ALU.add), [pg, cw, tt], [tt])
                P.op("dve", lambda e, tt=tt, pg=pg, fc=fc: e.scalar_tensor_tensor(
                    out=tt[:, 0:256], in0=pg[:, 2:258], scalar=cw[:, 2, fc:fc + 1], in1=tt[:, 0:256],
                    op0=ALU.mult, op1=ALU.add), [pg, cw, tt], [tt])
                P.op("act", lambda e, tt=tt: e.activation(out=tt[:, 256:512], in_=tt[:, 0:256],
                                                          func=AF.Gelu_apprx_tanh), [tt], [tt])
                P.op("dve", lambda e, tt=tt, pv=pv: e.tensor_tensor(
                    out=bf(tt)[:, 1536:1792], in0=tt[:, 256:512], in1=pv[:, 0:256], op=ALU.mult), [tt, pv], [tt])
                for mt in range(2):
                    for hf in range(2):
                        pb = po[mt * 2 + hf]
                        P.op("pe", lambda e, tt=tt, wo=wo, pb=pb, mt=mt, hf=hf, fc=fc: e.matmul(
                            pb[:, :], lhsT=bf(tt)[:, 1536 + mt * 128:1536 + (mt + 1) * 128],
                            rhs=bf(wo)[:, hf * 512:(hf + 1) * 512], start=(fc == 0), stop=(fc == NFC - 1)),
                            [tt, wo], [pb], inc=(fc == NFC - 1))
            for mt in range(2):
                r0 = j * 256 + mt * 128
                xt, xn = self.T4[14 + mt * 2], self.T4[15 + mt * 2]
                P.dma("sp", xt[:], Xin.ap()[r0:r0 + 128, :], reads=[Xin], writes=[xt])
                for hf in range(2):
                    pb = po[mt * 2 + hf]
                    P.op("dve", lambda e, xn=xn, pb=pb, hf=hf, m=m: e.tensor_tensor(
                        out=xn[:, hf * 512:(hf + 1) * 512], in0=pb[:, :],
                        in1=m[:, 5 * D + hf * 512:5 * D + (hf + 1) * 512], op=ALU.mult), [pb, m], [xn])
                P.op("pool", lambda e, xn=xn, xt=xt: e.tensor_tensor(
                    out=xn[:], in0=xn[:], in1=xt[:], op=ALU.add), [xn, xt], [xn])
                P.dma("pool", Xout.ap()[r0:r0 + 128, :], xn[:], reads=[xn], writes=[Xout])

    def final(self, Xin):
        P = self.P
        g = self.T4[6]
        self.bcast_load(g, g[:], self.I["final_g"], self.I["final_g"].ap())
        for t in range(2, self.nt):
            par = t % 2
            xt, h = self.T4[par * 3], self.T4[par * 3 + 1]
            ss = self.small("nss%d" % par, [128, 1])
            P.dma("sp", xt[:], Xin.ap()[t * 128:(t + 1) * 128, :], reads=[Xin], writes=[xt])
            self.rms_rstd(xt, h, ss, D, EPS)
            P.op("dve", lambda e, xt=xt, h=h, ss=ss: e.scalar_tensor_tensor(
                out=h[:], in0=xt[:], scalar=ss[:, 0:1], in1=g[:], op0=ALU.mult, op1=ALU.mult), [xt, ss, g], [h])
            P.dma("pool", self.out.ap()[(t - 2) * 128:(t - 1) * 128, :], h[:], reads=[h], writes=[self.out],
                  is_output=True)


def build(cfg):
    k = K(cfg)
    k.init()
    cur = 0
    for (i, kind) in cfg.get("layers", [(0, "ra"), (1, "ml"), (2, "da"), (3, "ra")]):
        k.adaln(i)
        if kind == "ra":
            k.rwkv(i, k.X[cur], k.X[1 - cur]); cur = 1 - cur
        elif kind == "ml":
            k.mlstm(i, k.X[cur], k.X[1 - cur]); cur = 1 - cur
        elif kind == "da":
            k.diffattn(i, k.X[cur], k.X[1 - cur]); cur = 1 - cur
        if cfg.get("ffn", True):
            k.ffn(i, k.X[cur], k.X[1 - cur]); cur = 1 - cur
    k.final(k.X[cur])
    k.P.finish()
    return k


def scratch(self, name, cols=D, dt=F32):
    if name not in self.S:
        self.S[name] = self.P.dram("S_" + name, [self.ntok, cols], dt)
    return self.S[name]


def load_wres(self, wbuf, wap, tiles, ncols=D):
    wb = self.cast_w("wres", wbuf, wap, D, ncols) if ncols == D else self.cast_w("wres%d" % ncols, wbuf, wap, D, ncols)
    for kc in range(8):
        self.P.dma("sp" if kc % 2 == 0 else "pool", bf(tiles[kc])[:, 0:ncols], wb.ap()[kc * 128:(kc + 1) * 128, :],
                   reads=[wb], writes=[tiles[kc]])


def proj_residual(self, zT_list, wres, Xin, Xout, t, par):
    P = self.P
    m = self.MOD[1 if t < 2 else 0]
    xt, xn = self.T4[26 + par * 2], self.T4[27 + par * 2]
    P.dma("sp", xt[:], Xin.ap()[t * 128:(t + 1) * 128, :], reads=[Xin], writes=[xt])
    for hf in range(2):
        pb = self.PS[6 + hf]
        for kc in range(8):
            zb, za = zT_list[kc]
            P.op("pe", lambda e, kc=kc, za=za, pb=pb, hf=hf: e.matmul(
                pb[:, :], lhsT=za, rhs=bf(wres[kc])[:, hf * 512:(hf + 1) * 512], start=(kc == 0), stop=(kc == 7)),
                [zb, wres[kc]], [pb], inc=(kc == 7))
        P.op("dve", lambda e, xn=xn, pb=pb, hf=hf, m=m: e.tensor_tensor(
            out=xn[:, hf * 512:(hf + 1) * 512], in0=pb[:, :], in1=m[:, 2 * D + hf * 512:2 * D + (hf + 1) * 512],
            op=ALU.mult), [pb, m], [xn])
    P.op("pool", lambda e, xn=xn, xt=xt: e.tensor_tensor(out=xn[:], in0=xn[:], in1=xt[:], op=ALU.add), [xn, xt], [xn])
    P.dma("pool", Xout.ap()[t * 128:(t + 1) * 128, :], xn[:], reads=[xn], writes=[Xout])


K.scratch = scratch
K.load_wres = load_wres
K.proj_residual = proj_residual


CDEC = math.exp(-0.5)


def interleave(gens):
    gens = list(gens)
    while gens:
        for g in list(gens):
            try:
                next(g)
            except StopIteration:
                gens.remove(g)


def rw_xs(self, hv, jm, mixc, xs_tiles, xx_tiles):
    P = self.P
    views = []
    for kc in range(8):
        hb, ha = hv(kc)
        xt = xs_tiles[kc // 4]
        xv = bf(xt)[:, (kc % 4) * 256:(kc % 4 + 1) * 256]
        xxb = xx_tiles[kc // 4]
        xxv = xxb[:, (kc % 4) * 256:(kc % 4 + 1) * 256]
        eng = "dve"
        P.op(eng, lambda e, xv=xv, xxv=xxv, ha=ha, kc=kc: e.scalar_tensor_tensor(
            out=xv, in0=xxv, scalar=mixc[:, jm * 8 + kc:jm * 8 + kc + 1], in1=ha[:, 1:257],
            op0=ALU.mult, op1=ALU.add), [xxb, mixc, hb], [xt])
        views.append((xt, xv))
    return views


def rw_xx(self, hv, xx_tiles):
    P = self.P
    for kc in range(8):
        hb, ha = hv(kc)
        xxb = xx_tiles[kc // 4]
        xxv = xxb[:, (kc % 4) * 256:(kc % 4 + 1) * 256]
        eng = "dve"
        P.op(eng, lambda e, xxv=xxv, ha=ha: e.tensor_tensor(out=xxv, in0=ha[:, 0:256], in1=ha[:, 2:258], op=ALU.add),
             [hb], [xxb])
        P.op(eng, lambda e, xxv=xxv, ha=ha: e.scalar_tensor_tensor(
            out=xxv, in0=xxv, scalar=0.5, in1=ha[:, 1:257], op0=ALU.mult, op1=ALU.subtract), [xxb, hb], [xxb])


def rwkv_pre(self, i, Xin):
    P = self.P
    j = i // 3
    I = self.I
    self.norm_to_HT(Xin, 1, 0)
    mixc = self.small("ra_mixc", [128, 48])
    self.load_cols(mixc, mixc[:], I["ra_mix"], I["ra_mix"].ap()[j].rearrange("m (kc p) -> (m kc) p", p=128), 48)
    T4 = self.T4
    wres = T4[0:8]
    SR, SV, SK, SKK = self.scratch("r"), self.scratch("v"), self.scratch("k"), self.scratch("kk")
    SG_ = self.scratch("g")
    SSG = [self.scratch("sg0"), self.scratch("sg1")]
    SKD = [self.scratch("kd0"), self.scratch("kd1")]
    SBA = [self.scratch("ba0"), self.scratch("ba1")]
    SRKS = self.scratch("rks", 16)

    def block_setup(jb, jm):
        hv = self.load_hTe(jb, T4[8 + (jb % 2) * 3:11 + (jb % 2) * 3])
        xx_tiles = T4[14:16]
        self.rw_xx(hv, xx_tiles)
        xs_tiles = T4[16 + (jb % 2) * 2:18 + (jb % 2) * 2]
        return self.rw_xs(hv, jm, mixc, xs_tiles, xx_tiles)

    kkb = T4[24]
    for (jm, proj, dst) in ((0, 0, SR), (2, 2, SV), (1, 1, SK)):
        self.load_wres(I["ra_w_rkv"], I["ra_w_rkv"].ap()[j, proj], wres)
        if proj == 1:
            self.bcast_load(kkb, kkb[:], I["ra_k_k"], I["ra_k_k"].ap()[j])
        for jb in range(self.nblk):
            xs = block_setup(jb, jm)
            for mt in range(2):
                ot = T4[20 + mt]
                for hf in range(2):
                    pb = self.PS[(mt * 2 + hf) % 4]
                    for kc in range(8):
                        xb, xv = xs[kc]
                        P.op("pe", lambda e, kc=kc, xv=xv, pb=pb, hf=hf, mt=mt: e.matmul(
                            pb[:, :], lhsT=xv[:, mt * 128:(mt + 1) * 128], rhs=bf(wres[kc])[:, hf * 512:(hf + 1) * 512],
                            start=(kc == 0), stop=(kc == 7)), [xb, wres[kc]], [pb], inc=(kc == 7))
                    P.op("act", lambda e, ot=ot, pb=pb, hf=hf: e.activation(
                        out=ot[:, hf * 512:(hf + 1) * 512], in_=pb[:, :], func=AF.Copy), [pb], [ot])
                r0 = jb * 256 + mt * 128
                P.dma("pool", dst.ap()[r0:r0 + 128, :], ot[:], reads=[ot], writes=[dst])
                if proj == 1:
                    kr, sq = T4[22], T4[23]
                    ssh = self.small("kk_ss", [128, 16])
                    P.op("dve", lambda e, kr=kr, ot=ot: e.tensor_tensor(out=kr[:], in0=ot[:], in1=kkb[:], op=ALU.mult),
                         [ot, kkb], [kr])
                    P.op("act", lambda e, kr=kr, sq=sq: e.activation(out=sq[:], in_=kr[:], func=AF.Square), [kr], [sq])
                    P.op("dve", lambda e, sq=sq: e.tensor_reduce(
                        out=ssh[:], in_=sq[:].rearrange("p (h k) -> p h k", k=64), axis=AX.X, op=ALU.add), [sq], [ssh])
                    P.op("act", lambda e: e.activation(out=ssh[:], in_=ssh[:], func=AF.Sqrt), [ssh], [ssh])
                    P.op("dve", lambda e: e.tensor_scalar_max(out=ssh[:], in0=ssh[:], scalar1=1e-12), [ssh], [ssh])
                    P.op("dve", lambda e: e.reciprocal(out=ssh[:], in_=ssh[:]), [ssh], [ssh])
                    P.op("dve", lambda e, kr=kr: e.tensor_tensor(
                        out=kr[:].rearrange("p (h k) -> p h k", k=64), in0=kr[:].rearrange("p (h k) -> p h k", k=64),
                        in1=ssh[:].unsqueeze(2).to_broadcast([128, 16, 64]), op=ALU.mult), [kr, ssh], [kr])
                    P.dma("sp", SKK.ap()[r0:r0 + 128, :], kr[:], reads=[kr], writes=[SKK])

    kab, rkb = T4[24], T4[25]
    for (jm, nm1, nm2, nm0, is_w) in ((3, "ra_w1", "ra_w2", "ra_w0", True), (4, "ra_a1", "ra_a2", "ra_a0", False)):
        l1, l2 = T4[0], T4[1]
        b0 = [T4[2], T4[3]]
        for z in range(2):
            wb1 = self.cast_w("lora1_%d" % z, I[nm1], I[nm1].ap()[j, z], D, 64)
            P.dma("sp", bf(l1)[:, 0:1024].rearrange("p (kc z l) -> p kc z l", z=2, l=64)[:, :, z, :],
                  wb1.ap().rearrange("(kc p) l -> p kc l", p=128), reads=[wb1], writes=[l1])
        P.dma("sp", l2[:], I[nm2].ap()[j].rearrange("z l n -> (z l) n"), reads=[I[nm2]], writes=[l2])
        for d in range(2):
            self.bcast_load(b0[d], b0[d][:], I[nm0], I[nm0].ap()[j, d])
        if not is_w:
            self.bcast_load(kab, kab[:], I["ra_k_a"], I["ra_k_a"].ap()[j])
            self.bcast_load(rkb, rkb[:], I["ra_r_k"], I["ra_r_k"].ap()[j].rearrange("h k -> (h k)"))
        for jb in range(self.nblk):
            xs = block_setup(jb, jm)
            pl = self.PS[4]
            for kc in range(8):
                xb, xv = xs[kc]
                P.op("pe", lambda e, kc=kc, xv=xv: e.matmul(
                    pl[:, 0:256], lhsT=bf(l1)[:, kc * 128:(kc + 1) * 128], rhs=xv, start=(kc == 0), stop=(kc == 7)),
                    [l1, xb], [pl], inc=(kc == 7))
            t1 = T4[4]
            P.op("act", lambda e: e.activation(out=t1[:, 0:256], in_=pl[:, 0:256],
                                               func=AF.Tanh if is_w else AF.Copy), [pl], [t1])
            for mt in range(2):
                r0 = jb * 256 + mt * 128
                if not is_w:
                    kt, kkt, rt = T4[5], T4[6], T4[7]
                    P.dma("sp", kt[:], SK.ap()[r0:r0 + 128, :], reads=[SK], writes=[kt])
                    P.dma("pool", kkt[:], SKK.ap()[r0:r0 + 128, :], reads=[SKK], writes=[kkt])
                    P.dma("sp", rt[:], SR.ap()[r0:r0 + 128, :], reads=[SR], writes=[rt])
                    kds = T4[26]
                for d in range(2):
                    ot = T4[20 + d]
                    for hf in range(2):
                        pb = self.PS[d * 2 + hf]
                        P.op("pe", lambda e, d=d, hf=hf, mt=mt, pb=pb: e.matmul(
                            pb[:, :], lhsT=t1[64 * d:64 * d + 64, mt * 128:(mt + 1) * 128],
                            rhs=l2[64 * d:64 * d + 64, hf * 512:(hf + 1) * 512], start=True, stop=True),
                            [t1, l2], [pb])
                        P.op("dve", lambda e, d=d, hf=hf, pb=pb, ot=ot: e.tensor_tensor(
                            out=ot[:, hf * 512:(hf + 1) * 512], in0=pb[:, :], in1=b0[d][:, hf * 512:(hf + 1) * 512],
                            op=ALU.add), [pb, b0[d]], [ot])
                    P.op("act", lambda e, ot=ot: e.activation(out=ot[:], in_=ot[:], func=AF.Sigmoid), [ot], [ot])
                    if is_w:
                        P.dma("pool", SSG[d].ap()[r0:r0 + 128, :], ot[:], reads=[ot], writes=[SSG[d]])
                    else:
                        kd, ba = T4[22 + d], T4[24 + 4 + d]
                        P.op("dve", lambda e, kd=kd, ot=ot: e.scalar_tensor_tensor(
                            out=kd[:], in0=ot[:], scalar=-1.0, in1=kab[:], op0=ALU.add, op1=ALU.mult), [ot, kab], [kd])
                        P.op("dve", lambda e, kd=kd: e.scalar_tensor_tensor(
                            out=kd[:], in0=kd[:], scalar=1.0, in1=kt[:], op0=ALU.add, op1=ALU.mult), [kd, kt], [kd])
                        P.op("pool", lambda e, ba=ba, ot=ot: e.tensor_tensor(out=ba[:], in0=ot[:], in1=kkt[:], op=ALU.mult),
                             [ot, kkt], [ba])
                        P.dma("sp", SKD[d].ap()[r0:r0 + 128, :], kd[:], reads=[kd], writes=[SKD[d]])
                        P.dma("pool", SBA[d].ap()[r0:r0 + 128, :], ba[:], reads=[ba], writes=[SBA[d]])
                if not is_w:
                    kd0, kd1 = T4[22], T4[23]
                    rks = self.small("rks_t", [128, 16])
                    P.op("dve", lambda e: e.tensor_tensor(out=kds[:], in0=kd0[:], in1=kd1[:], op=ALU.add), [kd0, kd1], [kds])
                    P.op("dve", lambda e: e.tensor_tensor(out=kds[:], in0=kds[:], in1=rt[:], op=ALU.mult), [kds, rt], [kds])
                    P.op("dve", lambda e: e.tensor_tensor(out=kds[:], in0=kds[:], in1=rkb[:], op=ALU.mult), [kds, rkb], [kds])
                    P.op("dve", lambda e: e.tensor_reduce(
                        out=rks[:], in_=kds[:].rearrange("p (h k) -> p h k", k=64), axis=AX.X, op=ALU.add), [kds], [rks])
                    P.dma("sp", SRKS.ap()[r0:r0 + 128, :], rks[:], reads=[rks], writes=[SRKS])

    g1a, g1b, g2a, g2b = T4[0], T4[1], T4[2], T4[3]
    wbg = self.cast_w("g1", I["ra_g1"], I["ra_g1"].ap()[j], D, 160)
    g1v = wbg.ap().rearrange("(kc p) l -> p kc l", p=128)
    P.dma("sp", bf(g1a)[:, 0:1024].rearrange("p (kc l) -> p kc l", l=128), g1v[:, :, 0:128], reads=[wbg], writes=[g1a])
    P.dma("sp", bf(g1b)[:, 0:256].rearrange("p (kc l) -> p kc l", l=32), g1v[:, :, 128:160], reads=[wbg], writes=[g1b])
    P.dma("sp", g2a[:], I["ra_g2"].ap()[j, 0:128, :], reads=[I["ra_g2"]], writes=[g2a])
    P.dma("sp", g2b[0:32, :], I["ra_g2"].ap()[j, 128:160, :], reads=[I["ra_g2"]], writes=[g2b])
    for jb in range(self.nblk):
        xs = block_setup(jb, 5)
        pa, pb2 = self.PS[4], self.PS[5]
        for kc in range(8):
            xb, xv = xs[kc]
            P.op("pe", lambda e, kc=kc, xv=xv: e.matmul(
                pa[:, 0:256], lhsT=bf(g1a)[:, kc * 128:(kc + 1) * 128], rhs=xv, start=(kc == 0), stop=(kc == 7)),
                [g1a, xb], [pa], inc=(kc == 7))
        for kc in range(8):
            xb, xv = xs[kc]
            P.op("pe", lambda e, kc=kc, xv=xv: e.matmul(
                pb2[0:32, 0:256], lhsT=bf(g1b)[:, kc * 32:(kc + 1) * 32], rhs=xv, start=(kc == 0), stop=(kc == 7)),
                [g1b, xb], [pb2], inc=(kc == 7))
        ta, tb = T4[4], T4[5]
        P.op("act", lambda e: e.activation(out=ta[:, 0:256], in_=pa[:, 0:256], func=AF.Sigmoid), [pa], [ta])
        P.op("act", lambda e: e.activation(out=tb[0:32, 0:256], in_=pb2[0:32, 0:256], func=AF.Sigmoid), [pb2], [tb])
        for mt in range(2):
            r0 = jb * 256 + mt * 128
            ot = T4[20 + mt]
            for hf in range(2):
                pb = self.PS[mt * 2 + hf]
                P.op("pe", lambda e, hf=hf, mt=mt, pb=pb: e.matmul(
                    pb[:, :], lhsT=ta[:, mt * 128:(mt + 1) * 128], rhs=g2a[:, hf * 512:(hf + 1) * 512],
                    start=True, stop=False), [ta, g2a], [pb], inc=False)
                P.op("pe", lambda e, hf=hf, mt=mt, pb=pb: e.matmul(
                    pb[:, :], lhsT=tb[0:32, mt * 128:(mt + 1) * 128], rhs=g2b[0:32, hf * 512:(hf + 1) * 512],
                    start=False, stop=True), [tb, g2b], [pb])
                P.op("act", lambda e, ot=ot, pb=pb, hf=hf: e.activation(
                    out=ot[:, hf * 512:(hf + 1) * 512], in_=pb[:, :], func=AF.Copy), [pb], [ot])
            P.dma("pool", SG_.ap()[r0:r0 + 128, :], ot[:], reads=[ot], writes=[SG_])


K.rw_xs = rw_xs
K.rw_xx = rw_xx
K.rwkv_pre = rwkv_pre


def rwkv_consts(self):
    if hasattr(self, "upC"):
        return
    P = self.P
    self.upC = P.sb("upC", [128, 128]); self.loC = P.sb("loC", [128, 128]); self.onesC = P.sb("onesC", [128, 128])
    for dst, src in ((self.upC, self.up), (self.loC, self.lo), (self.onesC, self.ones)):
        P.op("dve", lambda e, dst=dst, src=src: e.tensor_scalar(out=dst[:], in0=src[:], scalar1=-CDEC, scalar2=None,
                                                                 op0=ALU.mult), [src], [dst])
    self.MK1 = []; self.MK3 = []
    for d in range(2):
        strict_T, incl_T, strict_N = (self.ups, self.up, self.los) if d == 0 else (self.los, self.lo, self.ups)
        m1 = P.sb("mk1_%d" % d, [128, 512]); m3 = P.sb("mk3_%d" % d, [128, 512])
        for q in range(4):
            src = strict_T if q % 2 == 0 else incl_T
            P.op("dve", lambda e, m1=m1, q=q, src=src: e.tensor_copy(out=m1[:, q * 128:(q + 1) * 128], in_=src[:]), [src], [m1])
            P.op("dve", lambda e, m3=m3, q=q: e.tensor_copy(out=m3[:, q * 128:(q + 1) * 128], in_=strict_N[:]), [strict_N], [m3])
        self.MK1.append(m1); self.MK3.append(m3)


def rwkv_scan(self, d):
    P = self.P
    T4 = self.T4
    PS = self.PS
    SR, SV, SKK = self.scratch("r"), self.scratch("v"), self.scratch("kk")
    SSG, SKD, SBA = self.scratch("sg%d" % d), self.scratch("kd%d" % d), self.scratch("ba%d" % d)
    SY = self.scratch("y%d" % d)
    triC = self.upC if d == 0 else self.loC
    mk1, mk3 = self.MK1[d], self.MK3[d]
    ST = T4[29]
    stv = ST[:, 0:512].rearrange("p (q v) -> p q v", v=64)
    P.op("pool", lambda e: e.memset(ST[:], 0.0), [], [ST])
    order = [0, 1] + list(range(2, self.nt)) if d == 0 else [1, 0] + list(range(self.nt - 1, 1, -1))
    for c in order:
        r0 = c * 128
        Lr, Lv, Lkk, Lkd, Lba, Lsg = T4[0:6]
        for (dst, src, q) in ((Lsg, SSG, "sp"), (Lkk, SKK, "pool"), (Lba, SBA, "sp"), (Lkd, SKD, "pool"),
                              (Lr, SR, "sp"), (Lv, SV, "pool")):
            P.dma(q, dst[:], src.ap()[r0:r0 + 128, :], reads=[src], writes=[dst])
        for hf in range(2):
            P.op("pe", lambda e, hf=hf: e.matmul(PS[hf][:, :], lhsT=triC[:], rhs=Lsg[:, hf * 512:(hf + 1) * 512],
                                                 start=True, stop=True), [triC, Lsg], [PS[hf]])
            P.op("pe", lambda e, hf=hf: e.matmul(PS[2 + hf][:, :], lhsT=self.onesC[:], rhs=Lsg[:, hf * 512:(hf + 1) * 512],
                                                 start=True, stop=True), [self.onesC, Lsg], [PS[2 + hf]])
        gC = self.small("rw_gC", [128, 8])
        for p in range(8):
            P.op("pe", lambda e, p=p: e.matmul(PS[5][:, p:p + 1], lhsT=Lsg[:, p * 128:(p + 1) * 128],
                                               rhs=self.onesC[:, 0:1], start=True, stop=True),
                 [Lsg, self.onesC], [PS[5]], inc=(p == 7))
        P.op("act", lambda e: e.activation(out=gC[:], in_=PS[5][:, 0:8], func=AF.Exp), [PS[5]], [gC])
        lg, ex = T4[6], T4[7]
        Bt_, Kt_, Rt_, At_, Bh, Kh = T4[8:14]
        for hf in range(2):
            P.op("act", lambda e, hf=hf: e.activation(out=lg[:, hf * 512:(hf + 1) * 512], in_=PS[hf][:, :], func=AF.Copy),
                 [PS[hf]], [lg])
        P.op("act", lambda e: e.activation(out=ex[:], in_=lg[:], func=AF.Exp, scale=-1.0), [lg], [ex])
        P.op("dve", lambda e: e.tensor_tensor(out=Bt_[:], in0=Lba[:], in1=ex[:], op=ALU.mult), [Lba, ex], [Bt_])
        P.op("pool", lambda e: e.tensor_tensor(out=Kt_[:], in0=Lkd[:], in1=ex[:], op=ALU.mult), [Lkd, ex], [Kt_])
        P.op("act", lambda e: e.activation(out=ex[:], in_=lg[:], func=AF.Exp), [lg], [ex])
        P.op("dve", lambda e: e.tensor_tensor(out=Rt_[:], in0=Lr[:], in1=ex[:], op=ALU.mult), [Lr, ex], [Rt_])
        P.op("dve", lambda e: e.scalar_tensor_tensor(out=ex[:], in0=Lsg[:], scalar=CDEC, in1=lg[:], op0=ALU.mult,
                                                     op1=ALU.add), [Lsg, lg, Rt_], [ex])
        P.op("act", lambda e: e.activation(out=ex[:], in_=ex[:], func=AF.Exp), [ex], [ex])
        P.op("dve", lambda e: e.scalar_tensor_tensor(out=At_[:], in0=Lkk[:], scalar=-1.0, in1=ex[:], op0=ALU.mult,
                                                     op1=ALU.mult), [Lkk, ex], [At_])
        for hf in range(2):
            P.op("dve", lambda e, hf=hf: e.tensor_tensor(out=ex[:, hf * 512:(hf + 1) * 512], in0=PS[2 + hf][:, :],
                                                         in1=lg[:, hf * 512:(hf + 1) * 512], op=ALU.subtract),
                 [PS[2 + hf], lg, At_], [ex])
        P.op("act", lambda e: e.activation(out=ex[:], in_=ex[:], func=AF.Exp), [ex], [ex])
        P.op("dve", lambda e: e.tensor_tensor(out=Bh[:], in0=Lba[:], in1=ex[:], op=ALU.mult), [Lba, ex], [Bh])
        P.op("pool", lambda e: e.tensor_tensor(out=Kh[:], in0=Lkd[:], in1=ex[:], op=ALU.mult), [Lkd, ex], [Kh])
        AR0, AR1, BT, KT = T4[14], T4[15], T4[16], T4[17]
        bank = 4
        for (src, kind) in ((At_, 0), (Rt_, 1), (Bt_, 2), (Kt_, 3)):
            for hf in range(2):
                psa = PS[4 + (bank % 4)]; bank += 1
                if kind < 2:
                    art = AR0 if hf == 0 else AR1
                    dst = art
                    dap = art[:].rearrange("p (q a t) -> p q a t", a=2, t=128)[:, :, kind, :]
                else:
                    dst = BT if kind == 2 else KT
                    dap = dst[:, hf * 512:(hf + 1) * 512].rearrange("p (q t) -> p q t", t=128)
                for q in range(4):
                    P.op("pe", lambda e, q=q, psa=psa, src=src, hf=hf: e.transpose(
                        psa[:, q * 128:(q + 1) * 128], src[:, (hf * 4 + q) * 128:(hf * 4 + q + 1) * 128], self.ident[:]),
                        [src, self.ident], [psa], inc=(q == 3))
                eng = "act" if hf == 0 else "dve"
                if eng == "act":
                    P.op("act", lambda e, dap=dap, psa=psa: e.activation(
                        out=dap, in_=psa[:, :].rearrange("p (q t) -> p q t", t=128), func=AF.Copy), [psa], [dst])
                else:
                    P.op("dve", lambda e, dap=dap, psa=psa: e.tensor_copy(
                        out=dap, in_=psa[:, :].rearrange("p (q t) -> p q t", t=128)), [psa], [dst])

        def ar(p, a, rows):
            art = AR0 if p < 4 else AR1
            return art, art[:].rearrange("p (q a t) -> p q a t", a=2, t=128)[rows, p % 4, a, :]

        def ar2(p, rows):
            art = AR0 if p < 4 else AR1
            return art, art[rows, (p % 4) * 256:(p % 4 + 1) * 256]

        Yt = T4[18]
        for g in range(4):
            CH = [T4[19], T4[20]]
            MR, MA = T4[21], T4[22]
            chv = [t[:].rearrange("p (x h t) -> p x h t", x=2, t=128) for t in CH]
            mrv = MR[:].rearrange("p (x h t) -> p x h t", x=2, t=128)
            makv = MA[:, 0:512].rearrange("p (h t) -> p h t", t=128)
            Wt = MA[:, 512:768]
            for pp in range(2):
                p = g * 2 + pp
                pm1, pm2 = PS[pp], PS[2 + pp]
                for jj in range(2):
                    rows = slice(64 * jj, 64 * jj + 64)
                    ab, a2 = ar2(p, rows)
                    bt_ap = BT[rows, p * 128:(p + 1) * 128]
                    kt_ap = KT[rows, p * 128:(p + 1) * 128]
                    P.op("pe", lambda e, pm1=pm1, jj=jj, bt_ap=bt_ap, a2=a2: e.matmul(
                        pm1[:, jj * 256:(jj + 1) * 256], lhsT=bt_ap, rhs=a2, start=True, stop=True),
                        [BT, ab], [pm1], inc=(jj == 1))
                    P.op("pe", lambda e, pm2=pm2, jj=jj, kt_ap=kt_ap, a2=a2: e.matmul(
                        pm2[:, jj * 256:(jj + 1) * 256], lhsT=kt_ap, rhs=a2, start=True, stop=True),
                        [KT, ab], [pm2], inc=(jj == 1))
                    ab0, a0 = ar(p, 0, rows)
                    hh = pp * 2 + jj
                    P.op("pe", lambda e, hh=hh, a0=a0, bt_ap=bt_ap: e.matmul(
                        PS[4][:, hh * 128:(hh + 1) * 128], lhsT=a0, rhs=bt_ap, start=True, stop=True),
                        [ab0, BT], [PS[4]], inc=(hh == 3))
                pm1v = pm1[:, :].rearrange("p (h a t) -> p h a t", a=2, t=128)
                pm2v = pm2[:, :].rearrange("p (h a t) -> p h a t", a=2, t=128)
                mkv = mk1[:].rearrange("p (h a t) -> p h a t", a=2, t=128)
                hs = slice(pp * 2, pp * 2 + 2)
                P.op("dve", lambda e, pm1v=pm1v, hs=hs: e.tensor_tensor(
                    out=chv[0][:, 1, hs, :], in0=pm1v[:, :, 0, :], in1=mkv[:, :, 0, :], op=ALU.mult), [pm1, mk1], [CH[0]])
                P.op("dve", lambda e, pm1v=pm1v, hs=hs: e.tensor_tensor(
                    out=mrv[:, 0, hs, :], in0=pm1v[:, :, 1, :], in1=mkv[:, :, 1, :], op=ALU.mult), [pm1, mk1], [MR])
                P.op("dve", lambda e, pm2v=pm2v, hs=hs: e.tensor_tensor(
                    out=makv[:, hs, :], in0=pm2v[:, :, 0, :], in1=mkv[:, :, 0, :], op=ALU.mult), [pm2, mk1], [MA])
                P.op("dve", lambda e, pm2v=pm2v, hs=hs: e.tensor_tensor(
                    out=mrv[:, 1, hs, :], in0=pm2v[:, :, 1, :], in1=mkv[:, :, 1, :], op=ALU.mult), [pm2, mk1], [MR])
            P.op("dve", lambda e: e.tensor_tensor(out=chv[0][:, 0, :, :].rearrange("p h t -> p (h t)"),
                                                  in0=PS[4][:, :], in1=mk3[:], op=ALU.mult), [PS[4], mk3], [CH[0]])
            for hh in range(4):
                p = g * 2 + hh // 2
                rows = slice(64 * (hh % 2), 64 * (hh % 2) + 64)
                h = g * 4 + hh
                ab0, a0 = ar(p, 0, rows)
                P.op("pe", lambda e, hh=hh, a0=a0, rows=rows, p=p: e.matmul(
                    PS[7][:, hh * 64:(hh + 1) * 64], lhsT=a0, rhs=stv[rows, p, :], start=True, stop=False),
                    [ab0, ST], [PS[7]], inc=False)
                P.op("pe", lambda e, hh=hh, h=h: e.matmul(
                    PS[7][:, hh * 64:(hh + 1) * 64], lhsT=makv[:, hh, :], rhs=Lv[:, h * 64:(h + 1) * 64],
                    start=False, stop=True), [MA, Lv], [PS[7]], inc=(hh == 3))
            P.op("act", lambda e: e.activation(out=Wt, in_=PS[7][:, 0:256], func=AF.Copy), [PS[7]], [MA])
            for rd in range(7):
                cur, nxt = chv[rd % 2], chv[(rd + 1) % 2]
                cb, nb = CH[rd % 2], CH[(rd + 1) % 2]
                for hh in range(4):
                    P.op("pe", lambda e, hh=hh, cur=cur: e.matmul(
                        PS[7][:, hh * 64:(hh + 1) * 64], lhsT=cur[:, 1, hh, :], rhs=Wt[:, hh * 64:(hh + 1) * 64],
                        start=True, stop=True), [cb, MA], [PS[7]], inc=(hh == 3))
                P.op("dve", lambda e: e.tensor_tensor(out=Wt, in0=Wt, in1=PS[7][:, 0:256], op=ALU.add), [MA, PS[7]], [MA])
                if rd < 6:
                    for hh in range(4):
                        P.op("pe", lambda e, hh=hh, cur=cur: e.matmul(
                            PS[5][:, hh * 128:(hh + 1) * 128], lhsT=cur[:, 1, hh, :], rhs=cur[:, 0, hh, :],
                            start=True, stop=True), [cb], [PS[5]], inc=(hh == 3))
                    for hh in range(4):
                        P.op("pe", lambda e, hh=hh, cur=cur: e.matmul(
                            PS[6][:, hh * 128:(hh + 1) * 128], lhsT=cur[:, 0, hh, :], rhs=cur[:, 1, hh, :],
                            start=True, stop=True), [cb], [PS[6]], inc=(hh == 3))
                    P.op("act", lambda e, nxt=nxt: e.activation(
                        out=nxt[:, 0, :, :].rearrange("p h t -> p (h t)"), in_=PS[5][:, :], func=AF.Copy), [PS[5]], [nb])
                    P.op("dve", lambda e, nxt=nxt: e.tensor_copy(
                        out=nxt[:, 1, :, :].rearrange("p h t -> p (h t)"), in_=PS[6][:, :]), [PS[6]], [nb])
            for hh in range(4):
                p = g * 2 + hh // 2
                rows = slice(64 * (hh % 2), 64 * (hh % 2) + 64)
                h = g * 4 + hh
                ab1, a1 = ar(p, 1, rows)
                P.op("pe", lambda e, hh=hh, a1=a1, rows=rows, p=p: e.matmul(
                    PS[4][:, hh * 64:(hh + 1) * 64], lhsT=a1, rhs=stv[rows, p, :], start=True, stop=False),
                    [ab1, ST], [PS[4]], inc=False)
                P.op("pe", lambda e, hh=hh: e.matmul(
                    PS[4][:, hh * 64:(hh + 1) * 64], lhsT=mrv[:, 0, hh, :], rhs=Wt[:, hh * 64:(hh + 1) * 64],
                    start=False, stop=False), [MR, MA], [PS[4]], inc=False)
                P.op("pe", lambda e, hh=hh, h=h: e.matmul(
                    PS[4][:, hh * 64:(hh + 1) * 64], lhsT=mrv[:, 1, hh, :], rhs=Lv[:, h * 64:(h + 1) * 64],
                    start=False, stop=True), [MR, Lv], [PS[4]], inc=(hh == 3))
            P.op("act", lambda e, g=g: e.activation(out=Yt[:, g * 256:(g + 1) * 256], in_=PS[4][:, 0:256], func=AF.Copy),
                 [PS[4]], [Yt])
            for pp in range(2):
                p = g * 2 + pp
                P.op("pe", lambda e, pp=pp, p=p: e.matmul(
                    PS[0][:, pp * 128:(pp + 1) * 128], lhsT=Bh[:, p * 128:(p + 1) * 128],
                    rhs=Wt[:, pp * 128:(pp + 1) * 128], start=True, stop=False), [Bh, MA], [PS[0]], inc=False)
                P.op("pe", lambda e, pp=pp, p=p: e.matmul(
                    PS[0][:, pp * 128:(pp + 1) * 128], lhsT=Kh[:, p * 128:(p + 1) * 128],
                    rhs=Lv[:, p * 128:(p + 1) * 128], start=False, stop=True), [Kh, Lv], [PS[0]], inc=(pp == 1))
            for pp in range(2):
                p = g * 2 + pp
                for jj in range(2):
                    rows = slice(64 * jj, 64 * jj + 64)
                    P.op("dve", lambda e, rows=rows, p=p, pp=pp, jj=jj: e.scalar_tensor_tensor(
                        out=stv[rows, p, :], in0=stv[rows, p, :], scalar=gC[rows, p:p + 1],
                        in1=PS[0][rows, pp * 128 + jj * 64:pp * 128 + jj * 64 + 64], op0=ALU.mult, op1=ALU.add),
                        [ST, gC, PS[0]], [ST])
        P.dma("sp", SY.ap()[r0:r0 + 128, :], Yt[:], reads=[Yt], writes=[SY])


def rwkv_post(self, i, Xin, Xout):
    P = self.P
    j = i // 3
    I = self.I
    T4 = self.T4
    wres = T4[0:8]
    self.load_wres(I["ra_w_out"], I["ra_w_out"].ap()[j], wres)
    lng, lnb = T4[8], T4[9]
    self.bcast_load(lng, lng[:], I["ra_lnx_g"], I["ra_lnx_g"].ap()[j])
    self.bcast_load(lnb, lnb[:], I["ra_lnx_b"], I["ra_lnx_b"].ap()[j])
    SY0, SY1, SV, SG_, SRKS = (self.scratch("y0"), self.scratch("y1"), self.scratch("v"), self.scratch("g"),
                               self.scratch("rks", 16))
    t_start = 2 if i == DEPTH - 1 else 0
    for t in range(t_start, self.nt):
        par = t % 2
        r0 = t * 128
        y, y1, vt, gt_, sq = T4[10 + par * 5:15 + par * 5]
        rks = self.small("post_rks%d" % par, [128, 16])
        mu = self.small("post_mu%d" % par, [128, 16])
        var = self.small("post_var%d" % par, [128, 16])
        P.dma("sp", y[:], SY0.ap()[r0:r0 + 128, :], reads=[SY0], writes=[y])
        P.dma("pool", y1[:], SY1.ap()[r0:r0 + 128, :], reads=[SY1], writes=[y1])
        P.dma("sp", vt[:], SV.ap()[r0:r0 + 128, :], reads=[SV], writes=[vt])
        P.dma("pool", gt_[:], SG_.ap()[r0:r0 + 128, :], reads=[SG_], writes=[gt_])
        P.dma("sp", rks[:], SRKS.ap()[r0:r0 + 128, :], reads=[SRKS], writes=[rks])
        v3 = lambda tl: tl[:].rearrange("p (h k) -> p h k", k=64)
        bc = lambda sm_: sm_[:].unsqueeze(2).to_broadcast([128, 16, 64])
        P.op("dve", lambda e: e.tensor_tensor(out=y[:], in0=y[:], in1=y1[:], op=ALU.add), [y, y1], [y])
        P.op("dve", lambda e: e.tensor_reduce(out=mu[:], in_=v3(y), axis=AX.X, op=ALU.add), [y], [mu])
        P.op("dve", lambda e: e.tensor_scalar(out=mu[:], in0=mu[:], scalar1=1.0 / 64, scalar2=None, op0=ALU.mult), [mu], [mu])
        P.op("dve", lambda e: e.tensor_tensor(out=v3(y), in0=v3(y), in1=bc(mu), op=ALU.subtract), [y, mu], [y])
        P.op("act", lambda e: e.activation(out=sq[:], in_=y[:], func=AF.Square), [y], [sq])
        P.op("dve", lambda e: e.tensor_reduce(out=var[:], in_=v3(sq), axis=AX.X, op=ALU.add), [sq], [var])
        P.op("dve", lambda e: e.tensor_scalar(out=var[:], in0=var[:], scalar1=1.0 / 64, scalar2=6.4e-4, op0=ALU.mult,
                                              op1=ALU.add), [var], [var])
        P.op("act", lambda e: e.activation(out=var[:], in_=var[:], func=AF.Sqrt), [var], [var])
        P.op("dve", lambda e: e.reciprocal(out=var[:], in_=var[:]), [var], [var])
        P.op("dve", lambda e: e.tensor_tensor(out=v3(y), in0=v3(y), in1=bc(var), op=ALU.mult), [y, var], [y])
        P.op("pool", lambda e: e.tensor_tensor(out=y[:], in0=y[:], in1=lng[:], op=ALU.mult), [y, lng], [y])
        P.op("pool", lambda e: e.tensor_tensor(out=y[:], in0=y[:], in1=lnb[:], op=ALU.add), [y, lnb], [y])
        P.op("dve", lambda e: e.tensor_tensor(out=v3(vt), in0=v3(vt), in1=bc(rks), op=ALU.mult), [vt, rks], [vt])
        P.op("dve", lambda e: e.tensor_tensor(out=y[:], in0=y[:], in1=vt[:], op=ALU.add), [y, vt], [y])
        P.op("dve", lambda e: e.tensor_tensor(out=y[:], in0=y[:], in1=gt_[:], op=ALU.mult), [y, gt_], [y])
        zT = T4[20 + par]
        for hf in range(2):
            self.transpose_to(y, [y[:, (hf * 4 + q) * 128:(hf * 4 + q + 1) * 128] for q in range(4)],
                              zT, [(bf(zT)[:, hf * 512:(hf + 1) * 512], 0, 512)], self.PS[par * 2 + hf],
                              evac="act" if hf == 0 else "dve")
        self.proj_residual([(zT, bf(zT)[:, kc * 128:(kc + 1) * 128]) for kc in range(8)], wres, Xin, Xout, t, par)


def rwkv(self, i, Xin, Xout):
    ph = self.cfg.get("ra_phases", "pre,scan0,scan1,post")
    self.rwkv_consts()
    if "pre" in ph:
        self.rwkv_pre(i, Xin)
    for d in range(2):
        if "scan%d" % d in ph:
            self.rwkv_scan(d)
    if "post" in ph:
        self.rwkv_post(i, Xin, Xout)


K.rwkv_consts = rwkv_consts
K.rwkv_scan = rwkv_scan
K.rwkv_post = rwkv_post
K.rwkv = rwkv


def tm_pass(self, wbuf, wap2d, col0, ncols, bias_buf, bias_ap, dst, post=None, out_bf=False):
    P = self.P
    T4 = self.T4
    wres = T4[0:8]
    self.load_wres(wbuf, wap2d[:, col0:col0 + ncols], wres, ncols)
    bt = None
    if bias_buf is not None:
        bt = T4[24]
        self.bcast_load(bt, bt[:, 0:ncols], bias_buf, bias_ap)
    nb = (ncols + 511) // 512
    for jb in range(self.nblk):
        hv = self.load_hTe(jb, T4[8 + (jb % 2) * 3:11 + (jb % 2) * 3])
        for mt in range(2):
            ot = T4[20 + mt]
            for b in range(nb):
                w = min(512, ncols - b * 512)
                pb = self.PS[(mt * 2 + b) % 4]
                for kc in range(8):
                    hb, ha = hv(kc)
                    P.op("pe", lambda e, kc=kc, ha=ha, pb=pb, b=b, w=w, mt=mt: e.matmul(
                        pb[:, 0:w], lhsT=ha[:, 1 + mt * 128:1 + (mt + 1) * 128], rhs=bf(wres[kc])[:, b * 512:b * 512 + w],
                        start=(kc == 0), stop=(kc == 7)), [hb, wres[kc]], [pb], inc=(kc == 7))
                if bt is not None:
                    P.op("dve", lambda e, ot=ot, pb=pb, b=b, w=w: e.tensor_tensor(
                        out=ot[:, b * 512:b * 512 + w], in0=pb[:, 0:w], in1=bt[:, b * 512:b * 512 + w], op=ALU.add),
                        [pb, bt], [ot])
                else:
                    oap = bf(ot)[:, b * 512:b * 512 + w] if out_bf else ot[:, b * 512:b * 512 + w]
                    P.op("act", lambda e, oap=oap, pb=pb, w=w: e.activation(
                        out=oap, in_=pb[:, 0:w], func=AF.Copy), [pb], [ot])
            if post is not None:
                post(ot, jb, mt)
            r0 = jb * 256 + mt * 128
            if dst is not None:
                src = bf(ot)[:, 0:ncols] if out_bf else ot[:, 0:ncols]
                P.dma("pool", dst.ap()[r0:r0 + 128, 0:ncols], src, reads=[ot], writes=[dst])


def fm_pass(self, wbuf, wap2d, col0, nchunks, bias_cols, scales, dsts):
    P = self.P
    T4 = self.T4
    wres = T4[0:8]
    ncols = nchunks * 128
    self.load_wres(wbuf, wap2d[:, col0:col0 + ncols], wres, ncols)
    for jb in range(self.nblk):
        hv = self.load_hTe(jb, T4[8 + (jb % 2) * 3:11 + (jb % 2) * 3])
        for hc in range(nchunks):
            pb = self.PS[hc % 4]
            for kc in range(8):
                hb, ha = hv(kc)
                P.op("pe", lambda e, kc=kc, ha=ha, pb=pb, hc=hc: e.matmul(
                    pb[:, 0:256], lhsT=bf(wres[kc])[:, hc * 128:(hc + 1) * 128], rhs=ha[:, 1:257],
                    start=(kc == 0), stop=(kc == 7)), [hb, wres[kc]], [pb], inc=(kc == 7))
            ot = T4[20 + hc % 4]
            if bias_cols is not None:
                P.op("act", lambda e, ot=ot, pb=pb, hc=hc: e.activation(
                    out=ot[:, 0:256], in_=pb[:, 0:256], func=AF.Identity, scale=scales[hc],
                    bias=bias_cols[:, hc:hc + 1]), [pb, bias_cols], [ot])
            else:
                P.op("act", lambda e, ot=ot, pb=pb, hc=hc: e.activation(
                    out=ot[:, 0:256], in_=pb[:, 0:256], func=AF.Copy), [pb], [ot])
            dbuf, row0 = dsts[hc]
            P.dma("pool", dbuf.ap()[row0:row0 + 128, jb * 256:(jb + 1) * 256], ot[:, 0:256], reads=[ot], writes=[dbuf])


K.tm_pass = tm_pass
K.fm_pass = fm_pass


def fscratch(self, name, rows, dt=F32):
    if name not in self.S:
        self.S[name] = self.P.dram("S_" + name, [rows, self.ntok], dt)
    return self.S[name]


def mlstm_pre(self, i, Xin):
    P = self.P
    I = self.I
    self.norm_to_HT(Xin, 1, 0)
    w = I["ml_w_in"].ap()[0]
    b = I["ml_b_in"]
    QT, KT = self.fscratch("mlqT", 512), self.fscratch("mlkT", 512)
    MLA, MLB, MLC = self.scratch("mla"), self.scratch("mlb"), self.scratch("mlc")
    bc = self.small("ml_bc", [128, 8])
    self.load_cols(bc, bc[:], b, b.ap()[0, 0:1024].rearrange("(hc p) -> hc p", p=128), 8)
    sk = 128 ** -0.5
    P.op("dve", lambda e: e.tensor_scalar(out=bc[:, 4:8], in0=bc[:, 4:8], scalar1=sk, scalar2=None, op0=ALU.mult), [bc], [bc])
    self.fm_pass(I["ml_w_in"], w, 0, 8, bc, [1.0] * 4 + [sk] * 4,
                 [(QT, h * 128) for h in range(4)] + [(KT, h * 128) for h in range(4)])

    def scale_k(ot, jb, mt):
        P.op("dve", lambda e: e.tensor_scalar(out=ot[:, 0:512], in0=ot[:, 0:512], scalar1=sk, scalar2=None, op0=ALU.mult),
             [ot], [ot])
    self.tm_pass(I["ml_w_in"], w, 512, 1024, b, b.ap()[0, 512:1536], MLA, post=scale_k)
    self.tm_pass(I["ml_w_in"], w, 1536, 1024, b, b.ap()[0, 1536:2560], MLB)
    self.tm_pass(I["ml_w_in"], w, 2560, 528, b, b.ap()[0, 2560:3088], MLC)


def mlstm_scan(self, d):
    P = self.P
    T4 = self.T4
    PS = self.PS
    QT, KT = self.fscratch("mlqT", 512), self.fscratch("mlkT", 512)
    MLA, MLB, MLC = self.scratch("mla"), self.scratch("mlb"), self.scratch("mlc")
    HS = self.scratch("mlh%d" % d)
    tri = self.up if d == 0 else self.lo
    ecol_i = 127 if d == 0 else 0
    CN = [T4[28], T4[29]]
    V1 = [T4[26], T4[27]]
    cnv = [t[:, 0:514].rearrange("p (h v) -> p h v", v=257) for t in CN]
    v1v = [t[:, 0:514].rearrange("p (h v) -> p h v", v=257) for t in V1]
    for t in CN:
        P.op("pool", lambda e, t=t: e.memset(t[:], 0.0), [], [t])
    for t in V1:
        P.op("pool", lambda e, t=t: e.memset(t[:], 1.0), [], [t])
    order = [0, 1] + list(range(2, self.nt)) if d == 0 else [1, 0] + list(range(self.nt - 1, 1, -1))
    for c in order:
        r0 = c * 128
        qk = T4[0]
        qkv = qk[:].rearrange("p (x h t) -> p x h t", x=2, t=128)
        la, lb = T4[1], T4[2]
        gts = self.small("ml_g", [128, 16])
        lfa = self.small("ml_lf", [128, 16])
        P.dma("sp", qkv[:, 0, :, :], QT.ap().rearrange("(h p) c -> p h c", p=128)[:, :, r0:r0 + 128], reads=[QT], writes=[qk])
        P.dma("pool", qkv[:, 1, :, :], KT.ap().rearrange("(h p) c -> p h c", p=128)[:, :, r0:r0 + 128], reads=[KT], writes=[qk])
        P.dma("sp", la[:], MLA.ap()[r0:r0 + 128, :], reads=[MLA], writes=[la])
        P.dma("pool", lb[:, 0:512], MLB.ap()[r0:r0 + 128, 0:512], reads=[MLB], writes=[lb])
        P.dma("sp", gts[:], MLC.ap()[r0:r0 + 128, 512:528], reads=[MLC], writes=[gts])
        P.op("act", lambda e: e.activation(out=lfa[:], in_=gts[:], func=AF.Sigmoid), [gts], [lfa])
        P.op("act", lambda e: e.activation(out=lfa[:], in_=lfa[:], func=AF.Ln), [lfa], [lfa])
        P.op("dve", lambda e: e.tensor_copy(out=v1v[0][:, :, 0:256], in_=la[:, 512:1024].rearrange("p (h v) -> p h v", v=256)),
             [la], [V1[0]])
        P.op("dve", lambda e: e.tensor_copy(out=v1v[1][:, :, 0:256], in_=lb[:, 0:512].rearrange("p (h v) -> p h v", v=256)),
             [lb], [V1[1]])
        hout = T4[3]

        def head_gen(h):
            par = h % 2
            lfcol = lfa[:, d * 8 + 4 + h:d * 8 + 5 + h]
            licol = gts[:, d * 8 + h:d * 8 + h + 1]
            LFB, Dm, EB, PT, QS, KH = T4[4 + par * 6:10 + par * 6]
            sm = self.small("ml_sm%d" % par, [128, 4])
            psb, psq, pso, pss = PS[par * 4:par * 4 + 4]
            P.op("dve", lambda e: e.tensor_scalar(out=LFB[:, 0:128], in0=self.ones[:], scalar1=lfcol, scalar2=None,
                                                  op0=ALU.mult), [self.ones, lfa], [LFB])
            P.op("pe", lambda e: e.matmul(psb[:, 0:128], lhsT=LFB[:, 0:128], rhs=tri[:], start=True, stop=True),
                 [LFB, tri], [psb], inc=False)
            P.op("pe", lambda e: e.matmul(psb[:, 128:129], lhsT=tri[:], rhs=lfcol, start=True, stop=True),
                 [tri, lfa], [psb])
            yield
            P.op("dve", lambda e: e.tensor_tensor(out=sm[:, 0:1], in0=licol, in1=psb[:, 128:129], op=ALU.subtract),
                 [gts, psb], [sm])
            P.op("act", lambda e: e.activation(out=Dm[:, 0:128], in_=psb[:, 0:128], func=AF.Exp, bias=sm[:, 0:1]),
                 [psb, sm], [Dm])
            yield
            P.op("dve", lambda e: e.tensor_tensor(out=Dm[:, 0:128], in0=Dm[:, 0:128], in1=tri[:], op=ALU.mult), [Dm, tri], [Dm])
            P.op("act", lambda e: e.activation(out=EB[:, 0:128], in_=psb[:, 0:128], func=AF.Exp), [psb], [EB])
            P.op("pe", lambda e: e.matmul(psq[:, 0:128], lhsT=qkv[:, 1, h, :], rhs=qkv[:, 0, h, :], start=True, stop=True),
                 [qk], [psq])
            yield
            P.op("dve", lambda e: e.tensor_tensor(out=PT[:, 0:128], in0=Dm[:, 0:128], in1=psq[:, 0:128], op=ALU.mult),
                 [Dm, psq], [PT])
            P.op("pool", lambda e: e.tensor_tensor(out=QS[:, 0:128], in0=qkv[:, 0, h, :], in1=EB[:, 0:128], op=ALU.mult),
                 [qk, EB], [QS])
            yield
            vv = v1v[h // 2][:, h % 2, :]
            cv = cnv[h // 2][:, h % 2, :]
            P.op("pe", lambda e: e.matmul(pso[:, 0:257], lhsT=PT[:, 0:128], rhs=vv, start=True, stop=False),
                 [PT, V1[h // 2]], [pso], inc=False)
            P.op("pe", lambda e: e.matmul(pso[:, 0:257], lhsT=QS[:, 0:128], rhs=cv, start=False, stop=True),
                 [QS, CN[h // 2]], [pso])
            yield
            P.op("dve", lambda e: e.tensor_copy(out=sm[:, 3:4], in_=pso[:, 256:257]), [pso], [sm])
            P.op("dve", lambda e: e.scalar_tensor_tensor(out=sm[:, 1:2], in0=sm[:, 3:4], scalar=-1.0,
                                                         in1=sm[:, 3:4], op0=ALU.mult, op1=ALU.max), [sm], [sm])
            P.op("dve", lambda e: e.tensor_scalar_max(out=sm[:, 1:2], in0=sm[:, 1:2], scalar1=1.0), [sm], [sm])
            P.op("dve", lambda e: e.reciprocal(out=sm[:, 1:2], in_=sm[:, 1:2]), [sm], [sm])
            P.op("dve", lambda e: e.tensor_scalar(out=hout[:, h * 256:(h + 1) * 256], in0=pso[:, 0:256], scalar1=sm[:, 1:2],
                                                  scalar2=None, op0=ALU.mult), [pso, sm], [hout])
            yield
            P.op("act", lambda e: e.activation(out=sm[:, 2:3], in_=psb[:, ecol_i:ecol_i + 1], func=AF.Exp, bias=sm[:, 0:1]),
                 [psb, sm], [sm])
            P.op("dve", lambda e: e.tensor_scalar(out=KH[:, 0:128], in0=la[:, h * 128:(h + 1) * 128], scalar1=sm[:, 2:3],
                                                  scalar2=None, op0=ALU.mult), [la, sm], [KH])
            P.op("pe", lambda e: e.matmul(pss[:, 0:257], lhsT=KH[:, 0:128], rhs=vv, start=True, stop=True),
                 [KH, V1[h // 2]], [pss])
            yield
            P.op("dve", lambda e: e.scalar_tensor_tensor(out=cv, in0=cv, scalar=EB[:, ecol_i:ecol_i + 1], in1=pss[:, 0:257],
                                                         op0=ALU.mult, op1=ALU.add), [CN[h // 2], EB, pss], [CN[h // 2]])
        interleave([head_gen(0), head_gen(1)])
        interleave([head_gen(2), head_gen(3)])
        P.dma("sp", HS.ap()[r0:r0 + 128, :], hout[:], reads=[hout], writes=[HS])


def mlstm_post(self, i, Xin, Xout):
    P = self.P
    I = self.I
    T4 = self.T4
    wres = T4[0:8]
    self.load_wres(I["ml_w_out"], I["ml_w_out"].ap()[0], wres)
    ng = T4[8]
    self.bcast_load(ng, ng[:], I["ml_norm_g"], I["ml_norm_g"].ap()[0])
    HS0, HS1, MLB, MLC = self.scratch("mlh0"), self.scratch("mlh1"), self.scratch("mlb"), self.scratch("mlc")
    t_start = 2 if i == DEPTH - 1 else 0
    for t in range(t_start, self.nt):
        par = t % 2
        r0 = t * 128
        y, y1, ot, sq = T4[10 + par * 4:14 + par * 4]
        ss = self.small("mlp_ss%d" % par, [128, 4])
        P.dma("sp", y[:], HS0.ap()[r0:r0 + 128, :], reads=[HS0], writes=[y])
        P.dma("pool", y1[:], HS1.ap()[r0:r0 + 128, :], reads=[HS1], writes=[y1])
        P.dma("sp", ot[:, 0:512], MLB.ap()[r0:r0 + 128, 512:1024], reads=[MLB], writes=[ot])
        P.dma("pool", ot[:, 512:1024], MLC.ap()[r0:r0 + 128, 0:512], reads=[MLC], writes=[ot])
        v3 = lambda tl: tl[:].rearrange("p (h k) -> p h k", k=256)
        P.op("dve", lambda e: e.tensor_tensor(out=y[:], in0=y[:], in1=y1[:], op=ALU.add), [y, y1], [y])
        P.op("act", lambda e: e.activation(out=sq[:], in_=y[:], func=AF.Square), [y], [sq])
        P.op("dve", lambda e: e.tensor_reduce(out=ss[:], in_=v3(sq), axis=AX.X, op=ALU.add), [sq], [ss])
        P.op("dve", lambda e: e.tensor_scalar(out=ss[:], in0=ss[:], scalar1=1.0 / 256, scalar2=EPS, op0=ALU.mult,
                                              op1=ALU.add), [ss], [ss])
        P.op("act", lambda e: e.activation(out=ss[:], in_=ss[:], func=AF.Sqrt), [ss], [ss])
        P.op("dve", lambda e: e.reciprocal(out=ss[:], in_=ss[:]), [ss], [ss])
        P.op("dve", lambda e: e.tensor_tensor(out=v3(y), in0=v3(y), in1=ss[:].unsqueeze(2).to_broadcast([128, 4, 256]),
                                              op=ALU.mult), [y, ss], [y])
        P.op("act", lambda e: e.activation(out=ot[:], in_=ot[:], func=AF.Sigmoid), [ot], [ot])
        P.op("pool", lambda e: e.tensor_tensor(out=y[:], in0=y[:], in1=ng[:], op=ALU.mult), [y, ng], [y])
        P.op("dve", lambda e: e.tensor_tensor(out=y[:], in0=y[:], in1=ot[:], op=ALU.mult), [y, ot], [y])
        zT = T4[20 + par]
        for hf in range(2):
            self.transpose_to(y, [y[:, (hf * 4 + q) * 128:(hf * 4 + q + 1) * 128] for q in range(4)],
                              zT, [(bf(zT)[:, hf * 512:(hf + 1) * 512], 0, 512)], self.PS[par * 2 + hf],
                              evac="act" if hf == 0 else "dve")
        self.proj_residual([(zT, bf(zT)[:, kc * 128:(kc + 1) * 128]) for kc in range(8)], wres, Xin, Xout, t, par)


def mlstm(self, i, Xin, Xout):
    self.mlstm_pre(i, Xin)
    for d in range(2):
        self.mlstm_scan(d)
    self.mlstm_post(i, Xin, Xout)


K.fscratch = fscratch
K.mlstm_pre = mlstm_pre
K.mlstm_scan = mlstm_scan
K.mlstm_post = mlstm_post
K.mlstm = mlstm


I32 = mybir.dt.int32


def da_rope_tables(self):
    P = self.P
    T4 = self.T4
    nlt = self.seq // 128
    pi_ = self.small("rp_pi", [128, 1], I32); ti = self.small("rp_ti", [128, 1], I32)
    colv = self.small("rp_col", [128, 1]); rowp = self.small("rp_row", [128, 1])
    ji = self.small("rp_ji", [128, 16], I32); inv = self.small("rp_inv", [128, 16]); angc = self.small("rp_angc", [128, 16])
    taui = self.small("rp_taui", [128, 64], I32); rowf = self.small("rp_rowf", [128, 64])
    P.op("pool", lambda e: e.iota(pi_[:], [[0, 1]], base=0, channel_multiplier=1), [], [pi_])
    P.op("pool", lambda e: e.iota(ji[:], [[1, 16]], base=0, channel_multiplier=0), [], [ji])
    P.op("pool", lambda e: e.iota(taui[:], [[2, 64]], base=0, channel_multiplier=0), [], [taui])
    P.op("dve", lambda e: e.tensor_single_scalar(out=ti[:], in_=pi_[:], scalar=63, op=ALU.bitwise_and), [pi_], [ti])
    P.op("dve", lambda e: e.tensor_copy(out=colv[:], in_=ti[:]), [ti], [colv])
    P.op("dve", lambda e: e.tensor_single_scalar(out=ti[:], in_=pi_[:], scalar=6, op=ALU.arith_shift_right), [pi_, colv], [ti])
    P.op("dve", lambda e: e.tensor_copy(out=rowp[:], in_=ti[:]), [ti], [rowp])
    P.op("dve", lambda e: e.tensor_copy(out=inv[:], in_=ji[:]), [ji], [inv])
    P.op("act", lambda e: e.activation(out=inv[:], in_=inv[:], func=AF.Exp, scale=-math.log(10000.0) / 16), [inv], [inv])
    P.op("dve", lambda e: e.tensor_copy(out=rowf[:], in_=taui[:]), [taui], [rowf])
    P.op("dve", lambda e: e.tensor_scalar(out=rowf[:], in0=rowf[:], scalar1=rowp[:, 0:1], scalar2=None, op0=ALU.add),
         [rowf, rowp], [rowf])
    P.op("dve", lambda e: e.tensor_scalar(out=angc[:], in0=inv[:], scalar1=colv[:, 0:1], scalar2=None, op0=ALU.mult),
         [inv, colv], [angc])
    self.COS = [T4[25], T4[26]]
    self.SIN = [T4[27], T4[28]]
    ang, kf, tmp = T4[14], T4[15], T4[16]
    ki = self.P.sb("rp_ki", [128, 1024], I32)
    for half in range(2):
        n = min(32, nlt - half * 32)
        if n <= 0:
            break
        w = n * 32
        av = ang[:, 0:w].rearrange("p (t j) -> p t j", j=32)
        for tt in range(n):
            P.op("dve", lambda e, tt=tt: e.tensor_scalar(
                out=av[:, tt, 0:16], in0=inv[:], scalar1=rowf[:, half * 32 + tt:half * 32 + tt + 1], scalar2=None,
                op0=ALU.mult), [inv, rowf], [ang])
        P.op("dve", lambda e: e.tensor_copy(out=av[:, :, 16:32], in_=angc[:].unsqueeze(1).to_broadcast([128, n, 16])),
             [angc], [ang])
        for (dst, shift) in ((self.SIN[half], 0.0), (self.COS[half], math.pi / 2)):
            P.op("dve", lambda e: e.tensor_scalar(out=tmp[:, 0:w], in0=ang[:, 0:w], scalar1=shift, scalar2=None, op0=ALU.add),
                 [ang], [tmp])
            P.op("dve", lambda e: e.tensor_scalar(out=kf[:, 0:w], in0=tmp[:, 0:w], scalar1=1.0 / (2 * math.pi), scalar2=None,
                                                  op0=ALU.mult), [tmp], [kf])
            P.op("dve", lambda e: e.tensor_copy(out=ki[:, 0:w], in_=kf[:, 0:w]), [kf], [ki])
            P.op("dve", lambda e: e.tensor_copy(out=kf[:, 0:w], in_=ki[:, 0:w]), [ki], [kf])
            P.op("dve", lambda e: e.scalar_tensor_tensor(out=tmp[:, 0:w], in0=kf[:, 0:w], scalar=-2 * math.pi,
                                                         in1=tmp[:, 0:w], op0=ALU.mult, op1=ALU.add), [kf, tmp], [tmp])
            P.op("dve", lambda e: e.tensor_scalar(out=kf[:, 0:w], in0=tmp[:, 0:w], scalar1=math.pi, scalar2=None,
                                                  op0=ALU.is_gt), [tmp], [kf])
            P.op("dve", lambda e: e.scalar_tensor_tensor(out=tmp[:, 0:w], in0=kf[:, 0:w], scalar=-2 * math.pi,
                                                         in1=tmp[:, 0:w], op0=ALU.mult, op1=ALU.add), [kf, tmp], [tmp])
            P.op("act", lambda e, dst=dst: e.activation(out=dst[:, 0:w], in_=tmp[:, 0:w], func=AF.Sin), [tmp], [dst])


def da_pre(self, i, Xin):
    P = self.P
    I = self.I
    T4 = self.T4
    self.norm_to_HT(Xin, 1, 0)
    self.da_rope_tables()
    w = I["da_w_qkv"].ap()[0]
    QT, KT = self.fscratch("daqT", 1024, BF16), self.fscratch("dakT", 1024, BF16)
    V = self.scratch("dav", D, BF16)

    def mk_post(dstT):
        def post(ot, jb, mt):
            tau = jb * 2 + mt - 2
            src = ot
            if tau >= 0:
                t1, t2, rt = T4[14], T4[15], T4[16]
                ov = ot[:].rearrange("p (h x j) -> p h x j", x=2, j=32)
                rv = rt[:].rearrange("p (h x j) -> p h x j", x=2, j=32)
                t1v = t1[:, 0:512].rearrange("p (h j) -> p h j", j=32)
                t2v = t2[:, 0:512].rearrange("p (h j) -> p h j", j=32)
                cs = self.COS[tau // 32][:, (tau % 32) * 32:(tau % 32 + 1) * 32].unsqueeze(1).to_broadcast([128, 16, 32])
                sn = self.SIN[tau // 32][:, (tau % 32) * 32:(tau % 32 + 1) * 32].unsqueeze(1).to_broadcast([128, 16, 32])
                cb, sb_ = self.COS[tau // 32], self.SIN[tau // 32]
                P.op("dve", lambda e: e.tensor_tensor(out=t1v, in0=ov[:, :, 0, :], in1=cs, op=ALU.mult), [ot, cb], [t1])
                P.op("pool", lambda e: e.tensor_tensor(out=t2v, in0=ov[:, :, 1, :], in1=sn, op=ALU.mult), [ot, sb_], [t2])
                P.op("dve", lambda e: e.tensor_tensor(out=rv[:, :, 0, :], in0=t1v, in1=t2v, op=ALU.subtract), [t1, t2], [rt])
                P.op("dve", lambda e: e.tensor_tensor(out=t1v, in0=ov[:, :, 0, :], in1=sn, op=ALU.mult), [ot, sb_, rt], [t1])
                P.op("pool", lambda e: e.tensor_tensor(out=t2v, in0=ov[:, :, 1, :], in1=cs, op=ALU.mult), [ot, cb, rt], [t2])
                P.op("dve", lambda e: e.tensor_tensor(out=rv[:, :, 1, :], in0=t1v, in1=t2v, op=ALU.add), [t1, t2], [rt])
                src = rt
            zT = T4[17 + mt]
            for hf in range(2):
                self.transpose_to(src, [src[:, (hf * 4 + q) * 128:(hf * 4 + q + 1) * 128] for q in range(4)],
                                  zT, [(bf(zT)[:, hf * 512:(hf + 1) * 512], 0, 512)], self.PS[4 + mt * 2 + hf],
                                  evac="act" if hf == 0 else "dve")
            r0 = jb * 256 + mt * 128
            P.dma("sp", dstT.ap().rearrange("(h p) c -> p h c", p=128)[:, :, r0:r0 + 128],
                  bf(zT)[:, 0:1024].rearrange("p (h c) -> p h c", c=128), reads=[zT], writes=[dstT])
        return post
    self.tm_pass(I["da_w_qkv"], w, 0, 1024, None, None, None, post=mk_post(QT))
    self.tm_pass(I["da_w_qkv"], w, 1024, 1024, None, None, None, post=mk_post(KT))
    self.tm_pass(I["da_w_qkv"], w, 2048, 1024, None, None, V, out_bf=True)


def da_attn(self, i):
    P = self.P
    I = self.I
    T4 = self.T4
    PS = self.PS
    QT, KT, OT = self.fscratch("daqT", 1024, BF16), self.fscratch("dakT", 1024, BF16), self.fscratch("daoT", 1024, BF16)
    V = self.scratch("dav", D, BF16)
    lambda_init = 0.8 - 0.6 * math.exp(-0.3 * i)
    lt = T4[29]
    self.bcast_load(lt, lt[:, 0:256], I["da_lambda"], I["da_lambda"].ap()[0].rearrange("a k -> (a k)"))
    lam = self.small("da_lam", [128, 4])
    P.op("dve", lambda e: e.tensor_tensor(out=lt[:, 256:320], in0=lt[:, 0:64], in1=lt[:, 64:128], op=ALU.mult), [lt], [lt])
    P.op("dve", lambda e: e.tensor_tensor(out=lt[:, 320:384], in0=lt[:, 128:192], in1=lt[:, 192:256], op=ALU.mult), [lt], [lt])
    P.op("dve", lambda e: e.tensor_reduce(out=lam[:, 0:2], in_=lt[:, 256:384].rearrange("p (a k) -> p a k", k=64),
                                          axis=AX.X, op=ALU.add), [lt], [lam])
    P.op("act", lambda e: e.activation(out=lam[:, 0:2], in_=lam[:, 0:2], func=AF.Exp), [lam], [lam])
    P.op("dve", lambda e: e.tensor_tensor(out=lam[:, 2:3], in0=lam[:, 1:2], in1=lam[:, 0:1], op=ALU.subtract), [lam], [lam])
    P.op("dve", lambda e: e.tensor_scalar(out=lam[:, 2:3], in0=lam[:, 2:3], scalar1=-lambda_init, scalar2=None, op0=ALU.add),
         [lam], [lam])
    ngc = self.small("da_ngc", [128, 1])
    self.load_cols(ngc, ngc[:], I["da_norm_g"], I["da_norm_g"].ap()[0:1, :], 1)
    P.op("dve", lambda e: e.tensor_scalar(out=ngc[:], in0=ngc[:], scalar1=1.0 - lambda_init, scalar2=None, op0=ALU.mult),
         [ngc], [ngc])
    nkt = self.nt
    nkT = (self.ntok + 1023) // 1024
    for h in range(8):
        kres = T4[0:nkT]
        vres = T4[9:9 + nkT]
        for q in range(nkT):
            c0 = q * 1024
            wd = min(1024, self.ntok - c0)
            P.dma("sp", bf(kres[q])[:, 0:wd], KT.ap()[h * 128:(h + 1) * 128, c0:c0 + wd], reads=[KT], writes=[kres[q]])
            nk = wd // 128
            P.dma("pool", bf(vres[q])[:, 0:wd].rearrange("p (kt e) -> p kt e", e=128),
                  V.ap()[c0:c0 + wd, h * 128:(h + 1) * 128].rearrange("(kt p) e -> p kt e", p=128),
                  reads=[V], writes=[vres[q]])
        blocks = [(0, 256, [0, 1])] + [(CTX + b * 512, 512, list(range(nkt))) for b in range(self.seq // 512)]
        for bi, (q0, qn, kts) in enumerate(blocks):
            qt = T4[18 + bi % 2]
            P.dma("sp", bf(qt)[:, 0:qn], QT.ap()[h * 128:(h + 1) * 128, q0:q0 + qn], reads=[QT], writes=[qt])
            for ki_, kt in enumerate(kts):
                kq, ko = kt // 8, (kt % 8) * 128
                for m in range(2):
                    psc = PS[4 + (2 * ki_ + m) % 4]
                    et = T4[20 + (2 * ki_ + m) % 4]
                    rows = slice(64 * m, 64 * m + 64)
                    P.op("pe", lambda e: e.matmul(psc[:, 0:qn], lhsT=bf(kres[kq])[rows, ko:ko + 128], rhs=bf(qt)[rows, 0:qn],
                                                  start=True, stop=True), [kres[kq], qt], [psc])
                    P.op("act", lambda e: e.activation(out=bf(et)[:, 0:qn], in_=psc[:, 0:qn], func=AF.Exp, scale=0.125),
                         [psc], [et])
                    first, last = (ki_ == 0), (ki_ == len(kts) - 1)
                    P.op("pe", lambda e: e.matmul(PS[2 * m][:, 0:qn], lhsT=bf(vres[kq])[:, ko:ko + 128], rhs=bf(et)[:, 0:qn],
                                                  start=first, stop=last), [vres[kq], et], [PS[2 * m]], inc=False)
                    P.op("pe", lambda e: e.matmul(PS[2 * m + 1][:, 0:qn], lhsT=self.ones_bf[:], rhs=bf(et)[:, 0:qn],
                                                  start=first, stop=last), [self.ones_bf, et], [PS[2 * m + 1]])
            o0, o1, df = T4[24], T4[25], T4[26]
            for m, ob in ((0, o0), (1, o1)):
                P.op("dve", lambda e: e.reciprocal(out=ob[:, 0:qn], in_=PS[2 * m + 1][:, 0:qn]), [PS[2 * m + 1]], [ob])
                P.op("dve", lambda e: e.tensor_tensor(out=ob[:, 512:512 + qn], in0=PS[2 * m][:, 0:qn], in1=ob[:, 0:qn],
                                                      op=ALU.mult), [PS[2 * m], ob], [ob])
            P.op("dve", lambda e: e.scalar_tensor_tensor(out=df[:, 0:qn], in0=o1[:, 512:512 + qn], scalar=lam[:, 2:3],
                                                         in1=o0[:, 512:512 + qn], op0=ALU.mult, op1=ALU.add),
                 [o0, o1, lam], [df])
            P.op("act", lambda e: e.activation(out=df[:, 512:512 + qn], in_=df[:, 0:qn], func=AF.Square), [df], [df])
            P.op("pe", lambda e: e.matmul(PS[4][:, 0:qn], lhsT=self.ones[:], rhs=df[:, 512:512 + qn], start=True, stop=True),
                 [self.ones, df], [PS[4]])
            rs = T4[27]
            P.op("dve", lambda e: e.tensor_scalar(out=rs[:, 0:qn], in0=PS[4][:, 0:qn], scalar1=1.0 / 128, scalar2=1e-5,
                                                  op0=ALU.mult, op1=ALU.add), [PS[4]], [rs])
            P.op("act", lambda e: e.activation(out=rs[:, 0:qn], in_=rs[:, 0:qn], func=AF.Sqrt), [rs], [rs])
            P.op("dve", lambda e: e.reciprocal(out=rs[:, 0:qn], in_=rs[:, 0:qn]), [rs], [rs])
            P.op("dve", lambda e: e.scalar_tensor_tensor(out=bf(rs)[:, 1024:1024 + qn], in0=df[:, 0:qn], scalar=ngc[:, 0:1],
                                                         in1=rs[:, 0:qn], op0=ALU.mult, op1=ALU.mult), [df, ngc, rs], [rs])
            P.dma("pool", OT.ap()[h * 128:(h + 1) * 128, q0:q0 + qn], bf(rs)[:, 1024:1024 + qn], reads=[rs], writes=[OT])


def da_post(self, i, Xin, Xout):
    P = self.P
    I = self.I
    T4 = self.T4
    wres = T4[0:8]
    self.load_wres(I["da_w_out"], I["da_w_out"].ap()[0], wres)
    OT = self.fscratch("daoT", 1024, BF16)
    t_start = 2 if i == DEPTH - 1 else 0
    for t in range(t_start, self.nt):
        par = t % 2
        zT = T4[20 + par]
        P.dma("sp", bf(zT)[:, 0:1024].rearrange("p (h c) -> p h c", c=128),
              OT.ap().rearrange("(h p) c -> p h c", p=128)[:, :, t * 128:(t + 1) * 128], reads=[OT], writes=[zT])
        self.proj_residual([(zT, bf(zT)[:, kc * 128:(kc + 1) * 128]) for kc in range(8)], wres, Xin, Xout, t, par)


def diffattn(self, i, Xin, Xout):
    self.da_pre(i, Xin)
    self.da_attn(i)
    self.da_post(i, Xin, Xout)


K.da_rope_tables = da_rope_tables
K.da_pre = da_pre
K.da_attn = da_attn
K.da_post = da_post
K.diffattn = diffattn


N_CORES = 8
_CACHE = {}


def kernel(**inputs):
    if "k" not in _CACHE:
        _CACHE["k"] = build({})
    k = _CACHE["k"]
    maps = []
    for c in range(N_CORES):
        b = c % 2
        m = {}
        for name, _ in INPUT_SHAPES:
            a = np.asarray(inputs[name])
            if name in ("x", "c", "ctx"):
                a = a[b]
            m[name] = np.ascontiguousarray(a, dtype=np.float32)
        maps.append(m)
    res = run_bass_kernel_spmd(k.nc, maps, core_ids=list(range(N_CORES)))
    out = np.stack([np.asarray(res.results[b]["out"]) for b in range(2)], axis=0)
    return out.astype(np.float32)
```
